# Optimizing a Trainium2 kernel written in Bass

```python
import jax, jax.numpy as jnp
from jax import lax
import numpy as np

D_MODEL = 1024
BATCH = 8
SEQ = 4096
DEPTH = 1

GRID_W = 64
CTX_LEN = 256
EPS = 1e-6
ROPE_THETA = 10000.0
V_HEAD_DIM = 128
QK_NOPE_DIM = 128
QK_ROPE_DIM = 64
Q_LORA_RANK = 256
KV_LORA_RANK = 256
MLA_HEADS = D_MODEL // (2 * V_HEAD_DIM)
MLA_WIDTH = MLA_HEADS * V_HEAD_DIM
QK_HEAD_DIM = QK_NOPE_DIM + QK_ROPE_DIM
Q_BLOCK = 128
HGRN_KEY_DIM = 128
HGRN_VAL_DIM = 128
HGRN_HEADS = D_MODEL // (2 * HGRN_VAL_DIM)
HGRN_WIDTH = HGRN_HEADS * HGRN_KEY_DIM
CHUNK = 64
MIX_WIDTH = MLA_WIDTH + HGRN_HEADS * HGRN_VAL_DIM
IN_SIZES = (Q_LORA_RANK, KV_LORA_RANK, QK_ROPE_DIM,
            HGRN_WIDTH, HGRN_WIDTH, HGRN_WIDTH, HGRN_WIDTH, HGRN_WIDTH)
IN_COLS = sum(IN_SIZES)
D_FF = -(-8 * D_MODEL // (3 * 256)) * 256

kernel_name = "hymba_mla_hgrn2_dit_block"


def rmsnorm(x, g):
    x32 = x.astype(jnp.float32)
    y = x32 * lax.rsqrt(jnp.mean(x32 * x32, axis=-1, keepdims=True) + EPS)
    return (y * g.astype(jnp.float32)).astype(x.dtype)


def modulate(h, shift, scale):
    return h * (1.0 + scale) + shift


def axial_rope_2d(n):
    rows = n // GRID_W
    row = jnp.broadcast_to(jnp.arange(rows)[:, None], (rows, GRID_W)).reshape(n)
    col = jnp.broadcast_to(jnp.arange(GRID_W)[None, :], (rows, GRID_W)).reshape(n)
    axis_dim = QK_ROPE_DIM // 2
    inv = 1.0 / (ROPE_THETA ** (jnp.arange(0, axis_dim, 2, dtype=jnp.float32) / axis_dim))
    ang = jnp.concatenate([row.astype(jnp.float32)[:, None] * inv,
                           col.astype(jnp.float32)[:, None] * inv], axis=-1)
    return jnp.cos(ang), jnp.sin(ang)


def apply_rope(x, cos, sin):
    if x.ndim == 4:
        cos, sin = cos[:, None, :], sin[:, None, :]
    nf = QK_ROPE_DIM // 4
    x32 = x.astype(jnp.float32)
    outs = []
    for a in range(2):
        xa = x32[..., a * 2 * nf:(a + 1) * 2 * nf]
        c, s = cos[..., a * nf:(a + 1) * nf], sin[..., a * nf:(a + 1) * nf]
        x1, x2 = xa[..., :nf], xa[..., nf:]
        outs.append(jnp.concatenate([x1 * c - x2 * s, x2 * c + x1 * s], axis=-1))
    return jnp.concatenate(outs, axis=-1).astype(x.dtype)


def attend_blocks(q, k, v):
    B, N, H, dq = q.shape
    nb = N // Q_BLOCK
    scale = 1.0 / float(np.sqrt(dq))
    qb = jnp.moveaxis(q.reshape(B, nb, Q_BLOCK, H, dq), 1, 0)

    def one(qblk):
        s = jnp.einsum('bqhd,bkhd->bhqk', qblk, k).astype(jnp.float32) * scale
        p = jax.nn.softmax(s, axis=-1).astype(v.dtype)
        return jnp.einsum('bhqk,bkhd->bqhd', p, v)

    o = lax.map(one, qb)
    return jnp.moveaxis(o, 0, 1).reshape(B, N, H, v.shape[-1])


def gla_chunked(q, k, v, log_f, s0):
    B, T, H, dk = q.shape
    dv = v.shape[-1]
    nc = T // CHUNK

    def to_chunks(t):
        return jnp.moveaxis(t.reshape(B, nc, CHUNK, H, t.shape[-1]), 1, 0)

    mask = jnp.tril(jnp.ones((CHUNK, CHUNK), dtype=bool))

    def step(S, inp):
        qc, kc, vc, gc = inp
        b = jnp.cumsum(gc, axis=1)
        b_ref = b[:, CHUNK // 2 - 1][:, None]
        b_last = b[:, -1]
        o_inter = jnp.einsum('bchk,bhkv->bchv', qc * jnp.exp(b), S)
        A = jnp.einsum('bthk,bshk->bhts', qc * jnp.exp(b - b_ref), kc * jnp.exp(b_ref - b))
        A = jnp.where(mask, A, 0.0)
        o_intra = jnp.einsum('bhts,bshv->bthv', A, vc)
        S_new = jnp.exp(b_last)[..., None] * S + jnp.einsum(
            'bshk,bshv->bhkv', kc * jnp.exp(b_last[:, None] - b), vc)
        return S_new, o_inter + o_intra

    s_fin, o = lax.scan(step, s0, (to_chunks(q), to_chunks(k), to_chunks(v), to_chunks(log_f)))
    return s_fin, jnp.moveaxis(o, 0, 1).reshape(B, T, H, dv)


def scan_direction(q, k, v, log_f, s0, reverse):
    if reverse:
        q, k, v, log_f = (jnp.flip(t, axis=1) for t in (q, k, v, log_f))
    s_fin, o = gla_chunked(q, k, v, log_f, s0)
    if reverse:
        o = jnp.flip(o, axis=1)
    return s_fin, o


def token_mixer(h_lat, h_ctx, cos, sin, layer, w_in, g_qn, w_uq, g_kvn, w_ukv,
                lb_fwd, lb_bwd, g_on, w_out, with_ctx_out):
    B, N, _ = h_lat.shape
    L = h_ctx.shape[1]
    offs = [int(v) for v in np.cumsum(IN_SIZES)[:-1]]
    cq_l, ckv_l, kpe_l, hq_l, hi_l, hg_l, ff_l, fb_l = jnp.split(h_lat @ w_in, offs, axis=-1)
    cq_c, ckv_c, kpe_c, hq_c, hi_c, hg_c, ff_c, fb_c = jnp.split(h_ctx @ w_in, offs, axis=-1)

    def mla_q(cq, n):
        q = (rmsnorm(cq, g_qn) @ w_uq).reshape(B, n, MLA_HEADS, QK_HEAD_DIM)
        return q[..., :QK_NOPE_DIM], q[..., QK_NOPE_DIM:]

    def mla_kv(ckv, n):
        kv = (rmsnorm(ckv, g_kvn) @ w_ukv).reshape(B, n, MLA_HEADS, QK_NOPE_DIM + V_HEAD_DIM)
        return kv[..., :QK_NOPE_DIM], kv[..., QK_NOPE_DIM:]

    def full_k(k_nope, k_pe, n):
        k_pe = jnp.broadcast_to(k_pe[:, :, None, :], (B, n, MLA_HEADS, QK_ROPE_DIM))
        return jnp.concatenate([k_nope, k_pe], axis=-1)

    qn_l, qr_l = mla_q(cq_l, N)
    q_lat = jnp.concatenate([qn_l, apply_rope(qr_l, cos, sin)], axis=-1)
    kn_l, v_l = mla_kv(ckv_l, N)
    k_lat = full_k(kn_l, apply_rope(kpe_l, cos, sin), N)
    kn_c, v_c = mla_kv(ckv_c, L)
    k_ctx = full_k(kn_c, kpe_c, L)
    K = jnp.concatenate([k_ctx, k_lat], axis=1)
    V = jnp.concatenate([v_c, v_l], axis=1)
    o_mla_lat = attend_blocks(q_lat, K, V).reshape(B, N, MLA_WIDTH)

    def heads(t, n):
        return t.astype(jnp.float32).reshape(B, n, HGRN_HEADS, HGRN_KEY_DIM)

    def gates(fraw, lb_tab, n):
        lb = jnp.cumsum(jax.nn.softmax(lb_tab.astype(jnp.float32), axis=0), axis=0)[layer]
        f = lb + (1.0 - lb) * jax.nn.sigmoid(fraw.astype(jnp.float32))
        return heads(1.0 - f, n), heads(jnp.log(f), n)

    s0 = jnp.zeros((B, HGRN_HEADS, HGRN_KEY_DIM, HGRN_VAL_DIM), jnp.float32)
    q_hl, v_hl = heads(hq_l, N), heads(hi_l, N)
    q_hc, v_hc = heads(hq_c, L), heads(hi_c, L)
    o_h_lat = 0.0
    o_h_ctx = 0.0
    for fr_l, fr_c, lb_tab, rev in ((ff_l, ff_c, lb_fwd, False), (fb_l, fb_c, lb_bwd, True)):
        k_c, lf_c = gates(fr_c, lb_tab, L)
        s_ctx, o_c = scan_direction(q_hc, k_c, v_hc, lf_c, s0, rev)
        k_l, lf_l = gates(fr_l, lb_tab, N)
        _, o_l = scan_direction(q_hl, k_l, v_hl, lf_l, s_ctx, rev)
        o_h_lat = o_h_lat + o_l
        o_h_ctx = o_h_ctx + o_c

    def hgrn_out(o, hg, n):
        o = rmsnorm(o, g_on).astype(hg.dtype).reshape(B, n, HGRN_HEADS * HGRN_VAL_DIM)
        return o * jax.nn.silu(hg)

    lat_out = jnp.concatenate([o_mla_lat, hgrn_out(o_h_lat, hg_l, N)], axis=-1) @ w_out
    if not with_ctx_out:
        return lat_out, None
    qn_c, qr_c = mla_q(cq_c, L)
    q_ctx = jnp.concatenate([qn_c, qr_c], axis=-1)
    o_mla_ctx = attend_blocks(q_ctx, k_ctx, v_c).reshape(B, L, MLA_WIDTH)
    ctx_out = jnp.concatenate([o_mla_ctx, hgrn_out(o_h_ctx, hg_c, L)], axis=-1) @ w_out
    return lat_out, ctx_out


def swiglu(h, w_gate, w_up, w_down):
    return (jax.nn.silu(h @ w_gate) * (h @ w_up)) @ w_down


def setup_inputs(seed: int = 0) -> dict:
    key = jax.random.key(seed)
    ks = jax.random.split(key, 24)

    def nrm(k, shape, scale):
        return jax.random.normal(k, shape, jnp.float32) * scale

    def gain(k, shape):
        return 1.0 + nrm(k, shape, 0.05)

    return {
        "x": nrm(ks[0], (BATCH, SEQ, D_MODEL), 1.0),
        "c": nrm(ks[1], (BATCH, D_MODEL), 1.0),
        "ctx": nrm(ks[2], (BATCH, CTX_LEN, D_MODEL), 1.0),
        "c_ctx": nrm(ks[3], (D_MODEL,), 1.0),
        "w_mod": nrm(ks[4], (DEPTH, D_MODEL, 6 * D_MODEL), 0.5 * D_MODEL ** -0.5),
        "b_mod": nrm(ks[5], (DEPTH, 6 * D_MODEL), 0.02),
        "g_norm_mix": gain(ks[6], (DEPTH, D_MODEL)),
        "g_norm_ffn": gain(ks[7], (DEPTH, D_MODEL)),
        "w_in": nrm(ks[8], (DEPTH, D_MODEL, IN_COLS), D_MODEL ** -0.5),
        "g_q_norm": gain(ks[9], (DEPTH, Q_LORA_RANK)),
        "w_uq": nrm(ks[10], (DEPTH, Q_LORA_RANK, MLA_HEADS * QK_HEAD_DIM), Q_LORA_RANK ** -0.5),
        "g_kv_norm": gain(ks[11], (DEPTH, KV_LORA_RANK)),
        "w_ukv": nrm(ks[12], (DEPTH, KV_LORA_RANK, MLA_HEADS * (QK_NOPE_DIM + V_HEAD_DIM)), KV_LORA_RANK ** -0.5),
        "lb_fwd": nrm(ks[13], (DEPTH + 1, HGRN_WIDTH), 0.1),
        "lb_bwd": nrm(ks[14], (DEPTH + 1, HGRN_WIDTH), 0.1),
        "g_hgrn_norm": gain(ks[15], (DEPTH, HGRN_VAL_DIM)),
        "w_out": nrm(ks[16], (DEPTH, MIX_WIDTH, D_MODEL), MIX_WIDTH ** -0.5),
        "w_gate": nrm(ks[17], (DEPTH, D_MODEL, D_FF), D_MODEL ** -0.5),
        "w_up": nrm(ks[18], (DEPTH, D_MODEL, D_FF), D_MODEL ** -0.5),
        "w_down": nrm(ks[19], (DEPTH, D_FF, D_MODEL), D_FF ** -0.5),
        "g_final": gain(ks[20], (D_MODEL,)),
    }


def reference(x, c, ctx, c_ctx, w_mod, b_mod, g_norm_mix, g_norm_ffn, w_in, g_q_norm, w_uq,
              g_kv_norm, w_ukv, lb_fwd, lb_bwd, g_hgrn_norm, w_out, w_gate, w_up, w_down, g_final):
    N = x.shape[1]
    cos, sin = axial_rope_2d(N)
    for layer in range(DEPTH):
        mod = jax.nn.silu(c) @ w_mod[layer] + b_mod[layer]
        sh1, sc1, gt1, sh2, sc2, gt2 = jnp.split(mod[:, None, :], 6, axis=-1)
        mod_c = jax.nn.silu(c_ctx) @ w_mod[layer] + b_mod[layer]
        csh1, csc1, cgt1, csh2, csc2, cgt2 = jnp.split(mod_c, 6, axis=-1)
        update_ctx = layer < DEPTH - 1

        h = modulate(rmsnorm(x, g_norm_mix[layer]), sh1, sc1)
        hc = modulate(rmsnorm(ctx, g_norm_mix[layer]), csh1, csc1)
        mix_lat, mix_ctx = token_mixer(h, hc, cos, sin, layer, w_in[layer], g_q_norm[layer],
                                       w_uq[layer], g_kv_norm[layer], w_ukv[layer],
                                       lb_fwd, lb_bwd, g_hgrn_norm[layer], w_out[layer], update_ctx)
        x = x + gt1 * mix_lat
        h2 = modulate(rmsnorm(x, g_norm_ffn[layer]), sh2, sc2)
        x = x + gt2 * swiglu(h2, w_gate[layer], w_up[layer], w_down[layer])
        if update_ctx:
            ctx = ctx + cgt1 * mix_ctx
            hc2 = modulate(rmsnorm(ctx, g_norm_ffn[layer]), csh2, csc2)
            ctx = ctx + cgt2 * swiglu(hc2, w_gate[layer], w_up[layer], w_down[layer])
    return rmsnorm(x, g_final)
```

```python
import math
from contextlib import ExitStack

import numpy as np
import concourse.bass as bass
import concourse.mybir as mybir
from concourse.bass_utils import run_bass_kernel_spmd

F32 = mybir.dt.float32
BF16 = mybir.dt.bfloat16
AF = mybir.ActivationFunctionType
ALU = mybir.AluOpType
AX = mybir.AxisListType

NB, N, L, D = 8, 4096, 256, 1024
T = N + L
NT = T // 128
DFF = 2816
NCF = DFF // 128
EPS = 1e-6
INC = 3136
SCALE = 1.0 / math.sqrt(192.0)


class Sched:
    ENG = ('pe', 'act', 'dve', 'pool', 'sp')

    def __init__(self, nc, n_dma_sems=24):
        self.nc = nc
        self.ops = {e: [] for e in self.ENG}
        self.sems = {}
        for e in ('pe', 'act', 'dve', 'pool'):
            self.sems[('e', e)] = nc.alloc_semaphore(name=f"s_{e}")
        for i in range(n_dma_sems):
            self.sems[('d', i)] = nc.alloc_semaphore(name=f"s_dma{i}")
        self.nw = 6
        for i in range(self.nw):
            self.sems[('w', i)] = nc.alloc_semaphore(name=f"s_swdma{i}")
        self.wnext = 0
        self.cnt = {k: 0 for k in self.sems}
        self.nd = n_dma_sems
        self.dnext = 0
        self.waited = {e: {} for e in self.ENG}
        self.res = {}
        self.enabled = True

    def _deps(self, reads, writes):
        t = []
        for r in reads:
            st = self.res.get(r)
            if st and st['w']:
                t.append(st['w'])
        for w in writes:
            st = self.res.get(w)
            if st:
                if st['w']:
                    t.append(st['w'])
                t.extend(st['r'].values())
        return t

    def _need(self, eng, tickets):
        best = {}
        for key, val in tickets:
            if key == ('e', 'pe') and eng == 'pe':
                continue
            if self.waited[eng].get(key, 0) >= val:
                continue
            if best.get(key, 0) < val:
                best[key] = val
        for key, val in best.items():
            self.waited[eng][key] = val
        return list(best.items())

    def _commit(self, ticket, reads, writes):
        for r in reads:
            st = self.res.setdefault(r, {'w': None, 'r': {}})
            st['r'][ticket[0]] = ticket
        for w in writes:
            self.res[w] = {'w': ticket, 'r': {}}

    def op(self, eng, fn, reads=(), writes=()):
        if not self.enabled:
            return None
        waits = self._need(eng, self._deps(reads, writes))
        key = ('e', eng)
        self.cnt[key] += 1
        ticket = (key, self.cnt[key])
        self.ops[eng].append((waits, fn, key, 1))
        self._commit(ticket, reads, writes)
        return ticket

    def dma(self, q, out, in_, reads=(), writes=()):
        if not self.enabled:
            return None
        if q == 'pool':
            key = ('w', self.wnext)
            self.wnext = (self.wnext + 1) % self.nw
        else:
            key = ('d', self.dnext)
            self.dnext = (self.dnext + 1) % self.nd
        tickets = self._deps(reads, writes)
        if self.cnt[key] > 0:
            tickets.append((key, self.cnt[key]))
        waits = self._need(q, tickets)
        self.cnt[key] += 16
        ticket = (key, self.cnt[key])
        self.ops[q].append((waits, lambda e: e.dma_start(out=out, in_=in_), key, 16))
        self._commit(ticket, reads, writes)
        return ticket

    def barrier(self):
        allt = [(k, v) for k, v in self.cnt.items() if v > 0]
        for e in self.ENG:
            waits = self._need(e, allt)
            if waits:
                self.ops[e].append((waits, None, None, 0))

    def emit(self):
        self.barrier()
        with self.nc.Block() as block:
            def mk(engname):
                def body(e):
                    for waits, fn, semkey, inc in self.ops[engname]:
                        for key, val in waits:
                            e.wait_ge(self.sems[key], val)
                        if fn is not None:
                            fn(e).then_inc(self.sems[semkey], inc)
                return body
            block.tensor(mk('pe'))
            block.scalar(mk('act'))
            block.vector(mk('dve'))
            block.gpsimd(mk('pool'))
            block.sync(mk('sp'))
        self.ops = {e: [] for e in self.ENG}

    def mm(self, out, lhsT, rhs, start, stop, r, w):
        return self.op('pe', lambda e: e.matmul(out, lhsT=lhsT, rhs=rhs, start=start, stop=stop), r, w)

    def tr(self, out, in_, ident, r, w):
        return self.op('pe', lambda e: e.transpose(out, in_, ident), r, w)

    def act(self, out, in_, func, r, w, bias=None, scale=None, accum=None):
        kw = {}
        if bias is not None:
            kw['bias'] = bias
        if scale is not None:
            kw['scale'] = scale
        if accum is not None:
            kw['accum_out'] = accum
        return self.op('act', lambda e: e.activation(out=out, in_=in_, func=func, **kw), r, w)

    def tt(self, eng, out, in0, in1, op, r, w):
        return self.op(eng, lambda e: e.tensor_tensor(out=out, in0=in0, in1=in1, op=op), r, w)

    def ts(self, eng, out, in0, s1, s2, op0, op1, r, w):
        if s2 is None:
            return self.op(eng, lambda e: e.tensor_scalar(out=out, in0=in0, scalar1=s1, scalar2=None, op0=op0), r, w)
        return self.op(eng, lambda e: e.tensor_scalar(out=out, in0=in0, scalar1=s1, scalar2=s2, op0=op0, op1=op1), r, w)

    def stt(self, eng, out, in0, scalar, in1, op0, op1, r, w):
        return self.op(eng, lambda e: e.scalar_tensor_tensor(out=out, in0=in0, scalar=scalar, in1=in1, op0=op0, op1=op1), r, w)

    def copy(self, eng, out, in_, r, w):
        if eng == 'act':
            return self.act(out, in_, AF.Copy, r, w)
        return self.op(eng, lambda e: e.tensor_copy(out=out, in_=in_), r, w)

    def memset(self, eng, ap, val, w):
        return self.op(eng, lambda e: e.memset(ap, val), (), w)


def run_pipeline(n, stages):
    ctxs = [dict() for _ in range(n)]
    K = len(stages)
    for t in range(n + K - 1):
        for k in range(K - 1, -1, -1):
            i = t - k
            if 0 <= i < n:
                stages[k](ctxs[i], i)


class Rot:
    def __init__(self, alloc, name, shape, dt, n):
        self.t = [alloc(f"{name}{i}", shape, dt) for i in range(n)]
        self.k = [f"{name}{i}" for i in range(n)]
        self.i = 0

    def next(self):
        j = self.i % len(self.t)
        self.i += 1
        return self.t[j], self.k[j]


def _rstd(S, ss, out, dim, tag):
    S.act(out, ss, AF.Sqrt, [tag + 'ss'], [tag + 'rs'], bias=EPS, scale=1.0 / dim)
    S.op('dve', lambda e: e.reciprocal(out=out, in_=out), [tag + 'rs'], [tag + 'rs'])


def build_nc(stop_after=None, debug=False, a2_groups=None, a2_parts='abcde', STQ='sp'):
    nc = bass.Bass("TRN2", target_bir_lowering=False)
    S = Sched(nc)

    def din(name, shape):
        return nc.dram_tensor(name, shape, F32, kind="ExternalInput").ap()

    x = din("x", [N, D]); c = din("c", [D]); ctx = din("ctx", [L, D]); c_ctx = din("c_ctx", [D])
    w_mod = din("w_mod", [D, 6 * D]); b_mod = din("b_mod", [6 * D])
    g_mix = din("g_norm_mix", [D]); g_ffn = din("g_norm_ffn", [D])
    w_in = din("w_in", [D, INC]); g_qn = din("g_q_norm", [256]); w_uq = din("w_uq", [256, 768])
    g_kvn = din("g_kv_norm", [256]); w_ukv = din("w_ukv", [256, 1024])
    lb_f = din("lb_fwd", [2, 512]); lb_b = din("lb_bwd", [2, 512]); g_on = din("g_hgrn_norm", [128])
    w_out = din("w_out", [D, D]); w_gate = din("w_gate", [D, DFF]); w_up = din("w_up", [D, DFF])
    w_down = din("w_down", [DFF, D]); g_fin = din("g_final", [D])
    out = nc.dram_tensor("out", [N, D], F32, kind="ExternalOutput").ap()

    dbg = set(debug) if debug else set()

    def scr(name, shape, dt):
        return nc.dram_tensor(name, shape, dt, kind="ExternalOutput" if name in dbg else "Internal").ap()

    MOD = scr("s_mod", [8, 128, D], F32)
    HT = scr("s_ht", [128, 8, T], BF16)
    QN = scr("s_qn", [4, 128, N], BF16); QR = scr("s_qr", [4, 64, N], BF16)
    KN = scr("s_kn", [4, 128, T], BF16); KR = scr("s_kr", [64, T], BF16)
    VV = scr("s_v", [T, 512], BF16)
    HQ = scr("s_hq", [4, 128, T], F32); HV = scr("s_hv", [T, 512], BF16); HG = scr("s_hg", [N, 512], F32)
    GG = scr("s_g", [2, T, 512], F32); KG = scr("s_kg", [2, T, 512], F32)
    OO = scr("s_o", [2, N, 512], F32)
    MIXA = scr("s_mixa", [4, 128, N], BF16)
    X1 = scr("s_x1", [N, D], F32); H2T = scr("s_h2t", [128, 8, N], BF16)

    outer = ExitStack()
    with outer:
        def CT(name, shape, dt):
            return outer.enter_context(nc.sbuf_tensor(name, shape, dt))
        identb = CT("identb", [128, 128], BF16); identf = CT("identf", [128, 128], F32)
        onesb = CT("onesb", [128, 128], BF16); onesf = CT("onesf", [128, 128], F32)
        Uf = CT("Uf", [128, 128], F32); Ub = CT("Ub", [128, 128], F32)
        Rf = CT("Rf", [128, 128], F32); Rb = CT("Rb", [128, 128], F32)
        Mf = CT("Mf", [128, 128], F32); Mb = CT("Mb", [128, 128], F32)
        Ind = CT("Ind", [128, 2], F32)

        def sel(tile, pattern, cm, cmp_op, key):
            S.op('pool', lambda e: e.affine_select(out=tile[:], in_=tile[:], pattern=pattern, compare_op=cmp_op,
                                                   fill=0.0, base=0, channel_multiplier=cm), [key], [key])
        for tl, key in ((identb, 'identb'), (identf, 'identf')):
            S.memset('pool', tl[:], 1.0, [key])
            sel(tl, [[-1, 128]], 1, ALU.is_equal, key)
        S.memset('pool', onesb[:], 1.0, ['onesb'])
        S.memset('pool', onesf[:], 1.0, ['onesf'])
        S.memset('pool', Uf[:], 1.0, ['Uf']); sel(Uf, [[-1, 128]], 1, ALU.is_gt, 'Uf')
        S.memset('pool', Uf[64:128, 0:64], 0.0, ['Uf'])
        S.memset('pool', Mb[:], 1.0, ['Mb']); sel(Mb, [[-1, 128]], 1, ALU.is_ge, 'Mb')
        S.memset('pool', Mb[64:128, 0:64], 0.0, ['Mb'])
        S.memset('pool', Ub[:], 1.0, ['Ub']); sel(Ub, [[1, 128]], -1, ALU.is_gt, 'Ub')
        S.memset('pool', Ub[0:64, 64:128], 0.0, ['Ub'])
        S.memset('pool', Mf[:], 1.0, ['Mf']); sel(Mf, [[1, 128]], -1, ALU.is_ge, 'Mf')
        S.memset('pool', Mf[0:64, 64:128], 0.0, ['Mf'])
        S.ts('pool', Rf[:], Uf[:], -1.0, None, ALU.mult, None, ['Uf'], ['Rf'])
        S.ts('pool', Rb[:], Ub[:], -1.0, None, ALU.mult, None, ['Ub'], ['Rb'])
        S.memset('pool', Ind[:], 0.0, ['Ind'])
        S.memset('pool', Ind[0:64, 0:1], 1.0, ['Ind'])
        S.memset('pool', Ind[64:128, 1:2], 1.0, ['Ind'])

        with ExitStack() as es:
            def TT(name, shape, dt):
                return es.enter_context(nc.sbuf_tensor(name, shape, dt))

            def PP(name, shape, dt):
                return es.enter_context(nc.psum_tensor(name, shape, dt))
            crow = TT("crow", [128, 2, D], F32)
            cb = TT("cb", [128, 2, 8, 128], F32)
            bmod = TT("bmod", [128, 6 * D], F32)
            wm = [TT(f"wm{i}", [128, 8, 512], F32) for i in range(2)]
            modl = TT("modl", [128, 6 * D], F32); modc = TT("modc", [128, 2 * D], F32)
            gm = TT("gm", [128, D], F32); gf = TT("gf", [128, D], F32)
            tmpA = [TT(f"tmpA{i}", [128, D], F32) for i in range(3)]
            pcb = PP("pcb", [128, 128], F32)
            pm = [PP(f"pm{i}", [128, 512], F32) for i in range(2)]
            S.dma('sp', crow[:, 0, :], c.partition_broadcast(128), (), ['crow'])
            S.dma('sp', crow[:, 1, :], c_ctx.partition_broadcast(128), (), ['crow'])
            S.dma('sp', bmod[:], b_mod.partition_broadcast(128), (), ['bmod'])
            S.dma('sp', gm[:], g_mix.partition_broadcast(128), (), ['gm'])
            S.dma('sp', gf[:], g_ffn.partition_broadcast(128), (), ['gf'])
            S.act(crow[:], crow[:], AF.Silu, ['crow'], ['crow'])
            for w_ in range(2):
                for k in range(8):
                    S.mm(pcb[:], crow[:, w_, k * 128:(k + 1) * 128], identf[:], True, True, ['crow', 'identf'], ['pcb'])
                    S.copy('dve', cb[:, w_, k, :], pcb[:], ['pcb'], ['cb'])
            wmv = w_mod.rearrange("(k p) n -> p k n", p=128)
            for j in range(12):
                wt = wm[j % 2]; wk = f"wm{j % 2}"
                S.dma('sp', wt[:], wmv[:, :, j * 512:(j + 1) * 512], (), [wk])
                for k in range(8):
                    S.mm(pm[0][:], cb[:, 0, k, :], wt[:, k, :], k == 0, k == 7, ['cb', wk], ['pm0'])
                S.tt('dve', modl[:, j * 512:(j + 1) * 512], pm[0][:], bmod[:, j * 512:(j + 1) * 512], ALU.add,
                     ['pm0', 'bmod'], ['modl'])
                if j < 4:
                    for k in range(8):
                        S.mm(pm[1][:], cb[:, 1, k, :], wt[:, k, :], k == 0, k == 7, ['cb', wk], ['pm1'])
                    S.tt('dve', modc[:, j * 512:(j + 1) * 512], pm[1][:], bmod[:, j * 512:(j + 1) * 512], ALU.add,
                         ['pm1', 'bmod'], ['modc'])
            S.stt('dve', tmpA[0][:], modl[:, D:2 * D], 1.0, gm[:], ALU.add, ALU.mult, ['modl', 'gm'], ['tA0'])
            S.stt('dve', tmpA[1][:], modc[:, D:2 * D], 1.0, gm[:], ALU.add, ALU.mult, ['modc', 'gm'], ['tA1'])
            S.stt('dve', tmpA[2][:], modl[:, 4 * D:5 * D], 1.0, gf[:], ALU.add, ALU.mult, ['modl', 'gf'], ['tA2'])
            S.dma('sp', MOD[0], tmpA[0][:], ['tA0'], ())
            S.dma('sp', MOD[1], modl[:, 0:D], ['modl'], ())
            S.dma('sp', MOD[2], tmpA[1][:], ['tA1'], ())
            S.dma('sp', MOD[3], modc[:, 0:D], ['modc'], ())
            S.dma('sp', MOD[4], modl[:, 2 * D:3 * D], ['modl'], ())
            S.dma('sp', MOD[5], tmpA[2][:], ['tA2'], ())
            S.dma('sp', MOD[6], modl[:, 3 * D:4 * D], ['modl'], ())
            S.dma('sp', MOD[7], modl[:, 5 * D:6 * D], ['modl'], ())
            S.emit()
        if stop_after == 0:
            return nc

        with ExitStack() as es:
            def TT(name, shape, dt):
                return es.enter_context(nc.sbuf_tensor(name, shape, dt))

            def PP(name, shape, dt):
                return es.enter_context(nc.psum_tensor(name, shape, dt))
            mA = TT("mA", [128, 4, D], F32)
            junk = TT("junk", [128, D], BF16)
            xtR = Rot(TT, "xt", [128, D], F32, 3); ssR = Rot(TT, "ssa", [128, 1], F32, 3)
            t1R = Rot(TT, "t1_", [128, D], F32, 2); hbR = Rot(TT, "hb", [128, D], BF16, 3)
            hTR = Rot(TT, "hT", [128, 8, 128], BF16, 3); ptrR = Rot(PP, "ptr", [128, 8, 128], BF16, 2)
            for i in range(4):
                S.dma('sp', mA[:, i, :], MOD[i], (), ['mA'])

            def a1_s0(cx, ti):
                cx['xt'], cx['xk'] = xtR.next()
                src = ctx[ti * 128:(ti + 1) * 128, :] if ti < 2 else x[(ti - 2) * 128:(ti - 1) * 128, :]
                S.dma('sp', cx['xt'][:], src, (), [cx['xk']])

            def a1_s1(cx, ti):
                xt_, xk = cx['xt'], cx['xk']
                mi = 2 if ti < 2 else 0
                ss, sk = ssR.next(); t1, t1k = t1R.next(); hb, hbk = hbR.next()
                cx['hb'], cx['hbk'] = hb, hbk
                S.memset('dve', ss[:], 0.0, [sk])
                S.act(junk[:], xt_[:], AF.Square, [xk], ['junk', sk], accum=ss[:])
                S.act(ss[:], ss[:], AF.Sqrt, [sk], [sk], bias=EPS, scale=1.0 / D)
                S.op('dve', (lambda o_: (lambda e: e.reciprocal(out=o_[:], in_=o_[:])))(ss), [sk], [sk])
                S.stt('dve', t1[:], xt_[:], ss[:, 0:1], mA[:, mi, :], ALU.mult, ALU.mult, [xk, sk, 'mA'], [t1k])
                S.tt('pool', hb[:], t1[:], mA[:, mi + 1, :], ALU.add, [t1k, 'mA'], [hbk])

            def a1_s2(cx, ti):
                hb, hbk = cx['hb'], cx['hbk']
                pt, ptk = ptrR.next(); hT, hTk = hTR.next()
                for k in range(8):
                    S.tr(pt[:, k, :], hb[:, k * 128:(k + 1) * 128], identb[:], [hbk, 'identb'], [ptk])
                S.copy('dve' if ti % 2 else 'act', hT[:], pt[:], [ptk], [hTk])
                S.dma('sp', HT[:, :, ti * 128:(ti + 1) * 128], hT[:], [hTk], ())
            run_pipeline(NT, [a1_s0, a1_s1, a1_s2])
            S.emit()
        if stop_after == 1:
            return nc

        with ExitStack() as es:
            def TT(name, shape, dt):
                return es.enter_context(nc.sbuf_tensor(name, shape, dt))

            def PP(name, shape, dt):
                return es.enter_context(nc.psum_tensor(name, shape, dt))
            Win = TT("Win", [128, 8, INC], BF16)
            Wkpe = TT("Wkpe", [128, 8, 64], BF16); Wkrot = TT("Wkrot", [128, 8, 64], BF16)
            Wqn = TT("Wqn", [128, 2, 4, 128], BF16); Wqr = TT("Wqr", [128, 2, 4, 64], BF16)
            Wqrot = TT("Wqrot", [128, 2, 4, 64], BF16)
            Wkn = TT("Wkn", [128, 2, 4, 128], BF16); Wv = TT("Wv", [128, 2, 4, 128], BF16)
            Ct = TT("Ct", [64, N], F32); St = TT("St", [64, N], F32)
            lbt = TT("lbt", [128, 2, 512], F32); oml = TT("oml", [128, 2, 512], F32)
            with ExitStack() as es2:
                def T2(name, shape, dt):
                    return es2.enter_context(nc.sbuf_tensor(name, shape, dt))
                winv = w_in.rearrange("(k p) n -> p k n", p=128)
                for k in range(8):
                    S.dma('pool', Win[:, k, :], winv[:, k, :], (), ['Win'])
                    for f_ in range(2):
                        S.dma('pool', Wkpe[:, k, f_ * 32:(f_ + 1) * 32].rearrange("p (a i) -> p a i", a=2),
                              winv[:, k, 512:576].rearrange("p (a f i) -> p f a i", a=2, f=2)[:, f_, :, :], (), ['Wkpe'])
                S.ts('dve', Wkrot[:, :, 0:32], Wkpe[:, :, 32:64], -1.0, None, ALU.mult, None, ['Wkpe'], ['Wkrot'])
                S.copy('dve', Wkrot[:, :, 32:64], Wkpe[:, :, 0:32], ['Wkpe'], ['Wkrot'])
                stq = T2("stq", [128, 2, 768], F32); stkv = T2("stkv", [128, 2, 1024], F32)
                gq = T2("gq", [128, 2], F32); gkv = T2("gkv", [128, 2], F32)
                S.dma('sp', stq[:], w_uq.rearrange("(c p) n -> p c n", p=128), (), ['stq'])
                S.dma('sp', stkv[:], w_ukv.rearrange("(c p) n -> p c n", p=128), (), ['stkv'])
                for c_ in range(2):
                    S.dma('sp', gq[:, c_:c_ + 1], g_qn[c_ * 128:(c_ + 1) * 128].rearrange("(p o) -> p o", o=1), (), ['gq'])
                    S.dma('sp', gkv[:, c_:c_ + 1], g_kvn[c_ * 128:(c_ + 1) * 128].rearrange("(p o) -> p o", o=1), (), ['gkv'])
                for c_ in range(2):
                    sq_v = stq[:, c_, :].rearrange("p (h d) -> p h d", h=4)
                    S.ts('dve', Wqn[:, c_, :, :], sq_v[:, :, 0:128], gq[:, c_:c_ + 1], None, ALU.mult, None,
                         ['stq', 'gq'], ['Wqn'])
                    for f_ in range(2):
                        for a_ in range(2):
                            so = 128 + a_ * 32 + f_ * 16
                            do = f_ * 32 + a_ * 16
                            S.ts('dve', Wqr[:, c_, :, do:do + 16], sq_v[:, :, so:so + 16], gq[:, c_:c_ + 1], None,
                                 ALU.mult, None, ['stq', 'gq'], ['Wqr'])
                    S.ts('dve', Wqrot[:, c_, :, 0:32], Wqr[:, c_, :, 32:64], -1.0, None, ALU.mult, None, ['Wqr'], ['Wqrot'])
                    S.copy('dve', Wqrot[:, c_, :, 32:64], Wqr[:, c_, :, 0:32], ['Wqr'], ['Wqrot'])
                    skv_v = stkv[:, c_, :].rearrange("p (h t d) -> p h t d", h=4, t=2)
                    S.ts('dve', Wkn[:, c_, :, :], skv_v[:, :, 0, :], gkv[:, c_:c_ + 1], None, ALU.mult, None,
                         ['stkv', 'gkv'], ['Wkn'])
                    S.ts('dve', Wv[:, c_, :, :], skv_v[:, :, 1, :], gkv[:, c_:c_ + 1], None, ALU.mult, None,
                         ['stkv', 'gkv'], ['Wv'])
                lraw = T2("lraw", [128, 2, 2, 512], F32)
                for d_, lbx in enumerate((lb_f, lb_b)):
                    for r_ in range(2):
                        S.dma('sp', lraw[:, d_, r_, :], lbx[r_].partition_broadcast(128), (), ['lraw'])
                S.tt('dve', lbt[:], lraw[:, :, 0, :], lraw[:, :, 1, :], ALU.subtract, ['lraw'], ['lbt'])
                S.act(lbt[:], lbt[:], AF.Sigmoid, ['lbt'], ['lbt'])
                S.ts('dve', oml[:], lbt[:], -0.5, 0.5, ALU.mult, ALU.add, ['lbt'], ['oml'])
                S.tt('dve', lbt[:], lbt[:], oml[:], ALU.add, ['lbt', 'oml'], ['lbt'])
                pidx = T2("pidx", [64, 1], F32); i16 = T2("i16", [64, 1], F32); mrow = T2("mrow", [64, 1], F32)
                arow = T2("arow", [64, 1], F32); acol = T2("acol", [64, 1], F32)
                rowpos = T2("rowpos", [64, N], F32); colpos = T2("colpos", [64, N], F32); ang = T2("ang", [64, N], F32)
                S.op('pool', lambda e: e.iota(pidx[:], [[0, 1]], base=0, channel_multiplier=1,
                                              allow_small_or_imprecise_dtypes=True), (), ['pidx'])
                S.op('pool', lambda e: e.iota(rowpos[:], [[1, 64], [0, 64]], base=0, channel_multiplier=0,
                                              allow_small_or_imprecise_dtypes=True), (), ['rowpos'])
                S.op('pool', lambda e: e.iota(colpos[:], [[0, 64], [1, 64]], base=0, channel_multiplier=0,
                                              allow_small_or_imprecise_dtypes=True), (), ['colpos'])
                msk = T2("msk", [64, 3], F32)
                S.memset('pool', msk[:], 1.0, ['msk'])
                for j_ in range(3):
                    S.op('pool', (lambda jj: (lambda e: e.affine_select(
                        out=msk[:, jj:jj + 1], in_=msk[:, jj:jj + 1], pattern=[[0, 1]], compare_op=ALU.is_ge, fill=0.0,
                        base=-16 * (jj + 1), channel_multiplier=1)))(j_), ['msk'], ['msk'])
                S.tt('dve', mrow[:], msk[:, 0:1], msk[:, 1:2], ALU.add, ['msk'], ['mrow'])
                S.tt('dve', mrow[:], mrow[:], msk[:, 2:3], ALU.add, ['msk', 'mrow'], ['mrow'])
                S.stt('dve', i16[:], mrow[:], -16.0, pidx[:], ALU.mult, ALU.add, ['mrow', 'pidx'], ['i16'])
                S.tt('dve', mrow[:], msk[:, 1:2], msk[:, 0:1], ALU.subtract, ['msk'], ['mrow'])
                S.tt('dve', mrow[:], mrow[:], msk[:, 2:3], ALU.subtract, ['msk', 'mrow'], ['mrow'])
                S.ts('dve', mrow[:], mrow[:], 1.0, None, ALU.add, None, ['mrow'], ['mrow'])
                S.act(i16[:], i16[:], AF.Exp, ['i16'], ['i16'], scale=-math.log(10000.0) / 16.0)
                S.tt('dve', arow[:], i16[:], mrow[:], ALU.mult, ['i16', 'mrow'], ['arow'])
                S.tt('dve', acol[:], i16[:], arow[:], ALU.subtract, ['i16', 'arow'], ['acol'])
                S.ts('dve', ang[:], rowpos[:], arow[:, 0:1], None, ALU.mult, None, ['rowpos', 'arow'], ['ang'])
                S.stt('dve', ang[:], colpos[:], acol[:, 0:1], ang[:], ALU.mult, ALU.add, ['colpos', 'acol', 'ang'], ['ang'])
                sc_ = 1.0 - 1e-6
                ki = T2("ki", [64, N], mybir.dt.int32)
                for tab, shift in ((St, 0.0), (Ct, 0.5 * math.pi)):
                    S.ts('dve', rowpos[:], ang[:], shift, 1.0 / (2 * math.pi), ALU.add, ALU.mult, ['ang'], ['rowpos'])
                    S.copy('dve', ki[:], rowpos[:], ['rowpos'], ['ki'])
                    S.copy('dve', colpos[:], ki[:], ['ki'], ['colpos'])
                    S.ts('dve', rowpos[:], ang[:], shift, None, ALU.add, None, ['ang'], ['rowpos'])
                    S.stt('dve', rowpos[:], colpos[:], -2 * math.pi, rowpos[:], ALU.mult, ALU.add, ['colpos', 'rowpos'], ['rowpos'])
                    S.act(tab[:], rowpos[:], AF.Sin, ['rowpos'], ['St' if shift == 0.0 else 'Ct'], scale=sc_)
                if stop_after == 15:
                    for nm, tl, shp, dt_ in (("d_Ct", Ct, [64, N], F32), ("d_St", St, [64, N], F32),
                                             ("d_Wkpe", Wkpe, [128, 8, 64], BF16), ("d_Wkrot", Wkrot, [128, 8, 64], BF16),
                                             ("d_Wqn", Wqn, [128, 2, 4, 128], BF16), ("d_Wqr", Wqr, [128, 2, 4, 64], BF16),
                                             ("d_Wqrot", Wqrot, [128, 2, 4, 64], BF16), ("d_Wkn", Wkn, [128, 2, 4, 128], BF16),
                                             ("d_Wv", Wv, [128, 2, 4, 128], BF16), ("d_lbt", lbt, [128, 2, 512], F32),
                                             ("d_oml", oml, [128, 2, 512], F32), ("d_Win", Win, [128, 8, INC], BF16)):
                        dd = nc.dram_tensor(nm, shp, dt_, kind="ExternalOutput").ap()
                        S.dma('sp', dd, tl[:], [nm[2:]], ())
                S.emit()
                if stop_after == 15:
                    return nc

            hTg = Rot(TT, "hTg", [128, 8, 512], BF16, 2)
            cT = TT("cT", [128, 2, 512], BF16); sq = TT("sq", [128, 2, 512], BF16)
            rbc = TT("rbc", [128, 512], F32); rtk = TT("rtk", [128, 4], F32)
            o_bf = Rot(TT, "o_bf", [128, 512], BF16, 3)
            o_f = Rot(TT, "o_f", [128, 512], F32, 4)
            u_f = Rot(TT, "u_f", [64, 512], F32, 4)
            sgb = Rot(TT, "sgb", [128, 512], F32, 3); fb_ = Rot(TT, "fb_", [128, 512], F32, 3)
            pA = [PP(f"pA{i}", [128, 512], F32) for i in range(2)]
            pB = [PP(f"pB{i}", [128, 512], F32) for i in range(2)]
            pS = PP("pS", [128, 512], F32)
            pT = [PP(f"pT{i}", [128, 512], F32) for i in range(2)]
            pV = PP("pV", [128, 4, 128], F32)
            groups = [(0, 256)] + [(256 + i * 512, 512) for i in range(8)]
            if a2_groups is not None:
                groups = groups[:a2_groups]
            for (tok0, n) in groups:
                is_lat = tok0 >= L
                lo = tok0 - L
                nsub = n // 128
                S.enabled = True
                hT_, hk = hTg.next()
                S.dma('sp', hT_[:, :, :n], HT[:, :, tok0:tok0 + n], (), [hk])

                def fm_proj(ps, pk, wt, wk, col0, ncols):
                    for k in range(8):
                        S.mm(ps[0:ncols, :n], wt[:, k, col0:col0 + ncols], hT_[:, k, :n], k == 0, k == 7, [wk, hk], [pk])

                def lowrank(col0, want_tok):
                    for c_ in range(2):
                        fm_proj(pA[c_], f'pA{c_}', Win, 'Win', col0 + c_ * 128, 128)
                        S.copy('act', cT[:, c_, :n], pA[c_][:, :n], [f'pA{c_}'], ['cT'])
                        S.act(sq[:, c_, :n], pA[c_][:, :n], AF.Square, [f'pA{c_}'], ['sq'])
                    for c_ in range(2):
                        S.mm(pS[:, :n], onesb[:], sq[:, c_, :n], c_ == 0, c_ == 1, ['onesb', 'sq'], ['pS'])
                    S.act(rbc[:, :n], pS[:, :n], AF.Sqrt, ['pS'], ['rbc'], bias=EPS, scale=1.0 / 256)
                    S.op('dve', (lambda nn: (lambda e: e.reciprocal(out=rbc[:, :nn], in_=rbc[:, :nn])))(n), ['rbc'], ['rbc'])
                    if want_tok:
                        for s_ in range(nsub):
                            for c_ in range(2):
                                S.mm(pV[:, s_, :], sq[:, c_, s_ * 128:(s_ + 1) * 128], onesb[:], c_ == 0, c_ == 1,
                                     ['sq', 'onesb'], ['pV'])
                        S.act(rtk[:, :nsub], pV[:, :nsub, 0], AF.Sqrt, ['pV'], ['rtk'], bias=EPS, scale=1.0 / 256)
                        S.op('dve', (lambda ns: (lambda e: e.reciprocal(out=rtk[:, :ns], in_=rtk[:, :ns])))(nsub), ['rtk'], ['rtk'])

                def rope_out(p0, k0, p1, k1, dst, scale_rows):
                    u1, uk1 = u_f.next(); u2, uk2 = u_f.next()
                    S.tt('dve', u1[:, :n], p0[0:64, :n], Ct[:, lo:lo + n], ALU.mult, [k0, 'Ct'], [uk1])
                    S.tt('dve', u2[:, :n], p1[0:64, :n], St[:, lo:lo + n], ALU.mult, [k1, 'St'], [uk2])
                    ob, ok = o_bf.next()
                    if scale_rows:
                        S.tt('pool', u1[:, :n], u1[:, :n], u2[:, :n], ALU.add, [uk1, uk2], [uk1])
                        S.tt('pool', ob[0:64, :n], u1[:, :n], rbc[0:64, :n], ALU.mult, [uk1, 'rbc'], [ok])
                    else:
                        S.tt('pool', ob[0:64, :n], u1[:, :n], u2[:, :n], ALU.add, [uk1, uk2], [ok])
                    S.dma(STQ, dst, ob[0:64, :n], [ok], ())

                if is_lat and 'a' in a2_parts:
                    lowrank(0, False)
                    for h in range(4):
                        pb, pk = pB[h % 2], f'pB{h % 2}'
                        for c_ in range(2):
                            S.mm(pb[:, :n], Wqn[:, c_, h, :], cT[:, c_, :n], c_ == 0, c_ == 1, ['Wqn', 'cT'], [pk])
                        ob, ok = o_bf.next()
                        S.tt('dve', ob[:, :n], pb[:, :n], rbc[:, :n], ALU.mult, [pk, 'rbc'], [ok])
                        S.dma(STQ, QN[h][:, lo:lo + n], ob[:, :n], [ok], ())
                    for h in range(4):
                        for c_ in range(2):
                            S.mm(pB[0][0:64, :n], Wqr[:, c_, h, :], cT[:, c_, :n], c_ == 0, c_ == 1, ['Wqr', 'cT'], ['pB0'])
                        for c_ in range(2):
                            S.mm(pB[1][0:64, :n], Wqrot[:, c_, h, :], cT[:, c_, :n], c_ == 0, c_ == 1, ['Wqrot', 'cT'], ['pB1'])
                        rope_out(pB[0], 'pB0', pB[1], 'pB1', QR[h][:, lo:lo + n], True)
                S.enabled = 'b' in a2_parts
                lowrank(256, True)
                for h in range(4):
                    pb, pk = pB[h % 2], f'pB{h % 2}'
                    for c_ in range(2):
                        S.mm(pb[:, :n], Wkn[:, c_, h, :], cT[:, c_, :n], c_ == 0, c_ == 1, ['Wkn', 'cT'], [pk])
                    ob, ok = o_bf.next()
                    S.tt('dve', ob[:, :n], pb[:, :n], rbc[:, :n], ALU.mult, [pk, 'rbc'], [ok])
                    S.dma(STQ, KN[h][:, tok0:tok0 + n], ob[:, :n], [ok], ())
                Wv2 = Wv[:].rearrange("p c h d -> p c (h d)")
                for s_ in range(nsub):
                    pt, pk = pT[s_ % 2], f'pT{s_ % 2}'
                    for c_ in range(2):
                        S.mm(pt[:], cT[:, c_, s_ * 128:(s_ + 1) * 128], Wv2[:, c_, :], c_ == 0, c_ == 1, ['cT', 'Wv'], [pk])
                    ob, ok = o_bf.next()
                    S.act(ob[:], pt[:], AF.Copy, [pk, 'rtk'], [ok], scale=rtk[:, s_:s_ + 1])
                    S.dma(STQ, VV[tok0 + s_ * 128:tok0 + (s_ + 1) * 128, :], ob[:], [ok], ())
                S.enabled = 'c' in a2_parts
                fm_proj(pA[0], 'pA0', Wkpe, 'Wkpe', 0, 64)
                if is_lat:
                    fm_proj(pA[1], 'pA1', Wkrot, 'Wkrot', 0, 64)
                    rope_out(pA[0], 'pA0', pA[1], 'pA1', KR[:, tok0:tok0 + n], False)
                else:
                    ob, ok = o_bf.next()
                    S.copy('act', ob[0:64, :n], pA[0][0:64, :n], ['pA0'], [ok])
                    S.dma(STQ, KR[:, tok0:tok0 + n], ob[0:64, :n], [ok], ())
                S.enabled = 'd' in a2_parts
                for h in range(4):
                    pa, pk = pA[h % 2], f'pA{h % 2}'
                    fm_proj(pa, pk, Win, 'Win', 576 + h * 128, 128)
                    of, ok = o_f.next()
                    S.copy('act' if h % 2 else 'dve', of[:, :n], pa[:, :n], [pk], [ok])
                    S.dma(STQ, HQ[h][:, tok0:tok0 + n], of[:, :n], [ok], ())
                S.enabled = 'e' in a2_parts
                for s_ in range(nsub):
                    row0 = tok0 + s_ * 128

                    def tm_proj(ps, pk, col0):
                        for k in range(8):
                            S.mm(ps[:], hT_[:, k, s_ * 128:(s_ + 1) * 128], Win[:, k, col0:col0 + 512], k == 0, k == 7,
                                 [hk, 'Win'], [pk])
                    tm_proj(pT[0], 'pT0', 1088)
                    ob, ok = o_bf.next()
                    S.copy('dve', ob[:], pT[0][:], ['pT0'], [ok])
                    S.dma(STQ, HV[row0:row0 + 128, :], ob[:], [ok], ())
                    if is_lat:
                        tm_proj(pT[1], 'pT1', 1600)
                        th, thk = sgb.next(); uh, uhk = fb_.next()
                        S.act(th[:], pT[1][:], AF.Tanh, ['pT1'], [thk], scale=0.5)
                        S.act(uh[:], pT[1][:], AF.Copy, ['pT1'], [uhk], scale=0.5)
                        of, ok = o_f.next()
                        S.ts('pool', th[:], th[:], 1.0, None, ALU.add, None, [thk], [thk])
                        S.tt('pool', of[:], th[:], uh[:], ALU.mult, [thk, uhk], [ok])
                        S.dma(STQ, HG[row0 - L:row0 - L + 128, :], of[:], [ok], ())
                    fts = []
                    for d_ in range(2):
                        pt, pk = pT[d_], f'pT{d_}'
                        tm_proj(pt, pk, 2112 + d_ * 512)
                        sg, sk = sgb.next(); ff, fk = fb_.next()
                        S.act(sg[:], pt[:], AF.Tanh, [pk], [sk], scale=0.5)
                        S.tt('dve', ff[:], sg[:], oml[:, d_, :], ALU.mult, [sk, 'oml'], [fk])
                        S.tt('pool', ff[:], ff[:], lbt[:, d_, :], ALU.add, [fk, 'lbt'], [fk])
                        fts.append((ff, fk))
                    for d_ in range(2):
                        ff, fk = fts[d_]
                        of, ok = o_f.next()
                        S.act(of[:], ff[:], AF.Ln, [fk], [ok])
                        S.dma(STQ, GG[d_][row0:row0 + 128, :], of[:], [ok], ())
                        of2, ok2 = o_f.next()
                        S.ts('pool', of2[:], ff[:], -1.0, 1.0, ALU.mult, ALU.add, [fk], [ok2])
                        S.dma(STQ, KG[d_][row0:row0 + 128, :], of2[:], [ok2], ())
            S.enabled = True
            S.emit()
        if stop_after == 2:
            return nc

        with ExitStack() as es:
            def TT(name, shape, dt):
                return es.enter_context(nc.sbuf_tensor(name, shape, dt))

            def PP(name, shape, dt):
                return es.enter_context(nc.psum_tensor(name, shape, dt))

            bt = []
            for d_ in range(2):
                bt.append(dict(
                    g=Rot(TT, f"bg{d_}", [128, 512], F32, 2), kg=Rot(TT, f"bkg{d_}", [128, 512], F32, 2),
                    v=Rot(TT, f"bv{d_}", [128, 512], BF16, 3), hq=Rot(TT, f"bhq{d_}", [128, 4, 128], F32, 2),
                    Ek=Rot(TT, f"bEk{d_}", [128, 512], F32, 2), EqT=Rot(TT, f"bEq{d_}", [128, 4, 128], F32, 2),
                    eb=Rot(TT, f"beb{d_}", [128, 4, 2], F32, 3), K2=Rot(TT, f"bK2{d_}", [128, 512], BF16, 3),
                    QsT=Rot(TT, f"bQs{d_}", [128, 4, 128], BF16, 3), K2T=Rot(TT, f"bK2T{d_}", [128, 4, 128], BF16, 2),
                    Am=Rot(TT, f"bAm{d_}", [128, 4, 128], BF16, 3), S=TT(f"bS{d_}", [128, 4, 128], F32),
                    Sp=Rot(TT, f"bSp{d_}", [128, 4, 128], F32, 2), Spb=Rot(TT, f"bSpb{d_}", [128, 4, 128], BF16, 2),
                    osb=Rot(TT, f"bos{d_}", [64, 512], F32, 3)))
                S.memset('dve', bt[d_]['S'][:], 0.0, [f'bS{d_}'])
            pD1 = PP("pD1", [128, 512], F32); pD2 = PP("pD2", [128, 4, 128], F32)
            pKT = PP("pKT", [128, 4, 128], BF16); pBL = PP("pBL", [128, 4, 2], F32)
            pAT = PP("pAT", [128, 4, 128], F32)
            pOr = Rot(PP, "pO", [64, 4, 128], F32, 1)
            pSNr = Rot(PP, "pSN", [128, 4, 128], F32, 2)

            def hgrn_pre(cx, ti, d_):
                B_ = bt[d_]
                is_lat = ti >= 2
                row0 = ti * 128
                U_, R_, M_ = (Uf, Rf, Mf) if d_ == 0 else (Ub, Rb, Mb)
                uk, rk, mk_ = ('Uf', 'Rf', 'Mf') if d_ == 0 else ('Ub', 'Rb', 'Mb')
                g, gk = B_['g'].next(); kg, kgk = B_['kg'].next(); v, vk = B_['v'].next(); hq, hqk = B_['hq'].next()
                S.dma('sp', g[:], GG[d_][row0:row0 + 128, :], (), [gk])
                S.dma('sp', kg[:], KG[d_][row0:row0 + 128, :], (), [kgk])
                S.dma('sp', v[:], HV[row0:row0 + 128, :], (), [vk])
                S.dma('sp', hq[:], HQ[:, :, row0:row0 + 128].rearrange("h p t -> p h t"), (), [hqk])
                Ek, Ekk = B_['Ek'].next(); EqT, Eqk = B_['EqT'].next(); eb, ebk = B_['eb'].next()
                K2, K2k = B_['K2'].next(); QsT, Qsk = B_['QsT'].next(); K2T, K2Tk = B_['K2T'].next()
                S.mm(pD1[:], U_[:], g[:], True, True, [uk, gk], ['pD1'])
                for h in range(4):
                    S.mm(pD2[:, h, :], g[:, h * 128:(h + 1) * 128], R_[:], True, True, [gk, rk], ['pD2'])
                for h in range(4):
                    S.mm(pBL[:, h, :], g[:, h * 128:(h + 1) * 128], Ind[:], True, True, [gk, 'Ind'], ['pBL'])
                S.act(Ek[:], pD1[:], AF.Exp, ['pD1'], [Ekk])
                S.act(EqT[:], pD2[:], AF.Exp, ['pD2'], [Eqk])
                S.act(eb[:], pBL[:], AF.Exp, ['pBL'], [ebk])
                S.tt('dve', K2[:], kg[:], Ek[:], ALU.mult, [kgk, Ekk], [K2k])
                S.tt('pool', QsT[:], hq[:], EqT[:], ALU.mult, [hqk, Eqk], [Qsk])
                cx.update(v=v, vk=vk, eb=eb, ebk=ebk, K2=K2, K2k=K2k, QsT=QsT, Qsk=Qsk)
                if is_lat:
                    Am, Amk = B_['Am'].next()
                    for h in range(4):
                        S.tr(pKT[:, h, :], K2[:, h * 128:(h + 1) * 128], identb[:], [K2k, 'identb'], ['pKT'])
                    S.copy('act', K2T[:], pKT[:], ['pKT'], [K2Tk])
                    for h in range(4):
                        S.mm(pAT[:, h, :], K2T[:, h, :], QsT[:, h, :], True, True, [K2Tk, Qsk], ['pAT'])
                    S.tt('dve', Am[:], pAT[:], M_[:].unsqueeze(1).to_broadcast([128, 4, 128]), ALU.mult,
                         ['pAT', mk_], [Amk])
                    cx.update(Am=Am, Amk=Amk)

            def hgrn_chain(cx, ti, d_):
                B_ = bt[d_]
                is_lat = ti >= 2
                row0 = ti * 128
                St_, Sk = B_['S'], f'bS{d_}'
                v, vk, eb, ebk, K2, K2k, QsT, Qsk = (cx[k_] for k_ in ('v', 'vk', 'eb', 'ebk', 'K2', 'K2k', 'QsT', 'Qsk'))
                for c_ in ((0, 1) if d_ == 0 else (1, 0)):
                    lo_, hi_ = c_ * 64, (c_ + 1) * 64
                    ebb = eb[:, :, c_:c_ + 1].to_broadcast([128, 4, 128])
                    Sp, Spk = B_['Sp'].next()
                    psn, psnk = pSNr.next()
                    for h in range(4):
                        S.mm(psn[:, h, :], K2[lo_:hi_, h * 128:(h + 1) * 128], v[lo_:hi_, h * 128:(h + 1) * 128], True, True,
                             [K2k, vk], [psnk])
                    S.tt('pool', Sp[:], St_[:], ebb, ALU.mult, [Sk, ebk], [Spk])
                    if is_lat:
                        Am, Amk = cx['Am'], cx['Amk']
                        Spb, Spbk = B_['Spb'].next()
                        S.tt('dve', Spb[:], St_[:], ebb, ALU.mult, [Sk, ebk], [Spbk])
                    S.tt('dve', St_[:], Sp[:], psn[:], ALU.add, [Spk, psnk], [Sk])
                    if is_lat:
                        po, pok = pOr.next()
                        for h in range(4):
                            S.mm(po[:, h, :], QsT[:, h, lo_:hi_], Spb[:, h, :], True, False, [Qsk, Spbk], [pok])
                            S.mm(po[:, h, :], Am[lo_:hi_, h, lo_:hi_], v[lo_:hi_, h * 128:(h + 1) * 128], False, True,
                                 [Amk, vk], [pok])
                        ob, obk = B_['osb'].next()
                        S.copy('act', ob[:], po[:].rearrange("p h d -> p (h d)"), [pok], [obk])
                        r_ = row0 - L + lo_
                        S.dma('sp', OO[d_][r_:r_ + 64, :], ob[:], [obk], ())

            fwd_order = list(range(NT))
            bwd_order = [1, 0] + list(range(NT - 1, 1, -1))
            cxs = [[dict() for _ in range(NT)] for _ in range(2)]
            hgrn_pre(cxs[0][0], fwd_order[0], 0)
            hgrn_pre(cxs[1][0], bwd_order[0], 1)
            for i_ in range(NT):
                if i_ + 1 < NT:
                    hgrn_pre(cxs[0][i_ + 1], fwd_order[i_ + 1], 0)
                    hgrn_pre(cxs[1][i_ + 1], bwd_order[i_ + 1], 1)
                hgrn_chain(cxs[0][i_], fwd_order[i_], 0)
                hgrn_chain(cxs[1][i_], bwd_order[i_], 1)
            S.emit()
        if stop_after == 3:
            return nc

        with ExitStack() as es:
            def TT(name, shape, dt):
                return es.enter_context(nc.sbuf_tensor(name, shape, dt))

            def PP(name, shape, dt):
                return es.enter_context(nc.psum_tensor(name, shape, dt))

            KNs = TT("KNs", [128, 4, T], BF16); KRs = TT("KRs", [64, T], BF16); Vs = TT("Vs", [128, NT, 512], BF16)
            sqKR = TT("sqKR", [64, T], BF16)
            sqn = TT("sqn", [128, 512], BF16); sqr = TT("sqr", [64, 512], BF16)
            km2 = TT("km2", [128, 4], F32); tmx = TT("tmx", [128, 1], F32); nsh = TT("nsh", [128, 1], F32)
            qnr = Rot(TT, "cqn", [128, 512], BF16, 2); qrr = Rot(TT, "cqr", [64, 512], BF16, 2)
            PTr = Rot(TT, "cPT", [128, 1024], BF16, 4); osr = Rot(TT, "cos", [128, 512], BF16, 2)
            rinv = TT("rinv", [128, 512], F32)
            racc = [TT(f"racc{i}", [128, 1024], F32) for i in range(2)]
            rsum = [TT(f"rsum{i}", [128, 512], F32) for i in range(2)]
            pSc = [PP(f"pSc{i}", [128, 1024], F32) for i in range(2)]
            pOa = PP("pOa", [128, 512], F32); pRs = PP("pRs", [128, 512], F32); pNm = PP("pNm", [128, 512], F32)
            for h in range(4):
                S.dma('sp', KNs[:, h, :], KN[h], (), ['KNs'])
            S.dma('sp', KRs[:], KR, (), ['KRs'])
            VVv = VV.rearrange("(t p) n -> p t n", p=128)
            for j in range(0, NT, 4):
                je = min(NT, j + 4)
                S.dma('sp', Vs[:, j:je, :], VVv[:, j:je, :], (), ['Vs'])
            S.memset('dve', km2[:], 0.0, ['km2'])
            S.act(sqKR[:], KRs[:], AF.Square, ['KRs'], ['sqKR'])
            for j0 in range(0, T, 512):
                w_ = min(512, T - j0)
                for h in range(4):
                    S.act(sqn[:, :w_], KNs[:, h, j0:j0 + w_], AF.Square, ['KNs'], ['sqn'])
                    S.mm(pNm[:, :w_], onesb[:], sqn[:, :w_], True, False, ['onesb', 'sqn'], ['pNm'])
                    S.mm(pNm[:, :w_], onesb[0:64, :], sqKR[:, j0:j0 + w_], False, True, ['onesb', 'sqKR'], ['pNm'])
                    S.op('dve', (lambda ww: (lambda e: e.reduce_max(out=tmx[:], in_=pNm[:, :ww], axis=AX.X)))(w_),
                         ['pNm'], ['tmx'])
                    S.tt('dve', km2[:, h:h + 1], km2[:, h:h + 1], tmx[:], ALU.max, ['km2', 'tmx'], ['km2'])
            for g_ in range(8):
                q0 = g_ * 512
                for h in range(4):
                    qn, qnk = qnr.next(); qr, qrk = qrr.next()
                    S.dma('sp', qn[:], QN[h][:, q0:q0 + 512], (), [qnk])
                    S.dma('sp', qr[:], QR[h][:, q0:q0 + 512], (), [qrk])
                    S.act(sqn[:], qn[:], AF.Square, [qnk], ['sqn'])
                    S.act(sqr[:], qr[:], AF.Square, [qrk], ['sqr'])
                    S.mm(pNm[:], onesb[:], sqn[:], True, False, ['onesb', 'sqn'], ['pNm'])
                    S.mm(pNm[:], onesb[0:64, :], sqr[:], False, True, ['onesb', 'sqr'], ['pNm'])
                    S.op('dve', lambda e: e.reduce_max(out=tmx[:], in_=pNm[:], axis=AX.X), ['pNm'], ['tmx'])
                    S.ts('dve', nsh[:], tmx[:], km2[:, h:h + 1], -0.5 * SCALE, ALU.add, ALU.mult, ['tmx', 'km2'], ['nsh'])

                    NP_ = NT // 2

                    def qk(j):
                        ps, pk = pSc[j % 2], f'pSc{j % 2}'
                        for u_ in range(2):
                            kt = 2 * j + u_
                            S.mm(ps[:, u_ * 512:(u_ + 1) * 512], KNs[:, h, kt * 128:(kt + 1) * 128], qn[:], True, False,
                                 ['KNs', qnk], [pk])
                            S.mm(ps[:, u_ * 512:(u_ + 1) * 512], KRs[:, kt * 128:(kt + 1) * 128], qr[:], False, True,
                                 ['KRs', qrk], [pk])
                    qk(0)
                    for j in range(NP_):
                        if j + 1 < NP_:
                            qk(j + 1)
                        ps, pk = pSc[j % 2], f'pSc{j % 2}'
                        PT, ptk = PTr.next()
                        S.act(PT[:], ps[:], AF.Exp, [pk, 'nsh'], [ptk], bias=nsh[:], scale=SCALE)
                        for u_ in range(2):
                            kt = 2 * j + u_
                            S.mm(pOa[:], Vs[:, kt, h * 128:(h + 1) * 128], PT[:, u_ * 512:(u_ + 1) * 512],
                                 kt == 0, kt == NT - 1, ['Vs', ptk], ['pOa'])
                        ae = 'dve' if j % 2 == 0 else 'pool'
                        ra, rak = racc[j % 2], f'racc{j % 2}'
                        if j < 2:
                            S.copy(ae, ra[:], PT[:], [ptk], [rak])
                        else:
                            S.tt(ae, ra[:], ra[:], PT[:], ALU.add, [rak, ptk], [rak])
                    S.tt('dve', rsum[0][:], racc[0][:, 0:512], racc[0][:, 512:1024], ALU.add, ['racc0'], ['rsum0'])
                    S.tt('pool', rsum[1][:], racc[1][:, 0:512], racc[1][:, 512:1024], ALU.add, ['racc1'], ['rsum1'])
                    S.mm(pRs[:], onesf[:], rsum[0][:], True, False, ['onesf', 'rsum0'], ['pRs'])
                    S.mm(pRs[:], onesf[:], rsum[1][:], False, True, ['onesf', 'rsum1'], ['pRs'])
                    S.op('dve', lambda e: e.reciprocal(out=rinv[:], in_=pRs[:]), ['pRs'], ['rinv'])
                    ob, obk = osr.next()
                    S.tt('dve', ob[:], pOa[:], rinv[:], ALU.mult, ['pOa', 'rinv'], [obk])
                    S.dma('sp', MIXA[h][:, q0:q0 + 512], ob[:], [obk], ())
            S.emit()
        if stop_after == 4:
            return nc

        with ExitStack() as es:
            def TT(name, shape, dt):
                return es.enter_context(nc.sbuf_tensor(name, shape, dt))

            def PP(name, shape, dt):
                return es.enter_context(nc.psum_tensor(name, shape, dt))

            Wout = TT("Wout", [128, 8, D], BF16)
            mD = TT("mD", [128, 3, D], F32)
            gon = TT("gon", [128, 128], F32)
            woutv = w_out.rearrange("(k p) n -> p k n", p=128)
            for k in range(8):
                S.dma('pool', Wout[:, k, :], woutv[:, k, :], (), ['Wout'])
            for i, mi in enumerate((4, 5, 6)):
                S.dma('sp', mD[:, i, :], MOD[mi], (), ['mD'])
            S.dma('sp', gon[:], g_on.partition_broadcast(128), (), ['gon'])
            ofr = Rot(TT, "dof", [128, 512], F32, 3); obr = Rot(TT, "dob", [128, 512], F32, 3); hgr = Rot(TT, "dhg", [128, 512], F32, 3)
            mar = Rot(TT, "dma_", [128, 4, 128], BF16, 4); xtr = Rot(TT, "dxt", [128, D], F32, 5)
            osumr = Rot(TT, "osum", [128, 512], F32, 2); osqr = Rot(TT, "osq", [128, 512], F32, 2); ss4r = Rot(TT, "ss4", [128, 4], F32, 3)
            tBr = Rot(TT, "tB", [128, 512], F32, 2); hgbr = Rot(TT, "hgb", [128, 512], BF16, 3); mixBr = Rot(TT, "mixB", [128, 4, 128], BF16, 2)
            tmpDr = Rot(TT, "tmpD", [128, D], F32, 2); x1r = Rot(TT, "dx1", [128, D], F32, 3)
            junkD = TT("junkD", [128, D], BF16); ssDr = Rot(TT, "ssD", [128, 1], F32, 3)
            t2Dr = Rot(TT, "t2D", [128, D], F32, 2); h2r = Rot(TT, "h2", [128, D], BF16, 3); h2Tr = Rot(TT, "dh2T", [128, 8, 128], BF16, 3)
            pTBr = Rot(PP, "pTB", [128, 4, 128], BF16, 2)
            pLOr = Rot(PP, "pLO", [128, 512], F32, 4)
            pT8r = Rot(PP, "pT8", [128, 8, 128], BF16, 2)

            def d1_s0(cx, ti):
                r0 = ti * 128
                for nm, rr, src in (('of', ofr, OO[0][r0:r0 + 128, :]), ('ob', obr, OO[1][r0:r0 + 128, :]),
                                    ('hg', hgr, HG[r0:r0 + 128, :]),
                                    ('ma', mar, MIXA[:, :, r0:r0 + 128].rearrange("h p t -> p h t")),
                                    ('xt', xtr, x[r0:r0 + 128, :])):
                    cx[nm], cx[nm + 'k'] = rr.next()
                    S.dma('sp', cx[nm][:], src, (), [cx[nm + 'k']])

            def d1_s1(cx, ti):
                of, ofk, ob, obk, hg, hgk = cx['of'], cx['ofk'], cx['ob'], cx['obk'], cx['hg'], cx['hgk']
                osum, osumk = osumr.next(); osq, osqk = osqr.next(); ss4, ss4k = ss4r.next(); tB, tBk = tBr.next()
                hgb, hgbk = hgbr.next()
                cx['hgb'], cx['hgbk'] = hgb, hgbk
                S.tt('pool', osum[:], of[:], ob[:], ALU.add, [ofk, obk], [osumk])
                S.tt('pool', osq[:], osum[:], osum[:], ALU.mult, [osumk], [osqk])
                S.op('dve', (lambda o_, i_: (lambda e: e.reduce_sum(out=o_[:], in_=i_[:].rearrange("p (h d) -> p h d", h=4),
                                                                    axis=AX.X)))(ss4, osq), [osqk], [ss4k])
                S.act(ss4[:], ss4[:], AF.Sqrt, [ss4k], [ss4k], bias=EPS, scale=1.0 / 128)
                S.op('dve', (lambda o_: (lambda e: e.reciprocal(out=o_[:], in_=o_[:])))(ss4), [ss4k], [ss4k])
                o3 = osum[:].rearrange("p (h d) -> p h d", h=4)
                t3 = tB[:].rearrange("p (h d) -> p h d", h=4)
                S.tt('dve', t3, o3, ss4[:].unsqueeze(2).to_broadcast([128, 4, 128]), ALU.mult, [osumk, ss4k], [tBk])
                S.tt('pool', t3, t3, gon[:].unsqueeze(1).to_broadcast([128, 4, 128]), ALU.mult, [tBk, 'gon'], [tBk])
                S.tt('pool', hgb[:], tB[:], hg[:], ALU.mult, [tBk, hgk], [hgbk])

            def d1_s2(cx, ti):
                hgb, hgbk, ma, mak = cx['hgb'], cx['hgbk'], cx['ma'], cx['mak']
                pTB, pTBk = pTBr.next(); mixB, mixBk = mixBr.next()
                for h in range(4):
                    S.tr(pTB[:, h, :], hgb[:, h * 128:(h + 1) * 128], identb[:], [hgbk, 'identb'], [pTBk])
                S.copy('act', mixB[:], pTB[:], [pTBk], [mixBk])
                cx['pLO'] = []
                for hf in range(2):
                    pl, plk = pLOr.next()
                    cx['pLO'].append((pl, plk))
                    for k in range(4):
                        S.mm(pl[:], ma[:, k, :], Wout[:, k, hf * 512:(hf + 1) * 512], k == 0, False, [mak, 'Wout'], [plk])
                    for k in range(4):
                        S.mm(pl[:], mixB[:, k, :], Wout[:, 4 + k, hf * 512:(hf + 1) * 512], False, k == 3,
                             [mixBk, 'Wout'], [plk])

            def d1_s3(cx, ti):
                r0 = ti * 128
                xt_, xtk = cx['xt'], cx['xtk']
                tmpD, tmpDk = tmpDr.next(); ssD, ssDk = ssDr.next(); t2D, t2Dk = t2Dr.next(); h2, h2k = h2r.next()
                x1, x1k = x1r.next()
                cx['h2'], cx['h2k'] = h2, h2k
                for hf in range(2):
                    pl, plk = cx['pLO'][hf]
                    S.tt('dve', tmpD[:, hf * 512:(hf + 1) * 512], pl[:], mD[:, 0, hf * 512:(hf + 1) * 512], ALU.mult,
                         [plk, 'mD'], [tmpDk])
                S.tt('pool', x1[:], tmpD[:], xt_[:], ALU.add, [tmpDk, xtk], [x1k])
                S.dma('sp', X1[r0:r0 + 128, :], x1[:], [x1k], ())
                S.memset('dve', ssD[:], 0.0, [ssDk])
                S.act(junkD[:], x1[:], AF.Square, [x1k], ['junkD', ssDk], accum=ssD[:])
                S.act(ssD[:], ssD[:], AF.Sqrt, [ssDk], [ssDk], bias=EPS, scale=1.0 / D)
                S.op('dve', (lambda o_: (lambda e: e.reciprocal(out=o_[:], in_=o_[:])))(ssD), [ssDk], [ssDk])
                S.stt('dve', t2D[:], x1[:], ssD[:, 0:1], mD[:, 1, :], ALU.mult, ALU.mult, [x1k, ssDk, 'mD'], [t2Dk])
                S.tt('pool', h2[:], t2D[:], mD[:, 2, :], ALU.add, [t2Dk, 'mD'], [h2k])

            def d1_s4(cx, ti):
                r0 = ti * 128
                h2, h2k = cx['h2'], cx['h2k']
                pT8, pT8k = pT8r.next(); hT2, hT2k = h2Tr.next()
                for k in range(8):
                    S.tr(pT8[:, k, :], h2[:, k * 128:(k + 1) * 128], identb[:], [h2k, 'identb'], [pT8k])
                S.copy('dve' if ti % 2 else 'act', hT2[:], pT8[:], [pT8k], [hT2k])
                S.dma('sp', H2T[:, :, r0:r0 + 128], hT2[:], [hT2k], ())
            run_pipeline(N // 128, [d1_s0, d1_s1, d1_s2, d1_s3, d1_s4])
            S.emit()
        if stop_after == 5:
            return nc

        with ExitStack() as es:
            def TT(name, shape, dt):
                return es.enter_context(nc.sbuf_tensor(name, shape, dt))

            def PP(name, shape, dt):
                return es.enter_context(nc.psum_tensor(name, shape, dt))

            Wg = TT("Wg", [128, 8, DFF], BF16); Wu = TT("Wu", [128, 8, DFF], BF16); Wd = TT("Wd", [128, NCF, D], BF16)
            mE = TT("mE", [128, 2, D], F32)
            wgv = w_gate.rearrange("(k p) n -> p k n", p=128); wuv = w_up.rearrange("(k p) n -> p k n", p=128)
            wdv = w_down.rearrange("(c p) n -> p c n", p=128)
            for k in range(8):
                S.dma('pool', Wg[:, k, :], wgv[:, k, :], (), ['Wg'])
                S.dma('pool', Wu[:, k, :], wuv[:, k, :], (), ['Wu'])
            for c_ in range(NCF):
                S.dma('pool', Wd[:, c_, :], wdv[:, c_, :], (), ['Wd'])
            S.dma('sp', mE[:, 0, :], MOD[7], (), ['mE'])
            S.dma('sp', mE[:, 1, :], g_fin.partition_broadcast(128), (), ['mE'])
            GN = 256
            h2g = Rot(TT, "eh2", [128, 8, GN], BF16, 1)
            aT = TT("aT", [128, NCF, GN], BF16)
            sgr = Rot(TT, "esg", [128, GN], F32, 2)
            x1r = Rot(TT, "ex1", [128, D], F32, 2); tmr = Rot(TT, "etm", [128, D], F32, 2)
            junkE = TT("junkE", [128, D], BF16); ssE = TT("ssE", [128, 1], F32)
            pG = [PP(f"pG{i}", [128, GN], F32) for i in range(2)]
            pU = [PP(f"pU{i}", [128, GN], F32) for i in range(2)]
            pY = [PP(f"pY{i}", [128, 512], F32) for i in range(2)]
            for g_ in range(N // GN):
                q0 = g_ * GN
                hh, hhk = h2g.next()
                S.dma('sp', hh[:], H2T[:, :, q0:q0 + GN], (), [hhk])
                for c_ in range(NCF):
                    pg, pgk = pG[c_ % 2], f'pG{c_ % 2}'
                    pu, puk = pU[c_ % 2], f'pU{c_ % 2}'
                    for k in range(8):
                        S.mm(pg[:], Wg[:, k, c_ * 128:(c_ + 1) * 128], hh[:, k, :], k == 0, k == 7, ['Wg', hhk], [pgk])
                    for k in range(8):
                        S.mm(pu[:], Wu[:, k, c_ * 128:(c_ + 1) * 128], hh[:, k, :], k == 0, k == 7, ['Wu', hhk], [puk])
                    sg, sgk = sgr.next()
                    S.act(sg[:], pg[:], AF.Silu, [pgk], [sgk])
                    S.tt('dve', aT[:, c_, :], sg[:], pu[:], ALU.mult, [sgk, puk], ['aT'])
                for sb in range(GN // 128):
                    r0 = q0 + sb * 128
                    x1, x1k = x1r.next(); tm, tmk = tmr.next()
                    S.dma('sp', x1[:], X1[r0:r0 + 128, :], (), [x1k])
                    for hf in range(2):
                        for c_ in range(NCF):
                            S.mm(pY[hf][:], aT[:, c_, sb * 128:(sb + 1) * 128], Wd[:, c_, hf * 512:(hf + 1) * 512],
                                 c_ == 0, c_ == NCF - 1, ['aT', 'Wd'], [f'pY{hf}'])
                        S.tt('dve', tm[:, hf * 512:(hf + 1) * 512], pY[hf][:], mE[:, 0, hf * 512:(hf + 1) * 512], ALU.mult,
                             [f'pY{hf}', 'mE'], [tmk])
                    S.tt('pool', x1[:], tm[:], x1[:], ALU.add, [tmk, x1k], [x1k])
                    S.memset('dve', ssE[:], 0.0, ['ssE'])
                    S.act(junkE[:], x1[:], AF.Square, [x1k], ['junkE', 'ssE'], accum=ssE[:])
                    S.act(ssE[:], ssE[:], AF.Sqrt, ['ssE'], ['ssE'], bias=EPS, scale=1.0 / D)
                    S.op('dve', lambda e: e.reciprocal(out=ssE[:], in_=ssE[:]), ['ssE'], ['ssE'])
                    S.stt('dve', tm[:], x1[:], ssE[:, 0:1], mE[:, 1, :], ALU.mult, ALU.mult, [x1k, 'ssE', 'mE'], [tmk])
                    S.dma('sp', out[r0:r0 + 128, :], tm[:], [tmk], ())
            S.emit()
    return nc


_NC_CACHE = {}


def kernel(**inputs):
    if 'nc' not in _NC_CACHE:
        _NC_CACHE['nc'] = build_nc()
    nc = _NC_CACHE['nc']
    f = lambda a: np.ascontiguousarray(np.asarray(a, dtype=np.float32))
    shared = {
        "c_ctx": f(inputs["c_ctx"]), "w_mod": f(inputs["w_mod"][0]), "b_mod": f(inputs["b_mod"][0]),
        "g_norm_mix": f(inputs["g_norm_mix"][0]), "g_norm_ffn": f(inputs["g_norm_ffn"][0]),
        "w_in": f(inputs["w_in"][0]), "g_q_norm": f(inputs["g_q_norm"][0]), "w_uq": f(inputs["w_uq"][0]),
        "g_kv_norm": f(inputs["g_kv_norm"][0]), "w_ukv": f(inputs["w_ukv"][0]),
        "lb_fwd": f(inputs["lb_fwd"]), "lb_bwd": f(inputs["lb_bwd"]), "g_hgrn_norm": f(inputs["g_hgrn_norm"][0]),
        "w_out": f(inputs["w_out"][0]), "w_gate": f(inputs["w_gate"][0]), "w_up": f(inputs["w_up"][0]),
        "w_down": f(inputs["w_down"][0]), "g_final": f(inputs["g_final"]),
    }
    xs, cs, ctxs = f(inputs["x"]), f(inputs["c"]), f(inputs["ctx"])
    in_maps = []
    for b in range(NB):
        m = dict(shared)
        m["x"] = xs[b]; m["c"] = cs[b]; m["ctx"] = ctxs[b]
        in_maps.append(m)
    res = run_bass_kernel_spmd(nc, in_maps, core_ids=list(range(NB)))
    return np.stack([np.asarray(r["out"], dtype=np.float32) for r in res.results], axis=0)
```

```python
import math
from contextlib import ExitStack

import numpy as np
import concourse.bass as bass
import concourse.mybir as mybir
from concourse.bass_utils import run_bass_kernel_spmd

F32 = mybir.dt.float32
BF16 = mybir.dt.bfloat16
AF = mybir.ActivationFunctionType
ALU = mybir.AluOpType
AX = mybir.AxisListType

NB, N, L, D = 8, 4096, 256, 1024
T = N + L
NT = T // 128
DFF = 2816
NCF = DFF // 128
EPS = 1e-6
INC = 3136
SCALE = 1.0 / math.sqrt(192.0)


class Sched:
    ENG = ('pe', 'act', 'dve', 'pool', 'sp')

    def __init__(self, nc, n_dma_sems=24):
        self.nc = nc
        self.ops = {e: [] for e in self.ENG}
        self.sems = {}
        for e in ('pe', 'act', 'dve', 'pool'):
            self.sems[('e', e)] = nc.alloc_semaphore(name=f"s_{e}")
        for i in range(n_dma_sems):
            self.sems[('d', i)] = nc.alloc_semaphore(name=f"s_dma{i}")
        self.nw = 6
        for i in range(self.nw):
            self.sems[('w', i)] = nc.alloc_semaphore(name=f"s_swdma{i}")
        self.wnext = 0
        self.cnt = {k: 0 for k in self.sems}
        self.nd = n_dma_sems
        self.dnext = 0
        self.waited = {e: {} for e in self.ENG}
        self.res = {}
        self.enabled = True

    def _deps(self, reads, writes):
        t = []
        for r in reads:
            st = self.res.get(r)
            if st and st['w']:
                t.append(st['w'])
        for w in writes:
            st = self.res.get(w)
            if st:
                if st['w']:
                    t.append(st['w'])
                t.extend(st['r'].values())
        return t

    def _need(self, eng, tickets):
        best = {}
        for key, val in tickets:
            if key == ('e', 'pe') and eng == 'pe':
                continue
            if self.waited[eng].get(key, 0) >= val:
                continue
            if best.get(key, 0) < val:
                best[key] = val
        for key, val in best.items():
            self.waited[eng][key] = val
        return list(best.items())

    def _commit(self, ticket, reads, writes):
        for r in reads:
            st = self.res.setdefault(r, {'w': None, 'r': {}})
            st['r'][ticket[0]] = ticket
        for w in writes:
            self.res[w] = {'w': ticket, 'r': {}}

    def op(self, eng, fn, reads=(), writes=()):
        if not self.enabled:
            return None
        waits = self._need(eng, self._deps(reads, writes))
        key = ('e', eng)
        self.cnt[key] += 1
        ticket = (key, self.cnt[key])
        self.ops[eng].append((waits, fn, key, 1))
        self._commit(ticket, reads, writes)
        return ticket

    def dma(self, q, out, in_, reads=(), writes=()):
        if not self.enabled:
            return None
        if q == 'pool':
            key = ('w', self.wnext)
            self.wnext = (self.wnext + 1) % self.nw
        else:
            key = ('d', self.dnext)
            self.dnext = (self.dnext + 1) % self.nd
        tickets = self._deps(reads, writes)
        if self.cnt[key] > 0:
            tickets.append((key, self.cnt[key]))
        waits = self._need(q, tickets)
        self.cnt[key] += 16
        ticket = (key, self.cnt[key])
        self.ops[q].append((waits, lambda e: e.dma_start(out=out, in_=in_), key, 16))
        self._commit(ticket, reads, writes)
        return ticket

    def barrier(self):
        allt = [(k, v) for k, v in self.cnt.items() if v > 0]
        for e in self.ENG:
            waits = self._need(e, allt)
            if waits:
                self.ops[e].append((waits, None, None, 0))

    def emit(self):
        self.barrier()
        with self.nc.Block() as block:
            def mk(engname):
                def body(e):
                    for waits, fn, semkey, inc in self.ops[engname]:
                        for key, val in waits:
                            e.wait_ge(self.sems[key], val)
                        if fn is not None:
                            fn(e).then_inc(self.sems[semkey], inc)
                return body
            block.tensor(mk('pe'))
            block.scalar(mk('act'))
            block.vector(mk('dve'))
            block.gpsimd(mk('pool'))
            block.sync(mk('sp'))
        self.ops = {e: [] for e in self.ENG}

    def mm(self, out, lhsT, rhs, start, stop, r, w):
        return self.op('pe', lambda e: e.matmul(out, lhsT=lhsT, rhs=rhs, start=start, stop=stop), r, w)

    def tr(self, out, in_, ident, r, w):
        return self.op('pe', lambda e: e.transpose(out, in_, ident), r, w)

    def act(self, out, in_, func, r, w, bias=None, scale=None, accum=None):
        kw = {}
        if bias is not None:
            kw['bias'] = bias
        if scale is not None:
            kw['scale'] = scale
        if accum is not None:
            kw['accum_out'] = accum
        return self.op('act', lambda e: e.activation(out=out, in_=in_, func=func, **kw), r, w)

    def tt(self, eng, out, in0, in1, op, r, w):
        return self.op(eng, lambda e: e.tensor_tensor(out=out, in0=in0, in1=in1, op=op), r, w)

    def ts(self, eng, out, in0, s1, s2, op0, op1, r, w):
        if s2 is None:
            return self.op(eng, lambda e: e.tensor_scalar(out=out, in0=in0, scalar1=s1, scalar2=None, op0=op0), r, w)
        return self.op(eng, lambda e: e.tensor_scalar(out=out, in0=in0, scalar1=s1, scalar2=s2, op0=op0, op1=op1), r, w)

    def stt(self, eng, out, in0, scalar, in1, op0, op1, r, w):
        return self.op(eng, lambda e: e.scalar_tensor_tensor(out=out, in0=in0, scalar=scalar, in1=in1, op0=op0, op1=op1), r, w)

    def copy(self, eng, out, in_, r, w):
        if eng == 'act':
            return self.act(out, in_, AF.Copy, r, w)
        return self.op(eng, lambda e: e.tensor_copy(out=out, in_=in_), r, w)

    def memset(self, eng, ap, val, w):
        return self.op(eng, lambda e: e.memset(ap, val), (), w)


def run_pipeline(n, stages):
    ctxs = [dict() for _ in range(n)]
    K = len(stages)
    for t in range(n + K - 1):
        for k in range(K - 1, -1, -1):
            i = t - k
            if 0 <= i < n:
                stages[k](ctxs[i], i)


class Rot:
    def __init__(self, alloc, name, shape, dt, n):
        self.t = [alloc(f"{name}{i}", shape, dt) for i in range(n)]
        self.k = [f"{name}{i}" for i in range(n)]
        self.i = 0

    def next(self):
        j = self.i % len(self.t)
        self.i += 1
        return self.t[j], self.k[j]


def _rstd(S, ss, out, dim, tag):
    S.act(out, ss, AF.Sqrt, [tag + 'ss'], [tag + 'rs'], bias=EPS, scale=1.0 / dim)
    S.op('dve', lambda e: e.reciprocal(out=out, in_=out), [tag + 'rs'], [tag + 'rs'])


def build_nc(stop_after=None, debug=False, a2_groups=None, a2_parts='abcde', STQ='sp'):
    nc = bass.Bass("TRN2", target_bir_lowering=False)
    S = Sched(nc)

    def din(name, shape):
        return nc.dram_tensor(name, shape, F32, kind="ExternalInput").ap()

    x = din("x", [N, D]); c = din("c", [D]); ctx = din("ctx", [L, D]); c_ctx = din("c_ctx", [D])
    w_mod = din("w_mod", [D, 6 * D]); b_mod = din("b_mod", [6 * D])
    g_mix = din("g_norm_mix", [D]); g_ffn = din("g_norm_ffn", [D])
    w_in = din("w_in", [D, INC]); g_qn = din("g_q_norm", [256]); w_uq = din("w_uq", [256, 768])
    g_kvn = din("g_kv_norm", [256]); w_ukv = din("w_ukv", [256, 1024])
    lb_f = din("lb_fwd", [2, 512]); lb_b = din("lb_bwd", [2, 512]); g_on = din("g_hgrn_norm", [128])
    w_out = din("w_out", [D, D]); w_gate = din("w_gate", [D, DFF]); w_up = din("w_up", [D, DFF])
    w_down = din("w_down", [DFF, D]); g_fin = din("g_final", [D])
    out = nc.dram_tensor("out", [N, D], F32, kind="ExternalOutput").ap()

    dbg = set(debug) if debug else set()

    def scr(name, shape, dt):
        return nc.dram_tensor(name, shape, dt, kind="ExternalOutput" if name in dbg else "Internal").ap()

    MOD = scr("s_mod", [8, 128, D], F32)
    HT = scr("s_ht", [128, 8, T], BF16)
    QN = scr("s_qn", [4, 128, N], BF16); QR = scr("s_qr", [4, 64, N], BF16)
    KN = scr("s_kn", [4, 128, T], BF16); KR = scr("s_kr", [64, T], BF16)
    VV = scr("s_v", [T, 512], BF16)
    HQ = scr("s_hq", [4, 128, T], F32); HV = scr("s_hv", [T, 512], BF16); HG = scr("s_hg", [N, 512], F32)
    GG = scr("s_g", [2, T, 512], F32); KG = scr("s_kg", [2, T, 512], F32)
    OO = scr("s_o", [2, N, 512], F32)
    MIXA = scr("s_mixa", [4, 128, N], BF16)
    X1 = scr("s_x1", [N, D], F32); H2T = scr("s_h2t", [128, 8, N], BF16)

    outer = ExitStack()
    with outer:
        def CT(name, shape, dt):
            return outer.enter_context(nc.sbuf_tensor(name, shape, dt))
        identb = CT("identb", [128, 128], BF16); identf = CT("identf", [128, 128], F32)
        onesb = CT("onesb", [128, 128], BF16); onesf = CT("onesf", [128, 128], F32)
        Uf = CT("Uf", [128, 128], F32); Ub = CT("Ub", [128, 128], F32)
        Rf = CT("Rf", [128, 128], F32); Rb = CT("Rb", [128, 128], F32)
        Mf = CT("Mf", [128, 128], F32); Mb = CT("Mb", [128, 128], F32)
        Ind = CT("Ind", [128, 2], F32)

        def sel(tile, pattern, cm, cmp_op, key):
            S.op('pool', lambda e: e.affine_select(out=tile[:], in_=tile[:], pattern=pattern, compare_op=cmp_op,
                                                   fill=0.0, base=0, channel_multiplier=cm), [key], [key])
        for tl, key in ((identb, 'identb'), (identf, 'identf')):
            S.memset('pool', tl[:], 1.0, [key])
            sel(tl, [[-1, 128]], 1, ALU.is_equal, key)
        S.memset('pool', onesb[:], 1.0, ['onesb'])
        S.memset('pool', onesf[:], 1.0, ['onesf'])
        S.memset('pool', Uf[:], 1.0, ['Uf']); sel(Uf, [[-1, 128]], 1, ALU.is_gt, 'Uf')
        S.memset('pool', Uf[64:128, 0:64], 0.0, ['Uf'])
        S.memset('pool', Mb[:], 1.0, ['Mb']); sel(Mb, [[-1, 128]], 1, ALU.is_ge, 'Mb')
        S.memset('pool', Mb[64:128, 0:64], 0.0, ['Mb'])
        S.memset('pool', Ub[:], 1.0, ['Ub']); sel(Ub, [[1, 128]], -1, ALU.is_gt, 'Ub')
        S.memset('pool', Ub[0:64, 64:128], 0.0, ['Ub'])
        S.memset('pool', Mf[:], 1.0, ['Mf']); sel(Mf, [[1, 128]], -1, ALU.is_ge, 'Mf')
        S.memset('pool', Mf[0:64, 64:128], 0.0, ['Mf'])
        S.ts('pool', Rf[:], Uf[:], -1.0, None, ALU.mult, None, ['Uf'], ['Rf'])
        S.ts('pool', Rb[:], Ub[:], -1.0, None, ALU.mult, None, ['Ub'], ['Rb'])
        S.memset('pool', Ind[:], 0.0, ['Ind'])
        S.memset('pool', Ind[0:64, 0:1], 1.0, ['Ind'])
        S.memset('pool', Ind[64:128, 1:2], 1.0, ['Ind'])

        with ExitStack() as es:
            def TT(name, shape, dt):
                return es.enter_context(nc.sbuf_tensor(name, shape, dt))

            def PP(name, shape, dt):
                return es.enter_context(nc.psum_tensor(name, shape, dt))
            crow = TT("crow", [128, 2, D], F32)
            cb = TT("cb", [128, 2, 8, 128], F32)
            bmod = TT("bmod", [128, 6 * D], F32)
            wm = [TT(f"wm{i}", [128, 8, 512], F32) for i in range(2)]
            modl = TT("modl", [128, 6 * D], F32); modc = TT("modc", [128, 2 * D], F32)
            gm = TT("gm", [128, D], F32); gf = TT("gf", [128, D], F32)
            tmpA = [TT(f"tmpA{i}", [128, D], F32) for i in range(3)]
            pcb = PP("pcb", [128, 128], F32)
            pm = [PP(f"pm{i}", [128, 512], F32) for i in range(2)]
            S.dma('sp', crow[:, 0, :], c.partition_broadcast(128), (), ['crow'])
            S.dma('sp', crow[:, 1, :], c_ctx.partition_broadcast(128), (), ['crow'])
            S.dma('sp', bmod[:], b_mod.partition_broadcast(128), (), ['bmod'])
            S.dma('sp', gm[:], g_mix.partition_broadcast(128), (), ['gm'])
            S.dma('sp', gf[:], g_ffn.partition_broadcast(128), (), ['gf'])
            S.act(crow[:], crow[:], AF.Silu, ['crow'], ['crow'])
            for w_ in range(2):
                for k in range(8):
                    S.mm(pcb[:], crow[:, w_, k * 128:(k + 1) * 128], identf[:], True, True, ['crow', 'identf'], ['pcb'])
                    S.copy('dve', cb[:, w_, k, :], pcb[:], ['pcb'], ['cb'])
            wmv = w_mod.rearrange("(k p) n -> p k n", p=128)
            for j in range(12):
                wt = wm[j % 2]; wk = f"wm{j % 2}"
                S.dma('sp', wt[:], wmv[:, :, j * 512:(j + 1) * 512], (), [wk])
                for k in range(8):
                    S.mm(pm[0][:], cb[:, 0, k, :], wt[:, k, :], k == 0, k == 7, ['cb', wk], ['pm0'])
                S.tt('dve', modl[:, j * 512:(j + 1) * 512], pm[0][:], bmod[:, j * 512:(j + 1) * 512], ALU.add,
                     ['pm0', 'bmod'], ['modl'])
                if j < 4:
                    for k in range(8):
                        S.mm(pm[1][:], cb[:, 1, k, :], wt[:, k, :], k == 0, k == 7, ['cb', wk], ['pm1'])
                    S.tt('dve', modc[:, j * 512:(j + 1) * 512], pm[1][:], bmod[:, j * 512:(j + 1) * 512], ALU.add,
                         ['pm1', 'bmod'], ['modc'])
            S.stt('dve', tmpA[0][:], modl[:, D:2 * D], 1.0, gm[:], ALU.add, ALU.mult, ['modl', 'gm'], ['tA0'])
            S.stt('dve', tmpA[1][:], modc[:, D:2 * D], 1.0, gm[:], ALU.add, ALU.mult, ['modc', 'gm'], ['tA1'])
            S.stt('dve', tmpA[2][:], modl[:, 4 * D:5 * D], 1.0, gf[:], ALU.add, ALU.mult, ['modl', 'gf'], ['tA2'])
            S.dma('sp', MOD[0], tmpA[0][:], ['tA0'], ())
            S.dma('sp', MOD[1], modl[:, 0:D], ['modl'], ())
            S.dma('sp', MOD[2], tmpA[1][:], ['tA1'], ())
            S.dma('sp', MOD[3], modc[:, 0:D], ['modc'], ())
            S.dma('sp', MOD[4], modl[:, 2 * D:3 * D], ['modl'], ())
            S.dma('sp', MOD[5], tmpA[2][:], ['tA2'], ())
            S.dma('sp', MOD[6], modl[:, 3 * D:4 * D], ['modl'], ())
            S.dma('sp', MOD[7], modl[:, 5 * D:6 * D], ['modl'], ())
            S.emit()
        if stop_after == 0:
            return nc

        with ExitStack() as es:
            def TT(name, shape, dt):
                return es.enter_context(nc.sbuf_tensor(name, shape, dt))

            def PP(name, shape, dt):
                return es.enter_context(nc.psum_tensor(name, shape, dt))
            mA = TT("mA", [128, 4, D], F32)
            junk = TT("junk", [128, D], BF16)
            xtR = Rot(TT, "xt", [128, D], F32, 3); ssR = Rot(TT, "ssa", [128, 1], F32, 3)
            t1R = Rot(TT, "t1_", [128, D], F32, 2); hbR = Rot(TT, "hb", [128, D], BF16, 3)
            hTR = Rot(TT, "hT", [128, 8, 128], BF16, 3); ptrR = Rot(PP, "ptr", [128, 8, 128], BF16, 2)
            for i in range(4):
                S.dma('sp', mA[:, i, :], MOD[i], (), ['mA'])

            def a1_s0(cx, ti):
                cx['xt'], cx['xk'] = xtR.next()
                src = ctx[ti * 128:(ti + 1) * 128, :] if ti < 2 else x[(ti - 2) * 128:(ti - 1) * 128, :]
                S.dma('sp', cx['xt'][:], src, (), [cx['xk']])

            def a1_s1(cx, ti):
                xt_, xk = cx['xt'], cx['xk']
                mi = 2 if ti < 2 else 0
                ss, sk = ssR.next(); t1, t1k = t1R.next(); hb, hbk = hbR.next()
                cx['hb'], cx['hbk'] = hb, hbk
                S.memset('dve', ss[:], 0.0, [sk])
                S.act(junk[:], xt_[:], AF.Square, [xk], ['junk', sk], accum=ss[:])
                S.act(ss[:], ss[:], AF.Sqrt, [sk], [sk], bias=EPS, scale=1.0 / D)
                S.op('dve', (lambda o_: (lambda e: e.reciprocal(out=o_[:], in_=o_[:])))(ss), [sk], [sk])
                S.stt('dve', t1[:], xt_[:], ss[:, 0:1], mA[:, mi, :], ALU.mult, ALU.mult, [xk, sk, 'mA'], [t1k])
                S.tt('pool', hb[:], t1[:], mA[:, mi + 1, :], ALU.add, [t1k, 'mA'], [hbk])

            def a1_s2(cx, ti):
                hb, hbk = cx['hb'], cx['hbk']
                pt, ptk = ptrR.next(); hT, hTk = hTR.next()
                for k in range(8):
                    S.tr(pt[:, k, :], hb[:, k * 128:(k + 1) * 128], identb[:], [hbk, 'identb'], [ptk])
                S.copy('dve' if ti % 2 else 'act', hT[:], pt[:], [ptk], [hTk])
                S.dma('sp', HT[:, :, ti * 128:(ti + 1) * 128], hT[:], [hTk], ())
            run_pipeline(NT, [a1_s0, a1_s1, a1_s2])
            S.emit()
        if stop_after == 1:
            return nc

        with ExitStack() as es:
            def TT(name, shape, dt):
                return es.enter_context(nc.sbuf_tensor(name, shape, dt))

            def PP(name, shape, dt):
                return es.enter_context(nc.psum_tensor(name, shape, dt))
            Win = TT("Win", [128, 8, INC], BF16)
            Wkpe = TT("Wkpe", [128, 8, 64], BF16); Wkrot = TT("Wkrot", [128, 8, 64], BF16)
            Wqn = TT("Wqn", [128, 2, 4, 128], BF16); Wqr = TT("Wqr", [128, 2, 4, 64], BF16)
            Wqrot = TT("Wqrot", [128, 2, 4, 64], BF16)
            Wkn = TT("Wkn", [128, 2, 4, 128], BF16); Wv = TT("Wv", [128, 2, 4, 128], BF16)
            Ct = TT("Ct", [64, N], F32); St = TT("St", [64, N], F32)
            lbt = TT("lbt", [128, 2, 512], F32); oml = TT("oml", [128, 2, 512], F32)
            with ExitStack() as es2:
                def T2(name, shape, dt):
                    return es2.enter_context(nc.sbuf_tensor(name, shape, dt))
                winv = w_in.rearrange("(k p) n -> p k n", p=128)
                for k in range(8):
                    S.dma('pool', Win[:, k, :], winv[:, k, :], (), ['Win'])
                    for f_ in range(2):
                        S.dma('pool', Wkpe[:, k, f_ * 32:(f_ + 1) * 32].rearrange("p (a i) -> p a i", a=2),
                              winv[:, k, 512:576].rearrange("p (a f i) -> p f a i", a=2, f=2)[:, f_, :, :], (), ['Wkpe'])
                S.ts('dve', Wkrot[:, :, 0:32], Wkpe[:, :, 32:64], -1.0, None, ALU.mult, None, ['Wkpe'], ['Wkrot'])
                S.copy('dve', Wkrot[:, :, 32:64], Wkpe[:, :, 0:32], ['Wkpe'], ['Wkrot'])
                stq = T2("stq", [128, 2, 768], F32); stkv = T2("stkv", [128, 2, 1024], F32)
                gq = T2("gq", [128, 2], F32); gkv = T2("gkv", [128, 2], F32)
                S.dma('sp', stq[:], w_uq.rearrange("(c p) n -> p c n", p=128), (), ['stq'])
                S.dma('sp', stkv[:], w_ukv.rearrange("(c p) n -> p c n", p=128), (), ['stkv'])
                for c_ in range(2):
                    S.dma('sp', gq[:, c_:c_ + 1], g_qn[c_ * 128:(c_ + 1) * 128].rearrange("(p o) -> p o", o=1), (), ['gq'])
                    S.dma('sp', gkv[:, c_:c_ + 1], g_kvn[c_ * 128:(c_ + 1) * 128].rearrange("(p o) -> p o", o=1), (), ['gkv'])
                for c_ in range(2):
                    sq_v = stq[:, c_, :].rearrange("p (h d) -> p h d", h=4)
                    S.ts('dve', Wqn[:, c_, :, :], sq_v[:, :, 0:128], gq[:, c_:c_ + 1], None, ALU.mult, None,
                         ['stq', 'gq'], ['Wqn'])
                    for f_ in range(2):
                        for a_ in range(2):
                            so = 128 + a_ * 32 + f_ * 16
                            do = f_ * 32 + a_ * 16
                            S.ts('dve', Wqr[:, c_, :, do:do + 16], sq_v[:, :, so:so + 16], gq[:, c_:c_ + 1], None,
                                 ALU.mult, None, ['stq', 'gq'], ['Wqr'])
                    S.ts('dve', Wqrot[:, c_, :, 0:32], Wqr[:, c_, :, 32:64], -1.0, None, ALU.mult, None, ['Wqr'], ['Wqrot'])
                    S.copy('dve', Wqrot[:, c_, :, 32:64], Wqr[:, c_, :, 0:32], ['Wqr'], ['Wqrot'])
                    skv_v = stkv[:, c_, :].rearrange("p (h t d) -> p h t d", h=4, t=2)
                    S.ts('dve', Wkn[:, c_, :, :], skv_v[:, :, 0, :], gkv[:, c_:c_ + 1], None, ALU.mult, None,
                         ['stkv', 'gkv'], ['Wkn'])
                    S.ts('dve', Wv[:, c_, :, :], skv_v[:, :, 1, :], gkv[:, c_:c_ + 1], None, ALU.mult, None,
                         ['stkv', 'gkv'], ['Wv'])
                lraw = T2("lraw", [128, 2, 2, 512], F32)
                for d_, lbx in enumerate((lb_f, lb_b)):
                    for r_ in range(2):
                        S.dma('sp', lraw[:, d_, r_, :], lbx[r_].partition_broadcast(128), (), ['lraw'])
                S.tt('dve', lbt[:], lraw[:, :, 0, :], lraw[:, :, 1, :], ALU.subtract, ['lraw'], ['lbt'])
                S.act(lbt[:], lbt[:], AF.Sigmoid, ['lbt'], ['lbt'])
                S.ts('dve', oml[:], lbt[:], -0.5, 0.5, ALU.mult, ALU.add, ['lbt'], ['oml'])
                S.tt('dve', lbt[:], lbt[:], oml[:], ALU.add, ['lbt', 'oml'], ['lbt'])
                pidx = T2("pidx", [64, 1], F32); i16 = T2("i16", [64, 1], F32); mrow = T2("mrow", [64, 1], F32)
                arow = T2("arow", [64, 1], F32); acol = T2("acol", [64, 1], F32)
                rowpos = T2("rowpos", [64, N], F32); colpos = T2("colpos", [64, N], F32); ang = T2("ang", [64, N], F32)
                S.op('pool', lambda e: e.iota(pidx[:], [[0, 1]], base=0, channel_multiplier=1,
                                              allow_small_or_imprecise_dtypes=True), (), ['pidx'])
                S.op('pool', lambda e: e.iota(rowpos[:], [[1, 64], [0, 64]], base=0, channel_multiplier=0,
                                              allow_small_or_imprecise_dtypes=True), (), ['rowpos'])
                S.op('pool', lambda e: e.iota(colpos[:], [[0, 64], [1, 64]], base=0, channel_multiplier=0,
                                              allow_small_or_imprecise_dtypes=True), (), ['colpos'])
                msk = T2("msk", [64, 3], F32)
                S.memset('pool', msk[:], 1.0, ['msk'])
                for j_ in range(3):
                    S.op('pool', (lambda jj: (lambda e: e.affine_select(
                        out=msk[:, jj:jj + 1], in_=msk[:, jj:jj + 1], pattern=[[0, 1]], compare_op=ALU.is_ge, fill=0.0,
                        base=-16 * (jj + 1), channel_multiplier=1)))(j_), ['msk'], ['msk'])
                S.tt('dve', mrow[:], msk[:, 0:1], msk[:, 1:2], ALU.add, ['msk'], ['mrow'])
                S.tt('dve', mrow[:], mrow[:], msk[:, 2:3], ALU.add, ['msk', 'mrow'], ['mrow'])
                S.stt('dve', i16[:], mrow[:], -16.0, pidx[:], ALU.mult, ALU.add, ['mrow', 'pidx'], ['i16'])
                S.tt('dve', mrow[:], msk[:, 1:2], msk[:, 0:1], ALU.subtract, ['msk'], ['mrow'])
                S.tt('dve', mrow[:], mrow[:], msk[:, 2:3], ALU.subtract, ['msk', 'mrow'], ['mrow'])
                S.ts('dve', mrow[:], mrow[:], 1.0, None, ALU.add, None, ['mrow'], ['mrow'])
                S.act(i16[:], i16[:], AF.Exp, ['i16'], ['i16'], scale=-math.log(10000.0) / 16.0)
                S.tt('dve', arow[:], i16[:], mrow[:], ALU.mult, ['i16', 'mrow'], ['arow'])
                S.tt('dve', acol[:], i16[:], arow[:], ALU.subtract, ['i16', 'arow'], ['acol'])
                S.ts('dve', ang[:], rowpos[:], arow[:, 0:1], None, ALU.mult, None, ['rowpos', 'arow'], ['ang'])
                S.stt('dve', ang[:], colpos[:], acol[:, 0:1], ang[:], ALU.mult, ALU.add, ['colpos', 'acol', 'ang'], ['ang'])
                sc_ = 1.0 - 1e-6
                ki = T2("ki", [64, N], mybir.dt.int32)
                for tab, shift in ((St, 0.0), (Ct, 0.5 * math.pi)):
                    S.ts('dve', rowpos[:], ang[:], shift, 1.0 / (2 * math.pi), ALU.add, ALU.mult, ['ang'], ['rowpos'])
                    S.copy('dve', ki[:], rowpos[:], ['rowpos'], ['ki'])
                    S.copy('dve', colpos[:], ki[:], ['ki'], ['colpos'])
                    S.ts('dve', rowpos[:], ang[:], shift, None, ALU.add, None, ['ang'], ['rowpos'])
                    S.stt('dve', rowpos[:], colpos[:], -2 * math.pi, rowpos[:], ALU.mult, ALU.add, ['colpos', 'rowpos'], ['rowpos'])
                    S.act(tab[:], rowpos[:], AF.Sin, ['rowpos'], ['St' if shift == 0.0 else 'Ct'], scale=sc_)
                if stop_after == 15:
                    for nm, tl, shp, dt_ in (("d_Ct", Ct, [64, N], F32), ("d_St", St, [64, N], F32),
                                             ("d_Wkpe", Wkpe, [128, 8, 64], BF16), ("d_Wkrot", Wkrot, [128, 8, 64], BF16),
                                             ("d_Wqn", Wqn, [128, 2, 4, 128], BF16), ("d_Wqr", Wqr, [128, 2, 4, 64], BF16),
                                             ("d_Wqrot", Wqrot, [128, 2, 4, 64], BF16), ("d_Wkn", Wkn, [128, 2, 4, 128], BF16),
                                             ("d_Wv", Wv, [128, 2, 4, 128], BF16), ("d_lbt", lbt, [128, 2, 512], F32),
                                             ("d_oml", oml, [128, 2, 512], F32), ("d_Win", Win, [128, 8, INC], BF16)):
                        dd = nc.dram_tensor(nm, shp, dt_, kind="ExternalOutput").ap()
                        S.dma('sp', dd, tl[:], [nm[2:]], ())
                S.emit()
                if stop_after == 15:
                    return nc

            hTg = Rot(TT, "hTg", [128, 8, 512], BF16, 2)
            cT = TT("cT", [128, 2, 512], BF16); sq = TT("sq", [128, 2, 512], BF16)
            rbc = TT("rbc", [128, 512], F32); rtk = TT("rtk", [128, 4], F32)
            o_bf = Rot(TT, "o_bf", [128, 512], BF16, 3)
            o_f = Rot(TT, "o_f", [128, 512], F32, 4)
            u_f = Rot(TT, "u_f", [64, 512], F32, 4)
            sgb = Rot(TT, "sgb", [128, 512], F32, 3); fb_ = Rot(TT, "fb_", [128, 512], F32, 3)
            pA = [PP(f"pA{i}", [128, 512], F32) for i in range(2)]
            pB = [PP(f"pB{i}", [128, 512], F32) for i in range(2)]
            pS = PP("pS", [128, 512], F32)
            pT = [PP(f"pT{i}", [128, 512], F32) for i in range(2)]
            pV = PP("pV", [128, 4, 128], F32)
            groups = [(0, 256)] + [(256 + i * 512, 512) for i in range(8)]
            if a2_groups is not None:
                groups = groups[:a2_groups]
            for (tok0, n) in groups:
                is_lat = tok0 >= L
                lo = tok0 - L
                nsub = n // 128
                S.enabled = True
                hT_, hk = hTg.next()
                S.dma('sp', hT_[:, :, :n], HT[:, :, tok0:tok0 + n], (), [hk])

                def fm_proj(ps, pk, wt, wk, col0, ncols):
                    for k in range(8):
                        S.mm(ps[0:ncols, :n], wt[:, k, col0:col0 + ncols], hT_[:, k, :n], k == 0, k == 7, [wk, hk], [pk])

                def lowrank(col0, want_tok):
                    for c_ in range(2):
                        fm_proj(pA[c_], f'pA{c_}', Win, 'Win', col0 + c_ * 128, 128)
                        S.copy('act', cT[:, c_, :n], pA[c_][:, :n], [f'pA{c_}'], ['cT'])
                        S.act(sq[:, c_, :n], pA[c_][:, :n], AF.Square, [f'pA{c_}'], ['sq'])
                    for c_ in range(2):
                        S.mm(pS[:, :n], onesb[:], sq[:, c_, :n], c_ == 0, c_ == 1, ['onesb', 'sq'], ['pS'])
                    S.act(rbc[:, :n], pS[:, :n], AF.Sqrt, ['pS'], ['rbc'], bias=EPS, scale=1.0 / 256)
                    S.op('dve', (lambda nn: (lambda e: e.reciprocal(out=rbc[:, :nn], in_=rbc[:, :nn])))(n), ['rbc'], ['rbc'])
                    if want_tok:
                        for s_ in range(nsub):
                            for c_ in range(2):
                                S.mm(pV[:, s_, :], sq[:, c_, s_ * 128:(s_ + 1) * 128], onesb[:], c_ == 0, c_ == 1,
                                     ['sq', 'onesb'], ['pV'])
                        S.act(rtk[:, :nsub], pV[:, :nsub, 0], AF.Sqrt, ['pV'], ['rtk'], bias=EPS, scale=1.0 / 256)
                        S.op('dve', (lambda ns: (lambda e: e.reciprocal(out=rtk[:, :ns], in_=rtk[:, :ns])))(nsub), ['rtk'], ['rtk'])

                def rope_out(p0, k0, p1, k1, dst, scale_rows):
                    u1, uk1 = u_f.next(); u2, uk2 = u_f.next()
                    S.tt('dve', u1[:, :n], p0[0:64, :n], Ct[:, lo:lo + n], ALU.mult, [k0, 'Ct'], [uk1])
                    S.tt('dve', u2[:, :n], p1[0:64, :n], St[:, lo:lo + n], ALU.mult, [k1, 'St'], [uk2])
                    ob, ok = o_bf.next()
                    if scale_rows:
                        S.tt('pool', u1[:, :n], u1[:, :n], u2[:, :n], ALU.add, [uk1, uk2], [uk1])
                        S.tt('pool', ob[0:64, :n], u1[:, :n], rbc[0:64, :n], ALU.mult, [uk1, 'rbc'], [ok])
                    else:
                        S.tt('pool', ob[0:64, :n], u1[:, :n], u2[:, :n], ALU.add, [uk1, uk2], [ok])
                    S.dma(STQ, dst, ob[0:64, :n], [ok], ())

                if is_lat and 'a' in a2_parts:
                    lowrank(0, False)
                    for h in range(4):
                        pb, pk = pB[h % 2], f'pB{h % 2}'
                        for c_ in range(2):
                            S.mm(pb[:, :n], Wqn[:, c_, h, :], cT[:, c_, :n], c_ == 0, c_ == 1, ['Wqn', 'cT'], [pk])
                        ob, ok = o_bf.next()
                        S.tt('dve', ob[:, :n], pb[:, :n], rbc[:, :n], ALU.mult, [pk, 'rbc'], [ok])
                        S.dma(STQ, QN[h][:, lo:lo + n], ob[:, :n], [ok], ())
                    for h in range(4):
                        for c_ in range(2):
                            S.mm(pB[0][0:64, :n], Wqr[:, c_, h, :], cT[:, c_, :n], c_ == 0, c_ == 1, ['Wqr', 'cT'], ['pB0'])
                        for c_ in range(2):
                            S.mm(pB[1][0:64, :n], Wqrot[:, c_, h, :], cT[:, c_, :n], c_ == 0, c_ == 1, ['Wqrot', 'cT'], ['pB1'])
                        rope_out(pB[0], 'pB0', pB[1], 'pB1', QR[h][:, lo:lo + n], True)
                S.enabled = 'b' in a2_parts
                lowrank(256, True)
                for h in range(4):
                    pb, pk = pB[h % 2], f'pB{h % 2}'
                    for c_ in range(2):
                        S.mm(pb[:, :n], Wkn[:, c_, h, :], cT[:, c_, :n], c_ == 0, c_ == 1, ['Wkn', 'cT'], [pk])
                    ob, ok = o_bf.next()
                    S.tt('dve', ob[:, :n], pb[:, :n], rbc[:, :n], ALU.mult, [pk, 'rbc'], [ok])
                    S.dma(STQ, KN[h][:, tok0:tok0 + n], ob[:, :n], [ok], ())
                Wv2 = Wv[:].rearrange("p c h d -> p c (h d)")
                for s_ in range(nsub):
                    pt, pk = pT[s_ % 2], f'pT{s_ % 2}'
                    for c_ in range(2):
                        S.mm(pt[:], cT[:, c_, s_ * 128:(s_ + 1) * 128], Wv2[:, c_, :], c_ == 0, c_ == 1, ['cT', 'Wv'], [pk])
                    ob, ok = o_bf.next()
                    S.act(ob[:], pt[:], AF.Copy, [pk, 'rtk'], [ok], scale=rtk[:, s_:s_ + 1])
                    S.dma(STQ, VV[tok0 + s_ * 128:tok0 + (s_ + 1) * 128, :], ob[:], [ok], ())
                S.enabled = 'c' in a2_parts
                fm_proj(pA[0], 'pA0', Wkpe, 'Wkpe', 0, 64)
                if is_lat:
                    fm_proj(pA[1], 'pA1', Wkrot, 'Wkrot', 0, 64)
                    rope_out(pA[0], 'pA0', pA[1], 'pA1', KR[:, tok0:tok0 + n], False)
                else:
                    ob, ok = o_bf.next()
                    S.copy('act', ob[0:64, :n], pA[0][0:64, :n], ['pA0'], [ok])
                    S.dma(STQ, KR[:, tok0:tok0 + n], ob[0:64, :n], [ok], ())
                S.enabled = 'd' in a2_parts
                for h in range(4):
                    pa, pk = pA[h % 2], f'pA{h % 2}'
                    fm_proj(pa, pk, Win, 'Win', 576 + h * 128, 128)
                    of, ok = o_f.next()
                    S.copy('act' if h % 2 else 'dve', of[:, :n], pa[:, :n], [pk], [ok])
                    S.dma(STQ, HQ[h][:, tok0:tok0 + n], of[:, :n], [ok], ())
                S.enabled = 'e' in a2_parts
                for s_ in range(nsub):
                    row0 = tok0 + s_ * 128

                    def tm_proj(ps, pk, col0):
                        for k in range(8):
                            S.mm(ps[:], hT_[:, k, s_ * 128:(s_ + 1) * 128], Win[:, k, col0:col0 + 512], k == 0, k == 7,
                                 [hk, 'Win'], [pk])
                    tm_proj(pT[0], 'pT0', 1088)
                    ob, ok = o_bf.next()
                    S.copy('dve', ob[:], pT[0][:], ['pT0'], [ok])
                    S.dma(STQ, HV[row0:row0 + 128, :], ob[:], [ok], ())
                    if is_lat:
                        tm_proj(pT[1], 'pT1', 1600)
                        th, thk = sgb.next(); uh, uhk = fb_.next()
                        S.act(th[:], pT[1][:], AF.Tanh, ['pT1'], [thk], scale=0.5)
                        S.act(uh[:], pT[1][:], AF.Copy, ['pT1'], [uhk], scale=0.5)
                        of, ok = o_f.next()
                        S.tt('pool', th[:], th[:], uh[:], ALU.mult, [thk, uhk], [thk])
                        S.tt('pool', of[:], th[:], uh[:], ALU.add, [thk, uhk], [ok])
                        S.dma(STQ, HG[row0 - L:row0 - L + 128, :], of[:], [ok], ())
                    fts = []
                    for d_ in range(2):
                        pt, pk = pT[d_], f'pT{d_}'
                        tm_proj(pt, pk, 2112 + d_ * 512)
                        sg, sk = sgb.next(); ff, fk = fb_.next()
                        S.act(sg[:], pt[:], AF.Tanh, [pk], [sk], scale=0.5)
                        S.tt('dve', ff[:], sg[:], oml[:, d_, :], ALU.mult, [sk, 'oml'], [fk])
                        S.tt('pool', ff[:], ff[:], lbt[:, d_, :], ALU.add, [fk, 'lbt'], [fk])
                        fts.append((ff, fk))
                    for d_ in range(2):
                        ff, fk = fts[d_]
                        of, ok = o_f.next()
                        S.act(of[:], ff[:], AF.Ln, [fk], [ok])
                        S.dma(STQ, GG[d_][row0:row0 + 128, :], of[:], [ok], ())
                        of2, ok2 = o_f.next()
                        S.ts('pool', of2[:], ff[:], -1.0, 1.0, ALU.mult, ALU.add, [fk], [ok2])
                        S.dma(STQ, KG[d_][row0:row0 + 128, :], of2[:], [ok2], ())
            S.enabled = True
            S.emit()
        if stop_after == 2:
            return nc

        with ExitStack() as es:
            def TT(name, shape, dt):
                return es.enter_context(nc.sbuf_tensor(name, shape, dt))

            def PP(name, shape, dt):
                return es.enter_context(nc.psum_tensor(name, shape, dt))

            bt = []
            for d_ in range(2):
                bt.append(dict(
                    g=Rot(TT, f"bg{d_}", [128, 512], F32, 2), kg=Rot(TT, f"bkg{d_}", [128, 512], F32, 2),
                    v=Rot(TT, f"bv{d_}", [128, 512], BF16, 3), hq=Rot(TT, f"bhq{d_}", [128, 4, 128], F32, 2),
                    Ek=Rot(TT, f"bEk{d_}", [128, 512], F32, 2), EqT=Rot(TT, f"bEq{d_}", [128, 4, 128], F32, 2),
                    eb=Rot(TT, f"beb{d_}", [128, 4, 2], F32, 3), K2=Rot(TT, f"bK2{d_}", [128, 512], BF16, 3),
                    QsT=Rot(TT, f"bQs{d_}", [128, 4, 128], BF16, 3), K2T=Rot(TT, f"bK2T{d_}", [128, 4, 128], BF16, 2),
                    Am=Rot(TT, f"bAm{d_}", [128, 4, 128], BF16, 3), S=TT(f"bS{d_}", [128, 4, 128], F32),
                    Sp=Rot(TT, f"bSp{d_}", [128, 4, 128], F32, 2), Spb=Rot(TT, f"bSpb{d_}", [128, 4, 128], BF16, 2),
                    osb=Rot(TT, f"bos{d_}", [64, 512], F32, 3)))
                S.memset('dve', bt[d_]['S'][:], 0.0, [f'bS{d_}'])
            pD1 = PP("pD1", [128, 512], F32); pD2 = PP("pD2", [128, 4, 128], F32)
            pKT = PP("pKT", [128, 4, 128], BF16); pBL = PP("pBL", [128, 4, 2], F32)
            pAT = PP("pAT", [128, 4, 128], F32)
            pOr = Rot(PP, "pO", [64, 4, 128], F32, 1)
            pSNr = Rot(PP, "pSN", [128, 4, 128], F32, 2)

            def hgrn_pre(cx, ti, d_):
                B_ = bt[d_]
                is_lat = ti >= 2
                row0 = ti * 128
                U_, R_, M_ = (Uf, Rf, Mf) if d_ == 0 else (Ub, Rb, Mb)
                uk, rk, mk_ = ('Uf', 'Rf', 'Mf') if d_ == 0 else ('Ub', 'Rb', 'Mb')
                g, gk = B_['g'].next(); kg, kgk = B_['kg'].next(); v, vk = B_['v'].next(); hq, hqk = B_['hq'].next()
                S.dma('sp', g[:], GG[d_][row0:row0 + 128, :], (), [gk])
                S.dma('sp', kg[:], KG[d_][row0:row0 + 128, :], (), [kgk])
                S.dma('sp', v[:], HV[row0:row0 + 128, :], (), [vk])
                S.dma('sp', hq[:], HQ[:, :, row0:row0 + 128].rearrange("h p t -> p h t"), (), [hqk])
                Ek, Ekk = B_['Ek'].next(); EqT, Eqk = B_['EqT'].next(); eb, ebk = B_['eb'].next()
                K2, K2k = B_['K2'].next(); QsT, Qsk = B_['QsT'].next(); K2T, K2Tk = B_['K2T'].next()
                S.mm(pD1[:], U_[:], g[:], True, True, [uk, gk], ['pD1'])
                for h in range(4):
                    S.mm(pD2[:, h, :], g[:, h * 128:(h + 1) * 128], R_[:], True, True, [gk, rk], ['pD2'])
                for h in range(4):
                    S.mm(pBL[:, h, :], g[:, h * 128:(h + 1) * 128], Ind[:], True, True, [gk, 'Ind'], ['pBL'])
                S.act(Ek[:], pD1[:], AF.Exp, ['pD1'], [Ekk])
                S.act(EqT[:], pD2[:], AF.Exp, ['pD2'], [Eqk])
                S.act(eb[:], pBL[:], AF.Exp, ['pBL'], [ebk])
                S.tt('dve', K2[:], kg[:], Ek[:], ALU.mult, [kgk, Ekk], [K2k])
                S.tt('pool', QsT[:], hq[:], EqT[:], ALU.mult, [hqk, Eqk], [Qsk])
                cx.update(v=v, vk=vk, eb=eb, ebk=ebk, K2=K2, K2k=K2k, QsT=QsT, Qsk=Qsk)
                if is_lat:
                    Am, Amk = B_['Am'].next()
                    for h in range(4):
                        S.tr(pKT[:, h, :], K2[:, h * 128:(h + 1) * 128], identb[:], [K2k, 'identb'], ['pKT'])
                    S.copy('act', K2T[:], pKT[:], ['pKT'], [K2Tk])
                    for h in range(4):
                        S.mm(pAT[:, h, :], K2T[:, h, :], QsT[:, h, :], True, True, [K2Tk, Qsk], ['pAT'])
                    S.tt('dve', Am[:], pAT[:], M_[:].unsqueeze(1).to_broadcast([128, 4, 128]), ALU.mult,
                         ['pAT', mk_], [Amk])
                    cx.update(Am=Am, Amk=Amk)

            def hgrn_chain(cx, ti, d_):
                B_ = bt[d_]
                is_lat = ti >= 2
                row0 = ti * 128
                St_, Sk = B_['S'], f'bS{d_}'
                v, vk, eb, ebk, K2, K2k, QsT, Qsk = (cx[k_] for k_ in ('v', 'vk', 'eb', 'ebk', 'K2', 'K2k', 'QsT', 'Qsk'))
                for c_ in ((0, 1) if d_ == 0 else (1, 0)):
                    lo_, hi_ = c_ * 64, (c_ + 1) * 64
                    ebb = eb[:, :, c_:c_ + 1].to_broadcast([128, 4, 128])
                    Sp, Spk = B_['Sp'].next()
                    psn, psnk = pSNr.next()
                    for h in range(4):
                        S.mm(psn[:, h, :], K2[lo_:hi_, h * 128:(h + 1) * 128], v[lo_:hi_, h * 128:(h + 1) * 128], True, True,
                             [K2k, vk], [psnk])
                    S.tt('pool', Sp[:], St_[:], ebb, ALU.mult, [Sk, ebk], [Spk])
                    if is_lat:
                        Am, Amk = cx['Am'], cx['Amk']
                        Spb, Spbk = B_['Spb'].next()
                        S.tt('dve', Spb[:], St_[:], ebb, ALU.mult, [Sk, ebk], [Spbk])
                    S.tt('dve', St_[:], Sp[:], psn[:], ALU.add, [Spk, psnk], [Sk])
                    if is_lat:
                        po, pok = pOr.next()
                        for h in range(4):
                            S.mm(po[:, h, :], QsT[:, h, lo_:hi_], Spb[:, h, :], True, False, [Qsk, Spbk], [pok])
                            S.mm(po[:, h, :], Am[lo_:hi_, h, lo_:hi_], v[lo_:hi_, h * 128:(h + 1) * 128], False, True,
                                 [Amk, vk], [pok])
                        ob, obk = B_['osb'].next()
                        S.copy('act', ob[:], po[:].rearrange("p h d -> p (h d)"), [pok], [obk])
                        r_ = row0 - L + lo_
                        S.dma('sp', OO[d_][r_:r_ + 64, :], ob[:], [obk], ())

            fwd_order = list(range(NT))
            bwd_order = [1, 0] + list(range(NT - 1, 1, -1))
            cxs = [[dict() for _ in range(NT)] for _ in range(2)]
            hgrn_pre(cxs[0][0], fwd_order[0], 0)
            hgrn_pre(cxs[1][0], bwd_order[0], 1)
            for i_ in range(NT):
                if i_ + 1 < NT:
                    hgrn_pre(cxs[0][i_ + 1], fwd_order[i_ + 1], 0)
                    hgrn_pre(cxs[1][i_ + 1], bwd_order[i_ + 1], 1)
                hgrn_chain(cxs[0][i_], fwd_order[i_], 0)
                hgrn_chain(cxs[1][i_], bwd_order[i_], 1)
            S.emit()
        if stop_after == 3:
            return nc

        with ExitStack() as es:
            def TT(name, shape, dt):
                return es.enter_context(nc.sbuf_tensor(name, shape, dt))

            def PP(name, shape, dt):
                return es.enter_context(nc.psum_tensor(name, shape, dt))

            KNs = TT("KNs", [128, 4, T], BF16); KRs = TT("KRs", [128, T], BF16); Vs = TT("Vs", [128, NT, 512], BF16)
            sqKR = TT("sqKR", [64, T], BF16)
            sqn = TT("sqn", [128, 512], BF16); sqr = TT("sqr", [64, 512], BF16)
            km2 = TT("km2", [128, 4], F32); tmx = TT("tmx", [128, 1], F32); nsh = TT("nsh", [128, 1], F32)
            qnr = Rot(TT, "cqn", [128, 512], BF16, 2); qrr = Rot(TT, "cqr", [128, 512], BF16, 2)
            PTr = Rot(TT, "cPT", [128, 1024], BF16, 4); osr = Rot(TT, "cos", [128, 512], BF16, 2)
            rinv = TT("rinv", [128, 512], F32)
            racc = [TT(f"racc{i}", [128, 1024], F32) for i in range(2)]
            rsum = [TT(f"rsum{i}", [128, 512], F32) for i in range(2)]
            pSc = [PP(f"pSc{i}", [128, 1024], F32) for i in range(2)]
            pOa = PP("pOa", [128, 512], F32); pRs = PP("pRs", [128, 512], F32); pNm = PP("pNm", [128, 512], F32)
            for h in range(4):
                S.dma('sp', KNs[:, h, :], KN[h], (), ['KNs'])
            S.memset('pool', KRs[64:128, :], 0.0, ['KRs'])
            S.dma('sp', KRs[0:64, :], KR, (), ['KRs'])
            for i_ in range(2):
                S.memset('pool', qrr.t[i_][64:128, :], 0.0, [qrr.k[i_]])
            VVv = VV.rearrange("(t p) n -> p t n", p=128)
            for j in range(0, NT, 4):
                je = min(NT, j + 4)
                S.dma('sp', Vs[:, j:je, :], VVv[:, j:je, :], (), ['Vs'])
            S.memset('dve', km2[:], 0.0, ['km2'])
            S.act(sqKR[:], KRs[0:64, :], AF.Square, ['KRs'], ['sqKR'])
            for j0 in range(0, T, 512):
                w_ = min(512, T - j0)
                for h in range(4):
                    S.act(sqn[:, :w_], KNs[:, h, j0:j0 + w_], AF.Square, ['KNs'], ['sqn'])
                    S.mm(pNm[:, :w_], onesb[:], sqn[:, :w_], True, False, ['onesb', 'sqn'], ['pNm'])
                    S.mm(pNm[:, :w_], onesb[0:64, :], sqKR[:, j0:j0 + w_], False, True, ['onesb', 'sqKR'], ['pNm'])
                    S.op('dve', (lambda ww: (lambda e: e.reduce_max(out=tmx[:], in_=pNm[:, :ww], axis=AX.X)))(w_),
                         ['pNm'], ['tmx'])
                    S.tt('dve', km2[:, h:h + 1], km2[:, h:h + 1], tmx[:], ALU.max, ['km2', 'tmx'], ['km2'])
            for g_ in range(8):
                q0 = g_ * 512
                for h in range(4):
                    qn, qnk = qnr.next(); qr, qrk = qrr.next()
                    S.dma('sp', qn[:], QN[h][:, q0:q0 + 512], (), [qnk])
                    S.dma('sp', qr[0:64, :], QR[h][:, q0:q0 + 512], (), [qrk])
                    S.act(sqn[:], qn[:], AF.Square, [qnk], ['sqn'])
                    S.act(sqr[:], qr[0:64, :], AF.Square, [qrk], ['sqr'])
                    S.mm(pNm[:], onesb[:], sqn[:], True, False, ['onesb', 'sqn'], ['pNm'])
                    S.mm(pNm[:], onesb[0:64, :], sqr[:], False, True, ['onesb', 'sqr'], ['pNm'])
                    S.op('dve', lambda e: e.reduce_max(out=tmx[:], in_=pNm[:], axis=AX.X), ['pNm'], ['tmx'])
                    S.ts('dve', nsh[:], tmx[:], km2[:, h:h + 1], -0.5 * SCALE, ALU.add, ALU.mult, ['tmx', 'km2'], ['nsh'])

                    NP_ = NT // 2

                    def qk(j):
                        ps, pk = pSc[j % 2], f'pSc{j % 2}'
                        for u_ in range(2):
                            kt = 2 * j + u_
                            S.mm(ps[:, u_ * 512:(u_ + 1) * 512], KNs[:, h, kt * 128:(kt + 1) * 128], qn[:], True, False,
                                 ['KNs', qnk], [pk])
                            S.mm(ps[:, u_ * 512:(u_ + 1) * 512], KRs[:, kt * 128:(kt + 1) * 128], qr[:], False, True,
                                 ['KRs', qrk], [pk])
                    qk(0)
                    for j in range(NP_):
                        if j + 1 < NP_:
                            qk(j + 1)
                        ps, pk = pSc[j % 2], f'pSc{j % 2}'
                        PT, ptk = PTr.next()
                        S.act(PT[:], ps[:], AF.Exp, [pk, 'nsh'], [ptk], bias=nsh[:], scale=SCALE)
                        for u_ in range(2):
                            kt = 2 * j + u_
                            S.mm(pOa[:], Vs[:, kt, h * 128:(h + 1) * 128], PT[:, u_ * 512:(u_ + 1) * 512],
                                 kt == 0, kt == NT - 1, ['Vs', ptk], ['pOa'])
                        ae = 'dve' if j % 2 == 0 else 'pool'
                        ra, rak = racc[j % 2], f'racc{j % 2}'
                        if j < 2:
                            S.copy(ae, ra[:], PT[:], [ptk], [rak])
                        else:
                            S.tt(ae, ra[:], ra[:], PT[:], ALU.add, [rak, ptk], [rak])
                    S.tt('dve', rsum[0][:], racc[0][:, 0:512], racc[0][:, 512:1024], ALU.add, ['racc0'], ['rsum0'])
                    S.tt('pool', rsum[1][:], racc[1][:, 0:512], racc[1][:, 512:1024], ALU.add, ['racc1'], ['rsum1'])
                    S.mm(pRs[:], onesf[:], rsum[0][:], True, False, ['onesf', 'rsum0'], ['pRs'])
                    S.mm(pRs[:], onesf[:], rsum[1][:], False, True, ['onesf', 'rsum1'], ['pRs'])
                    S.op('dve', lambda e: e.reciprocal(out=rinv[:], in_=pRs[:]), ['pRs'], ['rinv'])
                    ob, obk = osr.next()
                    S.tt('dve', ob[:], pOa[:], rinv[:], ALU.mult, ['pOa', 'rinv'], [obk])
                    S.dma('sp', MIXA[h][:, q0:q0 + 512], ob[:], [obk], ())
            S.emit()
        if stop_after == 4:
            return nc

        with ExitStack() as es:
            def TT(name, shape, dt):
                return es.enter_context(nc.sbuf_tensor(name, shape, dt))

            def PP(name, shape, dt):
                return es.enter_context(nc.psum_tensor(name, shape, dt))

            Wout = TT("Wout", [128, 8, D], BF16)
            mD = TT("mD", [128, 3, D], F32)
            gon = TT("gon", [128, 128], F32)
            woutv = w_out.rearrange("(k p) n -> p k n", p=128)
            for k in range(8):
                S.dma('pool', Wout[:, k, :], woutv[:, k, :], (), ['Wout'])
            for i, mi in enumerate((4, 5, 6)):
                S.dma('sp', mD[:, i, :], MOD[mi], (), ['mD'])
            S.dma('sp', gon[:], g_on.partition_broadcast(128), (), ['gon'])
            ofr = Rot(TT, "dof", [128, 512], F32, 3); obr = Rot(TT, "dob", [128, 512], F32, 3); hgr = Rot(TT, "dhg", [128, 512], F32, 3)
            mar = Rot(TT, "dma_", [128, 4, 128], BF16, 4); xtr = Rot(TT, "dxt", [128, D], F32, 5)
            osumr = Rot(TT, "osum", [128, 512], F32, 2); osqr = Rot(TT, "osq", [128, 512], F32, 2); ss4r = Rot(TT, "ss4", [128, 4], F32, 3)
            tBr = Rot(TT, "tB", [128, 512], F32, 2); hgbr = Rot(TT, "hgb", [128, 512], BF16, 3); mixBr = Rot(TT, "mixB", [128, 4, 128], BF16, 2)
            tmpDr = Rot(TT, "tmpD", [128, D], F32, 2); x1r = Rot(TT, "dx1", [128, D], F32, 3)
            junkD = TT("junkD", [128, D], BF16); ssDr = Rot(TT, "ssD", [128, 1], F32, 3)
            t2Dr = Rot(TT, "t2D", [128, D], F32, 2); h2r = Rot(TT, "h2", [128, D], BF16, 3); h2Tr = Rot(TT, "dh2T", [128, 8, 128], BF16, 3)
            pTBr = Rot(PP, "pTB", [128, 4, 128], BF16, 2)
            pLOr = Rot(PP, "pLO", [128, 512], F32, 4)
            pT8r = Rot(PP, "pT8", [128, 8, 128], BF16, 2)

            def d1_s0(cx, ti):
                r0 = ti * 128
                for nm, rr, src in (('of', ofr, OO[0][r0:r0 + 128, :]), ('ob', obr, OO[1][r0:r0 + 128, :]),
                                    ('hg', hgr, HG[r0:r0 + 128, :]),
                                    ('ma', mar, MIXA[:, :, r0:r0 + 128].rearrange("h p t -> p h t")),
                                    ('xt', xtr, x[r0:r0 + 128, :])):
                    cx[nm], cx[nm + 'k'] = rr.next()
                    S.dma('sp', cx[nm][:], src, (), [cx[nm + 'k']])

            def d1_s1(cx, ti):
                of, ofk, ob, obk, hg, hgk = cx['of'], cx['ofk'], cx['ob'], cx['obk'], cx['hg'], cx['hgk']
                osum, osumk = osumr.next(); osq, osqk = osqr.next(); ss4, ss4k = ss4r.next(); tB, tBk = tBr.next()
                hgb, hgbk = hgbr.next()
                cx['hgb'], cx['hgbk'] = hgb, hgbk
                S.tt('pool', osum[:], of[:], ob[:], ALU.add, [ofk, obk], [osumk])
                S.tt('pool', osq[:], osum[:], osum[:], ALU.mult, [osumk], [osqk])
                S.op('dve', (lambda o_, i_: (lambda e: e.reduce_sum(out=o_[:], in_=i_[:].rearrange("p (h d) -> p h d", h=4),
                                                                    axis=AX.X)))(ss4, osq), [osqk], [ss4k])
                S.act(ss4[:], ss4[:], AF.Sqrt, [ss4k], [ss4k], bias=EPS, scale=1.0 / 128)
                S.op('dve', (lambda o_: (lambda e: e.reciprocal(out=o_[:], in_=o_[:])))(ss4), [ss4k], [ss4k])
                o3 = osum[:].rearrange("p (h d) -> p h d", h=4)
                t3 = tB[:].rearrange("p (h d) -> p h d", h=4)
                S.tt('dve', t3, o3, ss4[:].unsqueeze(2).to_broadcast([128, 4, 128]), ALU.mult, [osumk, ss4k], [tBk])
                S.tt('pool', t3, t3, gon[:].unsqueeze(1).to_broadcast([128, 4, 128]), ALU.mult, [tBk, 'gon'], [tBk])
                S.tt('pool', hgb[:], tB[:], hg[:], ALU.mult, [tBk, hgk], [hgbk])

            def d1_s2(cx, ti):
                hgb, hgbk, ma, mak = cx['hgb'], cx['hgbk'], cx['ma'], cx['mak']
                pTB, pTBk = pTBr.next(); mixB, mixBk = mixBr.next()
                for h in range(4):
                    S.tr(pTB[:, h, :], hgb[:, h * 128:(h + 1) * 128], identb[:], [hgbk, 'identb'], [pTBk])
                S.copy('act', mixB[:], pTB[:], [pTBk], [mixBk])
                cx['pLO'] = []
                for hf in range(2):
                    pl, plk = pLOr.next()
                    cx['pLO'].append((pl, plk))
                    for k in range(4):
                        S.mm(pl[:], ma[:, k, :], Wout[:, k, hf * 512:(hf + 1) * 512], k == 0, False, [mak, 'Wout'], [plk])
                    for k in range(4):
                        S.mm(pl[:], mixB[:, k, :], Wout[:, 4 + k, hf * 512:(hf + 1) * 512], False, k == 3,
                             [mixBk, 'Wout'], [plk])

            def d1_s3(cx, ti):
                r0 = ti * 128
                xt_, xtk = cx['xt'], cx['xtk']
                tmpD, tmpDk = tmpDr.next(); ssD, ssDk = ssDr.next(); t2D, t2Dk = t2Dr.next(); h2, h2k = h2r.next()
                x1, x1k = x1r.next()
                cx['h2'], cx['h2k'] = h2, h2k
                for hf in range(2):
                    pl, plk = cx['pLO'][hf]
                    S.tt('dve', tmpD[:, hf * 512:(hf + 1) * 512], pl[:], mD[:, 0, hf * 512:(hf + 1) * 512], ALU.mult,
                         [plk, 'mD'], [tmpDk])
                S.tt('pool', x1[:], tmpD[:], xt_[:], ALU.add, [tmpDk, xtk], [x1k])
                S.dma('sp', X1[r0:r0 + 128, :], x1[:], [x1k], ())
                S.memset('dve', ssD[:], 0.0, [ssDk])
                S.act(junkD[:], x1[:], AF.Square, [x1k], ['junkD', ssDk], accum=ssD[:])
                S.act(ssD[:], ssD[:], AF.Sqrt, [ssDk], [ssDk], bias=EPS, scale=1.0 / D)
                S.op('dve', (lambda o_: (lambda e: e.reciprocal(out=o_[:], in_=o_[:])))(ssD), [ssDk], [ssDk])
                S.stt('dve', t2D[:], x1[:], ssD[:, 0:1], mD[:, 1, :], ALU.mult, ALU.mult, [x1k, ssDk, 'mD'], [t2Dk])
                S.tt('pool', h2[:], t2D[:], mD[:, 2, :], ALU.add, [t2Dk, 'mD'], [h2k])

            def d1_s4(cx, ti):
                r0 = ti * 128
                h2, h2k = cx['h2'], cx['h2k']
                pT8, pT8k = pT8r.next(); hT2, hT2k = h2Tr.next()
                for k in range(8):
                    S.tr(pT8[:, k, :], h2[:, k * 128:(k + 1) * 128], identb[:], [h2k, 'identb'], [pT8k])
                S.copy('dve' if ti % 2 else 'act', hT2[:], pT8[:], [pT8k], [hT2k])
                S.dma('sp', H2T[:, :, r0:r0 + 128], hT2[:], [hT2k], ())
            run_pipeline(N // 128, [d1_s0, d1_s1, d1_s2, d1_s3, d1_s4])
            S.emit()
        if stop_after == 5:
            return nc

        with ExitStack() as es:
            def TT(name, shape, dt):
                return es.enter_context(nc.sbuf_tensor(name, shape, dt))

            def PP(name, shape, dt):
                return es.enter_context(nc.psum_tensor(name, shape, dt))

            Wg = TT("Wg", [128, 8, DFF], BF16); Wu = TT("Wu", [128, 8, DFF], BF16); Wd = TT("Wd", [128, NCF, D], BF16)
            mE = TT("mE", [128, 2, D], F32)
            wgv = w_gate.rearrange("(k p) n -> p k n", p=128); wuv = w_up.rearrange("(k p) n -> p k n", p=128)
            wdv = w_down.rearrange("(c p) n -> p c n", p=128)
            for k in range(8):
                S.dma('pool', Wg[:, k, :], wgv[:, k, :], (), ['Wg'])
                S.dma('pool', Wu[:, k, :], wuv[:, k, :], (), ['Wu'])
            for c_ in range(NCF):
                S.dma('pool', Wd[:, c_, :], wdv[:, c_, :], (), ['Wd'])
            S.dma('sp', mE[:, 0, :], MOD[7], (), ['mE'])
            S.dma('sp', mE[:, 1, :], g_fin.partition_broadcast(128), (), ['mE'])
            GN = 512
            h2g = Rot(TT, "eh2", [128, 8, GN], BF16, 1)
            aT = TT("aT", [128, NCF, GN], BF16)
            sgr = Rot(TT, "esg", [128, GN], F32, 2)
            x1r = Rot(TT, "ex1", [128, D], F32, 1); tmr = Rot(TT, "etm", [128, D], F32, 2)
            ssE = TT("ssE", [128, 1], F32)
            pG = [PP(f"pG{i}", [128, GN], F32) for i in range(2)]
            pU = [PP(f"pU{i}", [128, GN], F32) for i in range(2)]
            pY = [PP(f"pY{i}", [128, 512], F32) for i in range(2)]
            for g_ in range(N // GN):
                q0 = g_ * GN
                hh, hhk = h2g.next()
                S.dma('sp', hh[:], H2T[:, :, q0:q0 + GN], (), [hhk])
                for c_ in range(NCF):
                    pg, pgk = pG[c_ % 2], f'pG{c_ % 2}'
                    pu, puk = pU[c_ % 2], f'pU{c_ % 2}'
                    for k in range(8):
                        S.mm(pg[:], Wg[:, k, c_ * 128:(c_ + 1) * 128], hh[:, k, :], k == 0, k == 7, ['Wg', hhk], [pgk])
                    for k in range(8):
                        S.mm(pu[:], Wu[:, k, c_ * 128:(c_ + 1) * 128], hh[:, k, :], k == 0, k == 7, ['Wu', hhk], [puk])
                    sg, sgk = sgr.next()
                    S.act(sg[:], pg[:], AF.Silu, [pgk], [sgk])
                    S.tt('dve', aT[:, c_, :], sg[:], pu[:], ALU.mult, [sgk, puk], ['aT'])
                for sb in range(GN // 128):
                    r0 = q0 + sb * 128
                    x1, x1k = x1r.next(); tm, tmk = tmr.next()
                    S.dma('sp', x1[:], X1[r0:r0 + 128, :], (), [x1k])
                    for hf in range(2):
                        for c_ in range(NCF):
                            S.mm(pY[hf][:], aT[:, c_, sb * 128:(sb + 1) * 128], Wd[:, c_, hf * 512:(hf + 1) * 512],
                                 c_ == 0, c_ == NCF - 1, ['aT', 'Wd'], [f'pY{hf}'])
                        S.tt('dve', tm[:, hf * 512:(hf + 1) * 512], pY[hf][:], mE[:, 0, hf * 512:(hf + 1) * 512], ALU.mult,
                             [f'pY{hf}', 'mE'], [tmk])
                    S.tt('pool', x1[:], tm[:], x1[:], ALU.add, [tmk, x1k], [x1k])
                    S.memset('dve', ssE[:], 0.0, ['ssE'])
                    S.act(tm[:], x1[:], AF.Square, [x1k, tmk], [tmk, 'ssE'], accum=ssE[:])
                    S.act(ssE[:], ssE[:], AF.Sqrt, ['ssE'], ['ssE'], bias=EPS, scale=1.0 / D)
                    S.op('dve', lambda e: e.reciprocal(out=ssE[:], in_=ssE[:]), ['ssE'], ['ssE'])
                    S.stt('dve', tm[:], x1[:], ssE[:, 0:1], mE[:, 1, :], ALU.mult, ALU.mult, [x1k, 'ssE', 'mE'], [tmk])
                    S.dma('sp', out[r0:r0 + 128, :], tm[:], [tmk], ())
            S.emit()
    return nc


_NC_CACHE = {}


def kernel(**inputs):
    if 'nc' not in _NC_CACHE:
        _NC_CACHE['nc'] = build_nc()
    nc = _NC_CACHE['nc']
    f = lambda a: np.ascontiguousarray(np.asarray(a, dtype=np.float32))
    shared = {
        "c_ctx": f(inputs["c_ctx"]), "w_mod": f(inputs["w_mod"][0]), "b_mod": f(inputs["b_mod"][0]),
        "g_norm_mix": f(inputs["g_norm_mix"][0]), "g_norm_ffn": f(inputs["g_norm_ffn"][0]),
        "w_in": f(inputs["w_in"][0]), "g_q_norm": f(inputs["g_q_norm"][0]), "w_uq": f(inputs["w_uq"][0]),
        "g_kv_norm": f(inputs["g_kv_norm"][0]), "w_ukv": f(inputs["w_ukv"][0]),
        "lb_fwd": f(inputs["lb_fwd"]), "lb_bwd": f(inputs["lb_bwd"]), "g_hgrn_norm": f(inputs["g_hgrn_norm"][0]),
        "w_out": f(inputs["w_out"][0]), "w_gate": f(inputs["w_gate"][0]), "w_up": f(inputs["w_up"][0]),
        "w_down": f(inputs["w_down"][0]), "g_final": f(inputs["g_final"]),
    }
    xs, cs, ctxs = f(inputs["x"]), f(inputs["c"]), f(inputs["ctx"])
    in_maps = []
    for b in range(NB):
        m = dict(shared)
        m["x"] = xs[b]; m["c"] = cs[b]; m["ctx"] = ctxs[b]
        in_maps.append(m)
    res = run_bass_kernel_spmd(nc, in_maps, core_ids=list(range(NB)))
    return np.stack([np.asarray(r["out"], dtype=np.float32) for r in res.results], axis=0)
```

```python
import math
from contextlib import ExitStack

import numpy as np
import concourse.bass as bass
import concourse.mybir as mybir
from concourse.bass_utils import run_bass_kernel_spmd

F32 = mybir.dt.float32
BF16 = mybir.dt.bfloat16
AF = mybir.ActivationFunctionType
ALU = mybir.AluOpType
AX = mybir.AxisListType

NB, N, L, D = 8, 4096, 256, 1024
T = N + L
NT = T // 128
DFF = 2816
NCF = DFF // 128
EPS = 1e-6
INC = 3136
SCALE = 1.0 / math.sqrt(192.0)


class Sched:
    ENG = ('pe', 'act', 'dve', 'pool', 'sp')

    def __init__(self, nc, n_dma_sems=24):
        self.nc = nc
        self.ops = {e: [] for e in self.ENG}
        self.sems = {}
        for e in ('pe', 'act', 'dve', 'pool'):
            self.sems[('e', e)] = nc.alloc_semaphore(name=f"s_{e}")
        for i in range(n_dma_sems):
            self.sems[('d', i)] = nc.alloc_semaphore(name=f"s_dma{i}")
        self.nw = 6
        for i in range(self.nw):
            self.sems[('w', i)] = nc.alloc_semaphore(name=f"s_swdma{i}")
        self.wnext = 0
        self.cnt = {k: 0 for k in self.sems}
        self.nd = n_dma_sems
        self.dnext = 0
        self.waited = {e: {} for e in self.ENG}
        self.res = {}
        self.enabled = True

    def _deps(self, reads, writes):
        t = []
        for r in reads:
            st = self.res.get(r)
            if st and st['w']:
                t.append(st['w'])
        for w in writes:
            st = self.res.get(w)
            if st:
                if st['w']:
                    t.append(st['w'])
                t.extend(st['r'].values())
        return t

    def _need(self, eng, tickets):
        best = {}
        for key, val in tickets:
            if key == ('e', 'pe') and eng == 'pe':
                continue
            if self.waited[eng].get(key, 0) >= val:
                continue
            if best.get(key, 0) < val:
                best[key] = val
        for key, val in best.items():
            self.waited[eng][key] = val
        return list(best.items())

    def _commit(self, ticket, reads, writes):
        for r in reads:
            st = self.res.setdefault(r, {'w': None, 'r': {}})
            st['r'][ticket[0]] = ticket
        for w in writes:
            self.res[w] = {'w': ticket, 'r': {}}

    def op(self, eng, fn, reads=(), writes=()):
        if not self.enabled:
            return None
        waits = self._need(eng, self._deps(reads, writes))
        key = ('e', eng)
        self.cnt[key] += 1
        ticket = (key, self.cnt[key])
        self.ops[eng].append((waits, fn, key, 1))
        self._commit(ticket, reads, writes)
        return ticket

    def dma(self, q, out, in_, reads=(), writes=()):
        if not self.enabled:
            return None
        if q == 'pool':
            key = ('w', self.wnext)
            self.wnext = (self.wnext + 1) % self.nw
        else:
            key = ('d', self.dnext)
            self.dnext = (self.dnext + 1) % self.nd
        tickets = self._deps(reads, writes)
        if self.cnt[key] > 0:
            tickets.append((key, self.cnt[key]))
        waits = self._need(q, tickets)
        self.cnt[key] += 16
        ticket = (key, self.cnt[key])
        self.ops[q].append((waits, lambda e: e.dma_start(out=out, in_=in_), key, 16))
        self._commit(ticket, reads, writes)
        return ticket

    def barrier(self):
        allt = [(k, v) for k, v in self.cnt.items() if v > 0]
        for e in self.ENG:
            waits = self._need(e, allt)
            if waits:
                self.ops[e].append((waits, None, None, 0))

    def emit(self):
        self.barrier()
        with self.nc.Block() as block:
            def mk(engname):
                def body(e):
                    for waits, fn, semkey, inc in self.ops[engname]:
                        for key, val in waits:
                            e.wait_ge(self.sems[key], val)
                        if fn is not None:
                            fn(e).then_inc(self.sems[semkey], inc)
                return body
            block.tensor(mk('pe'))
            block.scalar(mk('act'))
            block.vector(mk('dve'))
            block.gpsimd(mk('pool'))
            block.sync(mk('sp'))
        self.ops = {e: [] for e in self.ENG}

    def mm(self, out, lhsT, rhs, start, stop, r, w):
        return self.op('pe', lambda e: e.matmul(out, lhsT=lhsT, rhs=rhs, start=start, stop=stop), r, w)

    def tr(self, out, in_, ident, r, w):
        return self.op('pe', lambda e: e.transpose(out, in_, ident), r, w)

    def act(self, out, in_, func, r, w, bias=None, scale=None, accum=None):
        kw = {}
        if bias is not None:
            kw['bias'] = bias
        if scale is not None:
            kw['scale'] = scale
        if accum is not None:
            kw['accum_out'] = accum
        return self.op('act', lambda e: e.activation(out=out, in_=in_, func=func, **kw), r, w)

    def tt(self, eng, out, in0, in1, op, r, w):
        return self.op(eng, lambda e: e.tensor_tensor(out=out, in0=in0, in1=in1, op=op), r, w)

    def ts(self, eng, out, in0, s1, s2, op0, op1, r, w):
        if s2 is None:
            return self.op(eng, lambda e: e.tensor_scalar(out=out, in0=in0, scalar1=s1, scalar2=None, op0=op0), r, w)
        return self.op(eng, lambda e: e.tensor_scalar(out=out, in0=in0, scalar1=s1, scalar2=s2, op0=op0, op1=op1), r, w)

    def stt(self, eng, out, in0, scalar, in1, op0, op1, r, w):
        return self.op(eng, lambda e: e.scalar_tensor_tensor(out=out, in0=in0, scalar=scalar, in1=in1, op0=op0, op1=op1), r, w)

    def copy(self, eng, out, in_, r, w):
        if eng == 'act':
            return self.act(out, in_, AF.Copy, r, w)
        return self.op(eng, lambda e: e.tensor_copy(out=out, in_=in_), r, w)

    def memset(self, eng, ap, val, w):
        return self.op(eng, lambda e: e.memset(ap, val), (), w)


def run_pipeline(n, stages):
    ctxs = [dict() for _ in range(n)]
    K = len(stages)
    for t in range(n + K - 1):
        for k in range(K - 1, -1, -1):
            i = t - k
            if 0 <= i < n:
                stages[k](ctxs[i], i)


class Rot:
    def __init__(self, alloc, name, shape, dt, n):
        self.t = [alloc(f"{name}{i}", shape, dt) for i in range(n)]
        self.k = [f"{name}{i}" for i in range(n)]
        self.i = 0

    def next(self):
        j = self.i % len(self.t)
        self.i += 1
        return self.t[j], self.k[j]


def _rstd(S, ss, out, dim, tag):
    S.act(out, ss, AF.Sqrt, [tag + 'ss'], [tag + 'rs'], bias=EPS, scale=1.0 / dim)
    S.op('dve', lambda e: e.reciprocal(out=out, in_=out), [tag + 'rs'], [tag + 'rs'])


def build_nc(stop_after=None, debug=False, a2_groups=None, a2_parts='abcde', STQ='sp'):
    nc = bass.Bass("TRN2", target_bir_lowering=False)
    S = Sched(nc)

    def din(name, shape):
        return nc.dram_tensor(name, shape, F32, kind="ExternalInput").ap()

    x = din("x", [N, D]); c = din("c", [D]); ctx = din("ctx", [L, D]); c_ctx = din("c_ctx", [D])
    w_mod = din("w_mod", [D, 6 * D]); b_mod = din("b_mod", [6 * D])
    g_mix = din("g_norm_mix", [D]); g_ffn = din("g_norm_ffn", [D])
    w_in = din("w_in", [D, INC]); g_qn = din("g_q_norm", [256]); w_uq = din("w_uq", [256, 768])
    g_kvn = din("g_kv_norm", [256]); w_ukv = din("w_ukv", [256, 1024])
    lb_f = din("lb_fwd", [2, 512]); lb_b = din("lb_bwd", [2, 512]); g_on = din("g_hgrn_norm", [128])
    w_out = din("w_out", [D, D]); w_gate = din("w_gate", [D, DFF]); w_up = din("w_up", [D, DFF])
    w_down = din("w_down", [DFF, D]); g_fin = din("g_final", [D])
    out = nc.dram_tensor("out", [N, D], F32, kind="ExternalOutput").ap()

    dbg = set(debug) if debug else set()

    def scr(name, shape, dt):
        return nc.dram_tensor(name, shape, dt, kind="ExternalOutput" if name in dbg else "Internal").ap()

    MOD = scr("s_mod", [8, 128, D], F32)
    HT = scr("s_ht", [128, 8, T], BF16)
    QN = scr("s_qn", [4, 128, N], BF16); QR = scr("s_qr", [4, 64, N], BF16)
    KN = scr("s_kn", [4, 128, T], BF16); KR = scr("s_kr", [64, T], BF16)
    VV = scr("s_v", [T, 512], BF16)
    HQ = scr("s_hq", [4, 128, T], F32); HV = scr("s_hv", [T, 512], BF16); HG = scr("s_hg", [N, 512], F32)
    GG = scr("s_g", [2, T, 512], F32); KG = scr("s_kg", [2, T, 512], F32)
    OO = scr("s_o", [2, N, 512], F32)
    MIXA = scr("s_mixa", [4, 128, N], BF16)
    X1 = scr("s_x1", [N, D], F32); H2T = scr("s_h2t", [128, 8, N], BF16)

    outer = ExitStack()
    with outer:
        def CT(name, shape, dt):
            return outer.enter_context(nc.sbuf_tensor(name, shape, dt))
        identb = CT("identb", [128, 128], BF16); identf = CT("identf", [128, 128], F32)
        onesb = CT("onesb", [128, 128], BF16); onesf = CT("onesf", [128, 128], F32)
        Uf = CT("Uf", [128, 128], F32); Ub = CT("Ub", [128, 128], F32)
        Rf = CT("Rf", [128, 128], F32); Rb = CT("Rb", [128, 128], F32)
        Mf = CT("Mf", [128, 128], F32); Mb = CT("Mb", [128, 128], F32)
        Ind = CT("Ind", [128, 2], F32)

        def sel(tile, pattern, cm, cmp_op, key):
            S.op('pool', lambda e: e.affine_select(out=tile[:], in_=tile[:], pattern=pattern, compare_op=cmp_op,
                                                   fill=0.0, base=0, channel_multiplier=cm), [key], [key])
        for tl, key in ((identb, 'identb'), (identf, 'identf')):
            S.memset('pool', tl[:], 1.0, [key])
            sel(tl, [[-1, 128]], 1, ALU.is_equal, key)
        S.memset('pool', onesb[:], 1.0, ['onesb'])
        S.memset('pool', onesf[:], 1.0, ['onesf'])
        S.memset('pool', Uf[:], 1.0, ['Uf']); sel(Uf, [[-1, 128]], 1, ALU.is_gt, 'Uf')
        S.memset('pool', Uf[64:128, 0:64], 0.0, ['Uf'])
        S.memset('pool', Mb[:], 1.0, ['Mb']); sel(Mb, [[-1, 128]], 1, ALU.is_ge, 'Mb')
        S.memset('pool', Mb[64:128, 0:64], 0.0, ['Mb'])
        S.memset('pool', Ub[:], 1.0, ['Ub']); sel(Ub, [[1, 128]], -1, ALU.is_gt, 'Ub')
        S.memset('pool', Ub[0:64, 64:128], 0.0, ['Ub'])
        S.memset('pool', Mf[:], 1.0, ['Mf']); sel(Mf, [[1, 128]], -1, ALU.is_ge, 'Mf')
        S.memset('pool', Mf[0:64, 64:128], 0.0, ['Mf'])
        S.ts('pool', Rf[:], Uf[:], -1.0, None, ALU.mult, None, ['Uf'], ['Rf'])
        S.ts('pool', Rb[:], Ub[:], -1.0, None, ALU.mult, None, ['Ub'], ['Rb'])
        S.memset('pool', Ind[:], 0.0, ['Ind'])
        S.memset('pool', Ind[0:64, 0:1], 1.0, ['Ind'])
        S.memset('pool', Ind[64:128, 1:2], 1.0, ['Ind'])

        with ExitStack() as es:
            def TT(name, shape, dt):
                return es.enter_context(nc.sbuf_tensor(name, shape, dt))

            def PP(name, shape, dt):
                return es.enter_context(nc.psum_tensor(name, shape, dt))
            crow = TT("crow", [128, 2, D], F32)
            cb = TT("cb", [128, 2, 8, 128], F32)
            bmod = TT("bmod", [128, 6 * D], F32)
            wm = [TT(f"wm{i}", [128, 8, 512], F32) for i in range(2)]
            modl = TT("modl", [128, 6 * D], F32); modc = TT("modc", [128, 2 * D], F32)
            gm = TT("gm", [128, D], F32); gf = TT("gf", [128, D], F32)
            tmpA = [TT(f"tmpA{i}", [128, D], F32) for i in range(3)]
            pcb = PP("pcb", [128, 128], F32)
            pm = [PP(f"pm{i}", [128, 512], F32) for i in range(2)]
            S.dma('sp', crow[:, 0, :], c.partition_broadcast(128), (), ['crow'])
            S.dma('sp', crow[:, 1, :], c_ctx.partition_broadcast(128), (), ['crow'])
            S.dma('sp', bmod[:], b_mod.partition_broadcast(128), (), ['bmod'])
            S.dma('sp', gm[:], g_mix.partition_broadcast(128), (), ['gm'])
            S.dma('sp', gf[:], g_ffn.partition_broadcast(128), (), ['gf'])
            S.act(crow[:], crow[:], AF.Silu, ['crow'], ['crow'])
            for w_ in range(2):
                for k in range(8):
                    S.mm(pcb[:], crow[:, w_, k * 128:(k + 1) * 128], identf[:], True, True, ['crow', 'identf'], ['pcb'])
                    S.copy('dve', cb[:, w_, k, :], pcb[:], ['pcb'], ['cb'])
            wmv = w_mod.rearrange("(k p) n -> p k n", p=128)
            for j in range(12):
                wt = wm[j % 2]; wk = f"wm{j % 2}"
                S.dma('sp', wt[:], wmv[:, :, j * 512:(j + 1) * 512], (), [wk])
                for k in range(8):
                    S.mm(pm[0][:], cb[:, 0, k, :], wt[:, k, :], k == 0, k == 7, ['cb', wk], ['pm0'])
                S.tt('dve', modl[:, j * 512:(j + 1) * 512], pm[0][:], bmod[:, j * 512:(j + 1) * 512], ALU.add,
                     ['pm0', 'bmod'], ['modl'])
                if j < 4:
                    for k in range(8):
                        S.mm(pm[1][:], cb[:, 1, k, :], wt[:, k, :], k == 0, k == 7, ['cb', wk], ['pm1'])
                    S.tt('dve', modc[:, j * 512:(j + 1) * 512], pm[1][:], bmod[:, j * 512:(j + 1) * 512], ALU.add,
                         ['pm1', 'bmod'], ['modc'])
            S.stt('dve', tmpA[0][:], modl[:, D:2 * D], 1.0, gm[:], ALU.add, ALU.mult, ['modl', 'gm'], ['tA0'])
            S.stt('dve', tmpA[1][:], modc[:, D:2 * D], 1.0, gm[:], ALU.add, ALU.mult, ['modc', 'gm'], ['tA1'])
            S.stt('dve', tmpA[2][:], modl[:, 4 * D:5 * D], 1.0, gf[:], ALU.add, ALU.mult, ['modl', 'gf'], ['tA2'])
            S.dma('sp', MOD[0], tmpA[0][:], ['tA0'], ())
            S.dma('sp', MOD[1], modl[:, 0:D], ['modl'], ())
            S.dma('sp', MOD[2], tmpA[1][:], ['tA1'], ())
            S.dma('sp', MOD[3], modc[:, 0:D], ['modc'], ())
            S.dma('sp', MOD[4], modl[:, 2 * D:3 * D], ['modl'], ())
            S.dma('sp', MOD[5], tmpA[2][:], ['tA2'], ())
            S.dma('sp', MOD[6], modl[:, 3 * D:4 * D], ['modl'], ())
            S.dma('sp', MOD[7], modl[:, 5 * D:6 * D], ['modl'], ())
            S.emit()
        if stop_after == 0:
            return nc

        with ExitStack() as es:
            def TT(name, shape, dt):
                return es.enter_context(nc.sbuf_tensor(name, shape, dt))

            def PP(name, shape, dt):
                return es.enter_context(nc.psum_tensor(name, shape, dt))
            mA = TT("mA", [128, 4, D], F32)
            junk = TT("junk", [128, D], BF16)
            xtR = Rot(TT, "xt", [128, D], F32, 3); ssR = Rot(TT, "ssa", [128, 1], F32, 3)
            t1R = Rot(TT, "t1_", [128, D], F32, 2); hbR = Rot(TT, "hb", [128, D], BF16, 3)
            hTR = Rot(TT, "hT", [128, 8, 128], BF16, 3); ptrR = Rot(PP, "ptr", [128, 8, 128], BF16, 2)
            for i in range(4):
                S.dma('sp', mA[:, i, :], MOD[i], (), ['mA'])

            def a1_s0(cx, ti):
                cx['xt'], cx['xk'] = xtR.next()
                src = ctx[ti * 128:(ti + 1) * 128, :] if ti < 2 else x[(ti - 2) * 128:(ti - 1) * 128, :]
                S.dma('sp', cx['xt'][:], src, (), [cx['xk']])

            def a1_s1(cx, ti):
                xt_, xk = cx['xt'], cx['xk']
                mi = 2 if ti < 2 else 0
                ss, sk = ssR.next(); t1, t1k = t1R.next(); hb, hbk = hbR.next()
                cx['hb'], cx['hbk'] = hb, hbk
                S.memset('dve', ss[:], 0.0, [sk])
                S.act(junk[:], xt_[:], AF.Square, [xk], ['junk', sk], accum=ss[:])
                S.act(ss[:], ss[:], AF.Sqrt, [sk], [sk], bias=EPS, scale=1.0 / D)
                S.op('dve', (lambda o_: (lambda e: e.reciprocal(out=o_[:], in_=o_[:])))(ss), [sk], [sk])
                S.stt('dve', t1[:], xt_[:], ss[:, 0:1], mA[:, mi, :], ALU.mult, ALU.mult, [xk, sk, 'mA'], [t1k])
                S.tt('pool', hb[:], t1[:], mA[:, mi + 1, :], ALU.add, [t1k, 'mA'], [hbk])

            def a1_s2(cx, ti):
                hb, hbk = cx['hb'], cx['hbk']
                pt, ptk = ptrR.next(); hT, hTk = hTR.next()
                for k in range(8):
                    S.tr(pt[:, k, :], hb[:, k * 128:(k + 1) * 128], identb[:], [hbk, 'identb'], [ptk])
                S.copy('dve' if ti % 2 else 'act', hT[:], pt[:], [ptk], [hTk])
                S.dma('sp', HT[:, :, ti * 128:(ti + 1) * 128], hT[:], [hTk], ())
            run_pipeline(NT, [a1_s0, a1_s1, a1_s2])
            S.emit()
        if stop_after == 1:
            return nc

        with ExitStack() as es:
            def TT(name, shape, dt):
                return es.enter_context(nc.sbuf_tensor(name, shape, dt))

            def PP(name, shape, dt):
                return es.enter_context(nc.psum_tensor(name, shape, dt))
            Win = TT("Win", [128, 8, INC], BF16)
            Wkpe = TT("Wkpe", [128, 8, 128], BF16); Wkrot = TT("Wkrot", [128, 8, 128], BF16)
            Wqn = TT("Wqn", [128, 2, 4, 128], BF16); Wqr = TT("Wqr", [128, 2, 4, 128], BF16)
            Wqrot = TT("Wqrot", [128, 2, 4, 128], BF16)
            Wkn = TT("Wkn", [128, 2, 4, 128], BF16); Wv = TT("Wv", [128, 2, 4, 128], BF16)
            Ct = TT("Ct", [64, N], F32); St = TT("St", [64, N], F32)
            lbt = TT("lbt", [128, 2, 512], F32); oml = TT("oml", [128, 2, 512], F32)
            with ExitStack() as es2:
                def T2(name, shape, dt):
                    return es2.enter_context(nc.sbuf_tensor(name, shape, dt))
                winv = w_in.rearrange("(k p) n -> p k n", p=128)
                for tl_, k_ in ((Wkpe, 'Wkpe'), (Wkrot, 'Wkrot')):
                    S.memset('dve', tl_[:, :, 64:128], 0.0, [k_])
                for tl_, k_ in ((Wqr, 'Wqr'), (Wqrot, 'Wqrot')):
                    S.memset('dve', tl_[:, :, :, 64:128], 0.0, [k_])
                for k in range(8):
                    S.dma('pool', Win[:, k, :], winv[:, k, :], (), ['Win'])
                    for f_ in range(2):
                        S.dma('pool', Wkpe[:, k, f_ * 32:(f_ + 1) * 32].rearrange("p (a i) -> p a i", a=2),
                              winv[:, k, 512:576].rearrange("p (a f i) -> p f a i", a=2, f=2)[:, f_, :, :], (), ['Wkpe'])
                S.ts('dve', Wkrot[:, :, 0:32], Wkpe[:, :, 32:64], -1.0, None, ALU.mult, None, ['Wkpe'], ['Wkrot'])
                S.copy('dve', Wkrot[:, :, 32:64], Wkpe[:, :, 0:32], ['Wkpe'], ['Wkrot'])
                stq = T2("stq", [128, 2, 768], F32); stkv = T2("stkv", [128, 2, 1024], F32)
                gq = T2("gq", [128, 2], F32); gkv = T2("gkv", [128, 2], F32)
                S.dma('sp', stq[:], w_uq.rearrange("(c p) n -> p c n", p=128), (), ['stq'])
                S.dma('sp', stkv[:], w_ukv.rearrange("(c p) n -> p c n", p=128), (), ['stkv'])
                for c_ in range(2):
                    S.dma('sp', gq[:, c_:c_ + 1], g_qn[c_ * 128:(c_ + 1) * 128].rearrange("(p o) -> p o", o=1), (), ['gq'])
                    S.dma('sp', gkv[:, c_:c_ + 1], g_kvn[c_ * 128:(c_ + 1) * 128].rearrange("(p o) -> p o", o=1), (), ['gkv'])
                for c_ in range(2):
                    sq_v = stq[:, c_, :].rearrange("p (h d) -> p h d", h=4)
                    S.ts('dve', Wqn[:, c_, :, :], sq_v[:, :, 0:128], gq[:, c_:c_ + 1], None, ALU.mult, None,
                         ['stq', 'gq'], ['Wqn'])
                    for f_ in range(2):
                        for a_ in range(2):
                            so = 128 + a_ * 32 + f_ * 16
                            do = f_ * 32 + a_ * 16
                            S.ts('dve', Wqr[:, c_, :, do:do + 16], sq_v[:, :, so:so + 16], gq[:, c_:c_ + 1], None,
                                 ALU.mult, None, ['stq', 'gq'], ['Wqr'])
                    S.ts('dve', Wqrot[:, c_, :, 0:32], Wqr[:, c_, :, 32:64], -1.0, None, ALU.mult, None, ['Wqr'], ['Wqrot'])
                    S.copy('dve', Wqrot[:, c_, :, 32:64], Wqr[:, c_, :, 0:32], ['Wqr'], ['Wqrot'])
                    skv_v = stkv[:, c_, :].rearrange("p (h t d) -> p h t d", h=4, t=2)
                    S.ts('dve', Wkn[:, c_, :, :], skv_v[:, :, 0, :], gkv[:, c_:c_ + 1], None, ALU.mult, None,
                         ['stkv', 'gkv'], ['Wkn'])
                    S.ts('dve', Wv[:, c_, :, :], skv_v[:, :, 1, :], gkv[:, c_:c_ + 1], None, ALU.mult, None,
                         ['stkv', 'gkv'], ['Wv'])
                lraw = T2("lraw", [128, 2, 2, 512], F32)
                for d_, lbx in enumerate((lb_f, lb_b)):
                    for r_ in range(2):
                        S.dma('sp', lraw[:, d_, r_, :], lbx[r_].partition_broadcast(128), (), ['lraw'])
                S.tt('dve', lbt[:], lraw[:, :, 0, :], lraw[:, :, 1, :], ALU.subtract, ['lraw'], ['lbt'])
                S.act(lbt[:], lbt[:], AF.Sigmoid, ['lbt'], ['lbt'])
                S.ts('dve', oml[:], lbt[:], -0.5, 0.5, ALU.mult, ALU.add, ['lbt'], ['oml'])
                S.tt('dve', lbt[:], lbt[:], oml[:], ALU.add, ['lbt', 'oml'], ['lbt'])
                pidx = T2("pidx", [64, 1], F32); i16 = T2("i16", [64, 1], F32); mrow = T2("mrow", [64, 1], F32)
                arow = T2("arow", [64, 1], F32); acol = T2("acol", [64, 1], F32)
                rowpos = T2("rowpos", [64, N], F32); colpos = T2("colpos", [64, N], F32); ang = T2("ang", [64, N], F32)
                S.op('pool', lambda e: e.iota(pidx[:], [[0, 1]], base=0, channel_multiplier=1,
                                              allow_small_or_imprecise_dtypes=True), (), ['pidx'])
                S.op('pool', lambda e: e.iota(rowpos[:], [[1, 64], [0, 64]], base=0, channel_multiplier=0,
                                              allow_small_or_imprecise_dtypes=True), (), ['rowpos'])
                S.op('pool', lambda e: e.iota(colpos[:], [[0, 64], [1, 64]], base=0, channel_multiplier=0,
                                              allow_small_or_imprecise_dtypes=True), (), ['colpos'])
                msk = T2("msk", [64, 3], F32)
                S.memset('pool', msk[:], 1.0, ['msk'])
                for j_ in range(3):
                    S.op('pool', (lambda jj: (lambda e: e.affine_select(
                        out=msk[:, jj:jj + 1], in_=msk[:, jj:jj + 1], pattern=[[0, 1]], compare_op=ALU.is_ge, fill=0.0,
                        base=-16 * (jj + 1), channel_multiplier=1)))(j_), ['msk'], ['msk'])
                S.tt('dve', mrow[:], msk[:, 0:1], msk[:, 1:2], ALU.add, ['msk'], ['mrow'])
                S.tt('dve', mrow[:], mrow[:], msk[:, 2:3], ALU.add, ['msk', 'mrow'], ['mrow'])
                S.stt('dve', i16[:], mrow[:], -16.0, pidx[:], ALU.mult, ALU.add, ['mrow', 'pidx'], ['i16'])
                S.tt('dve', mrow[:], msk[:, 1:2], msk[:, 0:1], ALU.subtract, ['msk'], ['mrow'])
                S.tt('dve', mrow[:], mrow[:], msk[:, 2:3], ALU.subtract, ['msk', 'mrow'], ['mrow'])
                S.ts('dve', mrow[:], mrow[:], 1.0, None, ALU.add, None, ['mrow'], ['mrow'])
                S.act(i16[:], i16[:], AF.Exp, ['i16'], ['i16'], scale=-math.log(10000.0) / 16.0)
                S.tt('dve', arow[:], i16[:], mrow[:], ALU.mult, ['i16', 'mrow'], ['arow'])
                S.tt('dve', acol[:], i16[:], arow[:], ALU.subtract, ['i16', 'arow'], ['acol'])
                S.ts('dve', ang[:], rowpos[:], arow[:, 0:1], None, ALU.mult, None, ['rowpos', 'arow'], ['ang'])
                S.stt('dve', ang[:], colpos[:], acol[:, 0:1], ang[:], ALU.mult, ALU.add, ['colpos', 'acol', 'ang'], ['ang'])
                sc_ = 1.0 - 1e-6
                ki = T2("ki", [64, N], mybir.dt.int32)
                for tab, shift in ((St, 0.0), (Ct, 0.5 * math.pi)):
                    S.ts('dve', rowpos[:], ang[:], shift, 1.0 / (2 * math.pi), ALU.add, ALU.mult, ['ang'], ['rowpos'])
                    S.copy('dve', ki[:], rowpos[:], ['rowpos'], ['ki'])
                    S.copy('dve', colpos[:], ki[:], ['ki'], ['colpos'])
                    S.ts('dve', rowpos[:], ang[:], shift, None, ALU.add, None, ['ang'], ['rowpos'])
                    S.stt('dve', rowpos[:], colpos[:], -2 * math.pi, rowpos[:], ALU.mult, ALU.add, ['colpos', 'rowpos'], ['rowpos'])
                    S.act(tab[:], rowpos[:], AF.Sin, ['rowpos'], ['St' if shift == 0.0 else 'Ct'], scale=sc_)
                if stop_after == 15:
                    for nm, tl, shp, dt_ in (("d_Ct", Ct, [64, N], F32), ("d_St", St, [64, N], F32),
                                             ("d_Wkpe", Wkpe, [128, 8, 128], BF16), ("d_Wkrot", Wkrot, [128, 8, 128], BF16),
                                             ("d_Wqn", Wqn, [128, 2, 4, 128], BF16), ("d_Wqr", Wqr, [128, 2, 4, 128], BF16),
                                             ("d_Wqrot", Wqrot, [128, 2, 4, 128], BF16), ("d_Wkn", Wkn, [128, 2, 4, 128], BF16),
                                             ("d_Wv", Wv, [128, 2, 4, 128], BF16), ("d_lbt", lbt, [128, 2, 512], F32),
                                             ("d_oml", oml, [128, 2, 512], F32), ("d_Win", Win, [128, 8, INC], BF16)):
                        dd = nc.dram_tensor(nm, shp, dt_, kind="ExternalOutput").ap()
                        S.dma('sp', dd, tl[:], [nm[2:]], ())
                S.emit()
                if stop_after == 15:
                    return nc

            hTg = Rot(TT, "hTg", [128, 8, 512], BF16, 2)
            cT = TT("cT", [128, 2, 512], BF16); sq = TT("sq", [128, 2, 512], BF16)
            rbc = TT("rbc", [128, 512], F32); rtk = TT("rtk", [128, 4], F32)
            o_bf = Rot(TT, "o_bf", [128, 512], BF16, 3)
            o_f = Rot(TT, "o_f", [128, 512], F32, 4)
            u_f = Rot(TT, "u_f", [64, 512], F32, 4)
            sgb = Rot(TT, "sgb", [128, 512], F32, 3); fb_ = Rot(TT, "fb_", [128, 512], F32, 3)
            pA = [PP(f"pA{i}", [128, 512], F32) for i in range(2)]
            pB = [PP(f"pB{i}", [128, 512], F32) for i in range(2)]
            pS = PP("pS", [128, 512], F32)
            pT = [PP(f"pT{i}", [128, 512], F32) for i in range(2)]
            pV = PP("pV", [128, 4, 128], F32)
            groups = [(0, 256)] + [(256 + i * 512, 512) for i in range(8)]
            if a2_groups is not None:
                groups = groups[:a2_groups]
            for (tok0, n) in groups:
                is_lat = tok0 >= L
                lo = tok0 - L
                nsub = n // 128
                S.enabled = True
                hT_, hk = hTg.next()
                S.dma('sp', hT_[:, :, :n], HT[:, :, tok0:tok0 + n], (), [hk])

                def fm_proj(ps, pk, wt, wk, col0, ncols):
                    for k in range(8):
                        S.mm(ps[0:ncols, :n], wt[:, k, col0:col0 + ncols], hT_[:, k, :n], k == 0, k == 7, [wk, hk], [pk])

                def lowrank(col0, want_tok):
                    for c_ in range(2):
                        fm_proj(pA[c_], f'pA{c_}', Win, 'Win', col0 + c_ * 128, 128)
                        S.copy('act', cT[:, c_, :n], pA[c_][:, :n], [f'pA{c_}'], ['cT'])
                        S.act(sq[:, c_, :n], pA[c_][:, :n], AF.Square, [f'pA{c_}'], ['sq'])
                    for c_ in range(2):
                        S.mm(pS[:, :n], onesb[:], sq[:, c_, :n], c_ == 0, c_ == 1, ['onesb', 'sq'], ['pS'])
                    S.act(rbc[:, :n], pS[:, :n], AF.Sqrt, ['pS'], ['rbc'], bias=EPS, scale=1.0 / 256)
                    S.op('dve', (lambda nn: (lambda e: e.reciprocal(out=rbc[:, :nn], in_=rbc[:, :nn])))(n), ['rbc'], ['rbc'])
                    if want_tok:
                        for s_ in range(nsub):
                            for c_ in range(2):
                                S.mm(pV[:, s_, :], sq[:, c_, s_ * 128:(s_ + 1) * 128], onesb[:], c_ == 0, c_ == 1,
                                     ['sq', 'onesb'], ['pV'])
                        S.act(rtk[:, :nsub], pV[:, :nsub, 0], AF.Sqrt, ['pV'], ['rtk'], bias=EPS, scale=1.0 / 256)
                        S.op('dve', (lambda ns: (lambda e: e.reciprocal(out=rtk[:, :ns], in_=rtk[:, :ns])))(nsub), ['rtk'], ['rtk'])

                def rope_out(p0, k0, p1, k1, dst, scale_rows):
                    u1, uk1 = u_f.next(); u2, uk2 = u_f.next()
                    S.tt('dve', u1[:, :n], p0[0:64, :n], Ct[:, lo:lo + n], ALU.mult, [k0, 'Ct'], [uk1])
                    S.tt('dve', u2[:, :n], p1[0:64, :n], St[:, lo:lo + n], ALU.mult, [k1, 'St'], [uk2])
                    ob, ok = o_bf.next()
                    if scale_rows:
                        S.tt('pool', u1[:, :n], u1[:, :n], u2[:, :n], ALU.add, [uk1, uk2], [uk1])
                        S.tt('pool', ob[0:64, :n], u1[:, :n], rbc[0:64, :n], ALU.mult, [uk1, 'rbc'], [ok])
                    else:
                        S.tt('pool', ob[0:64, :n], u1[:, :n], u2[:, :n], ALU.add, [uk1, uk2], [ok])
                    S.dma(STQ, dst, ob[0:64, :n], [ok], ())

                if is_lat and 'a' in a2_parts:
                    lowrank(0, False)
                    for h in range(4):
                        pb, pk = pB[h % 2], f'pB{h % 2}'
                        for c_ in range(2):
                            S.mm(pb[:, :n], Wqn[:, c_, h, :], cT[:, c_, :n], c_ == 0, c_ == 1, ['Wqn', 'cT'], [pk])
                        ob, ok = o_bf.next()
                        S.tt('dve', ob[:, :n], pb[:, :n], rbc[:, :n], ALU.mult, [pk, 'rbc'], [ok])
                        S.dma(STQ, QN[h][:, lo:lo + n], ob[:, :n], [ok], ())
                    for h in range(4):
                        for c_ in range(2):
                            S.mm(pB[0][:, :n], Wqr[:, c_, h, :], cT[:, c_, :n], c_ == 0, c_ == 1, ['Wqr', 'cT'], ['pB0'])
                        for c_ in range(2):
                            S.mm(pB[1][:, :n], Wqrot[:, c_, h, :], cT[:, c_, :n], c_ == 0, c_ == 1, ['Wqrot', 'cT'], ['pB1'])
                        rope_out(pB[0], 'pB0', pB[1], 'pB1', QR[h][:, lo:lo + n], True)
                S.enabled = 'b' in a2_parts
                lowrank(256, True)
                for h in range(4):
                    pb, pk = pB[h % 2], f'pB{h % 2}'
                    for c_ in range(2):
                        S.mm(pb[:, :n], Wkn[:, c_, h, :], cT[:, c_, :n], c_ == 0, c_ == 1, ['Wkn', 'cT'], [pk])
                    ob, ok = o_bf.next()
                    S.tt('dve', ob[:, :n], pb[:, :n], rbc[:, :n], ALU.mult, [pk, 'rbc'], [ok])
                    S.dma(STQ, KN[h][:, tok0:tok0 + n], ob[:, :n], [ok], ())
                Wv2 = Wv[:].rearrange("p c h d -> p c (h d)")
                for s_ in range(nsub):
                    pt, pk = pT[s_ % 2], f'pT{s_ % 2}'
                    for c_ in range(2):
                        S.mm(pt[:], cT[:, c_, s_ * 128:(s_ + 1) * 128], Wv2[:, c_, :], c_ == 0, c_ == 1, ['cT', 'Wv'], [pk])
                    ob, ok = o_bf.next()
                    S.act(ob[:], pt[:], AF.Copy, [pk, 'rtk'], [ok], scale=rtk[:, s_:s_ + 1])
                    S.dma(STQ, VV[tok0 + s_ * 128:tok0 + (s_ + 1) * 128, :], ob[:], [ok], ())
                S.enabled = 'c' in a2_parts
                fm_proj(pA[0], 'pA0', Wkpe, 'Wkpe', 0, 128)
                if is_lat:
                    fm_proj(pA[1], 'pA1', Wkrot, 'Wkrot', 0, 128)
                    rope_out(pA[0], 'pA0', pA[1], 'pA1', KR[:, tok0:tok0 + n], False)
                else:
                    ob, ok = o_bf.next()
                    S.copy('act', ob[0:64, :n], pA[0][0:64, :n], ['pA0'], [ok])
                    S.dma(STQ, KR[:, tok0:tok0 + n], ob[0:64, :n], [ok], ())
                S.enabled = 'd' in a2_parts
                for h in range(4):
                    pa, pk = pA[h % 2], f'pA{h % 2}'
                    fm_proj(pa, pk, Win, 'Win', 576 + h * 128, 128)
                    of, ok = o_f.next()
                    S.copy('act' if h % 2 else 'dve', of[:, :n], pa[:, :n], [pk], [ok])
                    S.dma(STQ, HQ[h][:, tok0:tok0 + n], of[:, :n], [ok], ())
                S.enabled = 'e' in a2_parts
                for s_ in range(nsub):
                    row0 = tok0 + s_ * 128

                    def tm_proj(ps, pk, col0):
                        for k in range(8):
                            S.mm(ps[:], hT_[:, k, s_ * 128:(s_ + 1) * 128], Win[:, k, col0:col0 + 512], k == 0, k == 7,
                                 [hk, 'Win'], [pk])
                    tm_proj(pT[0], 'pT0', 1088)
                    ob, ok = o_bf.next()
                    S.copy('dve', ob[:], pT[0][:], ['pT0'], [ok])
                    S.dma(STQ, HV[row0:row0 + 128, :], ob[:], [ok], ())
                    if is_lat:
                        tm_proj(pT[1], 'pT1', 1600)
                        th, thk = sgb.next(); uh, uhk = fb_.next()
                        S.act(th[:], pT[1][:], AF.Tanh, ['pT1'], [thk], scale=0.5)
                        S.act(uh[:], pT[1][:], AF.Copy, ['pT1'], [uhk], scale=0.5)
                        of, ok = o_f.next()
                        S.tt('pool', th[:], th[:], uh[:], ALU.mult, [thk, uhk], [thk])
                        S.tt('pool', of[:], th[:], uh[:], ALU.add, [thk, uhk], [ok])
                        S.dma(STQ, HG[row0 - L:row0 - L + 128, :], of[:], [ok], ())
                    fts = []
                    for d_ in range(2):
                        pt, pk = pT[d_], f'pT{d_}'
                        tm_proj(pt, pk, 2112 + d_ * 512)
                        sg, sk = sgb.next(); ff, fk = fb_.next()
                        S.act(sg[:], pt[:], AF.Tanh, [pk], [sk], scale=0.5)
                        S.tt('dve', ff[:], sg[:], oml[:, d_, :], ALU.mult, [sk, 'oml'], [fk])
                        S.tt('pool', ff[:], ff[:], lbt[:, d_, :], ALU.add, [fk, 'lbt'], [fk])
                        fts.append((ff, fk))
                    for d_ in range(2):
                        ff, fk = fts[d_]
                        of, ok = o_f.next()
                        S.act(of[:], ff[:], AF.Ln, [fk], [ok])
                        S.dma(STQ, GG[d_][row0:row0 + 128, :], of[:], [ok], ())
                        of2, ok2 = o_f.next()
                        S.ts('pool', of2[:], ff[:], -1.0, 1.0, ALU.mult, ALU.add, [fk], [ok2])
                        S.dma(STQ, KG[d_][row0:row0 + 128, :], of2[:], [ok2], ())
            S.enabled = True
            S.emit()
        if stop_after == 2:
            return nc

        with ExitStack() as es:
            def TT(name, shape, dt):
                return es.enter_context(nc.sbuf_tensor(name, shape, dt))

            def PP(name, shape, dt):
                return es.enter_context(nc.psum_tensor(name, shape, dt))

            bt = []
            for d_ in range(2):
                bt.append(dict(
                    g=Rot(TT, f"bg{d_}", [128, 512], F32, 2), kg=Rot(TT, f"bkg{d_}", [128, 512], F32, 2),
                    v=Rot(TT, f"bv{d_}", [128, 512], BF16, 3), hq=Rot(TT, f"bhq{d_}", [128, 4, 128], F32, 2),
                    Ek=Rot(TT, f"bEk{d_}", [128, 512], F32, 2), EqT=Rot(TT, f"bEq{d_}", [128, 4, 128], F32, 2),
                    eb=Rot(TT, f"beb{d_}", [128, 4, 2], F32, 3), K2=Rot(TT, f"bK2{d_}", [128, 512], BF16, 2),
                    K2m=[Rot(TT, f"bK2m{c_}{d_}", [128, 512], BF16, 2) for c_ in range(2)],
                    QsT=Rot(TT, f"bQs{d_}", [128, 4, 128], BF16, 2),
                    Qsm=[Rot(TT, f"bQsm{c_}{d_}", [128, 4, 128], BF16, 2) for c_ in range(2)],
                    K2T=Rot(TT, f"bK2T{d_}", [128, 4, 128], BF16, 2),
                    Am=Rot(TT, f"bAm{d_}", [128, 4, 128], BF16, 2), S=TT(f"bS{d_}", [128, 4, 128], F32),
                    Sp=Rot(TT, f"bSp{d_}", [128, 4, 128], F32, 2), Spb=Rot(TT, f"bSpb{d_}", [128, 4, 128], BF16, 4),
                    osb=Rot(TT, f"bos{d_}", [128, 512], F32, 2)))
                S.memset('dve', bt[d_]['S'][:], 0.0, [f'bS{d_}'])
                for c_ in range(2):
                    oth = slice(64, 128) if c_ == 0 else slice(0, 64)
                    for j_ in range(2):
                        S.memset('pool', bt[d_]['K2m'][c_].t[j_][oth, :], 0.0, [bt[d_]['K2m'][c_].k[j_]])
                        S.memset('pool', bt[d_]['Qsm'][c_].t[j_][:, :, oth], 0.0, [bt[d_]['Qsm'][c_].k[j_]])
            pD1 = PP("pD1", [128, 512], F32); pD2 = PP("pD2", [128, 4, 128], F32)
            pKT = PP("pKT", [128, 4, 128], BF16); pBL = PP("pBL", [128, 4, 2], F32)
            pAT = PP("pAT", [128, 4, 128], F32)
            pOb = PP("pOb", [128, 4, 128], F32)
            pSNr = Rot(PP, "pSN", [128, 4, 128], F32, 2)

            def hgrn_pre(cx, ti, d_):
                B_ = bt[d_]
                is_lat = ti >= 2
                row0 = ti * 128
                U_, R_, M_ = (Uf, Rf, Mf) if d_ == 0 else (Ub, Rb, Mb)
                uk, rk, mk_ = ('Uf', 'Rf', 'Mf') if d_ == 0 else ('Ub', 'Rb', 'Mb')
                g, gk = B_['g'].next(); kg, kgk = B_['kg'].next(); v, vk = B_['v'].next(); hq, hqk = B_['hq'].next()
                S.dma('sp', g[:], GG[d_][row0:row0 + 128, :], (), [gk])
                S.dma('sp', kg[:], KG[d_][row0:row0 + 128, :], (), [kgk])
                S.dma('sp', v[:], HV[row0:row0 + 128, :], (), [vk])
                S.dma('sp', hq[:], HQ[:, :, row0:row0 + 128].rearrange("h p t -> p h t"), (), [hqk])
                Ek, Ekk = B_['Ek'].next(); EqT, Eqk = B_['EqT'].next(); eb, ebk = B_['eb'].next()
                K2, K2k = B_['K2'].next(); QsT, Qsk = B_['QsT'].next(); K2T, K2Tk = B_['K2T'].next()
                S.mm(pD1[:], U_[:], g[:], True, True, [uk, gk], ['pD1'])
                for h in range(4):
                    S.mm(pD2[:, h, :], g[:, h * 128:(h + 1) * 128], R_[:], True, True, [gk, rk], ['pD2'])
                for h in range(4):
                    S.mm(pBL[:, h, :], g[:, h * 128:(h + 1) * 128], Ind[:], True, True, [gk, 'Ind'], ['pBL'])
                S.act(Ek[:], pD1[:], AF.Exp, ['pD1'], [Ekk])
                S.act(EqT[:], pD2[:], AF.Exp, ['pD2'], [Eqk])
                S.act(eb[:], pBL[:], AF.Exp, ['pBL'], [ebk])
                S.tt('dve', K2[:], kg[:], Ek[:], ALU.mult, [kgk, Ekk], [K2k])
                K2m = []
                for c_ in range(2):
                    rs_ = slice(c_ * 64, (c_ + 1) * 64)
                    km, kmk = B_['K2m'][c_].next()
                    S.copy('act', km[rs_, :], K2[rs_, :], [K2k], [kmk])
                    K2m.append((km, kmk))
                cx.update(v=v, vk=vk, eb=eb, ebk=ebk, K2m=K2m)
                if is_lat:
                    S.tt('pool', QsT[:], hq[:], EqT[:], ALU.mult, [hqk, Eqk], [Qsk])
                    Qsm = []
                    for c_ in range(2):
                        cs_ = slice(c_ * 64, (c_ + 1) * 64)
                        qm, qmk = B_['Qsm'][c_].next()
                        S.copy('pool', qm[:, :, cs_], QsT[:, :, cs_], [Qsk], [qmk])
                        Qsm.append((qm, qmk))
                    Am, Amk = B_['Am'].next()
                    for h in range(4):
                        S.tr(pKT[:, h, :], K2[:, h * 128:(h + 1) * 128], identb[:], [K2k, 'identb'], ['pKT'])
                    S.copy('act', K2T[:], pKT[:], ['pKT'], [K2Tk])
                    for h in range(4):
                        S.mm(pAT[:, h, :], K2T[:, h, :], QsT[:, h, :], True, True, [K2Tk, Qsk], ['pAT'])
                    S.tt('dve', Am[:], pAT[:], M_[:].unsqueeze(1).to_broadcast([128, 4, 128]), ALU.mult,
                         ['pAT', mk_], [Amk])
                    cx.update(Am=Am, Amk=Amk, Qsm=Qsm)

            def hgrn_chain(cx, ti, d_):
                B_ = bt[d_]
                is_lat = ti >= 2
                row0 = ti * 128
                St_, Sk = B_['S'], f'bS{d_}'
                v, vk, eb, ebk, K2m = (cx[k_] for k_ in ('v', 'vk', 'eb', 'ebk', 'K2m'))
                order = (0, 1) if d_ == 0 else (1, 0)
                spbs = {}
                for c_ in order:
                    km, kmk = K2m[c_]
                    ebb = eb[:, :, c_:c_ + 1].to_broadcast([128, 4, 128])
                    Sp, Spk = B_['Sp'].next()
                    psn, psnk = pSNr.next()
                    for h in range(4):
                        S.mm(psn[:, h, :], km[:, h * 128:(h + 1) * 128], v[:, h * 128:(h + 1) * 128], True, True,
                             [kmk, vk], [psnk])
                    S.tt('pool', Sp[:], St_[:], ebb, ALU.mult, [Sk, ebk], [Spk])
                    if is_lat:
                        Spb, Spbk = B_['Spb'].next()
                        S.tt('dve', Spb[:], St_[:], ebb, ALU.mult, [Sk, ebk], [Spbk])
                        spbs[c_] = (Spb, Spbk)
                    S.tt('dve', St_[:], Sp[:], psn[:], ALU.add, [Spk, psnk], [Sk])
                if is_lat:
                    Am, Amk, Qsm = cx['Am'], cx['Amk'], cx['Qsm']
                    for h in range(4):
                        S.mm(pOb[:, h, :], Am[:, h, :], v[:, h * 128:(h + 1) * 128], True, False, [Amk, vk], ['pOb'])
                        for n_, c_ in enumerate(order):
                            qm, qmk = Qsm[c_]
                            Spb, Spbk = spbs[c_]
                            S.mm(pOb[:, h, :], qm[:, h, :], Spb[:, h, :], False, n_ == 1, [qmk, Spbk], ['pOb'])
                    ob, obk = B_['osb'].next()
                    S.copy('act', ob[:], pOb[:].rearrange("p h d -> p (h d)"), ['pOb'], [obk])
                    S.dma('sp', OO[d_][row0 - L:row0 - L + 128, :], ob[:], [obk], ())

            fwd_order = list(range(NT))
            bwd_order = [1, 0] + list(range(NT - 1, 1, -1))
            cxs = [[dict() for _ in range(NT)] for _ in range(2)]
            hgrn_pre(cxs[0][0], fwd_order[0], 0)
            hgrn_pre(cxs[1][0], bwd_order[0], 1)
            for i_ in range(NT):
                if i_ + 1 < NT:
                    hgrn_pre(cxs[0][i_ + 1], fwd_order[i_ + 1], 0)
                    hgrn_pre(cxs[1][i_ + 1], bwd_order[i_ + 1], 1)
                hgrn_chain(cxs[0][i_], fwd_order[i_], 0)
                hgrn_chain(cxs[1][i_], bwd_order[i_], 1)
            S.emit()
        if stop_after == 3:
            return nc

        with ExitStack() as es:
            def TT(name, shape, dt):
                return es.enter_context(nc.sbuf_tensor(name, shape, dt))

            def PP(name, shape, dt):
                return es.enter_context(nc.psum_tensor(name, shape, dt))

            KNs = TT("KNs", [128, 4, T], BF16); KRs = TT("KRs", [128, T], BF16); Vs = TT("Vs", [128, NT, 512], BF16)
            sqKR = TT("sqKR", [64, T], BF16)
            sqn = TT("sqn", [128, 512], BF16); sqr = TT("sqr", [64, 512], BF16)
            km2 = TT("km2", [128, 4], F32); tmx = TT("tmx", [128, 1], F32); nsh = TT("nsh", [128, 1], F32)
            qnr = Rot(TT, "cqn", [128, 512], BF16, 2); qrr = Rot(TT, "cqr", [128, 512], BF16, 2)
            PTr = Rot(TT, "cPT", [128, 1024], BF16, 4); osr = Rot(TT, "cos", [128, 512], BF16, 2)
            rinv = TT("rinv", [128, 512], F32)
            racc = [TT(f"racc{i}", [128, 1024], F32) for i in range(2)]
            rsum = [TT(f"rsum{i}", [128, 512], F32) for i in range(2)]
            pSc = [PP(f"pSc{i}", [128, 1024], F32) for i in range(2)]
            pOa = PP("pOa", [128, 512], F32); pRs = PP("pRs", [128, 512], F32); pNm = PP("pNm", [128, 512], F32)
            for h in range(4):
                S.dma('sp', KNs[:, h, :], KN[h], (), ['KNs'])
            S.memset('pool', KRs[64:128, :], 0.0, ['KRs'])
            S.dma('sp', KRs[0:64, :], KR, (), ['KRs'])
            for i_ in range(2):
                S.memset('pool', qrr.t[i_][64:128, :], 0.0, [qrr.k[i_]])
            VVv = VV.rearrange("(t p) n -> p t n", p=128)
            for j in range(0, NT, 4):
                je = min(NT, j + 4)
                S.dma('sp', Vs[:, j:je, :], VVv[:, j:je, :], (), ['Vs'])
            S.memset('dve', km2[:], 0.0, ['km2'])
            S.act(sqKR[:], KRs[0:64, :], AF.Square, ['KRs'], ['sqKR'])
            for j0 in range(0, T, 512):
                w_ = min(512, T - j0)
                for h in range(4):
                    S.act(sqn[:, :w_], KNs[:, h, j0:j0 + w_], AF.Square, ['KNs'], ['sqn'])
                    S.mm(pNm[:, :w_], onesb[:], sqn[:, :w_], True, False, ['onesb', 'sqn'], ['pNm'])
                    S.mm(pNm[:, :w_], onesb[0:64, :], sqKR[:, j0:j0 + w_], False, True, ['onesb', 'sqKR'], ['pNm'])
                    S.op('dve', (lambda ww: (lambda e: e.reduce_max(out=tmx[:], in_=pNm[:, :ww], axis=AX.X)))(w_),
                         ['pNm'], ['tmx'])
                    S.tt('dve', km2[:, h:h + 1], km2[:, h:h + 1], tmx[:], ALU.max, ['km2', 'tmx'], ['km2'])
            for g_ in range(8):
                q0 = g_ * 512
                for h in range(4):
                    qn, qnk = qnr.next(); qr, qrk = qrr.next()
                    S.dma('sp', qn[:], QN[h][:, q0:q0 + 512], (), [qnk])
                    S.dma('sp', qr[0:64, :], QR[h][:, q0:q0 + 512], (), [qrk])
                    S.act(sqn[:], qn[:], AF.Square, [qnk], ['sqn'])
                    S.act(sqr[:], qr[0:64, :], AF.Square, [qrk], ['sqr'])
                    S.mm(pNm[:], onesb[:], sqn[:], True, False, ['onesb', 'sqn'], ['pNm'])
                    S.mm(pNm[:], onesb[0:64, :], sqr[:], False, True, ['onesb', 'sqr'], ['pNm'])
                    S.op('dve', lambda e: e.reduce_max(out=tmx[:], in_=pNm[:], axis=AX.X), ['pNm'], ['tmx'])
                    S.ts('dve', nsh[:], tmx[:], km2[:, h:h + 1], -0.5 * SCALE, ALU.add, ALU.mult, ['tmx', 'km2'], ['nsh'])

                    NP_ = NT // 2

                    def qk(j):
                        ps, pk = pSc[j % 2], f'pSc{j % 2}'
                        for u_ in range(2):
                            kt = 2 * j + u_
                            S.mm(ps[:, u_ * 512:(u_ + 1) * 512], KNs[:, h, kt * 128:(kt + 1) * 128], qn[:], True, False,
                                 ['KNs', qnk], [pk])
                            S.mm(ps[:, u_ * 512:(u_ + 1) * 512], KRs[:, kt * 128:(kt + 1) * 128], qr[:], False, True,
                                 ['KRs', qrk], [pk])
                    qk(0)
                    for j in range(NP_):
                        if j + 1 < NP_:
                            qk(j + 1)
                        ps, pk = pSc[j % 2], f'pSc{j % 2}'
                        PT, ptk = PTr.next()
                        S.act(PT[:], ps[:], AF.Exp, [pk, 'nsh'], [ptk], bias=nsh[:], scale=SCALE)
                        for u_ in range(2):
                            kt = 2 * j + u_
                            S.mm(pOa[:], Vs[:, kt, h * 128:(h + 1) * 128], PT[:, u_ * 512:(u_ + 1) * 512],
                                 kt == 0, kt == NT - 1, ['Vs', ptk], ['pOa'])
                        ae = 'dve' if j % 2 == 0 else 'pool'
                        ra, rak = racc[j % 2], f'racc{j % 2}'
                        if j < 2:
                            S.copy(ae, ra[:], PT[:], [ptk], [rak])
                        else:
                            S.tt(ae, ra[:], ra[:], PT[:], ALU.add, [rak, ptk], [rak])
                    S.tt('dve', rsum[0][:], racc[0][:, 0:512], racc[0][:, 512:1024], ALU.add, ['racc0'], ['rsum0'])
                    S.tt('pool', rsum[1][:], racc[1][:, 0:512], racc[1][:, 512:1024], ALU.add, ['racc1'], ['rsum1'])
                    S.mm(pRs[:], onesf[:], rsum[0][:], True, False, ['onesf', 'rsum0'], ['pRs'])
                    S.mm(pRs[:], onesf[:], rsum[1][:], False, True, ['onesf', 'rsum1'], ['pRs'])
                    S.op('dve', lambda e: e.reciprocal(out=rinv[:], in_=pRs[:]), ['pRs'], ['rinv'])
                    ob, obk = osr.next()
                    S.tt('dve', ob[:], pOa[:], rinv[:], ALU.mult, ['pOa', 'rinv'], [obk])
                    S.dma('sp', MIXA[h][:, q0:q0 + 512], ob[:], [obk], ())
            S.emit()
        if stop_after == 4:
            return nc

        with ExitStack() as es:
            def TT(name, shape, dt):
                return es.enter_context(nc.sbuf_tensor(name, shape, dt))

            def PP(name, shape, dt):
                return es.enter_context(nc.psum_tensor(name, shape, dt))

            Wout = TT("Wout", [128, 8, D], BF16)
            mD = TT("mD", [128, 3, D], F32)
            gon = TT("gon", [128, 128], F32)
            woutv = w_out.rearrange("(k p) n -> p k n", p=128)
            for k in range(8):
                S.dma('pool', Wout[:, k, :], woutv[:, k, :], (), ['Wout'])
            for i, mi in enumerate((4, 5, 6)):
                S.dma('sp', mD[:, i, :], MOD[mi], (), ['mD'])
            S.dma('sp', gon[:], g_on.partition_broadcast(128), (), ['gon'])
            ofr = Rot(TT, "dof", [128, 512], F32, 3); obr = Rot(TT, "dob", [128, 512], F32, 3); hgr = Rot(TT, "dhg", [128, 512], F32, 3)
            mar = Rot(TT, "dma_", [128, 4, 128], BF16, 4); xtr = Rot(TT, "dxt", [128, D], F32, 5)
            osumr = Rot(TT, "osum", [128, 512], F32, 2); osqr = Rot(TT, "osq", [128, 512], F32, 2); ss4r = Rot(TT, "ss4", [128, 4], F32, 3)
            tBr = Rot(TT, "tB", [128, 512], F32, 2); hgbr = Rot(TT, "hgb", [128, 512], BF16, 3); mixBr = Rot(TT, "mixB", [128, 4, 128], BF16, 2)
            tmpDr = Rot(TT, "tmpD", [128, D], F32, 2); x1r = Rot(TT, "dx1", [128, D], F32, 3)
            junkD = TT("junkD", [128, D], BF16); ssDr = Rot(TT, "ssD", [128, 1], F32, 3)
            t2Dr = Rot(TT, "t2D", [128, D], F32, 2); h2r = Rot(TT, "h2", [128, D], BF16, 3); h2Tr = Rot(TT, "dh2T", [128, 8, 128], BF16, 3)
            pTBr = Rot(PP, "pTB", [128, 4, 128], BF16, 2)
            pLOr = Rot(PP, "pLO", [128, 512], F32, 4)
            pT8r = Rot(PP, "pT8", [128, 8, 128], BF16, 2)

            def d1_s0(cx, ti):
                r0 = ti * 128
                for nm, rr, src in (('of', ofr, OO[0][r0:r0 + 128, :]), ('ob', obr, OO[1][r0:r0 + 128, :]),
                                    ('hg', hgr, HG[r0:r0 + 128, :]),
                                    ('ma', mar, MIXA[:, :, r0:r0 + 128].rearrange("h p t -> p h t")),
                                    ('xt', xtr, x[r0:r0 + 128, :])):
                    cx[nm], cx[nm + 'k'] = rr.next()
                    S.dma('sp', cx[nm][:], src, (), [cx[nm + 'k']])

            def d1_s1(cx, ti):
                of, ofk, ob, obk, hg, hgk = cx['of'], cx['ofk'], cx['ob'], cx['obk'], cx['hg'], cx['hgk']
                osum, osumk = osumr.next(); osq, osqk = osqr.next(); ss4, ss4k = ss4r.next(); tB, tBk = tBr.next()
                hgb, hgbk = hgbr.next()
                cx['hgb'], cx['hgbk'] = hgb, hgbk
                S.tt('pool', osum[:], of[:], ob[:], ALU.add, [ofk, obk], [osumk])
                S.tt('pool', osq[:], osum[:], osum[:], ALU.mult, [osumk], [osqk])
                S.op('dve', (lambda o_, i_: (lambda e: e.reduce_sum(out=o_[:], in_=i_[:].rearrange("p (h d) -> p h d", h=4),
                                                                    axis=AX.X)))(ss4, osq), [osqk], [ss4k])
                S.act(ss4[:], ss4[:], AF.Sqrt, [ss4k], [ss4k], bias=EPS, scale=1.0 / 128)
                S.op('dve', (lambda o_: (lambda e: e.reciprocal(out=o_[:], in_=o_[:])))(ss4), [ss4k], [ss4k])
                o3 = osum[:].rearrange("p (h d) -> p h d", h=4)
                t3 = tB[:].rearrange("p (h d) -> p h d", h=4)
                S.tt('dve', t3, o3, ss4[:].unsqueeze(2).to_broadcast([128, 4, 128]), ALU.mult, [osumk, ss4k], [tBk])
                S.tt('pool', t3, t3, gon[:].unsqueeze(1).to_broadcast([128, 4, 128]), ALU.mult, [tBk, 'gon'], [tBk])
                S.tt('pool', hgb[:], tB[:], hg[:], ALU.mult, [tBk, hgk], [hgbk])

            def d1_s2(cx, ti):
                hgb, hgbk, ma, mak = cx['hgb'], cx['hgbk'], cx['ma'], cx['mak']
                pTB, pTBk = pTBr.next(); mixB, mixBk = mixBr.next()
                for h in range(4):
                    S.tr(pTB[:, h, :], hgb[:, h * 128:(h + 1) * 128], identb[:], [hgbk, 'identb'], [pTBk])
                S.copy('act', mixB[:], pTB[:], [pTBk], [mixBk])
                cx['pLO'] = []
                for hf in range(2):
                    pl, plk = pLOr.next()
                    cx['pLO'].append((pl, plk))
                    for k in range(4):
                        S.mm(pl[:], ma[:, k, :], Wout[:, k, hf * 512:(hf + 1) * 512], k == 0, False, [mak, 'Wout'], [plk])
                    for k in range(4):
                        S.mm(pl[:], mixB[:, k, :], Wout[:, 4 + k, hf * 512:(hf + 1) * 512], False, k == 3,
                             [mixBk, 'Wout'], [plk])

            def d1_s3(cx, ti):
                r0 = ti * 128
                xt_, xtk = cx['xt'], cx['xtk']
                tmpD, tmpDk = tmpDr.next(); ssD, ssDk = ssDr.next(); t2D, t2Dk = t2Dr.next(); h2, h2k = h2r.next()
                x1, x1k = x1r.next()
                cx['h2'], cx['h2k'] = h2, h2k
                for hf in range(2):
                    pl, plk = cx['pLO'][hf]
                    S.tt('dve', tmpD[:, hf * 512:(hf + 1) * 512], pl[:], mD[:, 0, hf * 512:(hf + 1) * 512], ALU.mult,
                         [plk, 'mD'], [tmpDk])
                S.tt('pool', x1[:], tmpD[:], xt_[:], ALU.add, [tmpDk, xtk], [x1k])
                S.dma('sp', X1[r0:r0 + 128, :], x1[:], [x1k], ())
                S.memset('dve', ssD[:], 0.0, [ssDk])
                S.act(junkD[:], x1[:], AF.Square, [x1k], ['junkD', ssDk], accum=ssD[:])
                S.act(ssD[:], ssD[:], AF.Sqrt, [ssDk], [ssDk], bias=EPS, scale=1.0 / D)
                S.op('dve', (lambda o_: (lambda e: e.reciprocal(out=o_[:], in_=o_[:])))(ssD), [ssDk], [ssDk])
                S.stt('dve', t2D[:], x1[:], ssD[:, 0:1], mD[:, 1, :], ALU.mult, ALU.mult, [x1k, ssDk, 'mD'], [t2Dk])
                S.tt('pool', h2[:], t2D[:], mD[:, 2, :], ALU.add, [t2Dk, 'mD'], [h2k])

            def d1_s4(cx, ti):
                r0 = ti * 128
                h2, h2k = cx['h2'], cx['h2k']
                pT8, pT8k = pT8r.next(); hT2, hT2k = h2Tr.next()
                for k in range(8):
                    S.tr(pT8[:, k, :], h2[:, k * 128:(k + 1) * 128], identb[:], [h2k, 'identb'], [pT8k])
                S.copy('dve' if ti % 2 else 'act', hT2[:], pT8[:], [pT8k], [hT2k])
                S.dma('sp', H2T[:, :, r0:r0 + 128], hT2[:], [hT2k], ())
            run_pipeline(N // 128, [d1_s0, d1_s1, d1_s2, d1_s3, d1_s4])
            S.emit()
        if stop_after == 5:
            return nc

        with ExitStack() as es:
            def TT(name, shape, dt):
                return es.enter_context(nc.sbuf_tensor(name, shape, dt))

            def PP(name, shape, dt):
                return es.enter_context(nc.psum_tensor(name, shape, dt))

            Wg = TT("Wg", [128, 8, DFF], BF16); Wu = TT("Wu", [128, 8, DFF], BF16); Wd = TT("Wd", [128, NCF, D], BF16)
            mE = TT("mE", [128, 2, D], F32)
            wgv = w_gate.rearrange("(k p) n -> p k n", p=128); wuv = w_up.rearrange("(k p) n -> p k n", p=128)
            wdv = w_down.rearrange("(c p) n -> p c n", p=128)
            for k in range(8):
                S.dma('pool', Wg[:, k, :], wgv[:, k, :], (), ['Wg'])
                S.dma('pool', Wu[:, k, :], wuv[:, k, :], (), ['Wu'])
            for c_ in range(NCF):
                S.dma('pool', Wd[:, c_, :], wdv[:, c_, :], (), ['Wd'])
            S.dma('sp', mE[:, 0, :], MOD[7], (), ['mE'])
            S.dma('sp', mE[:, 1, :], g_fin.partition_broadcast(128), (), ['mE'])
            GN = 512
            h2g = Rot(TT, "eh2", [128, 8, GN], BF16, 1)
            aT = TT("aT", [128, NCF, GN], BF16)
            sgr = Rot(TT, "esg", [128, GN], F32, 2)
            x1r = Rot(TT, "ex1", [128, D], F32, 1); tmr = Rot(TT, "etm", [128, D], F32, 2)
            ssE = TT("ssE", [128, 1], F32)
            pG = [PP(f"pG{i}", [128, GN], F32) for i in range(2)]
            pU = [PP(f"pU{i}", [128, GN], F32) for i in range(2)]
            pY = [PP(f"pY{i}", [128, 512], F32) for i in range(2)]
            for g_ in range(N // GN):
                q0 = g_ * GN
                hh, hhk = h2g.next()
                S.dma('sp', hh[:], H2T[:, :, q0:q0 + GN], (), [hhk])
                for c_ in range(NCF):
                    pg, pgk = pG[c_ % 2], f'pG{c_ % 2}'
                    pu, puk = pU[c_ % 2], f'pU{c_ % 2}'
                    for k in range(8):
                        S.mm(pg[:], Wg[:, k, c_ * 128:(c_ + 1) * 128], hh[:, k, :], k == 0, k == 7, ['Wg', hhk], [pgk])
                    for k in range(8):
                        S.mm(pu[:], Wu[:, k, c_ * 128:(c_ + 1) * 128], hh[:, k, :], k == 0, k == 7, ['Wu', hhk], [puk])
                    sg, sgk = sgr.next()
                    S.act(sg[:], pg[:], AF.Silu, [pgk], [sgk])
                    S.tt('dve', aT[:, c_, :], sg[:], pu[:], ALU.mult, [sgk, puk], ['aT'])
                for sb in range(GN // 128):
                    r0 = q0 + sb * 128
                    x1, x1k = x1r.next(); tm, tmk = tmr.next()
                    S.dma('sp', x1[:], X1[r0:r0 + 128, :], (), [x1k])
                    for hf in range(2):
                        for c_ in range(NCF):
                            S.mm(pY[hf][:], aT[:, c_, sb * 128:(sb + 1) * 128], Wd[:, c_, hf * 512:(hf + 1) * 512],
                                 c_ == 0, c_ == NCF - 1, ['aT', 'Wd'], [f'pY{hf}'])
                        S.tt('dve', tm[:, hf * 512:(hf + 1) * 512], pY[hf][:], mE[:, 0, hf * 512:(hf + 1) * 512], ALU.mult,
                             [f'pY{hf}', 'mE'], [tmk])
                    S.tt('pool', x1[:], tm[:], x1[:], ALU.add, [tmk, x1k], [x1k])
                    S.memset('dve', ssE[:], 0.0, ['ssE'])
                    S.act(tm[:], x1[:], AF.Square, [x1k, tmk], [tmk, 'ssE'], accum=ssE[:])
                    S.act(ssE[:], ssE[:], AF.Sqrt, ['ssE'], ['ssE'], bias=EPS, scale=1.0 / D)
                    S.op('dve', lambda e: e.reciprocal(out=ssE[:], in_=ssE[:]), ['ssE'], ['ssE'])
                    S.stt('dve', tm[:], x1[:], ssE[:, 0:1], mE[:, 1, :], ALU.mult, ALU.mult, [x1k, 'ssE', 'mE'], [tmk])
                    S.dma('sp', out[r0:r0 + 128, :], tm[:], [tmk], ())
            S.emit()
    return nc


_NC_CACHE = {}


def kernel(**inputs):
    if 'nc' not in _NC_CACHE:
        _NC_CACHE['nc'] = build_nc()
    nc = _NC_CACHE['nc']
    f = lambda a: np.ascontiguousarray(np.asarray(a, dtype=np.float32))
    shared = {
        "c_ctx": f(inputs["c_ctx"]), "w_mod": f(inputs["w_mod"][0]), "b_mod": f(inputs["b_mod"][0]),
        "g_norm_mix": f(inputs["g_norm_mix"][0]), "g_norm_ffn": f(inputs["g_norm_ffn"][0]),
        "w_in": f(inputs["w_in"][0]), "g_q_norm": f(inputs["g_q_norm"][0]), "w_uq": f(inputs["w_uq"][0]),
        "g_kv_norm": f(inputs["g_kv_norm"][0]), "w_ukv": f(inputs["w_ukv"][0]),
        "lb_fwd": f(inputs["lb_fwd"]), "lb_bwd": f(inputs["lb_bwd"]), "g_hgrn_norm": f(inputs["g_hgrn_norm"][0]),
        "w_out": f(inputs["w_out"][0]), "w_gate": f(inputs["w_gate"][0]), "w_up": f(inputs["w_up"][0]),
        "w_down": f(inputs["w_down"][0]), "g_final": f(inputs["g_final"]),
    }
    xs, cs, ctxs = f(inputs["x"]), f(inputs["c"]), f(inputs["ctx"])
    in_maps = []
    for b in range(NB):
        m = dict(shared)
        m["x"] = xs[b]; m["c"] = cs[b]; m["ctx"] = ctxs[b]
        in_maps.append(m)
    res = run_bass_kernel_spmd(nc, in_maps, core_ids=list(range(NB)))
    return np.stack([np.asarray(r["out"], dtype=np.float32) for r in res.results], axis=0)
```

```python
import math
from contextlib import ExitStack

import numpy as np
import concourse.bass as bass
import concourse.mybir as mybir
from concourse.bass_utils import run_bass_kernel_spmd

F32 = mybir.dt.float32
BF16 = mybir.dt.bfloat16
AF = mybir.ActivationFunctionType
ALU = mybir.AluOpType
AX = mybir.AxisListType

NB, N, L, D = 8, 4096, 256, 1024
T = N + L
NT = T // 128
DFF = 2816
NCF = DFF // 128
EPS = 1e-6
INC = 3136
SCALE = 1.0 / math.sqrt(192.0)


class Sched:
    ENG = ('pe', 'act', 'dve', 'pool', 'sp')

    def __init__(self, nc, n_dma_sems=24):
        self.nc = nc
        self.ops = {e: [] for e in self.ENG}
        self.sems = {}
        for e in ('pe', 'act', 'dve', 'pool'):
            self.sems[('e', e)] = nc.alloc_semaphore(name=f"s_{e}")
        for i in range(n_dma_sems):
            self.sems[('d', i)] = nc.alloc_semaphore(name=f"s_dma{i}")
        self.nw = 6
        for i in range(self.nw):
            self.sems[('w', i)] = nc.alloc_semaphore(name=f"s_swdma{i}")
        self.wnext = 0
        self.cnt = {k: 0 for k in self.sems}
        self.nd = n_dma_sems
        self.dnext = 0
        self.waited = {e: {} for e in self.ENG}
        self.res = {}
        self.enabled = True

    def _deps(self, reads, writes):
        t = []
        for r in reads:
            st = self.res.get(r)
            if st and st['w']:
                t.append(st['w'])
        for w in writes:
            st = self.res.get(w)
            if st:
                if st['w']:
                    t.append(st['w'])
                t.extend(st['r'].values())
        return t

    def _need(self, eng, tickets):
        best = {}
        for key, val in tickets:
            if key == ('e', 'pe') and eng == 'pe':
                continue
            if self.waited[eng].get(key, 0) >= val:
                continue
            if best.get(key, 0) < val:
                best[key] = val
        for key, val in best.items():
            self.waited[eng][key] = val
        return list(best.items())

    def _commit(self, ticket, reads, writes):
        for r in reads:
            st = self.res.setdefault(r, {'w': None, 'r': {}})
            st['r'][ticket[0]] = ticket
        for w in writes:
            self.res[w] = {'w': ticket, 'r': {}}

    def op(self, eng, fn, reads=(), writes=()):
        if not self.enabled:
            return None
        waits = self._need(eng, self._deps(reads, writes))
        key = ('e', eng)
        self.cnt[key] += 1
        ticket = (key, self.cnt[key])
        self.ops[eng].append((waits, fn, key, 1))
        self._commit(ticket, reads, writes)
        return ticket

    def dma(self, q, out, in_, reads=(), writes=()):
        if not self.enabled:
            return None
        if q == 'pool':
            key = ('w', self.wnext)
            self.wnext = (self.wnext + 1) % self.nw
        else:
            key = ('d', self.dnext)
            self.dnext = (self.dnext + 1) % self.nd
        tickets = self._deps(reads, writes)
        if self.cnt[key] > 0:
            tickets.append((key, self.cnt[key]))
        waits = self._need(q, tickets)
        self.cnt[key] += 16
        ticket = (key, self.cnt[key])
        self.ops[q].append((waits, lambda e: e.dma_start(out=out, in_=in_), key, 16))
        self._commit(ticket, reads, writes)
        return ticket

    def barrier(self):
        allt = [(k, v) for k, v in self.cnt.items() if v > 0]
        for e in self.ENG:
            waits = self._need(e, allt)
            if waits:
                self.ops[e].append((waits, None, None, 0))

    def emit(self):
        self.barrier()
        with self.nc.Block() as block:
            def mk(engname):
                def body(e):
                    for waits, fn, semkey, inc in self.ops[engname]:
                        for key, val in waits:
                            e.wait_ge(self.sems[key], val)
                        if fn is not None:
                            fn(e).then_inc(self.sems[semkey], inc)
                return body
            block.tensor(mk('pe'))
            block.scalar(mk('act'))
            block.vector(mk('dve'))
            block.gpsimd(mk('pool'))
            block.sync(mk('sp'))
        self.ops = {e: [] for e in self.ENG}

    def mm(self, out, lhsT, rhs, start, stop, r, w):
        return self.op('pe', lambda e: e.matmul(out, lhsT=lhsT, rhs=rhs, start=start, stop=stop), r, w)

    def tr(self, out, in_, ident, r, w):
        return self.op('pe', lambda e: e.transpose(out, in_, ident), r, w)

    def act(self, out, in_, func, r, w, bias=None, scale=None, accum=None):
        kw = {}
        if bias is not None:
            kw['bias'] = bias
        if scale is not None:
            kw['scale'] = scale
        if accum is not None:
            kw['accum_out'] = accum
        return self.op('act', lambda e: e.activation(out=out, in_=in_, func=func, **kw), r, w)

    def tt(self, eng, out, in0, in1, op, r, w):
        return self.op(eng, lambda e: e.tensor_tensor(out=out, in0=in0, in1=in1, op=op), r, w)

    def ts(self, eng, out, in0, s1, s2, op0, op1, r, w):
        if s2 is None:
            return self.op(eng, lambda e: e.tensor_scalar(out=out, in0=in0, scalar1=s1, scalar2=None, op0=op0), r, w)
        return self.op(eng, lambda e: e.tensor_scalar(out=out, in0=in0, scalar1=s1, scalar2=s2, op0=op0, op1=op1), r, w)

    def stt(self, eng, out, in0, scalar, in1, op0, op1, r, w):
        return self.op(eng, lambda e: e.scalar_tensor_tensor(out=out, in0=in0, scalar=scalar, in1=in1, op0=op0, op1=op1), r, w)

    def copy(self, eng, out, in_, r, w):
        if eng == 'act':
            return self.act(out, in_, AF.Copy, r, w)
        return self.op(eng, lambda e: e.tensor_copy(out=out, in_=in_), r, w)

    def memset(self, eng, ap, val, w):
        return self.op(eng, lambda e: e.memset(ap, val), (), w)


def run_pipeline(n, stages):
    ctxs = [dict() for _ in range(n)]
    K = len(stages)
    for t in range(n + K - 1):
        for k in range(K - 1, -1, -1):
            i = t - k
            if 0 <= i < n:
                stages[k](ctxs[i], i)


class Rot:
    def __init__(self, alloc, name, shape, dt, n):
        self.t = [alloc(f"{name}{i}", shape, dt) for i in range(n)]
        self.k = [f"{name}{i}" for i in range(n)]
        self.i = 0

    def next(self):
        j = self.i % len(self.t)
        self.i += 1
        return self.t[j], self.k[j]


def _rstd(S, ss, out, dim, tag):
    S.act(out, ss, AF.Sqrt, [tag + 'ss'], [tag + 'rs'], bias=EPS, scale=1.0 / dim)
    S.op('dve', lambda e: e.reciprocal(out=out, in_=out), [tag + 'rs'], [tag + 'rs'])


def build_nc(stop_after=None, debug=False, a2_groups=None, a2_parts='abcde', STQ='sp'):
    nc = bass.Bass("TRN2", target_bir_lowering=False)
    S = Sched(nc)

    def din(name, shape):
        return nc.dram_tensor(name, shape, F32, kind="ExternalInput").ap()

    x = din("x", [N, D]); c = din("c", [D]); ctx = din("ctx", [L, D]); c_ctx = din("c_ctx", [D])
    w_mod = din("w_mod", [D, 6 * D]); b_mod = din("b_mod", [6 * D])
    g_mix = din("g_norm_mix", [D]); g_ffn = din("g_norm_ffn", [D])
    w_in = din("w_in", [D, INC]); g_qn = din("g_q_norm", [256]); w_uq = din("w_uq", [256, 768])
    g_kvn = din("g_kv_norm", [256]); w_ukv = din("w_ukv", [256, 1024])
    lb_f = din("lb_fwd", [2, 512]); lb_b = din("lb_bwd", [2, 512]); g_on = din("g_hgrn_norm", [128])
    w_out = din("w_out", [D, D]); w_gate = din("w_gate", [D, DFF]); w_up = din("w_up", [D, DFF])
    w_down = din("w_down", [DFF, D]); g_fin = din("g_final", [D])
    out = nc.dram_tensor("out", [N, D], F32, kind="ExternalOutput").ap()

    dbg = set(debug) if debug else set()

    def scr(name, shape, dt):
        return nc.dram_tensor(name, shape, dt, kind="ExternalOutput" if name in dbg else "Internal").ap()

    MOD = scr("s_mod", [8, 128, D], F32)
    HT = scr("s_ht", [128, 8, T], BF16)
    QN = scr("s_qn", [4, 128, N], BF16); QR = scr("s_qr", [4, 64, N], BF16)
    KN = scr("s_kn", [4, 128, T], BF16); KR = scr("s_kr", [64, T], BF16)
    VV = scr("s_v", [T, 512], BF16)
    HQ = scr("s_hq", [4, 128, T], F32); HV = scr("s_hv", [T, 512], BF16); HG = scr("s_hg", [N, 512], F32)
    GG = scr("s_g", [2, T, 512], F32); KG = scr("s_kg", [2, T, 512], F32)
    OO = scr("s_o", [2, N, 512], F32)
    MIXA = scr("s_mixa", [4, 128, N], BF16)
    X1 = scr("s_x1", [N, D], F32); H2T = scr("s_h2t", [128, 8, N], BF16)

    outer = ExitStack()
    with outer:
        def CT(name, shape, dt):
            return outer.enter_context(nc.sbuf_tensor(name, shape, dt))
        identb = CT("identb", [128, 128], BF16); identf = CT("identf", [128, 128], F32)
        onesb = CT("onesb", [128, 128], BF16); onesf = CT("onesf", [128, 128], F32)
        Uf = CT("Uf", [128, 128], F32); Ub = CT("Ub", [128, 128], F32)
        Rf = CT("Rf", [128, 128], F32); Rb = CT("Rb", [128, 128], F32)
        Mf = CT("Mf", [128, 128], F32); Mb = CT("Mb", [128, 128], F32)
        Ind = CT("Ind", [128, 2], F32)

        def sel(tile, pattern, cm, cmp_op, key):
            S.op('pool', lambda e: e.affine_select(out=tile[:], in_=tile[:], pattern=pattern, compare_op=cmp_op,
                                                   fill=0.0, base=0, channel_multiplier=cm), [key], [key])
        for tl, key in ((identb, 'identb'), (identf, 'identf')):
            S.memset('pool', tl[:], 1.0, [key])
            sel(tl, [[-1, 128]], 1, ALU.is_equal, key)
        S.memset('pool', onesb[:], 1.0, ['onesb'])
        S.memset('pool', onesf[:], 1.0, ['onesf'])
        S.memset('pool', Uf[:], 1.0, ['Uf']); sel(Uf, [[-1, 128]], 1, ALU.is_gt, 'Uf')
        S.memset('pool', Uf[64:128, 0:64], 0.0, ['Uf'])
        S.memset('pool', Mb[:], 1.0, ['Mb']); sel(Mb, [[-1, 128]], 1, ALU.is_ge, 'Mb')
        S.memset('pool', Mb[64:128, 0:64], 0.0, ['Mb'])
        S.memset('pool', Ub[:], 1.0, ['Ub']); sel(Ub, [[1, 128]], -1, ALU.is_gt, 'Ub')
        S.memset('pool', Ub[0:64, 64:128], 0.0, ['Ub'])
        S.memset('pool', Mf[:], 1.0, ['Mf']); sel(Mf, [[1, 128]], -1, ALU.is_ge, 'Mf')
        S.memset('pool', Mf[0:64, 64:128], 0.0, ['Mf'])
        S.ts('pool', Rf[:], Uf[:], -1.0, None, ALU.mult, None, ['Uf'], ['Rf'])
        S.ts('pool', Rb[:], Ub[:], -1.0, None, ALU.mult, None, ['Ub'], ['Rb'])
        S.memset('pool', Ind[:], 0.0, ['Ind'])
        S.memset('pool', Ind[0:64, 0:1], 1.0, ['Ind'])
        S.memset('pool', Ind[64:128, 1:2], 1.0, ['Ind'])

        with ExitStack() as es:
            def TT(name, shape, dt):
                return es.enter_context(nc.sbuf_tensor(name, shape, dt))

            def PP(name, shape, dt):
                return es.enter_context(nc.psum_tensor(name, shape, dt))
            crow = TT("crow", [128, 2, D], F32)
            cb = TT("cb", [128, 2, 8, 128], F32)
            bmod = TT("bmod", [128, 6 * D], F32)
            wm = [TT(f"wm{i}", [128, 8, 512], F32) for i in range(2)]
            modl = TT("modl", [128, 6 * D], F32); modc = TT("modc", [128, 2 * D], F32)
            gm = TT("gm", [128, D], F32); gf = TT("gf", [128, D], F32)
            tmpA = [TT(f"tmpA{i}", [128, D], F32) for i in range(3)]
            pcb = PP("pcb", [128, 128], F32)
            pm = [PP(f"pm{i}", [128, 512], F32) for i in range(2)]
            S.dma('sp', crow[:, 0, :], c.partition_broadcast(128), (), ['crow'])
            S.dma('sp', crow[:, 1, :], c_ctx.partition_broadcast(128), (), ['crow'])
            S.dma('sp', bmod[:], b_mod.partition_broadcast(128), (), ['bmod'])
            S.dma('sp', gm[:], g_mix.partition_broadcast(128), (), ['gm'])
            S.dma('sp', gf[:], g_ffn.partition_broadcast(128), (), ['gf'])
            S.act(crow[:], crow[:], AF.Silu, ['crow'], ['crow'])
            for w_ in range(2):
                for k in range(8):
                    S.mm(pcb[:], crow[:, w_, k * 128:(k + 1) * 128], identf[:], True, True, ['crow', 'identf'], ['pcb'])
                    S.copy('dve', cb[:, w_, k, :], pcb[:], ['pcb'], ['cb'])
            wmv = w_mod.rearrange("(k p) n -> p k n", p=128)
            for j in range(12):
                wt = wm[j % 2]; wk = f"wm{j % 2}"
                S.dma('sp', wt[:], wmv[:, :, j * 512:(j + 1) * 512], (), [wk])
                for k in range(8):
                    S.mm(pm[0][:], cb[:, 0, k, :], wt[:, k, :], k == 0, k == 7, ['cb', wk], ['pm0'])
                S.tt('dve', modl[:, j * 512:(j + 1) * 512], pm[0][:], bmod[:, j * 512:(j + 1) * 512], ALU.add,
                     ['pm0', 'bmod'], ['modl'])
                if j < 4:
                    for k in range(8):
                        S.mm(pm[1][:], cb[:, 1, k, :], wt[:, k, :], k == 0, k == 7, ['cb', wk], ['pm1'])
                    S.tt('dve', modc[:, j * 512:(j + 1) * 512], pm[1][:], bmod[:, j * 512:(j + 1) * 512], ALU.add,
                         ['pm1', 'bmod'], ['modc'])
            S.stt('dve', tmpA[0][:], modl[:, D:2 * D], 1.0, gm[:], ALU.add, ALU.mult, ['modl', 'gm'], ['tA0'])
            S.stt('dve', tmpA[1][:], modc[:, D:2 * D], 1.0, gm[:], ALU.add, ALU.mult, ['modc', 'gm'], ['tA1'])
            S.stt('dve', tmpA[2][:], modl[:, 4 * D:5 * D], 1.0, gf[:], ALU.add, ALU.mult, ['modl', 'gf'], ['tA2'])
            S.dma('sp', MOD[0], tmpA[0][:], ['tA0'], ())
            S.dma('sp', MOD[1], modl[:, 0:D], ['modl'], ())
            S.dma('sp', MOD[2], tmpA[1][:], ['tA1'], ())
            S.dma('sp', MOD[3], modc[:, 0:D], ['modc'], ())
            S.dma('sp', MOD[4], modl[:, 2 * D:3 * D], ['modl'], ())
            S.dma('sp', MOD[5], tmpA[2][:], ['tA2'], ())
            S.dma('sp', MOD[6], modl[:, 3 * D:4 * D], ['modl'], ())
            S.dma('sp', MOD[7], modl[:, 5 * D:6 * D], ['modl'], ())
            S.emit()
        if stop_after == 0:
            return nc

        with ExitStack() as es:
            def TT(name, shape, dt):
                return es.enter_context(nc.sbuf_tensor(name, shape, dt))

            def PP(name, shape, dt):
                return es.enter_context(nc.psum_tensor(name, shape, dt))
            mA = TT("mA", [128, 4, D], F32)
            junk = TT("junk", [128, D], BF16)
            xtR = Rot(TT, "xt", [128, D], F32, 3); ssR = Rot(TT, "ssa", [128, 1], F32, 3)
            t1R = Rot(TT, "t1_", [128, D], F32, 2); hbR = Rot(TT, "hb", [128, D], BF16, 3)
            hTR = Rot(TT, "hT", [128, 8, 128], BF16, 3); ptrR = Rot(PP, "ptr", [128, 8, 128], BF16, 2)
            for i in range(4):
                S.dma('sp', mA[:, i, :], MOD[i], (), ['mA'])

            def a1_s0(cx, ti):
                cx['xt'], cx['xk'] = xtR.next()
                src = ctx[ti * 128:(ti + 1) * 128, :] if ti < 2 else x[(ti - 2) * 128:(ti - 1) * 128, :]
                S.dma('sp', cx['xt'][:], src, (), [cx['xk']])

            def a1_s1(cx, ti):
                xt_, xk = cx['xt'], cx['xk']
                mi = 2 if ti < 2 else 0
                ss, sk = ssR.next(); t1, t1k = t1R.next(); hb, hbk = hbR.next()
                cx['hb'], cx['hbk'] = hb, hbk
                S.memset('dve', ss[:], 0.0, [sk])
                S.act(junk[:], xt_[:], AF.Square, [xk], ['junk', sk], accum=ss[:])
                S.act(ss[:], ss[:], AF.Sqrt, [sk], [sk], bias=EPS, scale=1.0 / D)
                S.op('dve', (lambda o_: (lambda e: e.reciprocal(out=o_[:], in_=o_[:])))(ss), [sk], [sk])
                S.stt('dve', t1[:], xt_[:], ss[:, 0:1], mA[:, mi, :], ALU.mult, ALU.mult, [xk, sk, 'mA'], [t1k])
                S.tt('pool', hb[:], t1[:], mA[:, mi + 1, :], ALU.add, [t1k, 'mA'], [hbk])

            def a1_s2(cx, ti):
                hb, hbk = cx['hb'], cx['hbk']
                pt, ptk = ptrR.next(); hT, hTk = hTR.next()
                for k in range(8):
                    S.tr(pt[:, k, :], hb[:, k * 128:(k + 1) * 128], identb[:], [hbk, 'identb'], [ptk])
                S.copy('dve' if ti % 2 else 'act', hT[:], pt[:], [ptk], [hTk])
                S.dma('sp', HT[:, :, ti * 128:(ti + 1) * 128], hT[:], [hTk], ())
            run_pipeline(NT, [a1_s0, a1_s1, a1_s2])
            S.emit()
        if stop_after == 1:
            return nc

        with ExitStack() as es:
            def TT(name, shape, dt):
                return es.enter_context(nc.sbuf_tensor(name, shape, dt))

            def PP(name, shape, dt):
                return es.enter_context(nc.psum_tensor(name, shape, dt))
            Win = TT("Win", [128, 8, INC], BF16)
            Wkpe = TT("Wkpe", [128, 8, 128], BF16); Wkrot = TT("Wkrot", [128, 8, 128], BF16)
            Wqn = TT("Wqn", [128, 2, 4, 128], BF16); Wqr = TT("Wqr", [128, 2, 4, 128], BF16)
            Wqrot = TT("Wqrot", [128, 2, 4, 128], BF16)
            Wkn = TT("Wkn", [128, 2, 4, 128], BF16); Wv = TT("Wv", [128, 2, 4, 128], BF16)
            Ct = TT("Ct", [64, N], F32); St = TT("St", [64, N], F32)
            lbt = TT("lbt", [128, 2, 512], F32); oml = TT("oml", [128, 2, 512], F32)
            with ExitStack() as es2:
                def T2(name, shape, dt):
                    return es2.enter_context(nc.sbuf_tensor(name, shape, dt))
                winv = w_in.rearrange("(k p) n -> p k n", p=128)
                for tl_, k_ in ((Wkpe, 'Wkpe'), (Wkrot, 'Wkrot')):
                    S.memset('dve', tl_[:, :, 64:128], 0.0, [k_])
                for tl_, k_ in ((Wqr, 'Wqr'), (Wqrot, 'Wqrot')):
                    S.memset('dve', tl_[:, :, :, 64:128], 0.0, [k_])
                for k in range(8):
                    S.dma('pool', Win[:, k, :], winv[:, k, :], (), ['Win'])
                    for f_ in range(2):
                        S.dma('pool', Wkpe[:, k, f_ * 32:(f_ + 1) * 32].rearrange("p (a i) -> p a i", a=2),
                              winv[:, k, 512:576].rearrange("p (a f i) -> p f a i", a=2, f=2)[:, f_, :, :], (), ['Wkpe'])
                S.ts('dve', Wkrot[:, :, 0:32], Wkpe[:, :, 32:64], -1.0, None, ALU.mult, None, ['Wkpe'], ['Wkrot'])
                S.copy('dve', Wkrot[:, :, 32:64], Wkpe[:, :, 0:32], ['Wkpe'], ['Wkrot'])
                stq = T2("stq", [128, 2, 768], F32); stkv = T2("stkv", [128, 2, 1024], F32)
                gq = T2("gq", [128, 2], F32); gkv = T2("gkv", [128, 2], F32)
                S.dma('sp', stq[:], w_uq.rearrange("(c p) n -> p c n", p=128), (), ['stq'])
                S.dma('sp', stkv[:], w_ukv.rearrange("(c p) n -> p c n", p=128), (), ['stkv'])
                for c_ in range(2):
                    S.dma('sp', gq[:, c_:c_ + 1], g_qn[c_ * 128:(c_ + 1) * 128].rearrange("(p o) -> p o", o=1), (), ['gq'])
                    S.dma('sp', gkv[:, c_:c_ + 1], g_kvn[c_ * 128:(c_ + 1) * 128].rearrange("(p o) -> p o", o=1), (), ['gkv'])
                for c_ in range(2):
                    sq_v = stq[:, c_, :].rearrange("p (h d) -> p h d", h=4)
                    S.ts('dve', Wqn[:, c_, :, :], sq_v[:, :, 0:128], gq[:, c_:c_ + 1], None, ALU.mult, None,
                         ['stq', 'gq'], ['Wqn'])
                    for f_ in range(2):
                        for a_ in range(2):
                            so = 128 + a_ * 32 + f_ * 16
                            do = f_ * 32 + a_ * 16
                            S.ts('dve', Wqr[:, c_, :, do:do + 16], sq_v[:, :, so:so + 16], gq[:, c_:c_ + 1], None,
                                 ALU.mult, None, ['stq', 'gq'], ['Wqr'])
                    S.ts('dve', Wqrot[:, c_, :, 0:32], Wqr[:, c_, :, 32:64], -1.0, None, ALU.mult, None, ['Wqr'], ['Wqrot'])
                    S.copy('dve', Wqrot[:, c_, :, 32:64], Wqr[:, c_, :, 0:32], ['Wqr'], ['Wqrot'])
                    skv_v = stkv[:, c_, :].rearrange("p (h t d) -> p h t d", h=4, t=2)
                    S.ts('dve', Wkn[:, c_, :, :], skv_v[:, :, 0, :], gkv[:, c_:c_ + 1], None, ALU.mult, None,
                         ['stkv', 'gkv'], ['Wkn'])
                    S.ts('dve', Wv[:, c_, :, :], skv_v[:, :, 1, :], gkv[:, c_:c_ + 1], None, ALU.mult, None,
                         ['stkv', 'gkv'], ['Wv'])
                lraw = T2("lraw", [128, 2, 2, 512], F32)
                for d_, lbx in enumerate((lb_f, lb_b)):
                    for r_ in range(2):
                        S.dma('sp', lraw[:, d_, r_, :], lbx[r_].partition_broadcast(128), (), ['lraw'])
                S.tt('dve', lbt[:], lraw[:, :, 0, :], lraw[:, :, 1, :], ALU.subtract, ['lraw'], ['lbt'])
                S.act(lbt[:], lbt[:], AF.Sigmoid, ['lbt'], ['lbt'])
                S.ts('dve', oml[:], lbt[:], -0.5, 0.5, ALU.mult, ALU.add, ['lbt'], ['oml'])
                S.tt('dve', lbt[:], lbt[:], oml[:], ALU.add, ['lbt', 'oml'], ['lbt'])
                pidx = T2("pidx", [64, 1], F32); i16 = T2("i16", [64, 1], F32); mrow = T2("mrow", [64, 1], F32)
                arow = T2("arow", [64, 1], F32); acol = T2("acol", [64, 1], F32)
                rowpos = T2("rowpos", [64, N], F32); colpos = T2("colpos", [64, N], F32); ang = T2("ang", [64, N], F32)
                S.op('pool', lambda e: e.iota(pidx[:], [[0, 1]], base=0, channel_multiplier=1,
                                              allow_small_or_imprecise_dtypes=True), (), ['pidx'])
                S.op('pool', lambda e: e.iota(rowpos[:], [[1, 64], [0, 64]], base=0, channel_multiplier=0,
                                              allow_small_or_imprecise_dtypes=True), (), ['rowpos'])
                S.op('pool', lambda e: e.iota(colpos[:], [[0, 64], [1, 64]], base=0, channel_multiplier=0,
                                              allow_small_or_imprecise_dtypes=True), (), ['colpos'])
                msk = T2("msk", [64, 3], F32)
                S.memset('pool', msk[:], 1.0, ['msk'])
                for j_ in range(3):
                    S.op('pool', (lambda jj: (lambda e: e.affine_select(
                        out=msk[:, jj:jj + 1], in_=msk[:, jj:jj + 1], pattern=[[0, 1]], compare_op=ALU.is_ge, fill=0.0,
                        base=-16 * (jj + 1), channel_multiplier=1)))(j_), ['msk'], ['msk'])
                S.tt('dve', mrow[:], msk[:, 0:1], msk[:, 1:2], ALU.add, ['msk'], ['mrow'])
                S.tt('dve', mrow[:], mrow[:], msk[:, 2:3], ALU.add, ['msk', 'mrow'], ['mrow'])
                S.stt('dve', i16[:], mrow[:], -16.0, pidx[:], ALU.mult, ALU.add, ['mrow', 'pidx'], ['i16'])
                S.tt('dve', mrow[:], msk[:, 1:2], msk[:, 0:1], ALU.subtract, ['msk'], ['mrow'])
                S.tt('dve', mrow[:], mrow[:], msk[:, 2:3], ALU.subtract, ['msk', 'mrow'], ['mrow'])
                S.ts('dve', mrow[:], mrow[:], 1.0, None, ALU.add, None, ['mrow'], ['mrow'])
                S.act(i16[:], i16[:], AF.Exp, ['i16'], ['i16'], scale=-math.log(10000.0) / 16.0)
                S.tt('dve', arow[:], i16[:], mrow[:], ALU.mult, ['i16', 'mrow'], ['arow'])
                S.tt('dve', acol[:], i16[:], arow[:], ALU.subtract, ['i16', 'arow'], ['acol'])
                S.ts('dve', ang[:], rowpos[:], arow[:, 0:1], None, ALU.mult, None, ['rowpos', 'arow'], ['ang'])
                S.stt('dve', ang[:], colpos[:], acol[:, 0:1], ang[:], ALU.mult, ALU.add, ['colpos', 'acol', 'ang'], ['ang'])
                sc_ = 1.0 - 1e-6
                ki = T2("ki", [64, N], mybir.dt.int32)
                for tab, shift in ((St, 0.0), (Ct, 0.5 * math.pi)):
                    S.ts('dve', rowpos[:], ang[:], shift, 1.0 / (2 * math.pi), ALU.add, ALU.mult, ['ang'], ['rowpos'])
                    S.copy('dve', ki[:], rowpos[:], ['rowpos'], ['ki'])
                    S.copy('dve', colpos[:], ki[:], ['ki'], ['colpos'])
                    S.ts('dve', rowpos[:], ang[:], shift, None, ALU.add, None, ['ang'], ['rowpos'])
                    S.stt('dve', rowpos[:], colpos[:], -2 * math.pi, rowpos[:], ALU.mult, ALU.add, ['colpos', 'rowpos'], ['rowpos'])
                    S.act(tab[:], rowpos[:], AF.Sin, ['rowpos'], ['St' if shift == 0.0 else 'Ct'], scale=sc_)
                if stop_after == 15:
                    for nm, tl, shp, dt_ in (("d_Ct", Ct, [64, N], F32), ("d_St", St, [64, N], F32),
                                             ("d_Wkpe", Wkpe, [128, 8, 128], BF16), ("d_Wkrot", Wkrot, [128, 8, 128], BF16),
                                             ("d_Wqn", Wqn, [128, 2, 4, 128], BF16), ("d_Wqr", Wqr, [128, 2, 4, 128], BF16),
                                             ("d_Wqrot", Wqrot, [128, 2, 4, 128], BF16), ("d_Wkn", Wkn, [128, 2, 4, 128], BF16),
                                             ("d_Wv", Wv, [128, 2, 4, 128], BF16), ("d_lbt", lbt, [128, 2, 512], F32),
                                             ("d_oml", oml, [128, 2, 512], F32), ("d_Win", Win, [128, 8, INC], BF16)):
                        dd = nc.dram_tensor(nm, shp, dt_, kind="ExternalOutput").ap()
                        S.dma('sp', dd, tl[:], [nm[2:]], ())
                S.emit()
                if stop_after == 15:
                    return nc

            hTg = Rot(TT, "hTg", [128, 8, 512], BF16, 2)
            cT = TT("cT", [128, 2, 512], BF16); sq = TT("sq", [128, 2, 512], BF16)
            rbc = TT("rbc", [128, 512], F32); rtk = TT("rtk", [128, 4], F32)
            o_bf = Rot(TT, "o_bf", [128, 512], BF16, 3)
            o_f = Rot(TT, "o_f", [128, 512], F32, 4)
            u_f = Rot(TT, "u_f", [64, 512], F32, 4)
            sgb = Rot(TT, "sgb", [128, 512], F32, 3); fb_ = Rot(TT, "fb_", [128, 512], F32, 3)
            pA = [PP(f"pA{i}", [128, 512], F32) for i in range(2)]
            pB = [PP(f"pB{i}", [128, 512], F32) for i in range(2)]
            pS = PP("pS", [128, 512], F32)
            pT = [PP(f"pT{i}", [128, 512], F32) for i in range(2)]
            pV = PP("pV", [128, 4, 128], F32)
            groups = [(0, 256)] + [(256 + i * 512, 512) for i in range(8)]
            if a2_groups is not None:
                groups = groups[:a2_groups]
            for (tok0, n) in groups:
                is_lat = tok0 >= L
                lo = tok0 - L
                nsub = n // 128
                S.enabled = True
                hT_, hk = hTg.next()
                S.dma('sp', hT_[:, :, :n], HT[:, :, tok0:tok0 + n], (), [hk])

                def fm_proj(ps, pk, wt, wk, col0, ncols):
                    for k in range(8):
                        S.mm(ps[0:ncols, :n], wt[:, k, col0:col0 + ncols], hT_[:, k, :n], k == 0, k == 7, [wk, hk], [pk])

                def lowrank(col0, want_tok):
                    for c_ in range(2):
                        fm_proj(pA[c_], f'pA{c_}', Win, 'Win', col0 + c_ * 128, 128)
                        S.copy('act', cT[:, c_, :n], pA[c_][:, :n], [f'pA{c_}'], ['cT'])
                        S.act(sq[:, c_, :n], pA[c_][:, :n], AF.Square, [f'pA{c_}'], ['sq'])
                    for c_ in range(2):
                        S.mm(pS[:, :n], onesb[:], sq[:, c_, :n], c_ == 0, c_ == 1, ['onesb', 'sq'], ['pS'])
                    S.act(rbc[:, :n], pS[:, :n], AF.Sqrt, ['pS'], ['rbc'], bias=EPS, scale=1.0 / 256)
                    S.op('dve', (lambda nn: (lambda e: e.reciprocal(out=rbc[:, :nn], in_=rbc[:, :nn])))(n), ['rbc'], ['rbc'])
                    if want_tok:
                        for s_ in range(nsub):
                            for c_ in range(2):
                                S.mm(pV[:, s_, :], sq[:, c_, s_ * 128:(s_ + 1) * 128], onesb[:], c_ == 0, c_ == 1,
                                     ['sq', 'onesb'], ['pV'])
                        S.act(rtk[:, :nsub], pV[:, :nsub, 0], AF.Sqrt, ['pV'], ['rtk'], bias=EPS, scale=1.0 / 256)
                        S.op('dve', (lambda ns: (lambda e: e.reciprocal(out=rtk[:, :ns], in_=rtk[:, :ns])))(nsub), ['rtk'], ['rtk'])

                def rope_out(p0, k0, p1, k1, dst, scale_rows):
                    u1, uk1 = u_f.next(); u2, uk2 = u_f.next()
                    S.tt('dve', u1[:, :n], p0[0:64, :n], Ct[:, lo:lo + n], ALU.mult, [k0, 'Ct'], [uk1])
                    S.tt('dve', u2[:, :n], p1[0:64, :n], St[:, lo:lo + n], ALU.mult, [k1, 'St'], [uk2])
                    ob, ok = o_bf.next()
                    if scale_rows:
                        S.tt('pool', u1[:, :n], u1[:, :n], u2[:, :n], ALU.add, [uk1, uk2], [uk1])
                        S.tt('pool', ob[0:64, :n], u1[:, :n], rbc[0:64, :n], ALU.mult, [uk1, 'rbc'], [ok])
                    else:
                        S.tt('pool', ob[0:64, :n], u1[:, :n], u2[:, :n], ALU.add, [uk1, uk2], [ok])
                    S.dma(STQ, dst, ob[0:64, :n], [ok], ())

                if is_lat and 'a' in a2_parts:
                    lowrank(0, False)
                    for h in range(4):
                        pb, pk = pB[h % 2], f'pB{h % 2}'
                        for c_ in range(2):
                            S.mm(pb[:, :n], Wqn[:, c_, h, :], cT[:, c_, :n], c_ == 0, c_ == 1, ['Wqn', 'cT'], [pk])
                        ob, ok = o_bf.next()
                        S.tt('dve', ob[:, :n], pb[:, :n], rbc[:, :n], ALU.mult, [pk, 'rbc'], [ok])
                        S.dma(STQ, QN[h][:, lo:lo + n], ob[:, :n], [ok], ())
                    for h in range(4):
                        for c_ in range(2):
                            S.mm(pB[0][:, :n], Wqr[:, c_, h, :], cT[:, c_, :n], c_ == 0, c_ == 1, ['Wqr', 'cT'], ['pB0'])
                        for c_ in range(2):
                            S.mm(pB[1][:, :n], Wqrot[:, c_, h, :], cT[:, c_, :n], c_ == 0, c_ == 1, ['Wqrot', 'cT'], ['pB1'])
                        rope_out(pB[0], 'pB0', pB[1], 'pB1', QR[h][:, lo:lo + n], True)
                S.enabled = 'b' in a2_parts
                lowrank(256, True)
                for h in range(4):
                    pb, pk = pB[h % 2], f'pB{h % 2}'
                    for c_ in range(2):
                        S.mm(pb[:, :n], Wkn[:, c_, h, :], cT[:, c_, :n], c_ == 0, c_ == 1, ['Wkn', 'cT'], [pk])
                    ob, ok = o_bf.next()
                    S.tt('dve', ob[:, :n], pb[:, :n], rbc[:, :n], ALU.mult, [pk, 'rbc'], [ok])
                    S.dma(STQ, KN[h][:, tok0:tok0 + n], ob[:, :n], [ok], ())
                Wv2 = Wv[:].rearrange("p c h d -> p c (h d)")
                for s_ in range(nsub):
                    pt, pk = pT[s_ % 2], f'pT{s_ % 2}'
                    for c_ in range(2):
                        S.mm(pt[:], cT[:, c_, s_ * 128:(s_ + 1) * 128], Wv2[:, c_, :], c_ == 0, c_ == 1, ['cT', 'Wv'], [pk])
                    ob, ok = o_bf.next()
                    S.act(ob[:], pt[:], AF.Copy, [pk, 'rtk'], [ok], scale=rtk[:, s_:s_ + 1])
                    S.dma(STQ, VV[tok0 + s_ * 128:tok0 + (s_ + 1) * 128, :], ob[:], [ok], ())
                S.enabled = 'c' in a2_parts
                fm_proj(pA[0], 'pA0', Wkpe, 'Wkpe', 0, 128)
                if is_lat:
                    fm_proj(pA[1], 'pA1', Wkrot, 'Wkrot', 0, 128)
                    rope_out(pA[0], 'pA0', pA[1], 'pA1', KR[:, tok0:tok0 + n], False)
                else:
                    ob, ok = o_bf.next()
                    S.copy('act', ob[0:64, :n], pA[0][0:64, :n], ['pA0'], [ok])
                    S.dma(STQ, KR[:, tok0:tok0 + n], ob[0:64, :n], [ok], ())
                S.enabled = 'd' in a2_parts
                for h in range(4):
                    pa, pk = pA[h % 2], f'pA{h % 2}'
                    fm_proj(pa, pk, Win, 'Win', 576 + h * 128, 128)
                    of, ok = o_f.next()
                    S.copy('act' if h % 2 else 'dve', of[:, :n], pa[:, :n], [pk], [ok])
                    S.dma(STQ, HQ[h][:, tok0:tok0 + n], of[:, :n], [ok], ())
                S.enabled = 'e' in a2_parts
                for s_ in range(nsub):
                    row0 = tok0 + s_ * 128

                    def tm_proj(ps, pk, col0):
                        for k in range(8):
                            S.mm(ps[:], hT_[:, k, s_ * 128:(s_ + 1) * 128], Win[:, k, col0:col0 + 512], k == 0, k == 7,
                                 [hk, 'Win'], [pk])
                    tm_proj(pT[0], 'pT0', 1088)
                    ob, ok = o_bf.next()
                    S.copy('dve', ob[:], pT[0][:], ['pT0'], [ok])
                    S.dma(STQ, HV[row0:row0 + 128, :], ob[:], [ok], ())
                    if is_lat:
                        tm_proj(pT[1], 'pT1', 1600)
                        th, thk = sgb.next(); uh, uhk = fb_.next()
                        S.act(th[:], pT[1][:], AF.Tanh, ['pT1'], [thk], scale=0.5)
                        S.act(uh[:], pT[1][:], AF.Copy, ['pT1'], [uhk], scale=0.5)
                        of, ok = o_f.next()
                        S.tt('pool', th[:], th[:], uh[:], ALU.mult, [thk, uhk], [thk])
                        S.tt('pool', of[:], th[:], uh[:], ALU.add, [thk, uhk], [ok])
                        S.dma(STQ, HG[row0 - L:row0 - L + 128, :], of[:], [ok], ())
                    fts = []
                    for d_ in range(2):
                        pt, pk = pT[d_], f'pT{d_}'
                        tm_proj(pt, pk, 2112 + d_ * 512)
                        sg, sk = sgb.next(); ff, fk = fb_.next()
                        S.act(sg[:], pt[:], AF.Tanh, [pk], [sk], scale=0.5)
                        S.tt('dve', ff[:], sg[:], oml[:, d_, :], ALU.mult, [sk, 'oml'], [fk])
                        S.tt('pool', ff[:], ff[:], lbt[:, d_, :], ALU.add, [fk, 'lbt'], [fk])
                        fts.append((ff, fk))
                    for d_ in range(2):
                        ff, fk = fts[d_]
                        of, ok = o_f.next()
                        S.act(of[:], ff[:], AF.Ln, [fk], [ok])
                        S.dma(STQ, GG[d_][row0:row0 + 128, :], of[:], [ok], ())
                        of2, ok2 = o_f.next()
                        S.ts('pool', of2[:], ff[:], -1.0, 1.0, ALU.mult, ALU.add, [fk], [ok2])
                        S.dma(STQ, KG[d_][row0:row0 + 128, :], of2[:], [ok2], ())
            S.enabled = True
            S.emit()
        if stop_after == 2:
            return nc

        with ExitStack() as es:
            def TT(name, shape, dt):
                return es.enter_context(nc.sbuf_tensor(name, shape, dt))

            def PP(name, shape, dt):
                return es.enter_context(nc.psum_tensor(name, shape, dt))

            bt = []
            for d_ in range(2):
                bt.append(dict(
                    g=Rot(TT, f"bg{d_}", [128, 512], F32, 2), kg=Rot(TT, f"bkg{d_}", [128, 512], F32, 2),
                    v=Rot(TT, f"bv{d_}", [128, 512], BF16, 3), hq=Rot(TT, f"bhq{d_}", [128, 4, 128], F32, 2),
                    Ek=Rot(TT, f"bEk{d_}", [128, 512], F32, 2), EqT=Rot(TT, f"bEq{d_}", [128, 4, 128], F32, 2),
                    eb=Rot(TT, f"beb{d_}", [128, 4, 2], F32, 3), K2=Rot(TT, f"bK2{d_}", [128, 512], BF16, 2),
                    K2m=[Rot(TT, f"bK2m{c_}{d_}", [128, 512], BF16, 2) for c_ in range(2)],
                    QsT=Rot(TT, f"bQs{d_}", [128, 4, 128], BF16, 2),
                    Qsm=[Rot(TT, f"bQsm{c_}{d_}", [128, 4, 128], BF16, 2) for c_ in range(2)],
                    K2T=Rot(TT, f"bK2T{d_}", [128, 4, 128], BF16, 2),
                    Am=Rot(TT, f"bAm{d_}", [128, 4, 128], BF16, 2), S=TT(f"bS{d_}", [128, 4, 128], F32),
                    Sp=Rot(TT, f"bSp{d_}", [128, 4, 128], F32, 2), Spb=Rot(TT, f"bSpb{d_}", [128, 4, 128], BF16, 4),
                    osb=Rot(TT, f"bos{d_}", [128, 512], F32, 2)))
                S.memset('dve', bt[d_]['S'][:], 0.0, [f'bS{d_}h{h}' for h in range(4)])
                for c_ in range(2):
                    oth = slice(64, 128) if c_ == 0 else slice(0, 64)
                    for j_ in range(2):
                        S.memset('pool', bt[d_]['K2m'][c_].t[j_][oth, :], 0.0, [bt[d_]['K2m'][c_].k[j_]])
                        S.memset('pool', bt[d_]['Qsm'][c_].t[j_][:, :, oth], 0.0, [bt[d_]['Qsm'][c_].k[j_]])
            pD1 = PP("pD1", [128, 512], F32); pD2 = PP("pD2", [128, 4, 128], F32)
            pKT = PP("pKT", [128, 4, 128], BF16); pBL = PP("pBL", [128, 4, 2], F32)
            pAT = PP("pAT", [128, 4, 128], F32)
            pOb = PP("pOb", [128, 4, 128], F32)
            pSNr = Rot(PP, "pSN", [128, 4, 128], F32, 2)

            def hgrn_pre(cx, ti, d_):
                B_ = bt[d_]
                is_lat = ti >= 2
                row0 = ti * 128
                U_, R_, M_ = (Uf, Rf, Mf) if d_ == 0 else (Ub, Rb, Mb)
                uk, rk, mk_ = ('Uf', 'Rf', 'Mf') if d_ == 0 else ('Ub', 'Rb', 'Mb')
                g, gk = B_['g'].next(); kg, kgk = B_['kg'].next(); v, vk = B_['v'].next(); hq, hqk = B_['hq'].next()
                S.dma('sp', g[:], GG[d_][row0:row0 + 128, :], (), [gk])
                S.dma('sp', kg[:], KG[d_][row0:row0 + 128, :], (), [kgk])
                S.dma('sp', v[:], HV[row0:row0 + 128, :], (), [vk])
                S.dma('sp', hq[:], HQ[:, :, row0:row0 + 128].rearrange("h p t -> p h t"), (), [hqk])
                Ek, Ekk = B_['Ek'].next(); EqT, Eqk = B_['EqT'].next(); eb, ebk = B_['eb'].next()
                K2, K2k = B_['K2'].next(); QsT, Qsk = B_['QsT'].next(); K2T, K2Tk = B_['K2T'].next()
                S.mm(pD1[:], U_[:], g[:], True, True, [uk, gk], ['pD1'])
                for h in range(4):
                    S.mm(pD2[:, h, :], g[:, h * 128:(h + 1) * 128], R_[:], True, True, [gk, rk], ['pD2'])
                for h in range(4):
                    S.mm(pBL[:, h, :], g[:, h * 128:(h + 1) * 128], Ind[:], True, True, [gk, 'Ind'], ['pBL'])
                S.act(Ek[:], pD1[:], AF.Exp, ['pD1'], [Ekk])
                S.act(EqT[:], pD2[:], AF.Exp, ['pD2'], [Eqk])
                S.act(eb[:], pBL[:], AF.Exp, ['pBL'], [ebk])
                S.tt('dve', K2[:], kg[:], Ek[:], ALU.mult, [kgk, Ekk], [K2k])
                K2m = []
                for c_ in range(2):
                    rs_ = slice(c_ * 64, (c_ + 1) * 64)
                    km, kmk = B_['K2m'][c_].next()
                    S.copy('act', km[rs_, :], K2[rs_, :], [K2k], [kmk])
                    K2m.append((km, kmk))
                cx.update(v=v, vk=vk, eb=eb, ebk=ebk, K2m=K2m)
                if is_lat:
                    S.tt('pool', QsT[:], hq[:], EqT[:], ALU.mult, [hqk, Eqk], [Qsk])
                    Qsm = []
                    for c_ in range(2):
                        cs_ = slice(c_ * 64, (c_ + 1) * 64)
                        qm, qmk = B_['Qsm'][c_].next()
                        S.copy('pool', qm[:, :, cs_], QsT[:, :, cs_], [Qsk], [qmk])
                        Qsm.append((qm, qmk))
                    Am, Amk = B_['Am'].next()
                    for h in range(4):
                        S.tr(pKT[:, h, :], K2[:, h * 128:(h + 1) * 128], identb[:], [K2k, 'identb'], ['pKT'])
                    S.copy('act', K2T[:], pKT[:], ['pKT'], [K2Tk])
                    for h in range(4):
                        S.mm(pAT[:, h, :], K2T[:, h, :], QsT[:, h, :], True, True, [K2Tk, Qsk], ['pAT'])
                    S.tt('dve', Am[:], pAT[:], M_[:].unsqueeze(1).to_broadcast([128, 4, 128]), ALU.mult,
                         ['pAT', mk_], [Amk])
                    cx.update(Am=Am, Amk=Amk, Qsm=Qsm)

            def hgrn_chain(cx, ti, d_):
                B_ = bt[d_]
                is_lat = ti >= 2
                row0 = ti * 128
                St_, Sk = B_['S'], f'bS{d_}'
                v, vk, eb, ebk, K2m = (cx[k_] for k_ in ('v', 'vk', 'eb', 'ebk', 'K2m'))
                order = (0, 1) if d_ == 0 else (1, 0)
                spbs = {}
                Skh = [f'bS{d_}h{h}' for h in range(4)]
                for c_ in order:
                    km, kmk = K2m[c_]
                    psn, psnk = pSNr.next()
                    for h in range(4):
                        S.mm(psn[:, h, :], km[:, h * 128:(h + 1) * 128], v[:, h * 128:(h + 1) * 128], True, True,
                             [kmk, vk], [psnk])
                    if is_lat:
                        Spb, Spbk = B_['Spb'].next()
                        S.tt('pool', Spb[:], St_[:], eb[:, :, c_:c_ + 1].to_broadcast([128, 4, 128]), ALU.mult,
                             Skh + [ebk], [Spbk])
                        spbs[c_] = (Spb, Spbk)
                    for h in range(4):
                        S.stt('dve', St_[:, h, :], St_[:, h, :], eb[:, h, c_:c_ + 1], psn[:, h, :], ALU.mult, ALU.add,
                              [Skh[h], ebk, psnk], [Skh[h]])
                if is_lat:
                    Am, Amk, Qsm = cx['Am'], cx['Amk'], cx['Qsm']
                    for h in range(4):
                        S.mm(pOb[:, h, :], Am[:, h, :], v[:, h * 128:(h + 1) * 128], True, False, [Amk, vk], ['pOb'])
                        for n_, c_ in enumerate(order):
                            qm, qmk = Qsm[c_]
                            Spb, Spbk = spbs[c_]
                            S.mm(pOb[:, h, :], qm[:, h, :], Spb[:, h, :], False, n_ == 1, [qmk, Spbk], ['pOb'])
                    ob, obk = B_['osb'].next()
                    S.copy('act', ob[:], pOb[:].rearrange("p h d -> p (h d)"), ['pOb'], [obk])
                    S.dma('sp', OO[d_][row0 - L:row0 - L + 128, :], ob[:], [obk], ())

            fwd_order = list(range(NT))
            bwd_order = [1, 0] + list(range(NT - 1, 1, -1))
            cxs = [[dict() for _ in range(NT)] for _ in range(2)]
            hgrn_pre(cxs[0][0], fwd_order[0], 0)
            hgrn_pre(cxs[1][0], bwd_order[0], 1)
            for i_ in range(NT):
                if i_ + 1 < NT:
                    hgrn_pre(cxs[0][i_ + 1], fwd_order[i_ + 1], 0)
                    hgrn_pre(cxs[1][i_ + 1], bwd_order[i_ + 1], 1)
                hgrn_chain(cxs[0][i_], fwd_order[i_], 0)
                hgrn_chain(cxs[1][i_], bwd_order[i_], 1)
            S.emit()
        if stop_after == 3:
            return nc

        with ExitStack() as es:
            def TT(name, shape, dt):
                return es.enter_context(nc.sbuf_tensor(name, shape, dt))

            def PP(name, shape, dt):
                return es.enter_context(nc.psum_tensor(name, shape, dt))

            KNs = TT("KNs", [128, 4, T], BF16); KRs = TT("KRs", [128, T], BF16); Vs = TT("Vs", [128, NT, 512], BF16)
            sqKR = TT("sqKR", [64, T], BF16)
            sqn = TT("sqn", [128, 512], BF16); sqr = TT("sqr", [64, 512], BF16)
            sqnR = Rot(TT, "csqn", [128, 512], BF16, 2); sqrR = Rot(TT, "csqr", [64, 512], BF16, 2)
            km2 = TT("km2", [128, 4], F32); tmx = TT("tmx", [128, 1], F32)
            tmxR = Rot(TT, "ctmx", [128, 1], F32, 3); nshR = Rot(TT, "cnsh", [128, 1], F32, 3)
            qnr = Rot(TT, "cqn", [128, 512], BF16, 3); qrr = Rot(TT, "cqr", [128, 512], BF16, 3)
            PTr = Rot(TT, "cPT", [128, 1024], BF16, 4); osr = Rot(TT, "cos", [128, 512], BF16, 2)
            rinv = TT("rinv", [128, 512], F32)
            raccR = [Rot(TT, f"racc{i}_", [128, 1024], F32, 2) for i in range(2)]
            rsumR = [Rot(TT, f"rsum{i}_", [128, 512], F32, 2) for i in range(2)]
            pScR = Rot(PP, "pSc", [128, 1024], F32, 2)
            pOaR = Rot(PP, "pOa", [128, 512], F32, 2)
            pMiR = Rot(PP, "pMi", [128, 512], F32, 2)
            pNm = pMiR.t[0]
            for h in range(4):
                S.dma('sp', KNs[:, h, :], KN[h], (), ['KNs'])
            S.memset('pool', KRs[64:128, :], 0.0, ['KRs'])
            S.dma('sp', KRs[0:64, :], KR, (), ['KRs'])
            for i_ in range(3):
                S.memset('pool', qrr.t[i_][64:128, :], 0.0, [qrr.k[i_]])
            VVv = VV.rearrange("(t p) n -> p t n", p=128)
            for j in range(0, NT, 4):
                je = min(NT, j + 4)
                S.dma('sp', Vs[:, j:je, :], VVv[:, j:je, :], (), ['Vs'])
            S.memset('dve', km2[:], 0.0, ['km2'])
            S.act(sqKR[:], KRs[0:64, :], AF.Square, ['KRs'], ['sqKR'])
            for j0 in range(0, T, 512):
                w_ = min(512, T - j0)
                for h in range(4):
                    S.act(sqn[:, :w_], KNs[:, h, j0:j0 + w_], AF.Square, ['KNs'], ['sqn'])
                    S.mm(pNm[:, :w_], onesb[:], sqn[:, :w_], True, False, ['onesb', 'sqn'], ['pMi0'])
                    S.mm(pNm[:, :w_], onesb[0:64, :], sqKR[:, j0:j0 + w_], False, True, ['onesb', 'sqKR'], ['pMi0'])
                    S.op('dve', (lambda ww: (lambda e: e.reduce_max(out=tmx[:], in_=pNm[:, :ww], axis=AX.X)))(w_),
                         ['pMi0'], ['tmx'])
                    S.tt('dve', km2[:, h:h + 1], km2[:, h:h + 1], tmx[:], ALU.max, ['km2', 'tmx'], ['km2'])
            items = [(g_, h) for g_ in range(8) for h in range(4)]
            cxs = [dict() for _ in items]
            NP_ = NT // 2

            def c_pro(i):
                g_, h = items[i]
                q0 = g_ * 512
                cx = cxs[i]
                qn, qnk = qnr.next(); qr, qrk = qrr.next()
                sqn_, sqnk = sqnR.next(); sqr_, sqrk = sqrR.next(); tm_, tmk = tmxR.next(); nsh, nshk = nshR.next()
                pm, pmk = pMiR.next()
                S.dma('sp', qn[:], QN[h][:, q0:q0 + 512], (), [qnk])
                S.dma('sp', qr[0:64, :], QR[h][:, q0:q0 + 512], (), [qrk])
                S.act(sqn_[:], qn[:], AF.Square, [qnk], [sqnk])
                S.act(sqr_[:], qr[0:64, :], AF.Square, [qrk], [sqrk])
                S.mm(pm[:], onesb[:], sqn_[:], True, False, ['onesb', sqnk], [pmk])
                S.mm(pm[:], onesb[0:64, :], sqr_[:], False, True, ['onesb', sqrk], [pmk])
                S.op('dve', (lambda o_, i_: (lambda e: e.reduce_max(out=o_[:], in_=i_[:], axis=AX.X)))(tm_, pm), [pmk], [tmk])
                S.ts('dve', nsh[:], tm_[:], km2[:, h:h + 1], -0.5 * SCALE, ALU.add, ALU.mult, [tmk, 'km2'], [nshk])
                cx.update(qn=qn, qnk=qnk, qr=qr, qrk=qrk, nsh=nsh, nshk=nshk, ps={})

            def c_qk(i, j):
                g_, h = items[i]
                cx = cxs[i]
                ps, pk = pScR.next()
                cx['ps'][j] = (ps, pk)
                for u_ in range(2):
                    kt = 2 * j + u_
                    S.mm(ps[:, u_ * 512:(u_ + 1) * 512], KNs[:, h, kt * 128:(kt + 1) * 128], cx['qn'][:], True, False,
                         ['KNs', cx['qnk']], [pk])
                    S.mm(ps[:, u_ * 512:(u_ + 1) * 512], KRs[:, kt * 128:(kt + 1) * 128], cx['qr'][:], False, True,
                         ['KRs', cx['qrk']], [pk])

            def c_main(i):
                g_, h = items[i]
                cx = cxs[i]
                pOa, pOak = pOaR.next()
                racc = [raccR[0].next(), raccR[1].next()]
                cx.update(pOa=pOa, pOak=pOak, racc=racc)
                for j in range(NP_):
                    if j + 1 < NP_:
                        c_qk(i, j + 1)
                    ps, pk = cx['ps'].pop(j)
                    PT, ptk = PTr.next()
                    S.act(PT[:], ps[:], AF.Exp, [pk, cx['nshk']], [ptk], bias=cx['nsh'][:], scale=SCALE)
                    for u_ in range(2):
                        kt = 2 * j + u_
                        S.mm(pOa[:], Vs[:, kt, h * 128:(h + 1) * 128], PT[:, u_ * 512:(u_ + 1) * 512],
                             kt == 0, kt == NT - 1, ['Vs', ptk], [pOak])
                    ae = 'dve' if j % 2 == 0 else 'pool'
                    ra, rak = racc[j % 2]
                    if j < 2:
                        S.copy(ae, ra[:], PT[:], [ptk], [rak])
                    else:
                        S.tt(ae, ra[:], ra[:], PT[:], ALU.add, [rak, ptk], [rak])

            def c_epi(i):
                g_, h = items[i]
                q0 = g_ * 512
                cx = cxs[i]
                pOa, pOak, racc = cx['pOa'], cx['pOak'], cx['racc']
                rs0, rs0k = rsumR[0].next(); rs1, rs1k = rsumR[1].next()
                pm, pmk = pMiR.next()
                S.tt('dve', rs0[:], racc[0][0][:, 0:512], racc[0][0][:, 512:1024], ALU.add, [racc[0][1]], [rs0k])
                S.tt('pool', rs1[:], racc[1][0][:, 0:512], racc[1][0][:, 512:1024], ALU.add, [racc[1][1]], [rs1k])
                S.mm(pm[:], onesf[:], rs0[:], True, False, ['onesf', rs0k], [pmk])
                S.mm(pm[:], onesf[:], rs1[:], False, True, ['onesf', rs1k], [pmk])
                S.op('dve', (lambda i_: (lambda e: e.reciprocal(out=rinv[:], in_=i_[:])))(pm), [pmk], ['rinv'])
                ob, obk = osr.next()
                S.tt('dve', ob[:], pOa[:], rinv[:], ALU.mult, [pOak, 'rinv'], [obk])
                S.dma('sp', MIXA[h][:, q0:q0 + 512], ob[:], [obk], ())
                cxs[i] = None

            c_pro(0)
            c_qk(0, 0)
            for i in range(len(items)):
                if i + 1 < len(items):
                    c_pro(i + 1)
                c_main(i)
                if i + 1 < len(items):
                    c_qk(i + 1, 0)
                c_epi(i)
            S.emit()
        if stop_after == 4:
            return nc

        with ExitStack() as es:
            def TT(name, shape, dt):
                return es.enter_context(nc.sbuf_tensor(name, shape, dt))

            def PP(name, shape, dt):
                return es.enter_context(nc.psum_tensor(name, shape, dt))

            Wout = TT("Wout", [128, 8, D], BF16)
            mD = TT("mD", [128, 3, D], F32)
            gon = TT("gon", [128, 128], F32)
            woutv = w_out.rearrange("(k p) n -> p k n", p=128)
            for k in range(8):
                S.dma('pool', Wout[:, k, :], woutv[:, k, :], (), ['Wout'])
            for i, mi in enumerate((4, 5, 6)):
                S.dma('sp', mD[:, i, :], MOD[mi], (), ['mD'])
            S.dma('sp', gon[:], g_on.partition_broadcast(128), (), ['gon'])
            ofr = Rot(TT, "dof", [128, 512], F32, 3); obr = Rot(TT, "dob", [128, 512], F32, 3); hgr = Rot(TT, "dhg", [128, 512], F32, 3)
            mar = Rot(TT, "dma_", [128, 4, 128], BF16, 4); xtr = Rot(TT, "dxt", [128, D], F32, 5)
            osumr = Rot(TT, "osum", [128, 512], F32, 2); osqr = Rot(TT, "osq", [128, 512], F32, 2); ss4r = Rot(TT, "ss4", [128, 4], F32, 3)
            tBr = Rot(TT, "tB", [128, 512], F32, 2); hgbr = Rot(TT, "hgb", [128, 512], BF16, 3); mixBr = Rot(TT, "mixB", [128, 4, 128], BF16, 2)
            tmpDr = Rot(TT, "tmpD", [128, D], F32, 2); x1r = Rot(TT, "dx1", [128, D], F32, 3)
            junkD = TT("junkD", [128, D], BF16); ssDr = Rot(TT, "ssD", [128, 1], F32, 3)
            t2Dr = Rot(TT, "t2D", [128, D], F32, 2); h2r = Rot(TT, "h2", [128, D], BF16, 3); h2Tr = Rot(TT, "dh2T", [128, 8, 128], BF16, 3)
            pTBr = Rot(PP, "pTB", [128, 4, 128], BF16, 2)
            pLOr = Rot(PP, "pLO", [128, 512], F32, 4)
            pT8r = Rot(PP, "pT8", [128, 8, 128], BF16, 2)

            def d1_s0(cx, ti):
                r0 = ti * 128
                for nm, rr, src in (('of', ofr, OO[0][r0:r0 + 128, :]), ('ob', obr, OO[1][r0:r0 + 128, :]),
                                    ('hg', hgr, HG[r0:r0 + 128, :]),
                                    ('ma', mar, MIXA[:, :, r0:r0 + 128].rearrange("h p t -> p h t")),
                                    ('xt', xtr, x[r0:r0 + 128, :])):
                    cx[nm], cx[nm + 'k'] = rr.next()
                    S.dma('sp', cx[nm][:], src, (), [cx[nm + 'k']])

            def d1_s1(cx, ti):
                of, ofk, ob, obk, hg, hgk = cx['of'], cx['ofk'], cx['ob'], cx['obk'], cx['hg'], cx['hgk']
                osum, osumk = osumr.next(); osq, osqk = osqr.next(); ss4, ss4k = ss4r.next(); tB, tBk = tBr.next()
                hgb, hgbk = hgbr.next()
                cx['hgb'], cx['hgbk'] = hgb, hgbk
                S.tt('pool', osum[:], of[:], ob[:], ALU.add, [ofk, obk], [osumk])
                S.tt('pool', osq[:], osum[:], osum[:], ALU.mult, [osumk], [osqk])
                S.op('dve', (lambda o_, i_: (lambda e: e.reduce_sum(out=o_[:], in_=i_[:].rearrange("p (h d) -> p h d", h=4),
                                                                    axis=AX.X)))(ss4, osq), [osqk], [ss4k])
                S.act(ss4[:], ss4[:], AF.Sqrt, [ss4k], [ss4k], bias=EPS, scale=1.0 / 128)
                S.op('dve', (lambda o_: (lambda e: e.reciprocal(out=o_[:], in_=o_[:])))(ss4), [ss4k], [ss4k])
                o3 = osum[:].rearrange("p (h d) -> p h d", h=4)
                t3 = tB[:].rearrange("p (h d) -> p h d", h=4)
                S.tt('dve', t3, o3, ss4[:].unsqueeze(2).to_broadcast([128, 4, 128]), ALU.mult, [osumk, ss4k], [tBk])
                S.tt('pool', t3, t3, gon[:].unsqueeze(1).to_broadcast([128, 4, 128]), ALU.mult, [tBk, 'gon'], [tBk])
                S.tt('pool', hgb[:], tB[:], hg[:], ALU.mult, [tBk, hgk], [hgbk])

            def d1_s2(cx, ti):
                hgb, hgbk, ma, mak = cx['hgb'], cx['hgbk'], cx['ma'], cx['mak']
                pTB, pTBk = pTBr.next(); mixB, mixBk = mixBr.next()
                for h in range(4):
                    S.tr(pTB[:, h, :], hgb[:, h * 128:(h + 1) * 128], identb[:], [hgbk, 'identb'], [pTBk])
                S.copy('act', mixB[:], pTB[:], [pTBk], [mixBk])
                cx['pLO'] = []
                for hf in range(2):
                    pl, plk = pLOr.next()
                    cx['pLO'].append((pl, plk))
                    for k in range(4):
                        S.mm(pl[:], ma[:, k, :], Wout[:, k, hf * 512:(hf + 1) * 512], k == 0, False, [mak, 'Wout'], [plk])
                    for k in range(4):
                        S.mm(pl[:], mixB[:, k, :], Wout[:, 4 + k, hf * 512:(hf + 1) * 512], False, k == 3,
                             [mixBk, 'Wout'], [plk])

            def d1_s3(cx, ti):
                r0 = ti * 128
                xt_, xtk = cx['xt'], cx['xtk']
                tmpD, tmpDk = tmpDr.next(); ssD, ssDk = ssDr.next(); t2D, t2Dk = t2Dr.next(); h2, h2k = h2r.next()
                x1, x1k = x1r.next()
                cx['h2'], cx['h2k'] = h2, h2k
                for hf in range(2):
                    pl, plk = cx['pLO'][hf]
                    S.tt('dve', tmpD[:, hf * 512:(hf + 1) * 512], pl[:], mD[:, 0, hf * 512:(hf + 1) * 512], ALU.mult,
                         [plk, 'mD'], [tmpDk])
                S.tt('pool', x1[:], tmpD[:], xt_[:], ALU.add, [tmpDk, xtk], [x1k])
                S.dma('sp', X1[r0:r0 + 128, :], x1[:], [x1k], ())
                S.memset('dve', ssD[:], 0.0, [ssDk])
                S.act(junkD[:], x1[:], AF.Square, [x1k], ['junkD', ssDk], accum=ssD[:])
                S.act(ssD[:], ssD[:], AF.Sqrt, [ssDk], [ssDk], bias=EPS, scale=1.0 / D)
                S.op('dve', (lambda o_: (lambda e: e.reciprocal(out=o_[:], in_=o_[:])))(ssD), [ssDk], [ssDk])
                S.stt('dve', t2D[:], x1[:], ssD[:, 0:1], mD[:, 1, :], ALU.mult, ALU.mult, [x1k, ssDk, 'mD'], [t2Dk])
                S.tt('pool', h2[:], t2D[:], mD[:, 2, :], ALU.add, [t2Dk, 'mD'], [h2k])

            def d1_s4(cx, ti):
                r0 = ti * 128
                h2, h2k = cx['h2'], cx['h2k']
                pT8, pT8k = pT8r.next(); hT2, hT2k = h2Tr.next()
                for k in range(8):
                    S.tr(pT8[:, k, :], h2[:, k * 128:(k + 1) * 128], identb[:], [h2k, 'identb'], [pT8k])
                S.copy('dve' if ti % 2 else 'act', hT2[:], pT8[:], [pT8k], [hT2k])
                S.dma('sp', H2T[:, :, r0:r0 + 128], hT2[:], [hT2k], ())
            run_pipeline(N // 128, [d1_s0, d1_s1, d1_s2, d1_s3, d1_s4])
            S.emit()
        if stop_after == 5:
            return nc

        with ExitStack() as es:
            def TT(name, shape, dt):
                return es.enter_context(nc.sbuf_tensor(name, shape, dt))

            def PP(name, shape, dt):
                return es.enter_context(nc.psum_tensor(name, shape, dt))

            Wg = TT("Wg", [128, 8, DFF], BF16); Wu = TT("Wu", [128, 8, DFF], BF16); Wd = TT("Wd", [128, NCF, D], BF16)
            mE = TT("mE", [128, 2, D], F32)
            wgv = w_gate.rearrange("(k p) n -> p k n", p=128); wuv = w_up.rearrange("(k p) n -> p k n", p=128)
            wdv = w_down.rearrange("(c p) n -> p c n", p=128)
            for k in range(8):
                S.dma('pool', Wg[:, k, :], wgv[:, k, :], (), ['Wg'])
                S.dma('pool', Wu[:, k, :], wuv[:, k, :], (), ['Wu'])
            for c_ in range(NCF):
                S.dma('pool', Wd[:, c_, :], wdv[:, c_, :], (), ['Wd'])
            S.dma('sp', mE[:, 0, :], MOD[7], (), ['mE'])
            S.dma('sp', mE[:, 1, :], g_fin.partition_broadcast(128), (), ['mE'])
            GN = 512
            h2g = Rot(TT, "eh2", [128, 8, GN], BF16, 1)
            aT = TT("aT", [128, NCF, GN], BF16)
            sgr = Rot(TT, "esg", [128, GN], F32, 2)
            x1r = Rot(TT, "ex1", [128, D], F32, 1); tmr = Rot(TT, "etm", [128, D], F32, 2)
            ssE = TT("ssE", [128, 1], F32)
            pG = [PP(f"pG{i}", [128, GN], F32) for i in range(2)]
            pU = [PP(f"pU{i}", [128, GN], F32) for i in range(2)]
            pY = [PP(f"pY{i}", [128, 512], F32) for i in range(2)]
            for g_ in range(N // GN):
                q0 = g_ * GN
                hh, hhk = h2g.next()
                S.dma('sp', hh[:], H2T[:, :, q0:q0 + GN], (), [hhk])
                for c_ in range(NCF):
                    pg, pgk = pG[c_ % 2], f'pG{c_ % 2}'
                    pu, puk = pU[c_ % 2], f'pU{c_ % 2}'
                    for k in range(8):
                        S.mm(pg[:], Wg[:, k, c_ * 128:(c_ + 1) * 128], hh[:, k, :], k == 0, k == 7, ['Wg', hhk], [pgk])
                    for k in range(8):
                        S.mm(pu[:], Wu[:, k, c_ * 128:(c_ + 1) * 128], hh[:, k, :], k == 0, k == 7, ['Wu', hhk], [puk])
                    sg, sgk = sgr.next()
                    S.act(sg[:], pg[:], AF.Silu, [pgk], [sgk])
                    S.tt('dve', aT[:, c_, :], sg[:], pu[:], ALU.mult, [sgk, puk], ['aT'])
                for sb in range(GN // 128):
                    r0 = q0 + sb * 128
                    x1, x1k = x1r.next(); tm, tmk = tmr.next()
                    S.dma('sp', x1[:], X1[r0:r0 + 128, :], (), [x1k])
                    for hf in range(2):
                        for c_ in range(NCF):
                            S.mm(pY[hf][:], aT[:, c_, sb * 128:(sb + 1) * 128], Wd[:, c_, hf * 512:(hf + 1) * 512],
                                 c_ == 0, c_ == NCF - 1, ['aT', 'Wd'], [f'pY{hf}'])
                        S.tt('dve', tm[:, hf * 512:(hf + 1) * 512], pY[hf][:], mE[:, 0, hf * 512:(hf + 1) * 512], ALU.mult,
                             [f'pY{hf}', 'mE'], [tmk])
                    S.tt('pool', x1[:], tm[:], x1[:], ALU.add, [tmk, x1k], [x1k])
                    S.memset('dve', ssE[:], 0.0, ['ssE'])
                    S.act(tm[:], x1[:], AF.Square, [x1k, tmk], [tmk, 'ssE'], accum=ssE[:])
                    S.act(ssE[:], ssE[:], AF.Sqrt, ['ssE'], ['ssE'], bias=EPS, scale=1.0 / D)
                    S.op('dve', lambda e: e.reciprocal(out=ssE[:], in_=ssE[:]), ['ssE'], ['ssE'])
                    S.stt('dve', tm[:], x1[:], ssE[:, 0:1], mE[:, 1, :], ALU.mult, ALU.mult, [x1k, 'ssE', 'mE'], [tmk])
                    S.dma('sp', out[r0:r0 + 128, :], tm[:], [tmk], ())
            S.emit()
    return nc


_NC_CACHE = {}


def kernel(**inputs):
    if 'nc' not in _NC_CACHE:
        _NC_CACHE['nc'] = build_nc()
    nc = _NC_CACHE['nc']
    f = lambda a: np.ascontiguousarray(np.asarray(a, dtype=np.float32))
    shared = {
        "c_ctx": f(inputs["c_ctx"]), "w_mod": f(inputs["w_mod"][0]), "b_mod": f(inputs["b_mod"][0]),
        "g_norm_mix": f(inputs["g_norm_mix"][0]), "g_norm_ffn": f(inputs["g_norm_ffn"][0]),
        "w_in": f(inputs["w_in"][0]), "g_q_norm": f(inputs["g_q_norm"][0]), "w_uq": f(inputs["w_uq"][0]),
        "g_kv_norm": f(inputs["g_kv_norm"][0]), "w_ukv": f(inputs["w_ukv"][0]),
        "lb_fwd": f(inputs["lb_fwd"]), "lb_bwd": f(inputs["lb_bwd"]), "g_hgrn_norm": f(inputs["g_hgrn_norm"][0]),
        "w_out": f(inputs["w_out"][0]), "w_gate": f(inputs["w_gate"][0]), "w_up": f(inputs["w_up"][0]),
        "w_down": f(inputs["w_down"][0]), "g_final": f(inputs["g_final"]),
    }
    xs, cs, ctxs = f(inputs["x"]), f(inputs["c"]), f(inputs["ctx"])
    in_maps = []
    for b in range(NB):
        m = dict(shared)
        m["x"] = xs[b]; m["c"] = cs[b]; m["ctx"] = ctxs[b]
        in_maps.append(m)
    res = run_bass_kernel_spmd(nc, in_maps, core_ids=list(range(NB)))
    return np.stack([np.asarray(r["out"], dtype=np.float32) for r in res.results], axis=0)
```

```python
import math
from contextlib import ExitStack

import numpy as np
import concourse.bass as bass
import concourse.mybir as mybir
from concourse.bass_utils import run_bass_kernel_spmd

F32 = mybir.dt.float32
BF16 = mybir.dt.bfloat16
AF = mybir.ActivationFunctionType
ALU = mybir.AluOpType
AX = mybir.AxisListType

NB, N, L, D = 8, 4096, 256, 1024
T = N + L
NT = T // 128
DFF = 2816
NCF = DFF // 128
EPS = 1e-6
INC = 3136
SCALE = 1.0 / math.sqrt(192.0)


class Sched:
    ENG = ('pe', 'act', 'dve', 'pool', 'sp')

    def __init__(self, nc, n_dma_sems=24):
        self.nc = nc
        self.ops = {e: [] for e in self.ENG}
        self.sems = {}
        for e in ('pe', 'act', 'dve', 'pool'):
            self.sems[('e', e)] = nc.alloc_semaphore(name=f"s_{e}")
        for i in range(n_dma_sems):
            self.sems[('d', i)] = nc.alloc_semaphore(name=f"s_dma{i}")
        self.nw = 6
        for i in range(self.nw):
            self.sems[('w', i)] = nc.alloc_semaphore(name=f"s_swdma{i}")
        self.wnext = 0
        self.cnt = {k: 0 for k in self.sems}
        self.nd = n_dma_sems
        self.dnext = 0
        self.waited = {e: {} for e in self.ENG}
        self.res = {}
        self.enabled = True

    def _deps(self, reads, writes):
        t = []
        for r in reads:
            st = self.res.get(r)
            if st and st['w']:
                t.append(st['w'])
        for w in writes:
            st = self.res.get(w)
            if st:
                if st['w']:
                    t.append(st['w'])
                t.extend(st['r'].values())
        return t

    def _need(self, eng, tickets):
        best = {}
        for key, val in tickets:
            if key == ('e', 'pe') and eng == 'pe':
                continue
            if self.waited[eng].get(key, 0) >= val:
                continue
            if best.get(key, 0) < val:
                best[key] = val
        for key, val in best.items():
            self.waited[eng][key] = val
        return list(best.items())

    def _commit(self, ticket, reads, writes):
        for r in reads:
            st = self.res.setdefault(r, {'w': None, 'r': {}})
            st['r'][ticket[0]] = ticket
        for w in writes:
            self.res[w] = {'w': ticket, 'r': {}}

    def op(self, eng, fn, reads=(), writes=()):
        if not self.enabled:
            return None
        waits = self._need(eng, self._deps(reads, writes))
        key = ('e', eng)
        self.cnt[key] += 1
        ticket = (key, self.cnt[key])
        self.ops[eng].append((waits, fn, key, 1))
        self._commit(ticket, reads, writes)
        return ticket

    def dma(self, q, out, in_, reads=(), writes=()):
        if not self.enabled:
            return None
        if q == 'pool':
            key = ('w', self.wnext)
            self.wnext = (self.wnext + 1) % self.nw
        else:
            key = ('d', self.dnext)
            self.dnext = (self.dnext + 1) % self.nd
        tickets = self._deps(reads, writes)
        if self.cnt[key] > 0:
            tickets.append((key, self.cnt[key]))
        waits = self._need(q, tickets)
        self.cnt[key] += 16
        ticket = (key, self.cnt[key])
        self.ops[q].append((waits, lambda e: e.dma_start(out=out, in_=in_), key, 16))
        self._commit(ticket, reads, writes)
        return ticket

    def barrier(self):
        allt = [(k, v) for k, v in self.cnt.items() if v > 0]
        for e in self.ENG:
            waits = self._need(e, allt)
            if waits:
                self.ops[e].append((waits, None, None, 0))

    def emit(self):
        self.barrier()
        with self.nc.Block() as block:
            def mk(engname):
                def body(e):
                    for waits, fn, semkey, inc in self.ops[engname]:
                        for key, val in waits:
                            e.wait_ge(self.sems[key], val)
                        if fn is not None:
                            fn(e).then_inc(self.sems[semkey], inc)
                return body
            block.tensor(mk('pe'))
            block.scalar(mk('act'))
            block.vector(mk('dve'))
            block.gpsimd(mk('pool'))
            block.sync(mk('sp'))
        self.ops = {e: [] for e in self.ENG}

    def mm(self, out, lhsT, rhs, start, stop, r, w):
        return self.op('pe', lambda e: e.matmul(out, lhsT=lhsT, rhs=rhs, start=start, stop=stop), r, w)

    def tr(self, out, in_, ident, r, w):
        return self.op('pe', lambda e: e.transpose(out, in_, ident), r, w)

    def act(self, out, in_, func, r, w, bias=None, scale=None, accum=None):
        kw = {}
        if bias is not None:
            kw['bias'] = bias
        if scale is not None:
            kw['scale'] = scale
        if accum is not None:
            kw['accum_out'] = accum
        return self.op('act', lambda e: e.activation(out=out, in_=in_, func=func, **kw), r, w)

    def tt(self, eng, out, in0, in1, op, r, w):
        return self.op(eng, lambda e: e.tensor_tensor(out=out, in0=in0, in1=in1, op=op), r, w)

    def ts(self, eng, out, in0, s1, s2, op0, op1, r, w):
        if s2 is None:
            return self.op(eng, lambda e: e.tensor_scalar(out=out, in0=in0, scalar1=s1, scalar2=None, op0=op0), r, w)
        return self.op(eng, lambda e: e.tensor_scalar(out=out, in0=in0, scalar1=s1, scalar2=s2, op0=op0, op1=op1), r, w)

    def stt(self, eng, out, in0, scalar, in1, op0, op1, r, w):
        return self.op(eng, lambda e: e.scalar_tensor_tensor(out=out, in0=in0, scalar=scalar, in1=in1, op0=op0, op1=op1), r, w)

    def copy(self, eng, out, in_, r, w):
        if eng == 'act':
            return self.act(out, in_, AF.Copy, r, w)
        return self.op(eng, lambda e: e.tensor_copy(out=out, in_=in_), r, w)

    def memset(self, eng, ap, val, w):
        return self.op(eng, lambda e: e.memset(ap, val), (), w)


def run_pipeline(n, stages):
    ctxs = [dict() for _ in range(n)]
    K = len(stages)
    for t in range(n + K - 1):
        for k in range(K - 1, -1, -1):
            i = t - k
            if 0 <= i < n:
                stages[k](ctxs[i], i)


class Rot:
    def __init__(self, alloc, name, shape, dt, n):
        self.t = [alloc(f"{name}{i}", shape, dt) for i in range(n)]
        self.k = [f"{name}{i}" for i in range(n)]
        self.i = 0

    def next(self):
        j = self.i % len(self.t)
        self.i += 1
        return self.t[j], self.k[j]


def _rstd(S, ss, out, dim, tag):
    S.act(out, ss, AF.Sqrt, [tag + 'ss'], [tag + 'rs'], bias=EPS, scale=1.0 / dim)
    S.op('dve', lambda e: e.reciprocal(out=out, in_=out), [tag + 'rs'], [tag + 'rs'])


def build_nc(stop_after=None, debug=False, a2_groups=None, a2_parts='abcde', STQ='sp'):
    nc = bass.Bass("TRN2", target_bir_lowering=False)
    S = Sched(nc)

    def din(name, shape):
        return nc.dram_tensor(name, shape, F32, kind="ExternalInput").ap()

    x = din("x", [N, D]); c = din("c", [D]); ctx = din("ctx", [L, D]); c_ctx = din("c_ctx", [D])
    w_mod = din("w_mod", [D, 6 * D]); b_mod = din("b_mod", [6 * D])
    g_mix = din("g_norm_mix", [D]); g_ffn = din("g_norm_ffn", [D])
    w_in = din("w_in", [D, INC]); g_qn = din("g_q_norm", [256]); w_uq = din("w_uq", [256, 768])
    g_kvn = din("g_kv_norm", [256]); w_ukv = din("w_ukv", [256, 1024])
    lb_f = din("lb_fwd", [2, 512]); lb_b = din("lb_bwd", [2, 512]); g_on = din("g_hgrn_norm", [128])
    w_out = din("w_out", [D, D]); w_gate = din("w_gate", [D, DFF]); w_up = din("w_up", [D, DFF])
    w_down = din("w_down", [DFF, D]); g_fin = din("g_final", [D])
    out = nc.dram_tensor("out", [N, D], F32, kind="ExternalOutput").ap()

    dbg = set(debug) if debug else set()

    def scr(name, shape, dt):
        return nc.dram_tensor(name, shape, dt, kind="ExternalOutput" if name in dbg else "Internal").ap()

    MOD = scr("s_mod", [8, 128, D], F32)
    HT = scr("s_ht", [128, 8, T], BF16)
    QN = scr("s_qn", [4, 128, N], BF16); QR = scr("s_qr", [4, 64, N], BF16)
    KN = scr("s_kn", [4, 128, T], BF16); KR = scr("s_kr", [64, T], BF16)
    VV = scr("s_v", [T, 512], BF16)
    HQ = scr("s_hq", [4, 128, T], F32); HV = scr("s_hv", [T, 512], BF16); HG = scr("s_hg", [N, 512], F32)
    GG = scr("s_g", [2, T, 512], F32); KG = scr("s_kg", [2, T, 512], F32)
    OO = scr("s_o", [2, N, 512], F32)
    MIXA = scr("s_mixa", [4, 128, N], BF16)
    X1 = scr("s_x1", [N, D], F32); H2T = scr("s_h2t", [128, 8, N], BF16)

    outer = ExitStack()
    with outer:
        def CT(name, shape, dt):
            return outer.enter_context(nc.sbuf_tensor(name, shape, dt))
        identb = CT("identb", [128, 128], BF16); identf = CT("identf", [128, 128], F32)
        onesb = CT("onesb", [128, 128], BF16); onesf = CT("onesf", [128, 128], F32)
        Uf = CT("Uf", [128, 128], F32); Ub = CT("Ub", [128, 128], F32)
        Rf = CT("Rf", [128, 128], F32); Rb = CT("Rb", [128, 128], F32)
        Mf = CT("Mf", [128, 128], F32); Mb = CT("Mb", [128, 128], F32)
        Ind = CT("Ind", [128, 2], F32)

        def sel(tile, pattern, cm, cmp_op, key):
            S.op('pool', lambda e: e.affine_select(out=tile[:], in_=tile[:], pattern=pattern, compare_op=cmp_op,
                                                   fill=0.0, base=0, channel_multiplier=cm), [key], [key])
        for tl, key in ((identb, 'identb'), (identf, 'identf')):
            S.memset('pool', tl[:], 1.0, [key])
            sel(tl, [[-1, 128]], 1, ALU.is_equal, key)
        S.memset('pool', onesb[:], 1.0, ['onesb'])
        S.memset('pool', onesf[:], 1.0, ['onesf'])
        S.memset('pool', Uf[:], 1.0, ['Uf']); sel(Uf, [[-1, 128]], 1, ALU.is_gt, 'Uf')
        S.memset('pool', Uf[64:128, 0:64], 0.0, ['Uf'])
        S.memset('pool', Mb[:], 1.0, ['Mb']); sel(Mb, [[-1, 128]], 1, ALU.is_ge, 'Mb')
        S.memset('pool', Mb[64:128, 0:64], 0.0, ['Mb'])
        S.memset('pool', Ub[:], 1.0, ['Ub']); sel(Ub, [[1, 128]], -1, ALU.is_gt, 'Ub')
        S.memset('pool', Ub[0:64, 64:128], 0.0, ['Ub'])
        S.memset('pool', Mf[:], 1.0, ['Mf']); sel(Mf, [[1, 128]], -1, ALU.is_ge, 'Mf')
        S.memset('pool', Mf[0:64, 64:128], 0.0, ['Mf'])
        S.ts('pool', Rf[:], Uf[:], -1.0, None, ALU.mult, None, ['Uf'], ['Rf'])
        S.ts('pool', Rb[:], Ub[:], -1.0, None, ALU.mult, None, ['Ub'], ['Rb'])
        S.memset('pool', Ind[:], 0.0, ['Ind'])
        S.memset('pool', Ind[0:64, 0:1], 1.0, ['Ind'])
        S.memset('pool', Ind[64:128, 1:2], 1.0, ['Ind'])

        wstack = ExitStack()
        Win = wstack.enter_context(nc.sbuf_tensor("Win", [128, 8, INC], BF16))
        Wkpe = wstack.enter_context(nc.sbuf_tensor("Wkpe", [128, 8, 128], BF16))
        winv = w_in.rearrange("(k p) n -> p k n", p=128)
        S.memset('dve', Wkpe[:, :, 64:128], 0.0, ['Wkpe'])
        for k in range(8):
            S.dma('pool', Win[:, k, :], winv[:, k, :], (), ['Win'])
            for f_ in range(2):
                S.dma('pool', Wkpe[:, k, f_ * 32:(f_ + 1) * 32].rearrange("p (a i) -> p a i", a=2),
                      winv[:, k, 512:576].rearrange("p (a f i) -> p f a i", a=2, f=2)[:, f_, :, :], (), ['Wkpe'])

        with ExitStack() as es:
            def TT(name, shape, dt):
                return es.enter_context(nc.sbuf_tensor(name, shape, dt))

            def PP(name, shape, dt):
                return es.enter_context(nc.psum_tensor(name, shape, dt))
            crow = TT("crow", [128, 2, D], F32)
            cb = TT("cb", [128, 2, 8, 128], F32)
            bmod = TT("bmod", [128, 6 * D], F32)
            wm = [TT(f"wm{i}", [128, 8, 512], F32) for i in range(2)]
            modl = TT("modl", [128, 6 * D], F32); modc = TT("modc", [128, 2 * D], F32)
            gm = TT("gm", [128, D], F32); gf = TT("gf", [128, D], F32)
            tmpA = [TT(f"tmpA{i}", [128, D], F32) for i in range(3)]
            pcb = PP("pcb", [128, 128], F32)
            pm = [PP(f"pm{i}", [128, 512], F32) for i in range(2)]
            S.dma('sp', crow[:, 0, :], c.partition_broadcast(128), (), ['crow'])
            S.dma('sp', crow[:, 1, :], c_ctx.partition_broadcast(128), (), ['crow'])
            S.dma('sp', bmod[:], b_mod.partition_broadcast(128), (), ['bmod'])
            S.dma('sp', gm[:], g_mix.partition_broadcast(128), (), ['gm'])
            S.dma('sp', gf[:], g_ffn.partition_broadcast(128), (), ['gf'])
            S.act(crow[:], crow[:], AF.Silu, ['crow'], ['crow'])
            for w_ in range(2):
                for k in range(8):
                    S.mm(pcb[:], crow[:, w_, k * 128:(k + 1) * 128], identf[:], True, True, ['crow', 'identf'], ['pcb'])
                    S.copy('dve', cb[:, w_, k, :], pcb[:], ['pcb'], ['cb'])
            wmv = w_mod.rearrange("(k p) n -> p k n", p=128)
            for j in range(12):
                wt = wm[j % 2]; wk = f"wm{j % 2}"
                S.dma('sp' if j % 2 == 0 else 'act', wt[:], wmv[:, :, j * 512:(j + 1) * 512], (), [wk])
                for k in range(8):
                    S.mm(pm[0][:], cb[:, 0, k, :], wt[:, k, :], k == 0, k == 7, ['cb', wk], ['pm0'])
                S.tt('dve', modl[:, j * 512:(j + 1) * 512], pm[0][:], bmod[:, j * 512:(j + 1) * 512], ALU.add,
                     ['pm0', 'bmod'], ['modl'])
                if j < 4:
                    for k in range(8):
                        S.mm(pm[1][:], cb[:, 1, k, :], wt[:, k, :], k == 0, k == 7, ['cb', wk], ['pm1'])
                    S.tt('dve', modc[:, j * 512:(j + 1) * 512], pm[1][:], bmod[:, j * 512:(j + 1) * 512], ALU.add,
                         ['pm1', 'bmod'], ['modc'])
            S.stt('dve', tmpA[0][:], modl[:, D:2 * D], 1.0, gm[:], ALU.add, ALU.mult, ['modl', 'gm'], ['tA0'])
            S.stt('dve', tmpA[1][:], modc[:, D:2 * D], 1.0, gm[:], ALU.add, ALU.mult, ['modc', 'gm'], ['tA1'])
            S.stt('dve', tmpA[2][:], modl[:, 4 * D:5 * D], 1.0, gf[:], ALU.add, ALU.mult, ['modl', 'gf'], ['tA2'])
            S.dma('sp', MOD[0], tmpA[0][:], ['tA0'], ())
            S.dma('sp', MOD[1], modl[:, 0:D], ['modl'], ())
            S.dma('sp', MOD[2], tmpA[1][:], ['tA1'], ())
            S.dma('sp', MOD[3], modc[:, 0:D], ['modc'], ())
            S.dma('sp', MOD[4], modl[:, 2 * D:3 * D], ['modl'], ())
            S.dma('sp', MOD[5], tmpA[2][:], ['tA2'], ())
            S.dma('sp', MOD[6], modl[:, 3 * D:4 * D], ['modl'], ())
            S.dma('sp', MOD[7], modl[:, 5 * D:6 * D], ['modl'], ())
            S.emit()
        if stop_after == 0:
            return nc

        with ExitStack() as es:
            def TT(name, shape, dt):
                return es.enter_context(nc.sbuf_tensor(name, shape, dt))

            def PP(name, shape, dt):
                return es.enter_context(nc.psum_tensor(name, shape, dt))
            mA = TT("mA", [128, 4, D], F32)
            junk = TT("junk", [128, D], BF16)
            xtR = Rot(TT, "xt", [128, D], F32, 4); ssR = Rot(TT, "ssa", [128, 1], F32, 4)
            t1R = Rot(TT, "t1_", [128, D], F32, 2); hbR = Rot(TT, "hb", [128, D], BF16, 3)
            hTR = Rot(TT, "hT", [128, 8, 128], BF16, 3); ptrR = Rot(PP, "ptr", [128, 8, 128], BF16, 2)
            for i in range(4):
                S.dma('sp', mA[:, i, :], MOD[i], (), ['mA'])

            def a1_s0(cx, ti):
                cx['xt'], cx['xk'] = xtR.next()
                src = ctx[ti * 128:(ti + 1) * 128, :] if ti < 2 else x[(ti - 2) * 128:(ti - 1) * 128, :]
                S.dma('sp', cx['xt'][:], src, (), [cx['xk']])

            def a1_s1(cx, ti):
                xt_, xk = cx['xt'], cx['xk']
                ss, sk = ssR.next()
                cx['ss'], cx['sk'] = ss, sk
                S.memset('dve', ss[:], 0.0, [sk])
                S.act(junk[:], xt_[:], AF.Square, [xk], ['junk', sk], accum=ss[:])
                S.act(ss[:], ss[:], AF.Sqrt, [sk], [sk], bias=EPS, scale=1.0 / D)
                S.op('dve', (lambda o_: (lambda e: e.reciprocal(out=o_[:], in_=o_[:])))(ss), [sk], [sk])

            def a1_s1b(cx, ti):
                xt_, xk, ss, sk = cx['xt'], cx['xk'], cx['ss'], cx['sk']
                mi = 2 if ti < 2 else 0
                t1, t1k = t1R.next(); hb, hbk = hbR.next()
                cx['hb'], cx['hbk'] = hb, hbk
                S.stt('dve', t1[:], xt_[:], ss[:, 0:1], mA[:, mi, :], ALU.mult, ALU.mult, [xk, sk, 'mA'], [t1k])
                S.tt('pool', hb[:], t1[:], mA[:, mi + 1, :], ALU.add, [t1k, 'mA'], [hbk])

            def a1_s2(cx, ti):
                hb, hbk = cx['hb'], cx['hbk']
                pt, ptk = ptrR.next(); hT, hTk = hTR.next()
                for k in range(8):
                    S.tr(pt[:, k, :], hb[:, k * 128:(k + 1) * 128], identb[:], [hbk, 'identb'], [ptk])
                S.copy('dve' if ti % 2 else 'act', hT[:], pt[:], [ptk], [hTk])
                S.dma('sp', HT[:, :, ti * 128:(ti + 1) * 128], hT[:], [hTk], ())
            run_pipeline(NT, [a1_s0, a1_s1, a1_s1b, a1_s2])
            S.emit()
        if stop_after == 1:
            return nc

        with ExitStack() as es:
            def TT(name, shape, dt):
                return es.enter_context(nc.sbuf_tensor(name, shape, dt))

            def PP(name, shape, dt):
                return es.enter_context(nc.psum_tensor(name, shape, dt))
            Wkrot = TT("Wkrot", [128, 8, 128], BF16)
            Wqn = TT("Wqn", [128, 2, 4, 128], BF16); Wqr = TT("Wqr", [128, 2, 4, 128], BF16)
            Wqrot = TT("Wqrot", [128, 2, 4, 128], BF16)
            Wkn = TT("Wkn", [128, 2, 4, 128], BF16); Wv = TT("Wv", [128, 2, 4, 128], BF16)
            Ct = TT("Ct", [64, N], F32); St = TT("St", [64, N], F32)
            lbt = TT("lbt", [128, 2, 512], F32); oml = TT("oml", [128, 2, 512], F32)
            with ExitStack() as es2:
                def T2(name, shape, dt):
                    return es2.enter_context(nc.sbuf_tensor(name, shape, dt))
                S.memset('dve', Wkrot[:, :, 64:128], 0.0, ['Wkrot'])
                for tl_, k_ in ((Wqr, 'Wqr'), (Wqrot, 'Wqrot')):
                    S.memset('dve', tl_[:, :, :, 64:128], 0.0, [k_])
                S.ts('dve', Wkrot[:, :, 0:32], Wkpe[:, :, 32:64], -1.0, None, ALU.mult, None, ['Wkpe'], ['Wkrot'])
                S.copy('dve', Wkrot[:, :, 32:64], Wkpe[:, :, 0:32], ['Wkpe'], ['Wkrot'])
                stq = T2("stq", [128, 2, 768], F32); stkv = T2("stkv", [128, 2, 1024], F32)
                gq = T2("gq", [128, 2], F32); gkv = T2("gkv", [128, 2], F32)
                S.dma('sp', stq[:], w_uq.rearrange("(c p) n -> p c n", p=128), (), ['stq'])
                S.dma('sp', stkv[:], w_ukv.rearrange("(c p) n -> p c n", p=128), (), ['stkv'])
                for c_ in range(2):
                    S.dma('sp', gq[:, c_:c_ + 1], g_qn[c_ * 128:(c_ + 1) * 128].rearrange("(p o) -> p o", o=1), (), ['gq'])
                    S.dma('sp', gkv[:, c_:c_ + 1], g_kvn[c_ * 128:(c_ + 1) * 128].rearrange("(p o) -> p o", o=1), (), ['gkv'])
                for c_ in range(2):
                    sq_v = stq[:, c_, :].rearrange("p (h d) -> p h d", h=4)
                    S.ts('dve', Wqn[:, c_, :, :], sq_v[:, :, 0:128], gq[:, c_:c_ + 1], None, ALU.mult, None,
                         ['stq', 'gq'], ['Wqn'])
                    for f_ in range(2):
                        for a_ in range(2):
                            so = 128 + a_ * 32 + f_ * 16
                            do = f_ * 32 + a_ * 16
                            S.ts('dve', Wqr[:, c_, :, do:do + 16], sq_v[:, :, so:so + 16], gq[:, c_:c_ + 1], None,
                                 ALU.mult, None, ['stq', 'gq'], ['Wqr'])
                    S.ts('dve', Wqrot[:, c_, :, 0:32], Wqr[:, c_, :, 32:64], -1.0, None, ALU.mult, None, ['Wqr'], ['Wqrot'])
                    S.copy('dve', Wqrot[:, c_, :, 32:64], Wqr[:, c_, :, 0:32], ['Wqr'], ['Wqrot'])
                    skv_v = stkv[:, c_, :].rearrange("p (h t d) -> p h t d", h=4, t=2)
                    S.ts('dve', Wkn[:, c_, :, :], skv_v[:, :, 0, :], gkv[:, c_:c_ + 1], None, ALU.mult, None,
                         ['stkv', 'gkv'], ['Wkn'])
                    S.ts('dve', Wv[:, c_, :, :], skv_v[:, :, 1, :], gkv[:, c_:c_ + 1], None, ALU.mult, None,
                         ['stkv', 'gkv'], ['Wv'])
                lraw = T2("lraw", [128, 2, 2, 512], F32)
                for d_, lbx in enumerate((lb_f, lb_b)):
                    for r_ in range(2):
                        S.dma('sp', lraw[:, d_, r_, :], lbx[r_].partition_broadcast(128), (), ['lraw'])
                S.tt('dve', lbt[:], lraw[:, :, 0, :], lraw[:, :, 1, :], ALU.subtract, ['lraw'], ['lbt'])
                S.act(lbt[:], lbt[:], AF.Sigmoid, ['lbt'], ['lbt'])
                S.ts('dve', oml[:], lbt[:], -0.5, 0.5, ALU.mult, ALU.add, ['lbt'], ['oml'])
                S.tt('dve', lbt[:], lbt[:], oml[:], ALU.add, ['lbt', 'oml'], ['lbt'])
                pidx = T2("pidx", [64, 1], F32); i16 = T2("i16", [64, 1], F32); mrow = T2("mrow", [64, 1], F32)
                arow = T2("arow", [64, 1], F32); acol = T2("acol", [64, 1], F32)
                rowpos = T2("rowpos", [64, N], F32); colpos = T2("colpos", [64, N], F32); ang = T2("ang", [64, N], F32)
                S.op('pool', lambda e: e.iota(pidx[:], [[0, 1]], base=0, channel_multiplier=1,
                                              allow_small_or_imprecise_dtypes=True), (), ['pidx'])
                S.op('pool', lambda e: e.iota(rowpos[:], [[1, 64], [0, 64]], base=0, channel_multiplier=0,
                                              allow_small_or_imprecise_dtypes=True), (), ['rowpos'])
                S.op('pool', lambda e: e.iota(colpos[:], [[0, 64], [1, 64]], base=0, channel_multiplier=0,
                                              allow_small_or_imprecise_dtypes=True), (), ['colpos'])
                msk = T2("msk", [64, 3], F32)
                S.memset('pool', msk[:], 1.0, ['msk'])
                for j_ in range(3):
                    S.op('pool', (lambda jj: (lambda e: e.affine_select(
                        out=msk[:, jj:jj + 1], in_=msk[:, jj:jj + 1], pattern=[[0, 1]], compare_op=ALU.is_ge, fill=0.0,
                        base=-16 * (jj + 1), channel_multiplier=1)))(j_), ['msk'], ['msk'])
                S.tt('dve', mrow[:], msk[:, 0:1], msk[:, 1:2], ALU.add, ['msk'], ['mrow'])
                S.tt('dve', mrow[:], mrow[:], msk[:, 2:3], ALU.add, ['msk', 'mrow'], ['mrow'])
                S.stt('dve', i16[:], mrow[:], -16.0, pidx[:], ALU.mult, ALU.add, ['mrow', 'pidx'], ['i16'])
                S.tt('dve', mrow[:], msk[:, 1:2], msk[:, 0:1], ALU.subtract, ['msk'], ['mrow'])
                S.tt('dve', mrow[:], mrow[:], msk[:, 2:3], ALU.subtract, ['msk', 'mrow'], ['mrow'])
                S.ts('dve', mrow[:], mrow[:], 1.0, None, ALU.add, None, ['mrow'], ['mrow'])
                S.act(i16[:], i16[:], AF.Exp, ['i16'], ['i16'], scale=-math.log(10000.0) / 16.0)
                S.tt('dve', arow[:], i16[:], mrow[:], ALU.mult, ['i16', 'mrow'], ['arow'])
                S.tt('dve', acol[:], i16[:], arow[:], ALU.subtract, ['i16', 'arow'], ['acol'])
                S.ts('dve', ang[:], rowpos[:], arow[:, 0:1], None, ALU.mult, None, ['rowpos', 'arow'], ['ang'])
                S.stt('dve', ang[:], colpos[:], acol[:, 0:1], ang[:], ALU.mult, ALU.add, ['colpos', 'acol', 'ang'], ['ang'])
                sc_ = 1.0 - 1e-6
                ki = T2("ki", [64, N], mybir.dt.int32)
                for tab, shift in ((St, 0.0), (Ct, 0.5 * math.pi)):
                    S.ts('dve', rowpos[:], ang[:], shift, 1.0 / (2 * math.pi), ALU.add, ALU.mult, ['ang'], ['rowpos'])
                    S.copy('dve', ki[:], rowpos[:], ['rowpos'], ['ki'])
                    S.copy('dve', colpos[:], ki[:], ['ki'], ['colpos'])
                    S.ts('dve', rowpos[:], ang[:], shift, None, ALU.add, None, ['ang'], ['rowpos'])
                    S.stt('dve', rowpos[:], colpos[:], -2 * math.pi, rowpos[:], ALU.mult, ALU.add, ['colpos', 'rowpos'], ['rowpos'])
                    S.act(tab[:], rowpos[:], AF.Sin, ['rowpos'], ['St' if shift == 0.0 else 'Ct'], scale=sc_)
                if stop_after == 15:
                    for nm, tl, shp, dt_ in (("d_Ct", Ct, [64, N], F32), ("d_St", St, [64, N], F32),
                                             ("d_Wkpe", Wkpe, [128, 8, 128], BF16), ("d_Wkrot", Wkrot, [128, 8, 128], BF16),
                                             ("d_Wqn", Wqn, [128, 2, 4, 128], BF16), ("d_Wqr", Wqr, [128, 2, 4, 128], BF16),
                                             ("d_Wqrot", Wqrot, [128, 2, 4, 128], BF16), ("d_Wkn", Wkn, [128, 2, 4, 128], BF16),
                                             ("d_Wv", Wv, [128, 2, 4, 128], BF16), ("d_lbt", lbt, [128, 2, 512], F32),
                                             ("d_oml", oml, [128, 2, 512], F32), ("d_Win", Win, [128, 8, INC], BF16)):
                        dd = nc.dram_tensor(nm, shp, dt_, kind="ExternalOutput").ap()
                        S.dma('sp', dd, tl[:], [nm[2:]], ())
                S.emit()
                if stop_after == 15:
                    return nc

            hTg = Rot(TT, "hTg", [128, 8, 512], BF16, 2)
            cT = TT("cT", [128, 2, 512], BF16); sq = TT("sq", [128, 2, 512], BF16)
            rbc = TT("rbc", [128, 512], F32); rtk = TT("rtk", [128, 4], F32)
            o_bf = Rot(TT, "o_bf", [128, 512], BF16, 3)
            o_f = Rot(TT, "o_f", [128, 512], F32, 4)
            u_f = Rot(TT, "u_f", [64, 512], F32, 4)
            sgb = Rot(TT, "sgb", [128, 512], F32, 3); fb_ = Rot(TT, "fb_", [128, 512], F32, 3)
            pA = [PP(f"pA{i}", [128, 512], F32) for i in range(2)]
            pB = [PP(f"pB{i}", [128, 512], F32) for i in range(2)]
            pS = PP("pS", [128, 512], F32)
            pT = [PP(f"pT{i}", [128, 512], F32) for i in range(2)]
            pV = PP("pV", [128, 4, 128], F32)
            groups = [(0, 256)] + [(256 + i * 512, 512) for i in range(8)]
            if a2_groups is not None:
                groups = groups[:a2_groups]
            for (tok0, n) in groups:
                is_lat = tok0 >= L
                lo = tok0 - L
                nsub = n // 128
                S.enabled = True
                hT_, hk = hTg.next()
                S.dma('sp', hT_[:, :, :n], HT[:, :, tok0:tok0 + n], (), [hk])

                def fm_proj(ps, pk, wt, wk, col0, ncols):
                    for k in range(8):
                        S.mm(ps[0:ncols, :n], wt[:, k, col0:col0 + ncols], hT_[:, k, :n], k == 0, k == 7, [wk, hk], [pk])

                def lowrank(col0, want_tok):
                    for c_ in range(2):
                        fm_proj(pA[c_], f'pA{c_}', Win, 'Win', col0 + c_ * 128, 128)
                        S.copy('act', cT[:, c_, :n], pA[c_][:, :n], [f'pA{c_}'], ['cT'])
                        S.act(sq[:, c_, :n], pA[c_][:, :n], AF.Square, [f'pA{c_}'], ['sq'])
                    for c_ in range(2):
                        S.mm(pS[:, :n], onesb[:], sq[:, c_, :n], c_ == 0, c_ == 1, ['onesb', 'sq'], ['pS'])
                    S.act(rbc[:, :n], pS[:, :n], AF.Sqrt, ['pS'], ['rbc'], bias=EPS, scale=1.0 / 256)
                    S.op('dve', (lambda nn: (lambda e: e.reciprocal(out=rbc[:, :nn], in_=rbc[:, :nn])))(n), ['rbc'], ['rbc'])
                    if want_tok:
                        for s_ in range(nsub):
                            for c_ in range(2):
                                S.mm(pV[:, s_, :], sq[:, c_, s_ * 128:(s_ + 1) * 128], onesb[:], c_ == 0, c_ == 1,
                                     ['sq', 'onesb'], ['pV'])
                        S.act(rtk[:, :nsub], pV[:, :nsub, 0], AF.Sqrt, ['pV'], ['rtk'], bias=EPS, scale=1.0 / 256)
                        S.op('dve', (lambda ns: (lambda e: e.reciprocal(out=rtk[:, :ns], in_=rtk[:, :ns])))(nsub), ['rtk'], ['rtk'])

                def rope_out(p0, k0, p1, k1, dst, scale_rows):
                    u1, uk1 = u_f.next(); u2, uk2 = u_f.next()
                    S.tt('dve', u1[:, :n], p0[0:64, :n], Ct[:, lo:lo + n], ALU.mult, [k0, 'Ct'], [uk1])
                    S.tt('dve', u2[:, :n], p1[0:64, :n], St[:, lo:lo + n], ALU.mult, [k1, 'St'], [uk2])
                    ob, ok = o_bf.next()
                    if scale_rows:
                        S.tt('pool', u1[:, :n], u1[:, :n], u2[:, :n], ALU.add, [uk1, uk2], [uk1])
                        S.tt('pool', ob[0:64, :n], u1[:, :n], rbc[0:64, :n], ALU.mult, [uk1, 'rbc'], [ok])
                    else:
                        S.tt('pool', ob[0:64, :n], u1[:, :n], u2[:, :n], ALU.add, [uk1, uk2], [ok])
                    S.dma(STQ, dst, ob[0:64, :n], [ok], ())

                if is_lat and 'a' in a2_parts:
                    lowrank(0, False)
                    for h in range(4):
                        pb, pk = pB[h % 2], f'pB{h % 2}'
                        for c_ in range(2):
                            S.mm(pb[:, :n], Wqn[:, c_, h, :], cT[:, c_, :n], c_ == 0, c_ == 1, ['Wqn', 'cT'], [pk])
                        ob, ok = o_bf.next()
                        S.tt('dve', ob[:, :n], pb[:, :n], rbc[:, :n], ALU.mult, [pk, 'rbc'], [ok])
                        S.dma(STQ, QN[h][:, lo:lo + n], ob[:, :n], [ok], ())
                    for h in range(4):
                        for c_ in range(2):
                            S.mm(pB[0][:, :n], Wqr[:, c_, h, :], cT[:, c_, :n], c_ == 0, c_ == 1, ['Wqr', 'cT'], ['pB0'])
                        for c_ in range(2):
                            S.mm(pB[1][:, :n], Wqrot[:, c_, h, :], cT[:, c_, :n], c_ == 0, c_ == 1, ['Wqrot', 'cT'], ['pB1'])
                        rope_out(pB[0], 'pB0', pB[1], 'pB1', QR[h][:, lo:lo + n], True)
                S.enabled = 'b' in a2_parts
                lowrank(256, True)
                for h in range(4):
                    pb, pk = pB[h % 2], f'pB{h % 2}'
                    for c_ in range(2):
                        S.mm(pb[:, :n], Wkn[:, c_, h, :], cT[:, c_, :n], c_ == 0, c_ == 1, ['Wkn', 'cT'], [pk])
                    ob, ok = o_bf.next()
                    S.tt('dve', ob[:, :n], pb[:, :n], rbc[:, :n], ALU.mult, [pk, 'rbc'], [ok])
                    S.dma(STQ, KN[h][:, tok0:tok0 + n], ob[:, :n], [ok], ())
                Wv2 = Wv[:].rearrange("p c h d -> p c (h d)")
                for s_ in range(nsub):
                    pt, pk = pT[s_ % 2], f'pT{s_ % 2}'
                    for c_ in range(2):
                        S.mm(pt[:], cT[:, c_, s_ * 128:(s_ + 1) * 128], Wv2[:, c_, :], c_ == 0, c_ == 1, ['cT', 'Wv'], [pk])
                    ob, ok = o_bf.next()
                    S.act(ob[:], pt[:], AF.Copy, [pk, 'rtk'], [ok], scale=rtk[:, s_:s_ + 1])
                    S.dma(STQ, VV[tok0 + s_ * 128:tok0 + (s_ + 1) * 128, :], ob[:], [ok], ())
                S.enabled = 'c' in a2_parts
                fm_proj(pA[0], 'pA0', Wkpe, 'Wkpe', 0, 128)
                if is_lat:
                    fm_proj(pA[1], 'pA1', Wkrot, 'Wkrot', 0, 128)
                    rope_out(pA[0], 'pA0', pA[1], 'pA1', KR[:, tok0:tok0 + n], False)
                else:
                    ob, ok = o_bf.next()
                    S.copy('act', ob[0:64, :n], pA[0][0:64, :n], ['pA0'], [ok])
                    S.dma(STQ, KR[:, tok0:tok0 + n], ob[0:64, :n], [ok], ())
                S.enabled = 'd' in a2_parts
                for h in range(4):
                    pa, pk = pA[h % 2], f'pA{h % 2}'
                    fm_proj(pa, pk, Win, 'Win', 576 + h * 128, 128)
                    of, ok = o_f.next()
                    S.copy('act' if h % 2 else 'dve', of[:, :n], pa[:, :n], [pk], [ok])
                    S.dma(STQ, HQ[h][:, tok0:tok0 + n], of[:, :n], [ok], ())
                S.enabled = 'e' in a2_parts
                for s_ in range(nsub):
                    row0 = tok0 + s_ * 128

                    def tm_proj(ps, pk, col0):
                        for k in range(8):
                            S.mm(ps[:], hT_[:, k, s_ * 128:(s_ + 1) * 128], Win[:, k, col0:col0 + 512], k == 0, k == 7,
                                 [hk, 'Win'], [pk])
                    tm_proj(pT[0], 'pT0', 1088)
                    ob, ok = o_bf.next()
                    S.copy('dve', ob[:], pT[0][:], ['pT0'], [ok])
                    S.dma(STQ, HV[row0:row0 + 128, :], ob[:], [ok], ())
                    if is_lat:
                        tm_proj(pT[1], 'pT1', 1600)
                        th, thk = sgb.next(); uh, uhk = fb_.next()
                        S.act(th[:], pT[1][:], AF.Tanh, ['pT1'], [thk], scale=0.5)
                        S.act(uh[:], pT[1][:], AF.Copy, ['pT1'], [uhk], scale=0.5)
                        of, ok = o_f.next()
                        S.tt('pool', th[:], th[:], uh[:], ALU.mult, [thk, uhk], [thk])
                        S.tt('pool', of[:], th[:], uh[:], ALU.add, [thk, uhk], [ok])
                        S.dma(STQ, HG[row0 - L:row0 - L + 128, :], of[:], [ok], ())
                    fts = []
                    for d_ in range(2):
                        pt, pk = pT[d_], f'pT{d_}'
                        tm_proj(pt, pk, 2112 + d_ * 512)
                        sg, sk = sgb.next(); ff, fk = fb_.next()
                        S.act(sg[:], pt[:], AF.Tanh, [pk], [sk], scale=0.5)
                        S.tt('dve', ff[:], sg[:], oml[:, d_, :], ALU.mult, [sk, 'oml'], [fk])
                        S.tt('pool', ff[:], ff[:], lbt[:, d_, :], ALU.add, [fk, 'lbt'], [fk])
                        fts.append((ff, fk))
                    for d_ in range(2):
                        ff, fk = fts[d_]
                        of, ok = o_f.next()
                        S.act(of[:], ff[:], AF.Ln, [fk], [ok])
                        S.dma(STQ, GG[d_][row0:row0 + 128, :], of[:], [ok], ())
                        of2, ok2 = o_f.next()
                        S.ts('pool', of2[:], ff[:], -1.0, 1.0, ALU.mult, ALU.add, [fk], [ok2])
                        S.dma(STQ, KG[d_][row0:row0 + 128, :], of2[:], [ok2], ())
            S.enabled = True
            S.emit()
        wstack.close()
        if stop_after == 2:
            return nc

        with ExitStack() as es:
            def TT(name, shape, dt):
                return es.enter_context(nc.sbuf_tensor(name, shape, dt))

            def PP(name, shape, dt):
                return es.enter_context(nc.psum_tensor(name, shape, dt))

            bt = []
            for d_ in range(2):
                bt.append(dict(
                    g=Rot(TT, f"bg{d_}", [128, 512], F32, 2), kg=Rot(TT, f"bkg{d_}", [128, 512], F32, 2),
                    v=Rot(TT, f"bv{d_}", [128, 512], BF16, 3), hq=Rot(TT, f"bhq{d_}", [128, 4, 128], F32, 2),
                    Ek=Rot(TT, f"bEk{d_}", [128, 512], F32, 2), EqT=Rot(TT, f"bEq{d_}", [128, 4, 128], F32, 2),
                    eb=Rot(TT, f"beb{d_}", [128, 4, 2], F32, 3), K2=Rot(TT, f"bK2{d_}", [128, 512], BF16, 2),
                    K2m=[Rot(TT, f"bK2m{c_}{d_}", [128, 512], BF16, 2) for c_ in range(2)],
                    QsT=Rot(TT, f"bQs{d_}", [128, 4, 128], BF16, 2),
                    Qsm=[Rot(TT, f"bQsm{c_}{d_}", [128, 4, 128], BF16, 2) for c_ in range(2)],
                    K2T=Rot(TT, f"bK2T{d_}", [128, 4, 128], BF16, 2),
                    Am=Rot(TT, f"bAm{d_}", [128, 4, 128], BF16, 2),
                    S=[TT(f"bS{d_}_{j_}", [128, 4, 128], F32) for j_ in range(2)], cur=[0],
                    Sp=Rot(TT, f"bSp{d_}", [128, 4, 128], F32, 2), Spb=Rot(TT, f"bSpb{d_}", [128, 4, 128], BF16, 4),
                    osb=Rot(TT, f"bos{d_}", [128, 512], F32, 2)))
                S.memset('dve', bt[d_]['S'][0][:], 0.0, [f'bS{d_}_0h{h}' for h in range(4)])
                for c_ in range(2):
                    oth = slice(64, 128) if c_ == 0 else slice(0, 64)
                    for j_ in range(2):
                        S.memset('pool', bt[d_]['K2m'][c_].t[j_][oth, :], 0.0, [bt[d_]['K2m'][c_].k[j_]])
                        S.memset('pool', bt[d_]['Qsm'][c_].t[j_][:, :, oth], 0.0, [bt[d_]['Qsm'][c_].k[j_]])
            pD1 = PP("pD1", [128, 512], F32); pD2 = PP("pD2", [128, 4, 128], F32)
            pKT = PP("pKT", [128, 4, 128], BF16); pBL = PP("pBL", [128, 4, 2], F32)
            pAT = PP("pAT", [128, 4, 128], F32)
            pOb = PP("pOb", [128, 4, 128], F32)
            pSNr = Rot(PP, "pSN", [128, 4, 128], F32, 2)

            def hgrn_pre(cx, ti, d_):
                B_ = bt[d_]
                is_lat = ti >= 2
                row0 = ti * 128
                U_, R_, M_ = (Uf, Rf, Mf) if d_ == 0 else (Ub, Rb, Mb)
                uk, rk, mk_ = ('Uf', 'Rf', 'Mf') if d_ == 0 else ('Ub', 'Rb', 'Mb')
                g, gk = B_['g'].next(); kg, kgk = B_['kg'].next(); v, vk = B_['v'].next(); hq, hqk = B_['hq'].next()
                S.dma('sp', g[:], GG[d_][row0:row0 + 128, :], (), [gk])
                S.dma('sp', kg[:], KG[d_][row0:row0 + 128, :], (), [kgk])
                S.dma('sp', v[:], HV[row0:row0 + 128, :], (), [vk])
                S.dma('sp', hq[:], HQ[:, :, row0:row0 + 128].rearrange("h p t -> p h t"), (), [hqk])
                Ek, Ekk = B_['Ek'].next(); EqT, Eqk = B_['EqT'].next(); eb, ebk = B_['eb'].next()
                K2, K2k = B_['K2'].next(); QsT, Qsk = B_['QsT'].next(); K2T, K2Tk = B_['K2T'].next()
                S.mm(pD1[:], U_[:], g[:], True, True, [uk, gk], ['pD1'])
                for h in range(4):
                    S.mm(pD2[:, h, :], g[:, h * 128:(h + 1) * 128], R_[:], True, True, [gk, rk], ['pD2'])
                for h in range(4):
                    S.mm(pBL[:, h, :], g[:, h * 128:(h + 1) * 128], Ind[:], True, True, [gk, 'Ind'], ['pBL'])
                S.act(Ek[:], pD1[:], AF.Exp, ['pD1'], [Ekk])
                S.act(EqT[:], pD2[:], AF.Exp, ['pD2'], [Eqk])
                S.act(eb[:], pBL[:], AF.Exp, ['pBL'], [ebk])
                S.tt('dve', K2[:], kg[:], Ek[:], ALU.mult, [kgk, Ekk], [K2k])
                K2m = []
                for c_ in range(2):
                    rs_ = slice(c_ * 64, (c_ + 1) * 64)
                    km, kmk = B_['K2m'][c_].next()
                    S.copy('act', km[rs_, :], K2[rs_, :], [K2k], [kmk])
                    K2m.append((km, kmk))
                cx.update(v=v, vk=vk, eb=eb, ebk=ebk, K2m=K2m)
                if is_lat:
                    S.tt('pool', QsT[:], hq[:], EqT[:], ALU.mult, [hqk, Eqk], [Qsk])
                    Qsm = []
                    for c_ in range(2):
                        cs_ = slice(c_ * 64, (c_ + 1) * 64)
                        qm, qmk = B_['Qsm'][c_].next()
                        S.copy('pool', qm[:, :, cs_], QsT[:, :, cs_], [Qsk], [qmk])
                        Qsm.append((qm, qmk))
                    Am, Amk = B_['Am'].next()
                    for h in range(4):
                        S.tr(pKT[:, h, :], K2[:, h * 128:(h + 1) * 128], identb[:], [K2k, 'identb'], ['pKT'])
                    S.copy('act', K2T[:], pKT[:], ['pKT'], [K2Tk])
                    for h in range(4):
                        S.mm(pAT[:, h, :], K2T[:, h, :], QsT[:, h, :], True, True, [K2Tk, Qsk], ['pAT'])
                    S.tt('dve', Am[:], pAT[:], M_[:].unsqueeze(1).to_broadcast([128, 4, 128]), ALU.mult,
                         ['pAT', mk_], [Amk])
                    cx.update(Am=Am, Amk=Amk, Qsm=Qsm)

            def hgrn_chain(cx, ti, d_):
                B_ = bt[d_]
                is_lat = ti >= 2
                row0 = ti * 128
                v, vk, eb, ebk, K2m = (cx[k_] for k_ in ('v', 'vk', 'eb', 'ebk', 'K2m'))
                order = (0, 1) if d_ == 0 else (1, 0)
                spbs = {}
                for c_ in order:
                    ci = B_['cur'][0]
                    Sc, Sn = B_['S'][ci], B_['S'][1 - ci]
                    Sck = [f'bS{d_}_{ci}h{h}' for h in range(4)]
                    Snk = [f'bS{d_}_{1 - ci}h{h}' for h in range(4)]
                    B_['cur'][0] = 1 - ci
                    km, kmk = K2m[c_]
                    psn, psnk = pSNr.next()
                    for h in range(4):
                        S.mm(psn[:, h, :], km[:, h * 128:(h + 1) * 128], v[:, h * 128:(h + 1) * 128], True, True,
                             [kmk, vk], [psnk])
                    for h in range(4):
                        S.stt('dve', Sn[:, h, :], Sc[:, h, :], eb[:, h, c_:c_ + 1], psn[:, h, :], ALU.mult, ALU.add,
                              [Sck[h], ebk, psnk], [Snk[h]])
                    if is_lat:
                        Spb, Spbk = B_['Spb'].next()
                        S.tt('pool', Spb[:], Sc[:], eb[:, :, c_:c_ + 1].to_broadcast([128, 4, 128]), ALU.mult,
                             Sck + [ebk], [Spbk])
                        spbs[c_] = (Spb, Spbk)
                if is_lat:
                    Am, Amk, Qsm = cx['Am'], cx['Amk'], cx['Qsm']
                    for h in range(4):
                        S.mm(pOb[:, h, :], Am[:, h, :], v[:, h * 128:(h + 1) * 128], True, False, [Amk, vk], ['pOb'])
                        for n_, c_ in enumerate(order):
                            qm, qmk = Qsm[c_]
                            Spb, Spbk = spbs[c_]
                            S.mm(pOb[:, h, :], qm[:, h, :], Spb[:, h, :], False, n_ == 1, [qmk, Spbk], ['pOb'])
                    ob, obk = B_['osb'].next()
                    S.copy('act', ob[:], pOb[:].rearrange("p h d -> p (h d)"), ['pOb'], [obk])
                    S.dma('sp', OO[d_][row0 - L:row0 - L + 128, :], ob[:], [obk], ())

            fwd_order = list(range(NT))
            bwd_order = [1, 0] + list(range(NT - 1, 1, -1))
            cxs = [[dict() for _ in range(NT)] for _ in range(2)]
            hgrn_pre(cxs[0][0], fwd_order[0], 0)
            hgrn_pre(cxs[1][0], bwd_order[0], 1)
            for i_ in range(NT):
                if i_ + 1 < NT:
                    hgrn_pre(cxs[0][i_ + 1], fwd_order[i_ + 1], 0)
                    hgrn_pre(cxs[1][i_ + 1], bwd_order[i_ + 1], 1)
                hgrn_chain(cxs[0][i_], fwd_order[i_], 0)
                hgrn_chain(cxs[1][i_], bwd_order[i_], 1)
            S.emit()
        if stop_after == 3:
            return nc

        with ExitStack() as es:
            def TT(name, shape, dt):
                return es.enter_context(nc.sbuf_tensor(name, shape, dt))

            def PP(name, shape, dt):
                return es.enter_context(nc.psum_tensor(name, shape, dt))

            KNs = TT("KNs", [128, 4, T], BF16); KRs = TT("KRs", [128, T], BF16); Vs = TT("Vs", [128, NT, 512], BF16)
            sqKR = TT("sqKR", [64, T], BF16)
            sqn = TT("sqn", [128, 512], BF16); sqr = TT("sqr", [64, 512], BF16)
            sqnR = Rot(TT, "csqn", [128, 512], BF16, 2); sqrR = Rot(TT, "csqr", [64, 512], BF16, 2)
            km2 = TT("km2", [128, 4], F32); tmx = TT("tmx", [128, 1], F32)
            tmxR = Rot(TT, "ctmx", [128, 1], F32, 3); nshR = Rot(TT, "cnsh", [128, 1], F32, 3)
            qnr = Rot(TT, "cqn", [128, 512], BF16, 3); qrr = Rot(TT, "cqr", [128, 512], BF16, 3)
            PTr = Rot(TT, "cPT", [128, 1024], BF16, 4); osr = Rot(TT, "cos", [128, 512], BF16, 2)
            rinv = TT("rinv", [128, 512], F32)
            raccR = [Rot(TT, f"racc{i}_", [128, 1024], F32, 2) for i in range(2)]
            rsumR = [Rot(TT, f"rsum{i}_", [128, 512], F32, 2) for i in range(2)]
            pScR = Rot(PP, "pSc", [128, 1024], F32, 2)
            pOaR = Rot(PP, "pOa", [128, 512], F32, 2)
            pMiR = Rot(PP, "pMi", [128, 512], F32, 2)
            pNm = pMiR.t[0]
            for h in range(4):
                S.dma('sp', KNs[:, h, :], KN[h], (), ['KNs'])
            S.memset('pool', KRs[64:128, :], 0.0, ['KRs'])
            S.dma('sp', KRs[0:64, :], KR, (), ['KRs'])
            for i_ in range(3):
                S.memset('pool', qrr.t[i_][64:128, :], 0.0, [qrr.k[i_]])
            VVv = VV.rearrange("(t p) n -> p t n", p=128)
            for j in range(0, NT, 4):
                je = min(NT, j + 4)
                S.dma('sp', Vs[:, j:je, :], VVv[:, j:je, :], (), ['Vs'])
            S.memset('dve', km2[:], 0.0, ['km2'])
            S.act(sqKR[:], KRs[0:64, :], AF.Square, ['KRs'], ['sqKR'])
            for j0 in range(0, T, 512):
                w_ = min(512, T - j0)
                for h in range(4):
                    S.act(sqn[:, :w_], KNs[:, h, j0:j0 + w_], AF.Square, ['KNs'], ['sqn'])
                    S.mm(pNm[:, :w_], onesb[:], sqn[:, :w_], True, False, ['onesb', 'sqn'], ['pMi0'])
                    S.mm(pNm[:, :w_], onesb[0:64, :], sqKR[:, j0:j0 + w_], False, True, ['onesb', 'sqKR'], ['pMi0'])
                    S.op('dve', (lambda ww: (lambda e: e.reduce_max(out=tmx[:], in_=pNm[:, :ww], axis=AX.X)))(w_),
                         ['pMi0'], ['tmx'])
                    S.tt('dve', km2[:, h:h + 1], km2[:, h:h + 1], tmx[:], ALU.max, ['km2', 'tmx'], ['km2'])
            items = [(g_, h) for g_ in range(8) for h in range(4)]
            cxs = [dict() for _ in items]
            NP_ = NT // 2

            def c_pro(i):
                g_, h = items[i]
                q0 = g_ * 512
                cx = cxs[i]
                qn, qnk = qnr.next(); qr, qrk = qrr.next()
                sqn_, sqnk = sqnR.next(); sqr_, sqrk = sqrR.next(); tm_, tmk = tmxR.next(); nsh, nshk = nshR.next()
                pm, pmk = pMiR.next()
                S.dma('sp', qn[:], QN[h][:, q0:q0 + 512], (), [qnk])
                S.dma('sp', qr[0:64, :], QR[h][:, q0:q0 + 512], (), [qrk])
                S.act(sqn_[:], qn[:], AF.Square, [qnk], [sqnk])
                S.act(sqr_[:], qr[0:64, :], AF.Square, [qrk], [sqrk])
                S.mm(pm[:], onesb[:], sqn_[:], True, False, ['onesb', sqnk], [pmk])
                S.mm(pm[:], onesb[0:64, :], sqr_[:], False, True, ['onesb', sqrk], [pmk])
                S.op('dve', (lambda o_, i_: (lambda e: e.reduce_max(out=o_[:], in_=i_[:], axis=AX.X)))(tm_, pm), [pmk], [tmk])
                S.ts('dve', nsh[:], tm_[:], km2[:, h:h + 1], -0.5 * SCALE, ALU.add, ALU.mult, [tmk, 'km2'], [nshk])
                cx.update(qn=qn, qnk=qnk, qr=qr, qrk=qrk, nsh=nsh, nshk=nshk, ps={})

            def c_qk(i, j):
                g_, h = items[i]
                cx = cxs[i]
                ps, pk = pScR.next()
                cx['ps'][j] = (ps, pk)
                for u_ in range(2):
                    kt = 2 * j + u_
                    S.mm(ps[:, u_ * 512:(u_ + 1) * 512], KNs[:, h, kt * 128:(kt + 1) * 128], cx['qn'][:], True, False,
                         ['KNs', cx['qnk']], [pk])
                    S.mm(ps[:, u_ * 512:(u_ + 1) * 512], KRs[:, kt * 128:(kt + 1) * 128], cx['qr'][:], False, True,
                         ['KRs', cx['qrk']], [pk])

            def c_main(i):
                g_, h = items[i]
                cx = cxs[i]
                pOa, pOak = pOaR.next()
                racc = [raccR[0].next(), raccR[1].next()]
                cx.update(pOa=pOa, pOak=pOak, racc=racc)
                for j in range(NP_):
                    if j + 1 < NP_:
                        c_qk(i, j + 1)
                    ps, pk = cx['ps'].pop(j)
                    PT, ptk = PTr.next()
                    S.act(PT[:], ps[:], AF.Exp, [pk, cx['nshk']], [ptk], bias=cx['nsh'][:], scale=SCALE)
                    for u_ in range(2):
                        kt = 2 * j + u_
                        S.mm(pOa[:], Vs[:, kt, h * 128:(h + 1) * 128], PT[:, u_ * 512:(u_ + 1) * 512],
                             kt == 0, kt == NT - 1, ['Vs', ptk], [pOak])
                    ae = 'dve' if j % 2 == 0 else 'pool'
                    ra, rak = racc[j % 2]
                    if j < 2:
                        S.copy(ae, ra[:], PT[:], [ptk], [rak])
                    else:
                        S.tt(ae, ra[:], ra[:], PT[:], ALU.add, [rak, ptk], [rak])

            def c_epi(i):
                g_, h = items[i]
                q0 = g_ * 512
                cx = cxs[i]
                pOa, pOak, racc = cx['pOa'], cx['pOak'], cx['racc']
                rs0, rs0k = rsumR[0].next(); rs1, rs1k = rsumR[1].next()
                pm, pmk = pMiR.next()
                S.tt('dve', rs0[:], racc[0][0][:, 0:512], racc[0][0][:, 512:1024], ALU.add, [racc[0][1]], [rs0k])
                S.tt('pool', rs1[:], racc[1][0][:, 0:512], racc[1][0][:, 512:1024], ALU.add, [racc[1][1]], [rs1k])
                S.mm(pm[:], onesf[:], rs0[:], True, False, ['onesf', rs0k], [pmk])
                S.mm(pm[:], onesf[:], rs1[:], False, True, ['onesf', rs1k], [pmk])
                S.op('dve', (lambda i_: (lambda e: e.reciprocal(out=rinv[:], in_=i_[:])))(pm), [pmk], ['rinv'])
                ob, obk = osr.next()
                S.tt('dve', ob[:], pOa[:], rinv[:], ALU.mult, [pOak, 'rinv'], [obk])
                S.dma('sp', MIXA[h][:, q0:q0 + 512], ob[:], [obk], ())
                cxs[i] = None

            c_pro(0)
            c_qk(0, 0)
            for i in range(len(items)):
                if i + 1 < len(items):
                    c_pro(i + 1)
                c_main(i)
                if i + 1 < len(items):
                    c_qk(i + 1, 0)
                c_epi(i)
            S.emit()
        if stop_after == 4:
            return nc

        with ExitStack() as es:
            def TT(name, shape, dt):
                return es.enter_context(nc.sbuf_tensor(name, shape, dt))

            def PP(name, shape, dt):
                return es.enter_context(nc.psum_tensor(name, shape, dt))

            Wout = TT("Wout", [128, 8, D], BF16)
            mD = TT("mD", [128, 3, D], F32)
            gon = TT("gon", [128, 128], F32)
            woutv = w_out.rearrange("(k p) n -> p k n", p=128)
            for k in range(8):
                S.dma('pool', Wout[:, k, :], woutv[:, k, :], (), ['Wout'])
            for i, mi in enumerate((4, 5, 6)):
                S.dma('sp', mD[:, i, :], MOD[mi], (), ['mD'])
            S.dma('sp', gon[:], g_on.partition_broadcast(128), (), ['gon'])
            ofr = Rot(TT, "dof", [128, 512], F32, 3); obr = Rot(TT, "dob", [128, 512], F32, 3); hgr = Rot(TT, "dhg", [128, 512], F32, 3)
            mar = Rot(TT, "dma_", [128, 4, 128], BF16, 4); xtr = Rot(TT, "dxt", [128, D], F32, 5)
            osumr = Rot(TT, "osum", [128, 512], F32, 2); osqr = Rot(TT, "osq", [128, 512], F32, 2); ss4r = Rot(TT, "ss4", [128, 4], F32, 3)
            tBr = Rot(TT, "tB", [128, 512], F32, 2); hgbr = Rot(TT, "hgb", [128, 512], BF16, 3); mixBr = Rot(TT, "mixB", [128, 4, 128], BF16, 2)
            tmpDr = Rot(TT, "tmpD", [128, D], F32, 2); x1r = Rot(TT, "dx1", [128, D], F32, 3)
            junkD = TT("junkD", [128, D], BF16); ssDr = Rot(TT, "ssD", [128, 1], F32, 3)
            t2Dr = Rot(TT, "t2D", [128, D], F32, 2); h2r = Rot(TT, "h2", [128, D], BF16, 3); h2Tr = Rot(TT, "dh2T", [128, 8, 128], BF16, 3)
            pTBr = Rot(PP, "pTB", [128, 4, 128], BF16, 2)
            pLOr = Rot(PP, "pLO", [128, 512], F32, 4)
            pT8r = Rot(PP, "pT8", [128, 8, 128], BF16, 2)

            def d1_s0(cx, ti):
                r0 = ti * 128
                for nm, rr, src in (('of', ofr, OO[0][r0:r0 + 128, :]), ('ob', obr, OO[1][r0:r0 + 128, :]),
                                    ('hg', hgr, HG[r0:r0 + 128, :]),
                                    ('ma', mar, MIXA[:, :, r0:r0 + 128].rearrange("h p t -> p h t")),
                                    ('xt', xtr, x[r0:r0 + 128, :])):
                    cx[nm], cx[nm + 'k'] = rr.next()
                    S.dma('sp', cx[nm][:], src, (), [cx[nm + 'k']])

            def d1_s1(cx, ti):
                of, ofk, ob, obk, hg, hgk = cx['of'], cx['ofk'], cx['ob'], cx['obk'], cx['hg'], cx['hgk']
                osum, osumk = osumr.next(); osq, osqk = osqr.next(); ss4, ss4k = ss4r.next(); tB, tBk = tBr.next()
                hgb, hgbk = hgbr.next()
                cx['hgb'], cx['hgbk'] = hgb, hgbk
                S.tt('pool', osum[:], of[:], ob[:], ALU.add, [ofk, obk], [osumk])
                S.tt('pool', osq[:], osum[:], osum[:], ALU.mult, [osumk], [osqk])
                S.op('dve', (lambda o_, i_: (lambda e: e.reduce_sum(out=o_[:], in_=i_[:].rearrange("p (h d) -> p h d", h=4),
                                                                    axis=AX.X)))(ss4, osq), [osqk], [ss4k])
                S.act(ss4[:], ss4[:], AF.Sqrt, [ss4k], [ss4k], bias=EPS, scale=1.0 / 128)
                S.op('dve', (lambda o_: (lambda e: e.reciprocal(out=o_[:], in_=o_[:])))(ss4), [ss4k], [ss4k])
                o3 = osum[:].rearrange("p (h d) -> p h d", h=4)
                t3 = tB[:].rearrange("p (h d) -> p h d", h=4)
                S.tt('dve', t3, o3, ss4[:].unsqueeze(2).to_broadcast([128, 4, 128]), ALU.mult, [osumk, ss4k], [tBk])
                S.tt('pool', t3, t3, gon[:].unsqueeze(1).to_broadcast([128, 4, 128]), ALU.mult, [tBk, 'gon'], [tBk])
                S.tt('pool', hgb[:], tB[:], hg[:], ALU.mult, [tBk, hgk], [hgbk])

            def d1_s2(cx, ti):
                hgb, hgbk, ma, mak = cx['hgb'], cx['hgbk'], cx['ma'], cx['mak']
                pTB, pTBk = pTBr.next(); mixB, mixBk = mixBr.next()
                for h in range(4):
                    S.tr(pTB[:, h, :], hgb[:, h * 128:(h + 1) * 128], identb[:], [hgbk, 'identb'], [pTBk])
                S.copy('act', mixB[:], pTB[:], [pTBk], [mixBk])
                cx['pLO'] = []
                for hf in range(2):
                    pl, plk = pLOr.next()
                    cx['pLO'].append((pl, plk))
                    for k in range(4):
                        S.mm(pl[:], ma[:, k, :], Wout[:, k, hf * 512:(hf + 1) * 512], k == 0, False, [mak, 'Wout'], [plk])
                    for k in range(4):
                        S.mm(pl[:], mixB[:, k, :], Wout[:, 4 + k, hf * 512:(hf + 1) * 512], False, k == 3,
                             [mixBk, 'Wout'], [plk])

            def d1_s3(cx, ti):
                r0 = ti * 128
                xt_, xtk = cx['xt'], cx['xtk']
                tmpD, tmpDk = tmpDr.next(); ssD, ssDk = ssDr.next(); t2D, t2Dk = t2Dr.next(); h2, h2k = h2r.next()
                x1, x1k = x1r.next()
                cx['h2'], cx['h2k'] = h2, h2k
                for hf in range(2):
                    pl, plk = cx['pLO'][hf]
                    S.tt('dve', tmpD[:, hf * 512:(hf + 1) * 512], pl[:], mD[:, 0, hf * 512:(hf + 1) * 512], ALU.mult,
                         [plk, 'mD'], [tmpDk])
                S.tt('pool', x1[:], tmpD[:], xt_[:], ALU.add, [tmpDk, xtk], [x1k])
                S.dma('sp', X1[r0:r0 + 128, :], x1[:], [x1k], ())
                S.memset('dve', ssD[:], 0.0, [ssDk])
                S.act(junkD[:], x1[:], AF.Square, [x1k], ['junkD', ssDk], accum=ssD[:])
                S.act(ssD[:], ssD[:], AF.Sqrt, [ssDk], [ssDk], bias=EPS, scale=1.0 / D)
                S.op('dve', (lambda o_: (lambda e: e.reciprocal(out=o_[:], in_=o_[:])))(ssD), [ssDk], [ssDk])
                S.stt('dve', t2D[:], x1[:], ssD[:, 0:1], mD[:, 1, :], ALU.mult, ALU.mult, [x1k, ssDk, 'mD'], [t2Dk])
                S.tt('pool', h2[:], t2D[:], mD[:, 2, :], ALU.add, [t2Dk, 'mD'], [h2k])

            def d1_s4(cx, ti):
                r0 = ti * 128
                h2, h2k = cx['h2'], cx['h2k']
                pT8, pT8k = pT8r.next(); hT2, hT2k = h2Tr.next()
                for k in range(8):
                    S.tr(pT8[:, k, :], h2[:, k * 128:(k + 1) * 128], identb[:], [h2k, 'identb'], [pT8k])
                S.copy('dve' if ti % 2 else 'act', hT2[:], pT8[:], [pT8k], [hT2k])
                S.dma('sp', H2T[:, :, r0:r0 + 128], hT2[:], [hT2k], ())
            run_pipeline(N // 128, [d1_s0, d1_s1, d1_s2, d1_s3, d1_s4])
            S.emit()
        if stop_after == 5:
            return nc

        with ExitStack() as es:
            def TT(name, shape, dt):
                return es.enter_context(nc.sbuf_tensor(name, shape, dt))

            def PP(name, shape, dt):
                return es.enter_context(nc.psum_tensor(name, shape, dt))

            Wg = TT("Wg", [128, 8, DFF], BF16); Wu = TT("Wu", [128, 8, DFF], BF16); Wd = TT("Wd", [128, NCF, D], BF16)
            mE = TT("mE", [128, 2, D], F32)
            wgv = w_gate.rearrange("(k p) n -> p k n", p=128); wuv = w_up.rearrange("(k p) n -> p k n", p=128)
            wdv = w_down.rearrange("(c p) n -> p c n", p=128)
            for k in range(8):
                S.dma('pool', Wg[:, k, :], wgv[:, k, :], (), ['Wg'])
                S.dma('pool', Wu[:, k, :], wuv[:, k, :], (), ['Wu'])
            for c_ in range(NCF):
                S.dma('pool', Wd[:, c_, :], wdv[:, c_, :], (), ['Wd'])
            S.dma('sp', mE[:, 0, :], MOD[7], (), ['mE'])
            S.dma('sp', mE[:, 1, :], g_fin.partition_broadcast(128), (), ['mE'])
            GN = 512
            h2g = Rot(TT, "eh2", [128, 8, GN], BF16, 1)
            aT = TT("aT", [128, NCF, GN], BF16)
            sgr = Rot(TT, "esg", [128, GN], F32, 2)
            x1r = Rot(TT, "ex1", [128, D], F32, 1); tmr = Rot(TT, "etm", [128, D], F32, 2)
            ssE = TT("ssE", [128, 1], F32)
            pG = [PP(f"pG{i}", [128, GN], F32) for i in range(2)]
            pU = [PP(f"pU{i}", [128, GN], F32) for i in range(2)]
            pY = [PP(f"pY{i}", [128, 512], F32) for i in range(2)]
            for g_ in range(N // GN):
                q0 = g_ * GN
                hh, hhk = h2g.next()
                S.dma('sp', hh[:], H2T[:, :, q0:q0 + GN], (), [hhk])
                for c_ in range(NCF):
                    pg, pgk = pG[c_ % 2], f'pG{c_ % 2}'
                    pu, puk = pU[c_ % 2], f'pU{c_ % 2}'
                    for k in range(8):
                        S.mm(pg[:], Wg[:, k, c_ * 128:(c_ + 1) * 128], hh[:, k, :], k == 0, k == 7, ['Wg', hhk], [pgk])
                    for k in range(8):
                        S.mm(pu[:], Wu[:, k, c_ * 128:(c_ + 1) * 128], hh[:, k, :], k == 0, k == 7, ['Wu', hhk], [puk])
                    sg, sgk = sgr.next()
                    S.act(sg[:], pg[:], AF.Silu, [pgk], [sgk])
                    S.tt('dve', aT[:, c_, :], sg[:], pu[:], ALU.mult, [sgk, puk], ['aT'])
                for sb in range(GN // 128):
                    r0 = q0 + sb * 128
                    x1, x1k = x1r.next(); tm, tmk = tmr.next()
                    S.dma('sp', x1[:], X1[r0:r0 + 128, :], (), [x1k])
                    for hf in range(2):
                        for c_ in range(NCF):
                            S.mm(pY[hf][:], aT[:, c_, sb * 128:(sb + 1) * 128], Wd[:, c_, hf * 512:(hf + 1) * 512],
                                 c_ == 0, c_ == NCF - 1, ['aT', 'Wd'], [f'pY{hf}'])
                        S.tt('dve', tm[:, hf * 512:(hf + 1) * 512], pY[hf][:], mE[:, 0, hf * 512:(hf + 1) * 512], ALU.mult,
                             [f'pY{hf}', 'mE'], [tmk])
                    S.tt('pool', x1[:], tm[:], x1[:], ALU.add, [tmk, x1k], [x1k])
                    S.memset('dve', ssE[:], 0.0, ['ssE'])
                    S.act(tm[:], x1[:], AF.Square, [x1k, tmk], [tmk, 'ssE'], accum=ssE[:])
                    S.act(ssE[:], ssE[:], AF.Sqrt, ['ssE'], ['ssE'], bias=EPS, scale=1.0 / D)
                    S.op('dve', lambda e: e.reciprocal(out=ssE[:], in_=ssE[:]), ['ssE'], ['ssE'])
                    S.stt('dve', tm[:], x1[:], ssE[:, 0:1], mE[:, 1, :], ALU.mult, ALU.mult, [x1k, 'ssE', 'mE'], [tmk])
                    S.dma('sp', out[r0:r0 + 128, :], tm[:], [tmk], ())
            S.emit()
    return nc


_NC_CACHE = {}


def kernel(**inputs):
    if 'nc' not in _NC_CACHE:
        _NC_CACHE['nc'] = build_nc()
    nc = _NC_CACHE['nc']
    f = lambda a: np.ascontiguousarray(np.asarray(a, dtype=np.float32))
    shared = {
        "c_ctx": f(inputs["c_ctx"]), "w_mod": f(inputs["w_mod"][0]), "b_mod": f(inputs["b_mod"][0]),
        "g_norm_mix": f(inputs["g_norm_mix"][0]), "g_norm_ffn": f(inputs["g_norm_ffn"][0]),
        "w_in": f(inputs["w_in"][0]), "g_q_norm": f(inputs["g_q_norm"][0]), "w_uq": f(inputs["w_uq"][0]),
        "g_kv_norm": f(inputs["g_kv_norm"][0]), "w_ukv": f(inputs["w_ukv"][0]),
        "lb_fwd": f(inputs["lb_fwd"]), "lb_bwd": f(inputs["lb_bwd"]), "g_hgrn_norm": f(inputs["g_hgrn_norm"][0]),
        "w_out": f(inputs["w_out"][0]), "w_gate": f(inputs["w_gate"][0]), "w_up": f(inputs["w_up"][0]),
        "w_down": f(inputs["w_down"][0]), "g_final": f(inputs["g_final"]),
    }
    xs, cs, ctxs = f(inputs["x"]), f(inputs["c"]), f(inputs["ctx"])
    in_maps = []
    for b in range(NB):
        m = dict(shared)
        m["x"] = xs[b]; m["c"] = cs[b]; m["ctx"] = ctxs[b]
        in_maps.append(m)
    res = run_bass_kernel_spmd(nc, in_maps, core_ids=list(range(NB)))
    return np.stack([np.asarray(r["out"], dtype=np.float32) for r in res.results], axis=0)
```

```python
import math
from contextlib import ExitStack

import numpy as np
import concourse.bass as bass
import concourse.mybir as mybir
from concourse.bass_utils import run_bass_kernel_spmd

F32 = mybir.dt.float32
BF16 = mybir.dt.bfloat16
AF = mybir.ActivationFunctionType
ALU = mybir.AluOpType
AX = mybir.AxisListType

NB, N, L, D = 8, 4096, 256, 1024
T = N + L
NT = T // 128
DFF = 2816
NCF = DFF // 128
EPS = 1e-6
INC = 3136
SCALE = 1.0 / math.sqrt(192.0)


class Sched:
    ENG = ('pe', 'act', 'dve', 'pool', 'sp')

    def __init__(self, nc, n_dma_sems=24):
        self.nc = nc
        self.ops = {e: [] for e in self.ENG}
        self.sems = {}
        for e in ('pe', 'act', 'dve', 'pool'):
            self.sems[('e', e)] = nc.alloc_semaphore(name=f"s_{e}")
        for i in range(n_dma_sems):
            self.sems[('d', i)] = nc.alloc_semaphore(name=f"s_dma{i}")
        self.nw = 6
        for i in range(self.nw):
            self.sems[('w', i)] = nc.alloc_semaphore(name=f"s_swdma{i}")
        self.wnext = 0
        self.cnt = {k: 0 for k in self.sems}
        self.nd = n_dma_sems
        self.dnext = 0
        self.waited = {e: {} for e in self.ENG}
        self.res = {}
        self.enabled = True
        self.load_q = None

    def _deps(self, reads, writes):
        t = []
        for r in reads:
            st = self.res.get(r)
            if st and st['w']:
                t.append(st['w'])
        for w in writes:
            st = self.res.get(w)
            if st:
                if st['w']:
                    t.append(st['w'])
                t.extend(st['r'].values())
        return t

    def _need(self, eng, tickets):
        best = {}
        for key, val in tickets:
            if key == ('e', 'pe') and eng == 'pe':
                continue
            if self.waited[eng].get(key, 0) >= val:
                continue
            if best.get(key, 0) < val:
                best[key] = val
        for key, val in best.items():
            self.waited[eng][key] = val
        return list(best.items())

    def _commit(self, ticket, reads, writes):
        for r in reads:
            st = self.res.setdefault(r, {'w': None, 'r': {}})
            st['r'][ticket[0]] = ticket
        for w in writes:
            self.res[w] = {'w': ticket, 'r': {}}

    def op(self, eng, fn, reads=(), writes=()):
        if not self.enabled:
            return None
        waits = self._need(eng, self._deps(reads, writes))
        key = ('e', eng)
        self.cnt[key] += 1
        ticket = (key, self.cnt[key])
        self.ops[eng].append((waits, fn, key, 1))
        self._commit(ticket, reads, writes)
        return ticket

    def dma(self, q, out, in_, reads=(), writes=()):
        if not self.enabled:
            return None
        if q == 'sp' and not reads and self.load_q:
            q = self.load_q
        if q == 'pool':
            key = ('w', self.wnext)
            self.wnext = (self.wnext + 1) % self.nw
        else:
            key = ('d', self.dnext)
            self.dnext = (self.dnext + 1) % self.nd
        tickets = self._deps(reads, writes)
        if self.cnt[key] > 0:
            tickets.append((key, self.cnt[key]))
        waits = self._need(q, tickets)
        self.cnt[key] += 16
        ticket = (key, self.cnt[key])
        self.ops[q].append((waits, lambda e: e.dma_start(out=out, in_=in_), key, 16))
        self._commit(ticket, reads, writes)
        return ticket

    def barrier(self):
        allt = [(k, v) for k, v in self.cnt.items() if v > 0]
        for e in self.ENG:
            waits = self._need(e, allt)
            if waits:
                self.ops[e].append((waits, None, None, 0))

    def emit(self):
        self.barrier()
        with self.nc.Block() as block:
            def mk(engname):
                def body(e):
                    for waits, fn, semkey, inc in self.ops[engname]:
                        for key, val in waits:
                            e.wait_ge(self.sems[key], val)
                        if fn is not None:
                            fn(e).then_inc(self.sems[semkey], inc)
                return body
            block.tensor(mk('pe'))
            block.scalar(mk('act'))
            block.vector(mk('dve'))
            block.gpsimd(mk('pool'))
            block.sync(mk('sp'))
        self.ops = {e: [] for e in self.ENG}

    def mm(self, out, lhsT, rhs, start, stop, r, w):
        return self.op('pe', lambda e: e.matmul(out, lhsT=lhsT, rhs=rhs, start=start, stop=stop), r, w)

    def tr(self, out, in_, ident, r, w):
        return self.op('pe', lambda e: e.transpose(out, in_, ident), r, w)

    def act(self, out, in_, func, r, w, bias=None, scale=None, accum=None):
        kw = {}
        if bias is not None:
            kw['bias'] = bias
        if scale is not None:
            kw['scale'] = scale
        if accum is not None:
            kw['accum_out'] = accum
        return self.op('act', lambda e: e.activation(out=out, in_=in_, func=func, **kw), r, w)

    def tt(self, eng, out, in0, in1, op, r, w):
        return self.op(eng, lambda e: e.tensor_tensor(out=out, in0=in0, in1=in1, op=op), r, w)

    def ts(self, eng, out, in0, s1, s2, op0, op1, r, w):
        if s2 is None:
            return self.op(eng, lambda e: e.tensor_scalar(out=out, in0=in0, scalar1=s1, scalar2=None, op0=op0), r, w)
        return self.op(eng, lambda e: e.tensor_scalar(out=out, in0=in0, scalar1=s1, scalar2=s2, op0=op0, op1=op1), r, w)

    def stt(self, eng, out, in0, scalar, in1, op0, op1, r, w):
        return self.op(eng, lambda e: e.scalar_tensor_tensor(out=out, in0=in0, scalar=scalar, in1=in1, op0=op0, op1=op1), r, w)

    def copy(self, eng, out, in_, r, w):
        if eng == 'act':
            return self.act(out, in_, AF.Copy, r, w)
        return self.op(eng, lambda e: e.tensor_copy(out=out, in_=in_), r, w)

    def memset(self, eng, ap, val, w):
        return self.op(eng, lambda e: e.memset(ap, val), (), w)


def run_pipeline(n, stages):
    ctxs = [dict() for _ in range(n)]
    K = len(stages)
    for t in range(n + K - 1):
        for k in range(K - 1, -1, -1):
            i = t - k
            if 0 <= i < n:
                stages[k](ctxs[i], i)


class Rot:
    def __init__(self, alloc, name, shape, dt, n):
        self.t = [alloc(f"{name}{i}", shape, dt) for i in range(n)]
        self.k = [f"{name}{i}" for i in range(n)]
        self.i = 0

    def next(self):
        j = self.i % len(self.t)
        self.i += 1
        return self.t[j], self.k[j]


def _rstd(S, ss, out, dim, tag):
    S.act(out, ss, AF.Sqrt, [tag + 'ss'], [tag + 'rs'], bias=EPS, scale=1.0 / dim)
    S.op('dve', lambda e: e.reciprocal(out=out, in_=out), [tag + 'rs'], [tag + 'rs'])


def build_nc(stop_after=None, debug=False, a2_groups=None, a2_parts='abcde', STQ='sp'):
    nc = bass.Bass("TRN2", target_bir_lowering=False)
    S = Sched(nc)

    def din(name, shape):
        return nc.dram_tensor(name, shape, F32, kind="ExternalInput").ap()

    x = din("x", [N, D]); c = din("c", [D]); ctx = din("ctx", [L, D]); c_ctx = din("c_ctx", [D])
    w_mod = din("w_mod", [D, 6 * D]); b_mod = din("b_mod", [6 * D])
    g_mix = din("g_norm_mix", [D]); g_ffn = din("g_norm_ffn", [D])
    w_in = din("w_in", [D, INC]); g_qn = din("g_q_norm", [256]); w_uq = din("w_uq", [256, 768])
    g_kvn = din("g_kv_norm", [256]); w_ukv = din("w_ukv", [256, 1024])
    lb_f = din("lb_fwd", [2, 512]); lb_b = din("lb_bwd", [2, 512]); g_on = din("g_hgrn_norm", [128])
    w_out = din("w_out", [D, D]); w_gate = din("w_gate", [D, DFF]); w_up = din("w_up", [D, DFF])
    w_down = din("w_down", [DFF, D]); g_fin = din("g_final", [D])
    out = nc.dram_tensor("out", [N, D], F32, kind="ExternalOutput").ap()

    dbg = set(debug) if debug else set()

    def scr(name, shape, dt):
        return nc.dram_tensor(name, shape, dt, kind="ExternalOutput" if name in dbg else "Internal").ap()

    MOD = scr("s_mod", [8, 128, D], F32)
    HT = scr("s_ht", [128, 8, T], BF16)
    QN = scr("s_qn", [4, 128, N], BF16); QR = scr("s_qr", [4, 64, N], BF16)
    KN = scr("s_kn", [4, 128, T], BF16); KR = scr("s_kr", [64, T], BF16)
    VV = scr("s_v", [T, 512], BF16)
    HQ = scr("s_hq", [4, 128, T], F32); HV = scr("s_hv", [T, 512], BF16); HG = scr("s_hg", [N, 512], F32)
    GG = scr("s_g", [2, T, 512], F32); KG = scr("s_kg", [2, T, 512], F32)
    OO = scr("s_o", [2, N, 512], F32)
    MIXA = scr("s_mixa", [4, 128, N], BF16)
    X1 = scr("s_x1", [N, D], F32); H2T = scr("s_h2t", [128, 8, N], BF16)

    outer = ExitStack()
    with outer:
        def CT(name, shape, dt):
            return outer.enter_context(nc.sbuf_tensor(name, shape, dt))
        identb = CT("identb", [128, 128], BF16); identf = CT("identf", [128, 128], F32)
        onesb = CT("onesb", [128, 128], BF16); onesf = CT("onesf", [128, 128], F32)
        Uf = CT("Uf", [128, 128], F32); Ub = CT("Ub", [128, 128], F32)
        Rf = CT("Rf", [128, 128], F32); Rb = CT("Rb", [128, 128], F32)
        Mf = CT("Mf", [128, 128], F32); Mb = CT("Mb", [128, 128], F32)
        Ind = CT("Ind", [128, 2], F32)

        def sel(tile, pattern, cm, cmp_op, key):
            S.op('pool', lambda e: e.affine_select(out=tile[:], in_=tile[:], pattern=pattern, compare_op=cmp_op,
                                                   fill=0.0, base=0, channel_multiplier=cm), [key], [key])
        for tl, key in ((identb, 'identb'), (identf, 'identf')):
            S.memset('pool', tl[:], 1.0, [key])
            sel(tl, [[-1, 128]], 1, ALU.is_equal, key)
        S.memset('pool', onesb[:], 1.0, ['onesb'])
        S.memset('pool', onesf[:], 1.0, ['onesf'])
        S.memset('pool', Uf[:], 1.0, ['Uf']); sel(Uf, [[-1, 128]], 1, ALU.is_gt, 'Uf')
        S.memset('pool', Uf[64:128, 0:64], 0.0, ['Uf'])
        S.memset('pool', Mb[:], 1.0, ['Mb']); sel(Mb, [[-1, 128]], 1, ALU.is_ge, 'Mb')
        S.memset('pool', Mb[64:128, 0:64], 0.0, ['Mb'])
        S.memset('pool', Ub[:], 1.0, ['Ub']); sel(Ub, [[1, 128]], -1, ALU.is_gt, 'Ub')
        S.memset('pool', Ub[0:64, 64:128], 0.0, ['Ub'])
        S.memset('pool', Mf[:], 1.0, ['Mf']); sel(Mf, [[1, 128]], -1, ALU.is_ge, 'Mf')
        S.memset('pool', Mf[0:64, 64:128], 0.0, ['Mf'])
        S.ts('pool', Rf[:], Uf[:], -1.0, None, ALU.mult, None, ['Uf'], ['Rf'])
        S.ts('pool', Rb[:], Ub[:], -1.0, None, ALU.mult, None, ['Ub'], ['Rb'])
        S.memset('pool', Ind[:], 0.0, ['Ind'])
        S.memset('pool', Ind[0:64, 0:1], 1.0, ['Ind'])
        S.memset('pool', Ind[64:128, 1:2], 1.0, ['Ind'])

        wstack = ExitStack()
        Win = wstack.enter_context(nc.sbuf_tensor("Win", [128, 8, INC], BF16))
        Wkpe = wstack.enter_context(nc.sbuf_tensor("Wkpe", [128, 8, 128], BF16))
        winv = w_in.rearrange("(k p) n -> p k n", p=128)
        S.memset('dve', Wkpe[:, :, 64:128], 0.0, ['Wkpe'])
        for k in range(8):
            S.dma('pool', Win[:, k, :], winv[:, k, :], (), ['Win'])
            for f_ in range(2):
                S.dma('pool', Wkpe[:, k, f_ * 32:(f_ + 1) * 32].rearrange("p (a i) -> p a i", a=2),
                      winv[:, k, 512:576].rearrange("p (a f i) -> p f a i", a=2, f=2)[:, f_, :, :], (), ['Wkpe'])

        with ExitStack() as es:
            def TT(name, shape, dt):
                return es.enter_context(nc.sbuf_tensor(name, shape, dt))

            def PP(name, shape, dt):
                return es.enter_context(nc.psum_tensor(name, shape, dt))
            crow = TT("crow", [128, 2, D], F32)
            cb = TT("cb", [128, 2, 8, 128], F32)
            bmod = TT("bmod", [128, 6 * D], F32)
            wm = [TT(f"wm{i}", [128, 8, 512], F32) for i in range(2)]
            modl = TT("modl", [128, 6 * D], F32); modc = TT("modc", [128, 2 * D], F32)
            gm = TT("gm", [128, D], F32); gf = TT("gf", [128, D], F32)
            tmpA = [TT(f"tmpA{i}", [128, D], F32) for i in range(3)]
            pcb = PP("pcb", [128, 128], F32)
            pm = [PP(f"pm{i}", [128, 512], F32) for i in range(2)]
            S.dma('sp', crow[:, 0, :], c.partition_broadcast(128), (), ['crow'])
            S.dma('sp', crow[:, 1, :], c_ctx.partition_broadcast(128), (), ['crow'])
            S.dma('sp', bmod[:], b_mod.partition_broadcast(128), (), ['bmod'])
            S.dma('sp', gm[:], g_mix.partition_broadcast(128), (), ['gm'])
            S.dma('sp', gf[:], g_ffn.partition_broadcast(128), (), ['gf'])
            S.act(crow[:], crow[:], AF.Silu, ['crow'], ['crow'])
            for w_ in range(2):
                for k in range(8):
                    S.mm(pcb[:], crow[:, w_, k * 128:(k + 1) * 128], identf[:], True, True, ['crow', 'identf'], ['pcb'])
                    S.copy('dve', cb[:, w_, k, :], pcb[:], ['pcb'], ['cb'])
            wmv = w_mod.rearrange("(k p) n -> p k n", p=128)
            for j in range(12):
                wt = wm[j % 2]; wk = f"wm{j % 2}"
                S.dma('sp' if j % 2 == 0 else 'act', wt[:], wmv[:, :, j * 512:(j + 1) * 512], (), [wk])
                for k in range(8):
                    S.mm(pm[0][:], cb[:, 0, k, :], wt[:, k, :], k == 0, k == 7, ['cb', wk], ['pm0'])
                S.tt('dve', modl[:, j * 512:(j + 1) * 512], pm[0][:], bmod[:, j * 512:(j + 1) * 512], ALU.add,
                     ['pm0', 'bmod'], ['modl'])
                if j < 4:
                    for k in range(8):
                        S.mm(pm[1][:], cb[:, 1, k, :], wt[:, k, :], k == 0, k == 7, ['cb', wk], ['pm1'])
                    S.tt('dve', modc[:, j * 512:(j + 1) * 512], pm[1][:], bmod[:, j * 512:(j + 1) * 512], ALU.add,
                         ['pm1', 'bmod'], ['modc'])
            S.stt('dve', tmpA[0][:], modl[:, D:2 * D], 1.0, gm[:], ALU.add, ALU.mult, ['modl', 'gm'], ['tA0'])
            S.stt('dve', tmpA[1][:], modc[:, D:2 * D], 1.0, gm[:], ALU.add, ALU.mult, ['modc', 'gm'], ['tA1'])
            S.stt('dve', tmpA[2][:], modl[:, 4 * D:5 * D], 1.0, gf[:], ALU.add, ALU.mult, ['modl', 'gf'], ['tA2'])
            S.dma('sp', MOD[0], tmpA[0][:], ['tA0'], ())
            S.dma('sp', MOD[1], modl[:, 0:D], ['modl'], ())
            S.dma('sp', MOD[2], tmpA[1][:], ['tA1'], ())
            S.dma('sp', MOD[3], modc[:, 0:D], ['modc'], ())
            S.dma('sp', MOD[4], modl[:, 2 * D:3 * D], ['modl'], ())
            S.dma('sp', MOD[5], tmpA[2][:], ['tA2'], ())
            S.dma('sp', MOD[6], modl[:, 3 * D:4 * D], ['modl'], ())
            S.dma('sp', MOD[7], modl[:, 5 * D:6 * D], ['modl'], ())
            S.emit()
        S.load_q = 'act'
        if stop_after == 0:
            return nc

        with ExitStack() as es:
            def TT(name, shape, dt):
                return es.enter_context(nc.sbuf_tensor(name, shape, dt))

            def PP(name, shape, dt):
                return es.enter_context(nc.psum_tensor(name, shape, dt))
            mA = TT("mA", [128, 4, D], F32)
            junk = TT("junk", [128, D], BF16)
            xtR = Rot(TT, "xt", [128, D], F32, 4); ssR = Rot(TT, "ssa", [128, 1], F32, 4)
            t1R = Rot(TT, "t1_", [128, D], F32, 2); hbR = Rot(TT, "hb", [128, D], BF16, 3)
            hTR = Rot(TT, "hT", [128, 8, 128], BF16, 3); ptrR = Rot(PP, "ptr", [128, 8, 128], BF16, 2)
            for i in range(4):
                S.dma('sp', mA[:, i, :], MOD[i], (), ['mA'])

            def a1_s0(cx, ti):
                cx['xt'], cx['xk'] = xtR.next()
                src = ctx[ti * 128:(ti + 1) * 128, :] if ti < 2 else x[(ti - 2) * 128:(ti - 1) * 128, :]
                S.dma('sp', cx['xt'][:], src, (), [cx['xk']])

            def a1_s1(cx, ti):
                xt_, xk = cx['xt'], cx['xk']
                ss, sk = ssR.next()
                cx['ss'], cx['sk'] = ss, sk
                S.memset('dve', ss[:], 0.0, [sk])
                S.act(junk[:], xt_[:], AF.Square, [xk], ['junk', sk], accum=ss[:])
                S.act(ss[:], ss[:], AF.Sqrt, [sk], [sk], bias=EPS, scale=1.0 / D)
                S.op('dve', (lambda o_: (lambda e: e.reciprocal(out=o_[:], in_=o_[:])))(ss), [sk], [sk])

            def a1_s1b(cx, ti):
                xt_, xk, ss, sk = cx['xt'], cx['xk'], cx['ss'], cx['sk']
                mi = 2 if ti < 2 else 0
                t1, t1k = t1R.next(); hb, hbk = hbR.next()
                cx['hb'], cx['hbk'] = hb, hbk
                S.stt('dve', t1[:], xt_[:], ss[:, 0:1], mA[:, mi, :], ALU.mult, ALU.mult, [xk, sk, 'mA'], [t1k])
                S.tt('pool', hb[:], t1[:], mA[:, mi + 1, :], ALU.add, [t1k, 'mA'], [hbk])

            def a1_s2(cx, ti):
                hb, hbk = cx['hb'], cx['hbk']
                pt, ptk = ptrR.next(); hT, hTk = hTR.next()
                for k in range(8):
                    S.tr(pt[:, k, :], hb[:, k * 128:(k + 1) * 128], identb[:], [hbk, 'identb'], [ptk])
                S.copy('dve' if ti % 2 else 'act', hT[:], pt[:], [ptk], [hTk])
                S.dma('sp', HT[:, :, ti * 128:(ti + 1) * 128], hT[:], [hTk], ())
            run_pipeline(NT, [a1_s0, a1_s1, a1_s1b, a1_s2])
            S.emit()
        if stop_after == 1:
            return nc

        with ExitStack() as es:
            def TT(name, shape, dt):
                return es.enter_context(nc.sbuf_tensor(name, shape, dt))

            def PP(name, shape, dt):
                return es.enter_context(nc.psum_tensor(name, shape, dt))
            Wkrot = TT("Wkrot", [128, 8, 128], BF16)
            Wqn = TT("Wqn", [128, 2, 4, 128], BF16); Wqr = TT("Wqr", [128, 2, 4, 128], BF16)
            Wqrot = TT("Wqrot", [128, 2, 4, 128], BF16)
            Wkn = TT("Wkn", [128, 2, 4, 128], BF16); Wv = TT("Wv", [128, 2, 4, 128], BF16)
            Ct = TT("Ct", [64, N], F32); St = TT("St", [64, N], F32)
            lbt = TT("lbt", [128, 2, 512], F32); oml = TT("oml", [128, 2, 512], F32)
            with ExitStack() as es2:
                def T2(name, shape, dt):
                    return es2.enter_context(nc.sbuf_tensor(name, shape, dt))
                S.memset('dve', Wkrot[:, :, 64:128], 0.0, ['Wkrot'])
                for tl_, k_ in ((Wqr, 'Wqr'), (Wqrot, 'Wqrot')):
                    S.memset('dve', tl_[:, :, :, 64:128], 0.0, [k_])
                S.ts('dve', Wkrot[:, :, 0:32], Wkpe[:, :, 32:64], -1.0, None, ALU.mult, None, ['Wkpe'], ['Wkrot'])
                S.copy('dve', Wkrot[:, :, 32:64], Wkpe[:, :, 0:32], ['Wkpe'], ['Wkrot'])
                stq = T2("stq", [128, 2, 768], F32); stkv = T2("stkv", [128, 2, 1024], F32)
                gq = T2("gq", [128, 2], F32); gkv = T2("gkv", [128, 2], F32)
                S.dma('sp', stq[:], w_uq.rearrange("(c p) n -> p c n", p=128), (), ['stq'])
                S.dma('sp', stkv[:], w_ukv.rearrange("(c p) n -> p c n", p=128), (), ['stkv'])
                for c_ in range(2):
                    S.dma('sp', gq[:, c_:c_ + 1], g_qn[c_ * 128:(c_ + 1) * 128].rearrange("(p o) -> p o", o=1), (), ['gq'])
                    S.dma('sp', gkv[:, c_:c_ + 1], g_kvn[c_ * 128:(c_ + 1) * 128].rearrange("(p o) -> p o", o=1), (), ['gkv'])
                for c_ in range(2):
                    sq_v = stq[:, c_, :].rearrange("p (h d) -> p h d", h=4)
                    S.ts('dve', Wqn[:, c_, :, :], sq_v[:, :, 0:128], gq[:, c_:c_ + 1], None, ALU.mult, None,
                         ['stq', 'gq'], ['Wqn'])
                    for f_ in range(2):
                        for a_ in range(2):
                            so = 128 + a_ * 32 + f_ * 16
                            do = f_ * 32 + a_ * 16
                            S.ts('dve', Wqr[:, c_, :, do:do + 16], sq_v[:, :, so:so + 16], gq[:, c_:c_ + 1], None,
                                 ALU.mult, None, ['stq', 'gq'], ['Wqr'])
                    S.ts('dve', Wqrot[:, c_, :, 0:32], Wqr[:, c_, :, 32:64], -1.0, None, ALU.mult, None, ['Wqr'], ['Wqrot'])
                    S.copy('dve', Wqrot[:, c_, :, 32:64], Wqr[:, c_, :, 0:32], ['Wqr'], ['Wqrot'])
                    skv_v = stkv[:, c_, :].rearrange("p (h t d) -> p h t d", h=4, t=2)
                    S.ts('dve', Wkn[:, c_, :, :], skv_v[:, :, 0, :], gkv[:, c_:c_ + 1], None, ALU.mult, None,
                         ['stkv', 'gkv'], ['Wkn'])
                    S.ts('dve', Wv[:, c_, :, :], skv_v[:, :, 1, :], gkv[:, c_:c_ + 1], None, ALU.mult, None,
                         ['stkv', 'gkv'], ['Wv'])
                lraw = T2("lraw", [128, 2, 2, 512], F32)
                for d_, lbx in enumerate((lb_f, lb_b)):
                    for r_ in range(2):
                        S.dma('sp', lraw[:, d_, r_, :], lbx[r_].partition_broadcast(128), (), ['lraw'])
                S.tt('dve', lbt[:], lraw[:, :, 0, :], lraw[:, :, 1, :], ALU.subtract, ['lraw'], ['lbt'])
                S.act(lbt[:], lbt[:], AF.Sigmoid, ['lbt'], ['lbt'])
                S.ts('dve', oml[:], lbt[:], -0.5, 0.5, ALU.mult, ALU.add, ['lbt'], ['oml'])
                S.tt('dve', lbt[:], lbt[:], oml[:], ALU.add, ['lbt', 'oml'], ['lbt'])
                pidx = T2("pidx", [64, 1], F32); i16 = T2("i16", [64, 1], F32); mrow = T2("mrow", [64, 1], F32)
                arow = T2("arow", [64, 1], F32); acol = T2("acol", [64, 1], F32)
                rowpos = T2("rowpos", [64, N], F32); colpos = T2("colpos", [64, N], F32); ang = T2("ang", [64, N], F32)
                S.op('pool', lambda e: e.iota(pidx[:], [[0, 1]], base=0, channel_multiplier=1,
                                              allow_small_or_imprecise_dtypes=True), (), ['pidx'])
                S.op('pool', lambda e: e.iota(rowpos[:], [[1, 64], [0, 64]], base=0, channel_multiplier=0,
                                              allow_small_or_imprecise_dtypes=True), (), ['rowpos'])
                S.op('pool', lambda e: e.iota(colpos[:], [[0, 64], [1, 64]], base=0, channel_multiplier=0,
                                              allow_small_or_imprecise_dtypes=True), (), ['colpos'])
                msk = T2("msk", [64, 3], F32)
                S.memset('pool', msk[:], 1.0, ['msk'])
                for j_ in range(3):
                    S.op('pool', (lambda jj: (lambda e: e.affine_select(
                        out=msk[:, jj:jj + 1], in_=msk[:, jj:jj + 1], pattern=[[0, 1]], compare_op=ALU.is_ge, fill=0.0,
                        base=-16 * (jj + 1), channel_multiplier=1)))(j_), ['msk'], ['msk'])
                S.tt('dve', mrow[:], msk[:, 0:1], msk[:, 1:2], ALU.add, ['msk'], ['mrow'])
                S.tt('dve', mrow[:], mrow[:], msk[:, 2:3], ALU.add, ['msk', 'mrow'], ['mrow'])
                S.stt('dve', i16[:], mrow[:], -16.0, pidx[:], ALU.mult, ALU.add, ['mrow', 'pidx'], ['i16'])
                S.tt('dve', mrow[:], msk[:, 1:2], msk[:, 0:1], ALU.subtract, ['msk'], ['mrow'])
                S.tt('dve', mrow[:], mrow[:], msk[:, 2:3], ALU.subtract, ['msk', 'mrow'], ['mrow'])
                S.ts('dve', mrow[:], mrow[:], 1.0, None, ALU.add, None, ['mrow'], ['mrow'])
                S.act(i16[:], i16[:], AF.Exp, ['i16'], ['i16'], scale=-math.log(10000.0) / 16.0)
                S.tt('dve', arow[:], i16[:], mrow[:], ALU.mult, ['i16', 'mrow'], ['arow'])
                S.tt('dve', acol[:], i16[:], arow[:], ALU.subtract, ['i16', 'arow'], ['acol'])
                S.ts('dve', ang[:], rowpos[:], arow[:, 0:1], None, ALU.mult, None, ['rowpos', 'arow'], ['ang'])
                S.stt('dve', ang[:], colpos[:], acol[:, 0:1], ang[:], ALU.mult, ALU.add, ['colpos', 'acol', 'ang'], ['ang'])
                sc_ = 1.0 - 1e-6
                ki = T2("ki", [64, N], mybir.dt.int32)
                for tab, shift in ((St, 0.0), (Ct, 0.5 * math.pi)):
                    S.ts('dve', rowpos[:], ang[:], shift, 1.0 / (2 * math.pi), ALU.add, ALU.mult, ['ang'], ['rowpos'])
                    S.copy('dve', ki[:], rowpos[:], ['rowpos'], ['ki'])
                    S.copy('dve', colpos[:], ki[:], ['ki'], ['colpos'])
                    S.ts('dve', rowpos[:], ang[:], shift, None, ALU.add, None, ['ang'], ['rowpos'])
                    S.stt('dve', rowpos[:], colpos[:], -2 * math.pi, rowpos[:], ALU.mult, ALU.add, ['colpos', 'rowpos'], ['rowpos'])
                    S.act(tab[:], rowpos[:], AF.Sin, ['rowpos'], ['St' if shift == 0.0 else 'Ct'], scale=sc_)
                if stop_after == 15:
                    for nm, tl, shp, dt_ in (("d_Ct", Ct, [64, N], F32), ("d_St", St, [64, N], F32),
                                             ("d_Wkpe", Wkpe, [128, 8, 128], BF16), ("d_Wkrot", Wkrot, [128, 8, 128], BF16),
                                             ("d_Wqn", Wqn, [128, 2, 4, 128], BF16), ("d_Wqr", Wqr, [128, 2, 4, 128], BF16),
                                             ("d_Wqrot", Wqrot, [128, 2, 4, 128], BF16), ("d_Wkn", Wkn, [128, 2, 4, 128], BF16),
                                             ("d_Wv", Wv, [128, 2, 4, 128], BF16), ("d_lbt", lbt, [128, 2, 512], F32),
                                             ("d_oml", oml, [128, 2, 512], F32), ("d_Win", Win, [128, 8, INC], BF16)):
                        dd = nc.dram_tensor(nm, shp, dt_, kind="ExternalOutput").ap()
                        S.dma('sp', dd, tl[:], [nm[2:]], ())
                S.emit()
                if stop_after == 15:
                    return nc

            hTg = Rot(TT, "hTg", [128, 8, 512], BF16, 2)
            cT = TT("cT", [128, 2, 512], BF16); sq = TT("sq", [128, 2, 512], BF16)
            rbc = TT("rbc", [128, 512], F32); rtk = TT("rtk", [128, 4], F32)
            o_bf = Rot(TT, "o_bf", [128, 512], BF16, 3)
            o_f = Rot(TT, "o_f", [128, 512], F32, 4)
            u_f = Rot(TT, "u_f", [64, 512], F32, 4)
            sgb = Rot(TT, "sgb", [128, 512], F32, 3); fb_ = Rot(TT, "fb_", [128, 512], F32, 3)
            pA = [PP(f"pA{i}", [128, 512], F32) for i in range(2)]
            pB = [PP(f"pB{i}", [128, 512], F32) for i in range(2)]
            pS = PP("pS", [128, 512], F32)
            pT = [PP(f"pT{i}", [128, 512], F32) for i in range(2)]
            pV = PP("pV", [128, 4, 128], F32)
            groups = [(0, 256)] + [(256 + i * 512, 512) for i in range(8)]
            if a2_groups is not None:
                groups = groups[:a2_groups]
            for (tok0, n) in groups:
                is_lat = tok0 >= L
                lo = tok0 - L
                nsub = n // 128
                S.enabled = True
                hT_, hk = hTg.next()
                S.dma('sp', hT_[:, :, :n], HT[:, :, tok0:tok0 + n], (), [hk])

                def fm_proj(ps, pk, wt, wk, col0, ncols):
                    for k in range(8):
                        S.mm(ps[0:ncols, :n], wt[:, k, col0:col0 + ncols], hT_[:, k, :n], k == 0, k == 7, [wk, hk], [pk])

                def lowrank(col0, want_tok):
                    for c_ in range(2):
                        fm_proj(pA[c_], f'pA{c_}', Win, 'Win', col0 + c_ * 128, 128)
                        S.copy('act', cT[:, c_, :n], pA[c_][:, :n], [f'pA{c_}'], ['cT'])
                        S.act(sq[:, c_, :n], pA[c_][:, :n], AF.Square, [f'pA{c_}'], ['sq'])
                    for c_ in range(2):
                        S.mm(pS[:, :n], onesb[:], sq[:, c_, :n], c_ == 0, c_ == 1, ['onesb', 'sq'], ['pS'])
                    S.act(rbc[:, :n], pS[:, :n], AF.Sqrt, ['pS'], ['rbc'], bias=EPS, scale=1.0 / 256)
                    S.op('dve', (lambda nn: (lambda e: e.reciprocal(out=rbc[:, :nn], in_=rbc[:, :nn])))(n), ['rbc'], ['rbc'])
                    if want_tok:
                        for s_ in range(nsub):
                            for c_ in range(2):
                                S.mm(pV[:, s_, :], sq[:, c_, s_ * 128:(s_ + 1) * 128], onesb[:], c_ == 0, c_ == 1,
                                     ['sq', 'onesb'], ['pV'])
                        S.act(rtk[:, :nsub], pV[:, :nsub, 0], AF.Sqrt, ['pV'], ['rtk'], bias=EPS, scale=1.0 / 256)
                        S.op('dve', (lambda ns: (lambda e: e.reciprocal(out=rtk[:, :ns], in_=rtk[:, :ns])))(nsub), ['rtk'], ['rtk'])

                def rope_out(p0, k0, p1, k1, dst, scale_rows):
                    u1, uk1 = u_f.next(); u2, uk2 = u_f.next()
                    S.tt('dve', u1[:, :n], p0[0:64, :n], Ct[:, lo:lo + n], ALU.mult, [k0, 'Ct'], [uk1])
                    S.tt('dve', u2[:, :n], p1[0:64, :n], St[:, lo:lo + n], ALU.mult, [k1, 'St'], [uk2])
                    ob, ok = o_bf.next()
                    if scale_rows:
                        S.tt('pool', u1[:, :n], u1[:, :n], u2[:, :n], ALU.add, [uk1, uk2], [uk1])
                        S.tt('pool', ob[0:64, :n], u1[:, :n], rbc[0:64, :n], ALU.mult, [uk1, 'rbc'], [ok])
                    else:
                        S.tt('pool', ob[0:64, :n], u1[:, :n], u2[:, :n], ALU.add, [uk1, uk2], [ok])
                    S.dma(STQ, dst, ob[0:64, :n], [ok], ())

                if is_lat and 'a' in a2_parts:
                    lowrank(0, False)
                    for h in range(4):
                        pb, pk = pB[h % 2], f'pB{h % 2}'
                        for c_ in range(2):
                            S.mm(pb[:, :n], Wqn[:, c_, h, :], cT[:, c_, :n], c_ == 0, c_ == 1, ['Wqn', 'cT'], [pk])
                        ob, ok = o_bf.next()
                        S.tt('dve', ob[:, :n], pb[:, :n], rbc[:, :n], ALU.mult, [pk, 'rbc'], [ok])
                        S.dma(STQ, QN[h][:, lo:lo + n], ob[:, :n], [ok], ())
                    for h in range(4):
                        for c_ in range(2):
                            S.mm(pB[0][:, :n], Wqr[:, c_, h, :], cT[:, c_, :n], c_ == 0, c_ == 1, ['Wqr', 'cT'], ['pB0'])
                        for c_ in range(2):
                            S.mm(pB[1][:, :n], Wqrot[:, c_, h, :], cT[:, c_, :n], c_ == 0, c_ == 1, ['Wqrot', 'cT'], ['pB1'])
                        rope_out(pB[0], 'pB0', pB[1], 'pB1', QR[h][:, lo:lo + n], True)
                S.enabled = 'b' in a2_parts
                lowrank(256, True)
                for h in range(4):
                    pb, pk = pB[h % 2], f'pB{h % 2}'
                    for c_ in range(2):
                        S.mm(pb[:, :n], Wkn[:, c_, h, :], cT[:, c_, :n], c_ == 0, c_ == 1, ['Wkn', 'cT'], [pk])
                    ob, ok = o_bf.next()
                    S.tt('dve', ob[:, :n], pb[:, :n], rbc[:, :n], ALU.mult, [pk, 'rbc'], [ok])
                    S.dma(STQ, KN[h][:, tok0:tok0 + n], ob[:, :n], [ok], ())
                Wv2 = Wv[:].rearrange("p c h d -> p c (h d)")
                for s_ in range(nsub):
                    pt, pk = pT[s_ % 2], f'pT{s_ % 2}'
                    for c_ in range(2):
                        S.mm(pt[:], cT[:, c_, s_ * 128:(s_ + 1) * 128], Wv2[:, c_, :], c_ == 0, c_ == 1, ['cT', 'Wv'], [pk])
                    ob, ok = o_bf.next()
                    S.act(ob[:], pt[:], AF.Copy, [pk, 'rtk'], [ok], scale=rtk[:, s_:s_ + 1])
                    S.dma(STQ, VV[tok0 + s_ * 128:tok0 + (s_ + 1) * 128, :], ob[:], [ok], ())
                S.enabled = 'c' in a2_parts
                fm_proj(pA[0], 'pA0', Wkpe, 'Wkpe', 0, 128)
                if is_lat:
                    fm_proj(pA[1], 'pA1', Wkrot, 'Wkrot', 0, 128)
                    rope_out(pA[0], 'pA0', pA[1], 'pA1', KR[:, tok0:tok0 + n], False)
                else:
                    ob, ok = o_bf.next()
                    S.copy('act', ob[0:64, :n], pA[0][0:64, :n], ['pA0'], [ok])
                    S.dma(STQ, KR[:, tok0:tok0 + n], ob[0:64, :n], [ok], ())
                S.enabled = 'd' in a2_parts
                for h in range(4):
                    pa, pk = pA[h % 2], f'pA{h % 2}'
                    fm_proj(pa, pk, Win, 'Win', 576 + h * 128, 128)
                    of, ok = o_f.next()
                    S.copy('act' if h % 2 else 'dve', of[:, :n], pa[:, :n], [pk], [ok])
                    S.dma(STQ, HQ[h][:, tok0:tok0 + n], of[:, :n], [ok], ())
                S.enabled = 'e' in a2_parts
                for s_ in range(nsub):
                    row0 = tok0 + s_ * 128

                    def tm_proj(ps, pk, col0):
                        for k in range(8):
                            S.mm(ps[:], hT_[:, k, s_ * 128:(s_ + 1) * 128], Win[:, k, col0:col0 + 512], k == 0, k == 7,
                                 [hk, 'Win'], [pk])
                    tm_proj(pT[0], 'pT0', 1088)
                    ob, ok = o_bf.next()
                    S.copy('dve', ob[:], pT[0][:], ['pT0'], [ok])
                    S.dma(STQ, HV[row0:row0 + 128, :], ob[:], [ok], ())
                    if is_lat:
                        tm_proj(pT[1], 'pT1', 1600)
                        th, thk = sgb.next(); uh, uhk = fb_.next()
                        S.act(th[:], pT[1][:], AF.Tanh, ['pT1'], [thk], scale=0.5)
                        S.act(uh[:], pT[1][:], AF.Copy, ['pT1'], [uhk], scale=0.5)
                        of, ok = o_f.next()
                        S.tt('pool', th[:], th[:], uh[:], ALU.mult, [thk, uhk], [thk])
                        S.tt('pool', of[:], th[:], uh[:], ALU.add, [thk, uhk], [ok])
                        S.dma(STQ, HG[row0 - L:row0 - L + 128, :], of[:], [ok], ())
                    fts = []
                    for d_ in range(2):
                        pt, pk = pT[d_], f'pT{d_}'
                        tm_proj(pt, pk, 2112 + d_ * 512)
                        sg, sk = sgb.next(); ff, fk = fb_.next()
                        S.act(sg[:], pt[:], AF.Tanh, [pk], [sk], scale=0.5)
                        S.tt('dve', ff[:], sg[:], oml[:, d_, :], ALU.mult, [sk, 'oml'], [fk])
                        S.tt('pool', ff[:], ff[:], lbt[:, d_, :], ALU.add, [fk, 'lbt'], [fk])
                        fts.append((ff, fk))
                    for d_ in range(2):
                        ff, fk = fts[d_]
                        of, ok = o_f.next()
                        S.act(of[:], ff[:], AF.Ln, [fk], [ok])
                        S.dma(STQ, GG[d_][row0:row0 + 128, :], of[:], [ok], ())
                        of2, ok2 = o_f.next()
                        S.ts('pool', of2[:], ff[:], -1.0, 1.0, ALU.mult, ALU.add, [fk], [ok2])
                        S.dma(STQ, KG[d_][row0:row0 + 128, :], of2[:], [ok2], ())
            S.enabled = True
            S.emit()
        wstack.close()
        if stop_after == 2:
            return nc

        with ExitStack() as es:
            def TT(name, shape, dt):
                return es.enter_context(nc.sbuf_tensor(name, shape, dt))

            def PP(name, shape, dt):
                return es.enter_context(nc.psum_tensor(name, shape, dt))

            bt = []
            for d_ in range(2):
                bt.append(dict(
                    g=Rot(TT, f"bg{d_}", [128, 512], F32, 2), kg=Rot(TT, f"bkg{d_}", [128, 512], F32, 2),
                    v=Rot(TT, f"bv{d_}", [128, 512], BF16, 3), hq=Rot(TT, f"bhq{d_}", [128, 4, 128], F32, 2),
                    Ek=Rot(TT, f"bEk{d_}", [128, 512], F32, 2), EqT=Rot(TT, f"bEq{d_}", [128, 4, 128], F32, 2),
                    eb=Rot(TT, f"beb{d_}", [128, 4, 2], F32, 3), K2=Rot(TT, f"bK2{d_}", [128, 512], BF16, 2),
                    K2m=[Rot(TT, f"bK2m{c_}{d_}", [128, 512], BF16, 2) for c_ in range(2)],
                    QsT=Rot(TT, f"bQs{d_}", [128, 4, 128], BF16, 2),
                    Qsm=[Rot(TT, f"bQsm{c_}{d_}", [128, 4, 128], BF16, 2) for c_ in range(2)],
                    K2T=Rot(TT, f"bK2T{d_}", [128, 4, 128], BF16, 2),
                    Am=Rot(TT, f"bAm{d_}", [128, 4, 128], BF16, 2),
                    S=[TT(f"bS{d_}_{j_}", [128, 4, 128], F32) for j_ in range(2)], cur=[0],
                    Sp=Rot(TT, f"bSp{d_}", [128, 4, 128], F32, 2), Spb=Rot(TT, f"bSpb{d_}", [128, 4, 128], BF16, 4),
                    osb=Rot(TT, f"bos{d_}", [128, 512], F32, 2)))
                S.memset('dve', bt[d_]['S'][0][:], 0.0, [f'bS{d_}_0h{h}' for h in range(4)])
                for c_ in range(2):
                    oth = slice(64, 128) if c_ == 0 else slice(0, 64)
                    for j_ in range(2):
                        S.memset('pool', bt[d_]['K2m'][c_].t[j_][oth, :], 0.0, [bt[d_]['K2m'][c_].k[j_]])
                        S.memset('pool', bt[d_]['Qsm'][c_].t[j_][:, :, oth], 0.0, [bt[d_]['Qsm'][c_].k[j_]])
            pD1 = PP("pD1", [128, 512], F32); pD2 = PP("pD2", [128, 4, 128], F32)
            pKT = PP("pKT", [128, 4, 128], BF16); pBL = PP("pBL", [128, 4, 2], F32)
            pAT = PP("pAT", [128, 4, 128], F32)
            pOb = PP("pOb", [128, 4, 128], F32)
            pSNr = Rot(PP, "pSN", [128, 4, 128], F32, 2)

            def hgrn_pre(cx, ti, d_):
                B_ = bt[d_]
                is_lat = ti >= 2
                row0 = ti * 128
                U_, R_, M_ = (Uf, Rf, Mf) if d_ == 0 else (Ub, Rb, Mb)
                uk, rk, mk_ = ('Uf', 'Rf', 'Mf') if d_ == 0 else ('Ub', 'Rb', 'Mb')
                g, gk = B_['g'].next(); kg, kgk = B_['kg'].next(); v, vk = B_['v'].next(); hq, hqk = B_['hq'].next()
                S.dma('sp', g[:], GG[d_][row0:row0 + 128, :], (), [gk])
                S.dma('sp', kg[:], KG[d_][row0:row0 + 128, :], (), [kgk])
                S.dma('sp', v[:], HV[row0:row0 + 128, :], (), [vk])
                S.dma('sp', hq[:], HQ[:, :, row0:row0 + 128].rearrange("h p t -> p h t"), (), [hqk])
                Ek, Ekk = B_['Ek'].next(); EqT, Eqk = B_['EqT'].next(); eb, ebk = B_['eb'].next()
                K2, K2k = B_['K2'].next(); QsT, Qsk = B_['QsT'].next(); K2T, K2Tk = B_['K2T'].next()
                S.mm(pD1[:], U_[:], g[:], True, True, [uk, gk], ['pD1'])
                for h in range(4):
                    S.mm(pD2[:, h, :], g[:, h * 128:(h + 1) * 128], R_[:], True, True, [gk, rk], ['pD2'])
                for h in range(4):
                    S.mm(pBL[:, h, :], g[:, h * 128:(h + 1) * 128], Ind[:], True, True, [gk, 'Ind'], ['pBL'])
                S.act(Ek[:], pD1[:], AF.Exp, ['pD1'], [Ekk])
                S.act(EqT[:], pD2[:], AF.Exp, ['pD2'], [Eqk])
                S.act(eb[:], pBL[:], AF.Exp, ['pBL'], [ebk])
                S.tt('dve', K2[:], kg[:], Ek[:], ALU.mult, [kgk, Ekk], [K2k])
                K2m = []
                for c_ in range(2):
                    rs_ = slice(c_ * 64, (c_ + 1) * 64)
                    km, kmk = B_['K2m'][c_].next()
                    S.copy('act', km[rs_, :], K2[rs_, :], [K2k], [kmk])
                    K2m.append((km, kmk))
                cx.update(v=v, vk=vk, eb=eb, ebk=ebk, K2m=K2m)
                if is_lat:
                    S.tt('pool', QsT[:], hq[:], EqT[:], ALU.mult, [hqk, Eqk], [Qsk])
                    Qsm = []
                    for c_ in range(2):
                        cs_ = slice(c_ * 64, (c_ + 1) * 64)
                        qm, qmk = B_['Qsm'][c_].next()
                        S.copy('pool', qm[:, :, cs_], QsT[:, :, cs_], [Qsk], [qmk])
                        Qsm.append((qm, qmk))
                    Am, Amk = B_['Am'].next()
                    for h in range(4):
                        S.tr(pKT[:, h, :], K2[:, h * 128:(h + 1) * 128], identb[:], [K2k, 'identb'], ['pKT'])
                    S.copy('act', K2T[:], pKT[:], ['pKT'], [K2Tk])
                    for h in range(4):
                        S.mm(pAT[:, h, :], K2T[:, h, :], QsT[:, h, :], True, True, [K2Tk, Qsk], ['pAT'])
                    S.tt('dve', Am[:], pAT[:], M_[:].unsqueeze(1).to_broadcast([128, 4, 128]), ALU.mult,
                         ['pAT', mk_], [Amk])
                    cx.update(Am=Am, Amk=Amk, Qsm=Qsm)

            def hgrn_chain(cx, ti, d_):
                B_ = bt[d_]
                is_lat = ti >= 2
                row0 = ti * 128
                v, vk, eb, ebk, K2m = (cx[k_] for k_ in ('v', 'vk', 'eb', 'ebk', 'K2m'))
                order = (0, 1) if d_ == 0 else (1, 0)
                spbs = {}
                for c_ in order:
                    ci = B_['cur'][0]
                    Sc, Sn = B_['S'][ci], B_['S'][1 - ci]
                    Sck = [f'bS{d_}_{ci}h{h}' for h in range(4)]
                    Snk = [f'bS{d_}_{1 - ci}h{h}' for h in range(4)]
                    B_['cur'][0] = 1 - ci
                    km, kmk = K2m[c_]
                    psn, psnk = pSNr.next()
                    for h in range(4):
                        S.mm(psn[:, h, :], km[:, h * 128:(h + 1) * 128], v[:, h * 128:(h + 1) * 128], True, True,
                             [kmk, vk], [psnk])
                    for h in range(4):
                        S.stt('dve', Sn[:, h, :], Sc[:, h, :], eb[:, h, c_:c_ + 1], psn[:, h, :], ALU.mult, ALU.add,
                              [Sck[h], ebk, psnk], [Snk[h]])
                    if is_lat:
                        Spb, Spbk = B_['Spb'].next()
                        S.tt('pool', Spb[:], Sc[:], eb[:, :, c_:c_ + 1].to_broadcast([128, 4, 128]), ALU.mult,
                             Sck + [ebk], [Spbk])
                        spbs[c_] = (Spb, Spbk)
                if is_lat:
                    Am, Amk, Qsm = cx['Am'], cx['Amk'], cx['Qsm']
                    for h in range(4):
                        S.mm(pOb[:, h, :], Am[:, h, :], v[:, h * 128:(h + 1) * 128], True, False, [Amk, vk], ['pOb'])
                        for n_, c_ in enumerate(order):
                            qm, qmk = Qsm[c_]
                            Spb, Spbk = spbs[c_]
                            S.mm(pOb[:, h, :], qm[:, h, :], Spb[:, h, :], False, n_ == 1, [qmk, Spbk], ['pOb'])
                    ob, obk = B_['osb'].next()
                    S.copy('act', ob[:], pOb[:].rearrange("p h d -> p (h d)"), ['pOb'], [obk])
                    S.dma('sp', OO[d_][row0 - L:row0 - L + 128, :], ob[:], [obk], ())

            fwd_order = list(range(NT))
            bwd_order = [1, 0] + list(range(NT - 1, 1, -1))
            cxs = [[dict() for _ in range(NT)] for _ in range(2)]
            hgrn_pre(cxs[0][0], fwd_order[0], 0)
            hgrn_pre(cxs[1][0], bwd_order[0], 1)
            for i_ in range(NT):
                if i_ + 1 < NT:
                    hgrn_pre(cxs[0][i_ + 1], fwd_order[i_ + 1], 0)
                    hgrn_pre(cxs[1][i_ + 1], bwd_order[i_ + 1], 1)
                hgrn_chain(cxs[0][i_], fwd_order[i_], 0)
                hgrn_chain(cxs[1][i_], bwd_order[i_], 1)
            S.emit()
        if stop_after == 3:
            return nc

        with ExitStack() as es:
            def TT(name, shape, dt):
                return es.enter_context(nc.sbuf_tensor(name, shape, dt))

            def PP(name, shape, dt):
                return es.enter_context(nc.psum_tensor(name, shape, dt))

            KNs = TT("KNs", [128, 4, T], BF16); KRs = TT("KRs", [128, T], BF16); Vs = TT("Vs", [128, NT, 512], BF16)
            sqKR = TT("sqKR", [64, T], BF16)
            sqn = TT("sqn", [128, 512], BF16); sqr = TT("sqr", [64, 512], BF16)
            sqnR = Rot(TT, "csqn", [128, 512], BF16, 2); sqrR = Rot(TT, "csqr", [64, 512], BF16, 2)
            km2 = TT("km2", [128, 4], F32); tmx = TT("tmx", [128, 1], F32)
            tmxR = Rot(TT, "ctmx", [128, 1], F32, 3); nshR = Rot(TT, "cnsh", [128, 1], F32, 3)
            qnr = Rot(TT, "cqn", [128, 512], BF16, 3); qrr = Rot(TT, "cqr", [128, 512], BF16, 3)
            PTr = Rot(TT, "cPT", [128, 1024], BF16, 4); osr = Rot(TT, "cos", [128, 512], BF16, 2)
            rinv = TT("rinv", [128, 512], F32)
            raccR = [Rot(TT, f"racc{i}_", [128, 1024], F32, 2) for i in range(2)]
            rsumR = [Rot(TT, f"rsum{i}_", [128, 512], F32, 2) for i in range(2)]
            pScR = Rot(PP, "pSc", [128, 1024], F32, 2)
            pOaR = Rot(PP, "pOa", [128, 512], F32, 2)
            pMiR = Rot(PP, "pMi", [128, 512], F32, 2)
            pNm = pMiR.t[0]
            for h in range(4):
                S.dma('sp', KNs[:, h, :], KN[h], (), ['KNs'])
            S.memset('pool', KRs[64:128, :], 0.0, ['KRs'])
            S.dma('sp', KRs[0:64, :], KR, (), ['KRs'])
            for i_ in range(3):
                S.memset('pool', qrr.t[i_][64:128, :], 0.0, [qrr.k[i_]])
            VVv = VV.rearrange("(t p) n -> p t n", p=128)
            for j in range(0, NT, 4):
                je = min(NT, j + 4)
                S.dma('sp', Vs[:, j:je, :], VVv[:, j:je, :], (), ['Vs'])
            S.memset('dve', km2[:], 0.0, ['km2'])
            S.act(sqKR[:], KRs[0:64, :], AF.Square, ['KRs'], ['sqKR'])
            for j0 in range(0, T, 512):
                w_ = min(512, T - j0)
                for h in range(4):
                    S.act(sqn[:, :w_], KNs[:, h, j0:j0 + w_], AF.Square, ['KNs'], ['sqn'])
                    S.mm(pNm[:, :w_], onesb[:], sqn[:, :w_], True, False, ['onesb', 'sqn'], ['pMi0'])
                    S.mm(pNm[:, :w_], onesb[0:64, :], sqKR[:, j0:j0 + w_], False, True, ['onesb', 'sqKR'], ['pMi0'])
                    S.op('dve', (lambda ww: (lambda e: e.reduce_max(out=tmx[:], in_=pNm[:, :ww], axis=AX.X)))(w_),
                         ['pMi0'], ['tmx'])
                    S.tt('dve', km2[:, h:h + 1], km2[:, h:h + 1], tmx[:], ALU.max, ['km2', 'tmx'], ['km2'])
            items = [(g_, h) for g_ in range(8) for h in range(4)]
            cxs = [dict() for _ in items]
            NP_ = NT // 2

            def c_pro(i):
                g_, h = items[i]
                q0 = g_ * 512
                cx = cxs[i]
                qn, qnk = qnr.next(); qr, qrk = qrr.next()
                sqn_, sqnk = sqnR.next(); sqr_, sqrk = sqrR.next(); tm_, tmk = tmxR.next(); nsh, nshk = nshR.next()
                pm, pmk = pMiR.next()
                S.dma('sp', qn[:], QN[h][:, q0:q0 + 512], (), [qnk])
                S.dma('sp', qr[0:64, :], QR[h][:, q0:q0 + 512], (), [qrk])
                S.act(sqn_[:], qn[:], AF.Square, [qnk], [sqnk])
                S.act(sqr_[:], qr[0:64, :], AF.Square, [qrk], [sqrk])
                S.mm(pm[:], onesb[:], sqn_[:], True, False, ['onesb', sqnk], [pmk])
                S.mm(pm[:], onesb[0:64, :], sqr_[:], False, True, ['onesb', sqrk], [pmk])
                S.op('dve', (lambda o_, i_: (lambda e: e.reduce_max(out=o_[:], in_=i_[:], axis=AX.X)))(tm_, pm), [pmk], [tmk])
                S.ts('dve', nsh[:], tm_[:], km2[:, h:h + 1], -0.5 * SCALE, ALU.add, ALU.mult, [tmk, 'km2'], [nshk])
                cx.update(qn=qn, qnk=qnk, qr=qr, qrk=qrk, nsh=nsh, nshk=nshk, ps={})

            def c_qk(i, j):
                g_, h = items[i]
                cx = cxs[i]
                ps, pk = pScR.next()
                cx['ps'][j] = (ps, pk)
                for u_ in range(2):
                    kt = 2 * j + u_
                    S.mm(ps[:, u_ * 512:(u_ + 1) * 512], KNs[:, h, kt * 128:(kt + 1) * 128], cx['qn'][:], True, False,
                         ['KNs', cx['qnk']], [pk])
                    S.mm(ps[:, u_ * 512:(u_ + 1) * 512], KRs[:, kt * 128:(kt + 1) * 128], cx['qr'][:], False, True,
                         ['KRs', cx['qrk']], [pk])

            def c_main(i):
                g_, h = items[i]
                cx = cxs[i]
                pOa, pOak = pOaR.next()
                racc = [raccR[0].next(), raccR[1].next()]
                cx.update(pOa=pOa, pOak=pOak, racc=racc)
                for j in range(NP_):
                    if j + 1 < NP_:
                        c_qk(i, j + 1)
                    ps, pk = cx['ps'].pop(j)
                    PT, ptk = PTr.next()
                    S.act(PT[:], ps[:], AF.Exp, [pk, cx['nshk']], [ptk], bias=cx['nsh'][:], scale=SCALE)
                    for u_ in range(2):
                        kt = 2 * j + u_
                        S.mm(pOa[:], Vs[:, kt, h * 128:(h + 1) * 128], PT[:, u_ * 512:(u_ + 1) * 512],
                             kt == 0, kt == NT - 1, ['Vs', ptk], [pOak])
                    ae = 'dve' if j % 2 == 0 else 'pool'
                    ra, rak = racc[j % 2]
                    if j < 2:
                        S.copy(ae, ra[:], PT[:], [ptk], [rak])
                    else:
                        S.tt(ae, ra[:], ra[:], PT[:], ALU.add, [rak, ptk], [rak])

            def c_epi(i):
                g_, h = items[i]
                q0 = g_ * 512
                cx = cxs[i]
                pOa, pOak, racc = cx['pOa'], cx['pOak'], cx['racc']
                rs0, rs0k = rsumR[0].next(); rs1, rs1k = rsumR[1].next()
                pm, pmk = pMiR.next()
                S.tt('dve', rs0[:], racc[0][0][:, 0:512], racc[0][0][:, 512:1024], ALU.add, [racc[0][1]], [rs0k])
                S.tt('pool', rs1[:], racc[1][0][:, 0:512], racc[1][0][:, 512:1024], ALU.add, [racc[1][1]], [rs1k])
                S.mm(pm[:], onesf[:], rs0[:], True, False, ['onesf', rs0k], [pmk])
                S.mm(pm[:], onesf[:], rs1[:], False, True, ['onesf', rs1k], [pmk])
                S.op('dve', (lambda i_: (lambda e: e.reciprocal(out=rinv[:], in_=i_[:])))(pm), [pmk], ['rinv'])
                ob, obk = osr.next()
                S.tt('dve', ob[:], pOa[:], rinv[:], ALU.mult, [pOak, 'rinv'], [obk])
                S.dma('sp', MIXA[h][:, q0:q0 + 512], ob[:], [obk], ())
                cxs[i] = None

            c_pro(0)
            c_qk(0, 0)
            for i in range(len(items)):
                if i + 1 < len(items):
                    c_pro(i + 1)
                c_main(i)
                if i + 1 < len(items):
                    c_qk(i + 1, 0)
                c_epi(i)
            S.emit()
        if stop_after == 4:
            return nc

        with ExitStack() as es:
            def TT(name, shape, dt):
                return es.enter_context(nc.sbuf_tensor(name, shape, dt))

            def PP(name, shape, dt):
                return es.enter_context(nc.psum_tensor(name, shape, dt))

            Wout = TT("Wout", [128, 8, D], BF16)
            mD = TT("mD", [128, 3, D], F32)
            gon = TT("gon", [128, 128], F32)
            woutv = w_out.rearrange("(k p) n -> p k n", p=128)
            for k in range(8):
                S.dma('pool', Wout[:, k, :], woutv[:, k, :], (), ['Wout'])
            for i, mi in enumerate((4, 5, 6)):
                S.dma('sp', mD[:, i, :], MOD[mi], (), ['mD'])
            S.dma('sp', gon[:], g_on.partition_broadcast(128), (), ['gon'])
            ofr = Rot(TT, "dof", [128, 512], F32, 3); obr = Rot(TT, "dob", [128, 512], F32, 3); hgr = Rot(TT, "dhg", [128, 512], F32, 3)
            mar = Rot(TT, "dma_", [128, 4, 128], BF16, 4); xtr = Rot(TT, "dxt", [128, D], F32, 5)
            osumr = Rot(TT, "osum", [128, 512], F32, 2); osqr = Rot(TT, "osq", [128, 512], F32, 2); ss4r = Rot(TT, "ss4", [128, 4], F32, 3)
            tBr = Rot(TT, "tB", [128, 512], F32, 2); hgbr = Rot(TT, "hgb", [128, 512], BF16, 3); mixBr = Rot(TT, "mixB", [128, 4, 128], BF16, 2)
            tmpDr = Rot(TT, "tmpD", [128, D], F32, 2); x1r = Rot(TT, "dx1", [128, D], F32, 3)
            junkD = TT("junkD", [128, D], BF16); ssDr = Rot(TT, "ssD", [128, 1], F32, 3)
            t2Dr = Rot(TT, "t2D", [128, D], F32, 2); h2r = Rot(TT, "h2", [128, D], BF16, 3); h2Tr = Rot(TT, "dh2T", [128, 8, 128], BF16, 3)
            pTBr = Rot(PP, "pTB", [128, 4, 128], BF16, 2)
            pLOr = Rot(PP, "pLO", [128, 512], F32, 4)
            pT8r = Rot(PP, "pT8", [128, 8, 128], BF16, 2)

            def d1_s0(cx, ti):
                r0 = ti * 128
                for nm, rr, src in (('of', ofr, OO[0][r0:r0 + 128, :]), ('ob', obr, OO[1][r0:r0 + 128, :]),
                                    ('hg', hgr, HG[r0:r0 + 128, :]),
                                    ('ma', mar, MIXA[:, :, r0:r0 + 128].rearrange("h p t -> p h t")),
                                    ('xt', xtr, x[r0:r0 + 128, :])):
                    cx[nm], cx[nm + 'k'] = rr.next()
                    S.dma('sp', cx[nm][:], src, (), [cx[nm + 'k']])

            def d1_s1(cx, ti):
                of, ofk, ob, obk, hg, hgk = cx['of'], cx['ofk'], cx['ob'], cx['obk'], cx['hg'], cx['hgk']
                osum, osumk = osumr.next(); osq, osqk = osqr.next(); ss4, ss4k = ss4r.next(); tB, tBk = tBr.next()
                hgb, hgbk = hgbr.next()
                cx['hgb'], cx['hgbk'] = hgb, hgbk
                S.tt('pool', osum[:], of[:], ob[:], ALU.add, [ofk, obk], [osumk])
                S.tt('pool', osq[:], osum[:], osum[:], ALU.mult, [osumk], [osqk])
                S.op('dve', (lambda o_, i_: (lambda e: e.reduce_sum(out=o_[:], in_=i_[:].rearrange("p (h d) -> p h d", h=4),
                                                                    axis=AX.X)))(ss4, osq), [osqk], [ss4k])
                S.act(ss4[:], ss4[:], AF.Sqrt, [ss4k], [ss4k], bias=EPS, scale=1.0 / 128)
                S.op('dve', (lambda o_: (lambda e: e.reciprocal(out=o_[:], in_=o_[:])))(ss4), [ss4k], [ss4k])
                o3 = osum[:].rearrange("p (h d) -> p h d", h=4)
                t3 = tB[:].rearrange("p (h d) -> p h d", h=4)
                S.tt('dve', t3, o3, ss4[:].unsqueeze(2).to_broadcast([128, 4, 128]), ALU.mult, [osumk, ss4k], [tBk])
                S.tt('pool', t3, t3, gon[:].unsqueeze(1).to_broadcast([128, 4, 128]), ALU.mult, [tBk, 'gon'], [tBk])
                S.tt('pool', hgb[:], tB[:], hg[:], ALU.mult, [tBk, hgk], [hgbk])

            def d1_s2(cx, ti):
                hgb, hgbk, ma, mak = cx['hgb'], cx['hgbk'], cx['ma'], cx['mak']
                pTB, pTBk = pTBr.next(); mixB, mixBk = mixBr.next()
                for h in range(4):
                    S.tr(pTB[:, h, :], hgb[:, h * 128:(h + 1) * 128], identb[:], [hgbk, 'identb'], [pTBk])
                S.copy('act', mixB[:], pTB[:], [pTBk], [mixBk])
                cx['pLO'] = []
                for hf in range(2):
                    pl, plk = pLOr.next()
                    cx['pLO'].append((pl, plk))
                    for k in range(4):
                        S.mm(pl[:], ma[:, k, :], Wout[:, k, hf * 512:(hf + 1) * 512], k == 0, False, [mak, 'Wout'], [plk])
                    for k in range(4):
                        S.mm(pl[:], mixB[:, k, :], Wout[:, 4 + k, hf * 512:(hf + 1) * 512], False, k == 3,
                             [mixBk, 'Wout'], [plk])

            def d1_s3(cx, ti):
                r0 = ti * 128
                xt_, xtk = cx['xt'], cx['xtk']
                tmpD, tmpDk = tmpDr.next(); ssD, ssDk = ssDr.next(); t2D, t2Dk = t2Dr.next(); h2, h2k = h2r.next()
                x1, x1k = x1r.next()
                cx['h2'], cx['h2k'] = h2, h2k
                for hf in range(2):
                    pl, plk = cx['pLO'][hf]
                    S.tt('dve', tmpD[:, hf * 512:(hf + 1) * 512], pl[:], mD[:, 0, hf * 512:(hf + 1) * 512], ALU.mult,
                         [plk, 'mD'], [tmpDk])
                S.tt('pool', x1[:], tmpD[:], xt_[:], ALU.add, [tmpDk, xtk], [x1k])
                S.dma('sp', X1[r0:r0 + 128, :], x1[:], [x1k], ())
                S.memset('dve', ssD[:], 0.0, [ssDk])
                S.act(junkD[:], x1[:], AF.Square, [x1k], ['junkD', ssDk], accum=ssD[:])
                S.act(ssD[:], ssD[:], AF.Sqrt, [ssDk], [ssDk], bias=EPS, scale=1.0 / D)
                S.op('dve', (lambda o_: (lambda e: e.reciprocal(out=o_[:], in_=o_[:])))(ssD), [ssDk], [ssDk])
                S.stt('dve', t2D[:], x1[:], ssD[:, 0:1], mD[:, 1, :], ALU.mult, ALU.mult, [x1k, ssDk, 'mD'], [t2Dk])
                S.tt('pool', h2[:], t2D[:], mD[:, 2, :], ALU.add, [t2Dk, 'mD'], [h2k])

            def d1_s4(cx, ti):
                r0 = ti * 128
                h2, h2k = cx['h2'], cx['h2k']
                pT8, pT8k = pT8r.next(); hT2, hT2k = h2Tr.next()
                for k in range(8):
                    S.tr(pT8[:, k, :], h2[:, k * 128:(k + 1) * 128], identb[:], [h2k, 'identb'], [pT8k])
                S.copy('dve' if ti % 2 else 'act', hT2[:], pT8[:], [pT8k], [hT2k])
                S.dma('sp', H2T[:, :, r0:r0 + 128], hT2[:], [hT2k], ())
            run_pipeline(N // 128, [d1_s0, d1_s1, d1_s2, d1_s3, d1_s4])
            S.emit()
        if stop_after == 5:
            return nc

        with ExitStack() as es:
            def TT(name, shape, dt):
                return es.enter_context(nc.sbuf_tensor(name, shape, dt))

            def PP(name, shape, dt):
                return es.enter_context(nc.psum_tensor(name, shape, dt))

            Wg = TT("Wg", [128, 8, DFF], BF16); Wu = TT("Wu", [128, 8, DFF], BF16); Wd = TT("Wd", [128, NCF, D], BF16)
            mE = TT("mE", [128, 2, D], F32)
            wgv = w_gate.rearrange("(k p) n -> p k n", p=128); wuv = w_up.rearrange("(k p) n -> p k n", p=128)
            wdv = w_down.rearrange("(c p) n -> p c n", p=128)
            for k in range(8):
                S.dma('pool', Wg[:, k, :], wgv[:, k, :], (), ['Wg'])
                S.dma('pool', Wu[:, k, :], wuv[:, k, :], (), ['Wu'])
            for c_ in range(NCF):
                S.dma('pool', Wd[:, c_, :], wdv[:, c_, :], (), ['Wd'])
            S.dma('sp', mE[:, 0, :], MOD[7], (), ['mE'])
            S.dma('sp', mE[:, 1, :], g_fin.partition_broadcast(128), (), ['mE'])
            GN = 512
            h2g = Rot(TT, "eh2", [128, 8, GN], BF16, 1)
            aT = TT("aT", [128, NCF, GN], BF16)
            sgr = Rot(TT, "esg", [128, GN], F32, 2)
            x1r = Rot(TT, "ex1", [128, D], F32, 1); tmr = Rot(TT, "etm", [128, D], F32, 2)
            ssE = TT("ssE", [128, 1], F32)
            pG = [PP(f"pG{i}", [128, GN], F32) for i in range(2)]
            pU = [PP(f"pU{i}", [128, GN], F32) for i in range(2)]
            pY = [PP(f"pY{i}", [128, 512], F32) for i in range(2)]
            for g_ in range(N // GN):
                q0 = g_ * GN
                hh, hhk = h2g.next()
                S.dma('sp', hh[:], H2T[:, :, q0:q0 + GN], (), [hhk])
                for c_ in range(NCF):
                    pg, pgk = pG[c_ % 2], f'pG{c_ % 2}'
                    pu, puk = pU[c_ % 2], f'pU{c_ % 2}'
                    for k in range(8):
                        S.mm(pg[:], Wg[:, k, c_ * 128:(c_ + 1) * 128], hh[:, k, :], k == 0, k == 7, ['Wg', hhk], [pgk])
                    for k in range(8):
                        S.mm(pu[:], Wu[:, k, c_ * 128:(c_ + 1) * 128], hh[:, k, :], k == 0, k == 7, ['Wu', hhk], [puk])
                    sg, sgk = sgr.next()
                    S.act(sg[:], pg[:], AF.Silu, [pgk], [sgk])
                    S.tt('dve', aT[:, c_, :], sg[:], pu[:], ALU.mult, [sgk, puk], ['aT'])
                for sb in range(GN // 128):
                    r0 = q0 + sb * 128
                    x1, x1k = x1r.next(); tm, tmk = tmr.next()
                    S.dma('sp', x1[:], X1[r0:r0 + 128, :], (), [x1k])
                    for hf in range(2):
                        for c_ in range(NCF):
                            S.mm(pY[hf][:], aT[:, c_, sb * 128:(sb + 1) * 128], Wd[:, c_, hf * 512:(hf + 1) * 512],
                                 c_ == 0, c_ == NCF - 1, ['aT', 'Wd'], [f'pY{hf}'])
                        S.tt('dve', tm[:, hf * 512:(hf + 1) * 512], pY[hf][:], mE[:, 0, hf * 512:(hf + 1) * 512], ALU.mult,
                             [f'pY{hf}', 'mE'], [tmk])
                    S.tt('pool', x1[:], tm[:], x1[:], ALU.add, [tmk, x1k], [x1k])
                    S.memset('dve', ssE[:], 0.0, ['ssE'])
                    S.act(tm[:], x1[:], AF.Square, [x1k, tmk], [tmk, 'ssE'], accum=ssE[:])
                    S.act(ssE[:], ssE[:], AF.Sqrt, ['ssE'], ['ssE'], bias=EPS, scale=1.0 / D)
                    S.op('dve', lambda e: e.reciprocal(out=ssE[:], in_=ssE[:]), ['ssE'], ['ssE'])
                    S.stt('dve', tm[:], x1[:], ssE[:, 0:1], mE[:, 1, :], ALU.mult, ALU.mult, [x1k, 'ssE', 'mE'], [tmk])
                    S.dma('sp', out[r0:r0 + 128, :], tm[:], [tmk], ())
            S.emit()
    return nc


_NC_CACHE = {}


def kernel(**inputs):
    if 'nc' not in _NC_CACHE:
        _NC_CACHE['nc'] = build_nc()
    nc = _NC_CACHE['nc']
    f = lambda a: np.ascontiguousarray(np.asarray(a, dtype=np.float32))
    shared = {
        "c_ctx": f(inputs["c_ctx"]), "w_mod": f(inputs["w_mod"][0]), "b_mod": f(inputs["b_mod"][0]),
        "g_norm_mix": f(inputs["g_norm_mix"][0]), "g_norm_ffn": f(inputs["g_norm_ffn"][0]),
        "w_in": f(inputs["w_in"][0]), "g_q_norm": f(inputs["g_q_norm"][0]), "w_uq": f(inputs["w_uq"][0]),
        "g_kv_norm": f(inputs["g_kv_norm"][0]), "w_ukv": f(inputs["w_ukv"][0]),
        "lb_fwd": f(inputs["lb_fwd"]), "lb_bwd": f(inputs["lb_bwd"]), "g_hgrn_norm": f(inputs["g_hgrn_norm"][0]),
        "w_out": f(inputs["w_out"][0]), "w_gate": f(inputs["w_gate"][0]), "w_up": f(inputs["w_up"][0]),
        "w_down": f(inputs["w_down"][0]), "g_final": f(inputs["g_final"]),
    }
    xs, cs, ctxs = f(inputs["x"]), f(inputs["c"]), f(inputs["ctx"])
    in_maps = []
    for b in range(NB):
        m = dict(shared)
        m["x"] = xs[b]; m["c"] = cs[b]; m["ctx"] = ctxs[b]
        in_maps.append(m)
    res = run_bass_kernel_spmd(nc, in_maps, core_ids=list(range(NB)))
    return np.stack([np.asarray(r["out"], dtype=np.float32) for r in res.results], axis=0)
```

```python
import math
from contextlib import ExitStack

import numpy as np
import concourse.bass as bass
import concourse.mybir as mybir
from concourse.bass_utils import run_bass_kernel_spmd

F32 = mybir.dt.float32
BF16 = mybir.dt.bfloat16
AF = mybir.ActivationFunctionType
ALU = mybir.AluOpType
AX = mybir.AxisListType

NB, N, L, D = 8, 4096, 256, 1024
T = N + L
NT = T // 128
DFF = 2816
NCF = DFF // 128
EPS = 1e-6
INC = 3136
SCALE = 1.0 / math.sqrt(192.0)


class Sched:
    ENG = ('pe', 'act', 'dve', 'pool', 'sp')

    def __init__(self, nc, n_dma_sems=24):
        self.nc = nc
        self.ops = {e: [] for e in self.ENG}
        self.sems = {}
        for e in ('pe', 'act', 'dve', 'pool'):
            self.sems[('e', e)] = nc.alloc_semaphore(name=f"s_{e}")
        for i in range(n_dma_sems):
            self.sems[('d', i)] = nc.alloc_semaphore(name=f"s_dma{i}")
        self.nw = 6
        for i in range(self.nw):
            self.sems[('w', i)] = nc.alloc_semaphore(name=f"s_swdma{i}")
        self.wnext = 0
        self.cnt = {k: 0 for k in self.sems}
        self.nd = n_dma_sems
        self.dnext = 0
        self.waited = {e: {} for e in self.ENG}
        self.res = {}
        self.enabled = True
        self.load_q = None

    def _deps(self, reads, writes):
        t = []
        for r in reads:
            st = self.res.get(r)
            if st and st['w']:
                t.append(st['w'])
        for w in writes:
            st = self.res.get(w)
            if st:
                if st['w']:
                    t.append(st['w'])
                t.extend(st['r'].values())
        return t

    def _need(self, eng, tickets):
        best = {}
        for key, val in tickets:
            if key == ('e', 'pe') and eng == 'pe':
                continue
            if self.waited[eng].get(key, 0) >= val:
                continue
            if best.get(key, 0) < val:
                best[key] = val
        for key, val in best.items():
            self.waited[eng][key] = val
        return list(best.items())

    def _commit(self, ticket, reads, writes):
        for r in reads:
            st = self.res.setdefault(r, {'w': None, 'r': {}})
            st['r'][ticket[0]] = ticket
        for w in writes:
            self.res[w] = {'w': ticket, 'r': {}}

    def op(self, eng, fn, reads=(), writes=()):
        if not self.enabled:
            return None
        waits = self._need(eng, self._deps(reads, writes))
        key = ('e', eng)
        self.cnt[key] += 1
        ticket = (key, self.cnt[key])
        self.ops[eng].append((waits, fn, key, 1))
        self._commit(ticket, reads, writes)
        return ticket

    def dma(self, q, out, in_, reads=(), writes=()):
        if not self.enabled:
            return None
        if q == 'sp!':
            q = 'sp'
        elif q == 'sp' and not reads and self.load_q:
            q = self.load_q
        if q == 'pool':
            key = ('w', self.wnext)
            self.wnext = (self.wnext + 1) % self.nw
        else:
            key = ('d', self.dnext)
            self.dnext = (self.dnext + 1) % self.nd
        tickets = self._deps(reads, writes)
        if self.cnt[key] > 0:
            tickets.append((key, self.cnt[key]))
        waits = self._need(q, tickets)
        self.cnt[key] += 16
        ticket = (key, self.cnt[key])
        self.ops[q].append((waits, lambda e: e.dma_start(out=out, in_=in_), key, 16))
        self._commit(ticket, reads, writes)
        return ticket

    def barrier(self):
        allt = [(k, v) for k, v in self.cnt.items() if v > 0]
        for e in self.ENG:
            waits = self._need(e, allt)
            if waits:
                self.ops[e].append((waits, None, None, 0))

    def emit(self):
        self.barrier()
        with self.nc.Block() as block:
            def mk(engname):
                def body(e):
                    for waits, fn, semkey, inc in self.ops[engname]:
                        for key, val in waits:
                            e.wait_ge(self.sems[key], val)
                        if fn is not None:
                            fn(e).then_inc(self.sems[semkey], inc)
                return body
            block.tensor(mk('pe'))
            block.scalar(mk('act'))
            block.vector(mk('dve'))
            block.gpsimd(mk('pool'))
            block.sync(mk('sp'))
        self.ops = {e: [] for e in self.ENG}

    def mm(self, out, lhsT, rhs, start, stop, r, w):
        return self.op('pe', lambda e: e.matmul(out, lhsT=lhsT, rhs=rhs, start=start, stop=stop), r, w)

    def tr(self, out, in_, ident, r, w):
        return self.op('pe', lambda e: e.transpose(out, in_, ident), r, w)

    def act(self, out, in_, func, r, w, bias=None, scale=None, accum=None):
        kw = {}
        if bias is not None:
            kw['bias'] = bias
        if scale is not None:
            kw['scale'] = scale
        if accum is not None:
            kw['accum_out'] = accum
        return self.op('act', lambda e: e.activation(out=out, in_=in_, func=func, **kw), r, w)

    def tt(self, eng, out, in0, in1, op, r, w):
        return self.op(eng, lambda e: e.tensor_tensor(out=out, in0=in0, in1=in1, op=op), r, w)

    def ts(self, eng, out, in0, s1, s2, op0, op1, r, w):
        if s2 is None:
            return self.op(eng, lambda e: e.tensor_scalar(out=out, in0=in0, scalar1=s1, scalar2=None, op0=op0), r, w)
        return self.op(eng, lambda e: e.tensor_scalar(out=out, in0=in0, scalar1=s1, scalar2=s2, op0=op0, op1=op1), r, w)

    def stt(self, eng, out, in0, scalar, in1, op0, op1, r, w):
        return self.op(eng, lambda e: e.scalar_tensor_tensor(out=out, in0=in0, scalar=scalar, in1=in1, op0=op0, op1=op1), r, w)

    def copy(self, eng, out, in_, r, w):
        if eng == 'act':
            return self.act(out, in_, AF.Copy, r, w)
        return self.op(eng, lambda e: e.tensor_copy(out=out, in_=in_), r, w)

    def memset(self, eng, ap, val, w):
        return self.op(eng, lambda e: e.memset(ap, val), (), w)


def run_pipeline(n, stages):
    ctxs = [dict() for _ in range(n)]
    K = len(stages)
    for t in range(n + K - 1):
        for k in range(K - 1, -1, -1):
            i = t - k
            if 0 <= i < n:
                stages[k](ctxs[i], i)


class Rot:
    def __init__(self, alloc, name, shape, dt, n):
        self.t = [alloc(f"{name}{i}", shape, dt) for i in range(n)]
        self.k = [f"{name}{i}" for i in range(n)]
        self.i = 0

    def next(self):
        j = self.i % len(self.t)
        self.i += 1
        return self.t[j], self.k[j]


def _rstd(S, ss, out, dim, tag):
    S.act(out, ss, AF.Sqrt, [tag + 'ss'], [tag + 'rs'], bias=EPS, scale=1.0 / dim)
    S.op('dve', lambda e: e.reciprocal(out=out, in_=out), [tag + 'rs'], [tag + 'rs'])


def build_nc(stop_after=None, debug=False, a2_groups=None, a2_parts='abcde', STQ='sp'):
    nc = bass.Bass("TRN2", target_bir_lowering=False)
    S = Sched(nc)

    def din(name, shape):
        return nc.dram_tensor(name, shape, F32, kind="ExternalInput").ap()

    x = din("x", [N, D]); c = din("c", [D]); ctx = din("ctx", [L, D]); c_ctx = din("c_ctx", [D])
    w_mod = din("w_mod", [D, 6 * D]); b_mod = din("b_mod", [6 * D])
    g_mix = din("g_norm_mix", [D]); g_ffn = din("g_norm_ffn", [D])
    w_in = din("w_in", [D, INC]); g_qn = din("g_q_norm", [256]); w_uq = din("w_uq", [256, 768])
    g_kvn = din("g_kv_norm", [256]); w_ukv = din("w_ukv", [256, 1024])
    lb_f = din("lb_fwd", [2, 512]); lb_b = din("lb_bwd", [2, 512]); g_on = din("g_hgrn_norm", [128])
    w_out = din("w_out", [D, D]); w_gate = din("w_gate", [D, DFF]); w_up = din("w_up", [D, DFF])
    w_down = din("w_down", [DFF, D]); g_fin = din("g_final", [D])
    out = nc.dram_tensor("out", [N, D], F32, kind="ExternalOutput").ap()

    dbg = set(debug) if debug else set()

    def scr(name, shape, dt):
        return nc.dram_tensor(name, shape, dt, kind="ExternalOutput" if name in dbg else "Internal").ap()

    MOD = scr("s_mod", [8, 128, D], F32)
    HT = scr("s_ht", [128, 8, T], BF16)
    QN = scr("s_qn", [4, 128, N], BF16); QR = scr("s_qr", [4, 64, N], BF16)
    KN = scr("s_kn", [4, 128, T], BF16); KR = scr("s_kr", [64, T], BF16)
    VV = scr("s_v", [T, 512], BF16)
    HQ = scr("s_hq", [4, 128, T], F32); HV = scr("s_hv", [T, 512], BF16); HG = scr("s_hg", [N, 512], F32)
    GG = scr("s_g", [2, T, 512], F32); KG = scr("s_kg", [2, T, 512], F32)
    OO = scr("s_o", [2, N, 512], F32)
    MIXA = scr("s_mixa", [4, 128, N], BF16)
    X1 = scr("s_x1", [N, D], F32); H2T = scr("s_h2t", [128, 8, N], BF16)

    outer = ExitStack()
    with outer:
        def CT(name, shape, dt):
            return outer.enter_context(nc.sbuf_tensor(name, shape, dt))
        identb = CT("identb", [128, 128], BF16); identf = CT("identf", [128, 128], F32)
        onesb = CT("onesb", [128, 128], BF16); onesf = CT("onesf", [128, 128], F32)
        Uf = CT("Uf", [128, 128], F32); Ub = CT("Ub", [128, 128], F32)
        Rf = CT("Rf", [128, 128], F32); Rb = CT("Rb", [128, 128], F32)
        Mf = CT("Mf", [128, 128], F32); Mb = CT("Mb", [128, 128], F32)
        Ind = CT("Ind", [128, 2], F32)

        def sel(tile, pattern, cm, cmp_op, key):
            S.op('pool', lambda e: e.affine_select(out=tile[:], in_=tile[:], pattern=pattern, compare_op=cmp_op,
                                                   fill=0.0, base=0, channel_multiplier=cm), [key], [key])
        for tl, key in ((identb, 'identb'), (identf, 'identf')):
            S.memset('pool', tl[:], 1.0, [key])
            sel(tl, [[-1, 128]], 1, ALU.is_equal, key)
        S.memset('pool', onesb[:], 1.0, ['onesb'])
        S.memset('pool', onesf[:], 1.0, ['onesf'])
        S.memset('pool', Uf[:], 1.0, ['Uf']); sel(Uf, [[-1, 128]], 1, ALU.is_gt, 'Uf')
        S.memset('pool', Uf[64:128, 0:64], 0.0, ['Uf'])
        S.memset('pool', Mb[:], 1.0, ['Mb']); sel(Mb, [[-1, 128]], 1, ALU.is_ge, 'Mb')
        S.memset('pool', Mb[64:128, 0:64], 0.0, ['Mb'])
        S.memset('pool', Ub[:], 1.0, ['Ub']); sel(Ub, [[1, 128]], -1, ALU.is_gt, 'Ub')
        S.memset('pool', Ub[0:64, 64:128], 0.0, ['Ub'])
        S.memset('pool', Mf[:], 1.0, ['Mf']); sel(Mf, [[1, 128]], -1, ALU.is_ge, 'Mf')
        S.memset('pool', Mf[0:64, 64:128], 0.0, ['Mf'])
        S.ts('pool', Rf[:], Uf[:], -1.0, None, ALU.mult, None, ['Uf'], ['Rf'])
        S.ts('pool', Rb[:], Ub[:], -1.0, None, ALU.mult, None, ['Ub'], ['Rb'])
        S.memset('pool', Ind[:], 0.0, ['Ind'])
        S.memset('pool', Ind[0:64, 0:1], 1.0, ['Ind'])
        S.memset('pool', Ind[64:128, 1:2], 1.0, ['Ind'])

        wstack = ExitStack()
        Win = wstack.enter_context(nc.sbuf_tensor("Win", [128, 8, INC], BF16))
        Wkpe = wstack.enter_context(nc.sbuf_tensor("Wkpe", [128, 8, 128], BF16))
        winv = w_in.rearrange("(k p) n -> p k n", p=128)
        S.memset('dve', Wkpe[:, :, 64:128], 0.0, ['Wkpe'])
        for k in range(8):
            S.dma('pool', Win[:, k, :], winv[:, k, :], (), ['Win'])
            for f_ in range(2):
                S.dma('pool', Wkpe[:, k, f_ * 32:(f_ + 1) * 32].rearrange("p (a i) -> p a i", a=2),
                      winv[:, k, 512:576].rearrange("p (a f i) -> p f a i", a=2, f=2)[:, f_, :, :], (), ['Wkpe'])

        with ExitStack() as es:
            def TT(name, shape, dt):
                return es.enter_context(nc.sbuf_tensor(name, shape, dt))

            def PP(name, shape, dt):
                return es.enter_context(nc.psum_tensor(name, shape, dt))
            crow = TT("crow", [128, 2, D], F32)
            cb = TT("cb", [128, 2, 8, 128], F32)
            bmod = TT("bmod", [128, 6 * D], F32)
            wm = [TT(f"wm{i}", [128, 8, 512], F32) for i in range(2)]
            modl = TT("modl", [128, 6 * D], F32); modc = TT("modc", [128, 2 * D], F32)
            gm = TT("gm", [128, D], F32); gf = TT("gf", [128, D], F32)
            tmpA = [TT(f"tmpA{i}", [128, D], F32) for i in range(3)]
            pcb = PP("pcb", [128, 128], F32)
            pm = [PP(f"pm{i}", [128, 512], F32) for i in range(2)]
            S.dma('sp', crow[:, 0, :], c.partition_broadcast(128), (), ['crow'])
            S.dma('sp', crow[:, 1, :], c_ctx.partition_broadcast(128), (), ['crow'])
            S.dma('sp', bmod[:], b_mod.partition_broadcast(128), (), ['bmod'])
            S.dma('sp', gm[:], g_mix.partition_broadcast(128), (), ['gm'])
            S.dma('sp', gf[:], g_ffn.partition_broadcast(128), (), ['gf'])
            S.act(crow[:], crow[:], AF.Silu, ['crow'], ['crow'])
            for w_ in range(2):
                for k in range(8):
                    S.mm(pcb[:], crow[:, w_, k * 128:(k + 1) * 128], identf[:], True, True, ['crow', 'identf'], ['pcb'])
                    S.copy('dve', cb[:, w_, k, :], pcb[:], ['pcb'], ['cb'])
            wmv = w_mod.rearrange("(k p) n -> p k n", p=128)
            for j in range(12):
                wt = wm[j % 2]; wk = f"wm{j % 2}"
                S.dma('sp' if j % 2 == 0 else 'act', wt[:], wmv[:, :, j * 512:(j + 1) * 512], (), [wk])
                for k in range(8):
                    S.mm(pm[0][:], cb[:, 0, k, :], wt[:, k, :], k == 0, k == 7, ['cb', wk], ['pm0'])
                S.tt('dve', modl[:, j * 512:(j + 1) * 512], pm[0][:], bmod[:, j * 512:(j + 1) * 512], ALU.add,
                     ['pm0', 'bmod'], ['modl'])
                if j < 4:
                    for k in range(8):
                        S.mm(pm[1][:], cb[:, 1, k, :], wt[:, k, :], k == 0, k == 7, ['cb', wk], ['pm1'])
                    S.tt('dve', modc[:, j * 512:(j + 1) * 512], pm[1][:], bmod[:, j * 512:(j + 1) * 512], ALU.add,
                         ['pm1', 'bmod'], ['modc'])
            S.stt('dve', tmpA[0][:], modl[:, D:2 * D], 1.0, gm[:], ALU.add, ALU.mult, ['modl', 'gm'], ['tA0'])
            S.stt('dve', tmpA[1][:], modc[:, D:2 * D], 1.0, gm[:], ALU.add, ALU.mult, ['modc', 'gm'], ['tA1'])
            S.stt('dve', tmpA[2][:], modl[:, 4 * D:5 * D], 1.0, gf[:], ALU.add, ALU.mult, ['modl', 'gf'], ['tA2'])
            S.dma('sp', MOD[0], tmpA[0][:], ['tA0'], ())
            S.dma('sp', MOD[1], modl[:, 0:D], ['modl'], ())
            S.dma('sp', MOD[2], tmpA[1][:], ['tA1'], ())
            S.dma('sp', MOD[3], modc[:, 0:D], ['modc'], ())
            S.dma('sp', MOD[4], modl[:, 2 * D:3 * D], ['modl'], ())
            S.dma('sp', MOD[5], tmpA[2][:], ['tA2'], ())
            S.dma('sp', MOD[6], modl[:, 3 * D:4 * D], ['modl'], ())
            S.dma('sp', MOD[7], modl[:, 5 * D:6 * D], ['modl'], ())
            S.emit()
        S.load_q = 'act'
        if stop_after == 0:
            return nc

        with ExitStack() as es:
            def TT(name, shape, dt):
                return es.enter_context(nc.sbuf_tensor(name, shape, dt))

            def PP(name, shape, dt):
                return es.enter_context(nc.psum_tensor(name, shape, dt))
            mA = TT("mA", [128, 4, D], F32)
            junk = TT("junk", [128, D], BF16)
            xtR = Rot(TT, "xt", [128, D], F32, 4); ssR = Rot(TT, "ssa", [128, 1], F32, 4)
            t1R = Rot(TT, "t1_", [128, D], F32, 2); hbR = Rot(TT, "hb", [128, D], BF16, 3)
            hTR = Rot(TT, "hT", [128, 8, 128], BF16, 3); ptrR = Rot(PP, "ptr", [128, 8, 128], BF16, 2)
            for i in range(4):
                S.dma('sp', mA[:, i, :], MOD[i], (), ['mA'])

            def a1_s0(cx, ti):
                cx['xt'], cx['xk'] = xtR.next()
                src = ctx[ti * 128:(ti + 1) * 128, :] if ti < 2 else x[(ti - 2) * 128:(ti - 1) * 128, :]
                S.dma('sp', cx['xt'][:], src, (), [cx['xk']])

            def a1_s1(cx, ti):
                xt_, xk = cx['xt'], cx['xk']
                ss, sk = ssR.next()
                cx['ss'], cx['sk'] = ss, sk
                S.memset('dve', ss[:], 0.0, [sk])
                S.act(junk[:], xt_[:], AF.Square, [xk], ['junk', sk], accum=ss[:])
                S.act(ss[:], ss[:], AF.Sqrt, [sk], [sk], bias=EPS, scale=1.0 / D)
                S.op('dve', (lambda o_: (lambda e: e.reciprocal(out=o_[:], in_=o_[:])))(ss), [sk], [sk])

            def a1_s1b(cx, ti):
                xt_, xk, ss, sk = cx['xt'], cx['xk'], cx['ss'], cx['sk']
                mi = 2 if ti < 2 else 0
                t1, t1k = t1R.next(); hb, hbk = hbR.next()
                cx['hb'], cx['hbk'] = hb, hbk
                S.stt('dve', t1[:], xt_[:], ss[:, 0:1], mA[:, mi, :], ALU.mult, ALU.mult, [xk, sk, 'mA'], [t1k])
                S.tt('pool', hb[:], t1[:], mA[:, mi + 1, :], ALU.add, [t1k, 'mA'], [hbk])

            def a1_s2(cx, ti):
                hb, hbk = cx['hb'], cx['hbk']
                pt, ptk = ptrR.next(); hT, hTk = hTR.next()
                for k in range(8):
                    S.tr(pt[:, k, :], hb[:, k * 128:(k + 1) * 128], identb[:], [hbk, 'identb'], [ptk])
                S.copy('dve' if ti % 2 else 'act', hT[:], pt[:], [ptk], [hTk])
                S.dma('sp', HT[:, :, ti * 128:(ti + 1) * 128], hT[:], [hTk], ())
            run_pipeline(NT, [a1_s0, a1_s1, a1_s1b, a1_s2])
            S.emit()
        if stop_after == 1:
            return nc

        with ExitStack() as es:
            def TT(name, shape, dt):
                return es.enter_context(nc.sbuf_tensor(name, shape, dt))

            def PP(name, shape, dt):
                return es.enter_context(nc.psum_tensor(name, shape, dt))
            Wkrot = TT("Wkrot", [128, 8, 128], BF16)
            Wqn = TT("Wqn", [128, 2, 4, 128], BF16); Wqr = TT("Wqr", [128, 2, 4, 128], BF16)
            Wqrot = TT("Wqrot", [128, 2, 4, 128], BF16)
            Wkn = TT("Wkn", [128, 2, 4, 128], BF16); Wv = TT("Wv", [128, 2, 4, 128], BF16)
            Ct = TT("Ct", [64, N], F32); St = TT("St", [64, N], F32)
            lbt = TT("lbt", [128, 2, 512], F32); oml = TT("oml", [128, 2, 512], F32)
            with ExitStack() as es2:
                def T2(name, shape, dt):
                    return es2.enter_context(nc.sbuf_tensor(name, shape, dt))
                S.memset('dve', Wkrot[:, :, 64:128], 0.0, ['Wkrot'])
                for tl_, k_ in ((Wqr, 'Wqr'), (Wqrot, 'Wqrot')):
                    S.memset('dve', tl_[:, :, :, 64:128], 0.0, [k_])
                S.ts('dve', Wkrot[:, :, 0:32], Wkpe[:, :, 32:64], -1.0, None, ALU.mult, None, ['Wkpe'], ['Wkrot'])
                S.copy('dve', Wkrot[:, :, 32:64], Wkpe[:, :, 0:32], ['Wkpe'], ['Wkrot'])
                stq = T2("stq", [128, 2, 768], F32); stkv = T2("stkv", [128, 2, 1024], F32)
                gq = T2("gq", [128, 2], F32); gkv = T2("gkv", [128, 2], F32)
                S.dma('sp', stq[:], w_uq.rearrange("(c p) n -> p c n", p=128), (), ['stq'])
                S.dma('sp', stkv[:], w_ukv.rearrange("(c p) n -> p c n", p=128), (), ['stkv'])
                for c_ in range(2):
                    S.dma('sp', gq[:, c_:c_ + 1], g_qn[c_ * 128:(c_ + 1) * 128].rearrange("(p o) -> p o", o=1), (), ['gq'])
                    S.dma('sp', gkv[:, c_:c_ + 1], g_kvn[c_ * 128:(c_ + 1) * 128].rearrange("(p o) -> p o", o=1), (), ['gkv'])
                for c_ in range(2):
                    sq_v = stq[:, c_, :].rearrange("p (h d) -> p h d", h=4)
                    S.ts('dve', Wqn[:, c_, :, :], sq_v[:, :, 0:128], gq[:, c_:c_ + 1], None, ALU.mult, None,
                         ['stq', 'gq'], ['Wqn'])
                    for f_ in range(2):
                        for a_ in range(2):
                            so = 128 + a_ * 32 + f_ * 16
                            do = f_ * 32 + a_ * 16
                            S.ts('dve', Wqr[:, c_, :, do:do + 16], sq_v[:, :, so:so + 16], gq[:, c_:c_ + 1], None,
                                 ALU.mult, None, ['stq', 'gq'], ['Wqr'])
                    S.ts('dve', Wqrot[:, c_, :, 0:32], Wqr[:, c_, :, 32:64], -1.0, None, ALU.mult, None, ['Wqr'], ['Wqrot'])
                    S.copy('dve', Wqrot[:, c_, :, 32:64], Wqr[:, c_, :, 0:32], ['Wqr'], ['Wqrot'])
                    skv_v = stkv[:, c_, :].rearrange("p (h t d) -> p h t d", h=4, t=2)
                    S.ts('dve', Wkn[:, c_, :, :], skv_v[:, :, 0, :], gkv[:, c_:c_ + 1], None, ALU.mult, None,
                         ['stkv', 'gkv'], ['Wkn'])
                    S.ts('dve', Wv[:, c_, :, :], skv_v[:, :, 1, :], gkv[:, c_:c_ + 1], None, ALU.mult, None,
                         ['stkv', 'gkv'], ['Wv'])
                lraw = T2("lraw", [128, 2, 2, 512], F32)
                for d_, lbx in enumerate((lb_f, lb_b)):
                    for r_ in range(2):
                        S.dma('sp', lraw[:, d_, r_, :], lbx[r_].partition_broadcast(128), (), ['lraw'])
                S.tt('dve', lbt[:], lraw[:, :, 0, :], lraw[:, :, 1, :], ALU.subtract, ['lraw'], ['lbt'])
                S.act(lbt[:], lbt[:], AF.Sigmoid, ['lbt'], ['lbt'])
                S.ts('dve', oml[:], lbt[:], -0.5, 0.5, ALU.mult, ALU.add, ['lbt'], ['oml'])
                S.tt('dve', lbt[:], lbt[:], oml[:], ALU.add, ['lbt', 'oml'], ['lbt'])
                pidx = T2("pidx", [64, 1], F32); i16 = T2("i16", [64, 1], F32); mrow = T2("mrow", [64, 1], F32)
                arow = T2("arow", [64, 1], F32); acol = T2("acol", [64, 1], F32)
                rowpos = T2("rowpos", [64, N], F32); colpos = T2("colpos", [64, N], F32); ang = T2("ang", [64, N], F32)
                S.op('pool', lambda e: e.iota(pidx[:], [[0, 1]], base=0, channel_multiplier=1,
                                              allow_small_or_imprecise_dtypes=True), (), ['pidx'])
                S.op('pool', lambda e: e.iota(rowpos[:], [[1, 64], [0, 64]], base=0, channel_multiplier=0,
                                              allow_small_or_imprecise_dtypes=True), (), ['rowpos'])
                S.op('pool', lambda e: e.iota(colpos[:], [[0, 64], [1, 64]], base=0, channel_multiplier=0,
                                              allow_small_or_imprecise_dtypes=True), (), ['colpos'])
                msk = T2("msk", [64, 3], F32)
                S.memset('pool', msk[:], 1.0, ['msk'])
                for j_ in range(3):
                    S.op('pool', (lambda jj: (lambda e: e.affine_select(
                        out=msk[:, jj:jj + 1], in_=msk[:, jj:jj + 1], pattern=[[0, 1]], compare_op=ALU.is_ge, fill=0.0,
                        base=-16 * (jj + 1), channel_multiplier=1)))(j_), ['msk'], ['msk'])
                S.tt('dve', mrow[:], msk[:, 0:1], msk[:, 1:2], ALU.add, ['msk'], ['mrow'])
                S.tt('dve', mrow[:], mrow[:], msk[:, 2:3], ALU.add, ['msk', 'mrow'], ['mrow'])
                S.stt('dve', i16[:], mrow[:], -16.0, pidx[:], ALU.mult, ALU.add, ['mrow', 'pidx'], ['i16'])
                S.tt('dve', mrow[:], msk[:, 1:2], msk[:, 0:1], ALU.subtract, ['msk'], ['mrow'])
                S.tt('dve', mrow[:], mrow[:], msk[:, 2:3], ALU.subtract, ['msk', 'mrow'], ['mrow'])
                S.ts('dve', mrow[:], mrow[:], 1.0, None, ALU.add, None, ['mrow'], ['mrow'])
                S.act(i16[:], i16[:], AF.Exp, ['i16'], ['i16'], scale=-math.log(10000.0) / 16.0)
                S.tt('dve', arow[:], i16[:], mrow[:], ALU.mult, ['i16', 'mrow'], ['arow'])
                S.tt('dve', acol[:], i16[:], arow[:], ALU.subtract, ['i16', 'arow'], ['acol'])
                S.ts('dve', ang[:], rowpos[:], arow[:, 0:1], None, ALU.mult, None, ['rowpos', 'arow'], ['ang'])
                S.stt('dve', ang[:], colpos[:], acol[:, 0:1], ang[:], ALU.mult, ALU.add, ['colpos', 'acol', 'ang'], ['ang'])
                sc_ = 1.0 - 1e-6
                ki = T2("ki", [64, N], mybir.dt.int32)
                for tab, shift in ((St, 0.0), (Ct, 0.5 * math.pi)):
                    S.ts('dve', rowpos[:], ang[:], shift, 1.0 / (2 * math.pi), ALU.add, ALU.mult, ['ang'], ['rowpos'])
                    S.copy('dve', ki[:], rowpos[:], ['rowpos'], ['ki'])
                    S.copy('dve', colpos[:], ki[:], ['ki'], ['colpos'])
                    S.ts('dve', rowpos[:], ang[:], shift, None, ALU.add, None, ['ang'], ['rowpos'])
                    S.stt('dve', rowpos[:], colpos[:], -2 * math.pi, rowpos[:], ALU.mult, ALU.add, ['colpos', 'rowpos'], ['rowpos'])
                    S.act(tab[:], rowpos[:], AF.Sin, ['rowpos'], ['St' if shift == 0.0 else 'Ct'], scale=sc_)
                if stop_after == 15:
                    for nm, tl, shp, dt_ in (("d_Ct", Ct, [64, N], F32), ("d_St", St, [64, N], F32),
                                             ("d_Wkpe", Wkpe, [128, 8, 128], BF16), ("d_Wkrot", Wkrot, [128, 8, 128], BF16),
                                             ("d_Wqn", Wqn, [128, 2, 4, 128], BF16), ("d_Wqr", Wqr, [128, 2, 4, 128], BF16),
                                             ("d_Wqrot", Wqrot, [128, 2, 4, 128], BF16), ("d_Wkn", Wkn, [128, 2, 4, 128], BF16),
                                             ("d_Wv", Wv, [128, 2, 4, 128], BF16), ("d_lbt", lbt, [128, 2, 512], F32),
                                             ("d_oml", oml, [128, 2, 512], F32), ("d_Win", Win, [128, 8, INC], BF16)):
                        dd = nc.dram_tensor(nm, shp, dt_, kind="ExternalOutput").ap()
                        S.dma('sp', dd, tl[:], [nm[2:]], ())
                S.emit()
                if stop_after == 15:
                    return nc

            hTg = Rot(TT, "hTg", [128, 8, 512], BF16, 2)
            cT = TT("cT", [128, 2, 512], BF16); sq = TT("sq", [128, 2, 512], BF16)
            rbc = TT("rbc", [128, 512], F32); rtk = TT("rtk", [128, 4], F32)
            o_bf = Rot(TT, "o_bf", [128, 512], BF16, 3)
            o_f = Rot(TT, "o_f", [128, 512], F32, 4)
            u_f = Rot(TT, "u_f", [64, 512], F32, 4)
            sgb = Rot(TT, "sgb", [128, 512], F32, 3); fb_ = Rot(TT, "fb_", [128, 512], F32, 3)
            pA = [PP(f"pA{i}", [128, 512], F32) for i in range(2)]
            pB = [PP(f"pB{i}", [128, 512], F32) for i in range(2)]
            pS = PP("pS", [128, 512], F32)
            pT = [PP(f"pT{i}", [128, 512], F32) for i in range(2)]
            pV = PP("pV", [128, 4, 128], F32)
            groups = [(0, 256)] + [(256 + i * 512, 512) for i in range(8)]
            if a2_groups is not None:
                groups = groups[:a2_groups]
            for (tok0, n) in groups:
                is_lat = tok0 >= L
                lo = tok0 - L
                nsub = n // 128
                S.enabled = True
                hT_, hk = hTg.next()
                S.dma('sp', hT_[:, :, :n], HT[:, :, tok0:tok0 + n], (), [hk])

                def fm_proj(ps, pk, wt, wk, col0, ncols):
                    for k in range(8):
                        S.mm(ps[0:ncols, :n], wt[:, k, col0:col0 + ncols], hT_[:, k, :n], k == 0, k == 7, [wk, hk], [pk])

                def lowrank(col0, want_tok):
                    for c_ in range(2):
                        fm_proj(pA[c_], f'pA{c_}', Win, 'Win', col0 + c_ * 128, 128)
                        S.copy('act', cT[:, c_, :n], pA[c_][:, :n], [f'pA{c_}'], ['cT'])
                        S.act(sq[:, c_, :n], pA[c_][:, :n], AF.Square, [f'pA{c_}'], ['sq'])
                    for c_ in range(2):
                        S.mm(pS[:, :n], onesb[:], sq[:, c_, :n], c_ == 0, c_ == 1, ['onesb', 'sq'], ['pS'])
                    S.act(rbc[:, :n], pS[:, :n], AF.Sqrt, ['pS'], ['rbc'], bias=EPS, scale=1.0 / 256)
                    S.op('dve', (lambda nn: (lambda e: e.reciprocal(out=rbc[:, :nn], in_=rbc[:, :nn])))(n), ['rbc'], ['rbc'])
                    if want_tok:
                        for s_ in range(nsub):
                            for c_ in range(2):
                                S.mm(pV[:, s_, :], sq[:, c_, s_ * 128:(s_ + 1) * 128], onesb[:], c_ == 0, c_ == 1,
                                     ['sq', 'onesb'], ['pV'])
                        S.act(rtk[:, :nsub], pV[:, :nsub, 0], AF.Sqrt, ['pV'], ['rtk'], bias=EPS, scale=1.0 / 256)
                        S.op('dve', (lambda ns: (lambda e: e.reciprocal(out=rtk[:, :ns], in_=rtk[:, :ns])))(nsub), ['rtk'], ['rtk'])

                def rope_out(p0, k0, p1, k1, dst, scale_rows):
                    u1, uk1 = u_f.next(); u2, uk2 = u_f.next()
                    S.tt('dve', u1[:, :n], p0[0:64, :n], Ct[:, lo:lo + n], ALU.mult, [k0, 'Ct'], [uk1])
                    S.tt('dve', u2[:, :n], p1[0:64, :n], St[:, lo:lo + n], ALU.mult, [k1, 'St'], [uk2])
                    ob, ok = o_bf.next()
                    if scale_rows:
                        S.tt('pool', u1[:, :n], u1[:, :n], u2[:, :n], ALU.add, [uk1, uk2], [uk1])
                        S.tt('pool', ob[0:64, :n], u1[:, :n], rbc[0:64, :n], ALU.mult, [uk1, 'rbc'], [ok])
                    else:
                        S.tt('pool', ob[0:64, :n], u1[:, :n], u2[:, :n], ALU.add, [uk1, uk2], [ok])
                    S.dma(STQ, dst, ob[0:64, :n], [ok], ())

                if is_lat and 'a' in a2_parts:
                    lowrank(0, False)
                    for h in range(4):
                        pb, pk = pB[h % 2], f'pB{h % 2}'
                        for c_ in range(2):
                            S.mm(pb[:, :n], Wqn[:, c_, h, :], cT[:, c_, :n], c_ == 0, c_ == 1, ['Wqn', 'cT'], [pk])
                        ob, ok = o_bf.next()
                        S.tt('dve', ob[:, :n], pb[:, :n], rbc[:, :n], ALU.mult, [pk, 'rbc'], [ok])
                        S.dma(STQ, QN[h][:, lo:lo + n], ob[:, :n], [ok], ())
                    for h in range(4):
                        for c_ in range(2):
                            S.mm(pB[0][:, :n], Wqr[:, c_, h, :], cT[:, c_, :n], c_ == 0, c_ == 1, ['Wqr', 'cT'], ['pB0'])
                        for c_ in range(2):
                            S.mm(pB[1][:, :n], Wqrot[:, c_, h, :], cT[:, c_, :n], c_ == 0, c_ == 1, ['Wqrot', 'cT'], ['pB1'])
                        rope_out(pB[0], 'pB0', pB[1], 'pB1', QR[h][:, lo:lo + n], True)
                S.enabled = 'b' in a2_parts
                lowrank(256, True)
                for h in range(4):
                    pb, pk = pB[h % 2], f'pB{h % 2}'
                    for c_ in range(2):
                        S.mm(pb[:, :n], Wkn[:, c_, h, :], cT[:, c_, :n], c_ == 0, c_ == 1, ['Wkn', 'cT'], [pk])
                    ob, ok = o_bf.next()
                    S.tt('dve', ob[:, :n], pb[:, :n], rbc[:, :n], ALU.mult, [pk, 'rbc'], [ok])
                    S.dma(STQ, KN[h][:, tok0:tok0 + n], ob[:, :n], [ok], ())
                Wv2 = Wv[:].rearrange("p c h d -> p c (h d)")
                for s_ in range(nsub):
                    pt, pk = pT[s_ % 2], f'pT{s_ % 2}'
                    for c_ in range(2):
                        S.mm(pt[:], cT[:, c_, s_ * 128:(s_ + 1) * 128], Wv2[:, c_, :], c_ == 0, c_ == 1, ['cT', 'Wv'], [pk])
                    ob, ok = o_bf.next()
                    S.act(ob[:], pt[:], AF.Copy, [pk, 'rtk'], [ok], scale=rtk[:, s_:s_ + 1])
                    S.dma(STQ, VV[tok0 + s_ * 128:tok0 + (s_ + 1) * 128, :], ob[:], [ok], ())
                S.enabled = 'c' in a2_parts
                fm_proj(pA[0], 'pA0', Wkpe, 'Wkpe', 0, 128)
                if is_lat:
                    fm_proj(pA[1], 'pA1', Wkrot, 'Wkrot', 0, 128)
                    rope_out(pA[0], 'pA0', pA[1], 'pA1', KR[:, tok0:tok0 + n], False)
                else:
                    ob, ok = o_bf.next()
                    S.copy('act', ob[0:64, :n], pA[0][0:64, :n], ['pA0'], [ok])
                    S.dma(STQ, KR[:, tok0:tok0 + n], ob[0:64, :n], [ok], ())
                S.enabled = 'd' in a2_parts
                for h in range(4):
                    pa, pk = pA[h % 2], f'pA{h % 2}'
                    fm_proj(pa, pk, Win, 'Win', 576 + h * 128, 128)
                    of, ok = o_f.next()
                    S.copy('act' if h % 2 else 'dve', of[:, :n], pa[:, :n], [pk], [ok])
                    S.dma(STQ, HQ[h][:, tok0:tok0 + n], of[:, :n], [ok], ())
                S.enabled = 'e' in a2_parts
                for s_ in range(nsub):
                    row0 = tok0 + s_ * 128

                    def tm_proj(ps, pk, col0):
                        for k in range(8):
                            S.mm(ps[:], hT_[:, k, s_ * 128:(s_ + 1) * 128], Win[:, k, col0:col0 + 512], k == 0, k == 7,
                                 [hk, 'Win'], [pk])
                    tm_proj(pT[0], 'pT0', 1088)
                    ob, ok = o_bf.next()
                    S.copy('dve', ob[:], pT[0][:], ['pT0'], [ok])
                    S.dma(STQ, HV[row0:row0 + 128, :], ob[:], [ok], ())
                    if is_lat:
                        tm_proj(pT[1], 'pT1', 1600)
                        th, thk = sgb.next(); uh, uhk = fb_.next()
                        S.act(th[:], pT[1][:], AF.Tanh, ['pT1'], [thk], scale=0.5)
                        S.act(uh[:], pT[1][:], AF.Copy, ['pT1'], [uhk], scale=0.5)
                        of, ok = o_f.next()
                        S.tt('pool', th[:], th[:], uh[:], ALU.mult, [thk, uhk], [thk])
                        S.tt('pool', of[:], th[:], uh[:], ALU.add, [thk, uhk], [ok])
                        S.dma(STQ, HG[row0 - L:row0 - L + 128, :], of[:], [ok], ())
                    fts = []
                    for d_ in range(2):
                        pt, pk = pT[d_], f'pT{d_}'
                        tm_proj(pt, pk, 2112 + d_ * 512)
                        sg, sk = sgb.next(); ff, fk = fb_.next()
                        S.act(sg[:], pt[:], AF.Tanh, [pk], [sk], scale=0.5)
                        S.tt('dve', ff[:], sg[:], oml[:, d_, :], ALU.mult, [sk, 'oml'], [fk])
                        S.tt('pool', ff[:], ff[:], lbt[:, d_, :], ALU.add, [fk, 'lbt'], [fk])
                        fts.append((ff, fk))
                    for d_ in range(2):
                        ff, fk = fts[d_]
                        of, ok = o_f.next()
                        S.act(of[:], ff[:], AF.Ln, [fk], [ok])
                        S.dma(STQ, GG[d_][row0:row0 + 128, :], of[:], [ok], ())
                        of2, ok2 = o_f.next()
                        S.ts('pool', of2[:], ff[:], -1.0, 1.0, ALU.mult, ALU.add, [fk], [ok2])
                        S.dma(STQ, KG[d_][row0:row0 + 128, :], of2[:], [ok2], ())
            S.enabled = True
            S.emit()
        wstack.close()
        if stop_after == 2:
            return nc

        with ExitStack() as es:
            def TT(name, shape, dt):
                return es.enter_context(nc.sbuf_tensor(name, shape, dt))

            def PP(name, shape, dt):
                return es.enter_context(nc.psum_tensor(name, shape, dt))

            bt = []
            for d_ in range(2):
                bt.append(dict(
                    g=Rot(TT, f"bg{d_}", [128, 512], F32, 4), kg=Rot(TT, f"bkg{d_}", [128, 512], F32, 4),
                    v=Rot(TT, f"bv{d_}", [128, 512], BF16, 5), hq=Rot(TT, f"bhq{d_}", [128, 4, 128], F32, 4),
                    Ek=Rot(TT, f"bEk{d_}", [128, 512], F32, 2), EqT=Rot(TT, f"bEq{d_}", [128, 4, 128], F32, 2),
                    eb=Rot(TT, f"beb{d_}", [128, 4, 2], F32, 3), K2=Rot(TT, f"bK2{d_}", [128, 512], BF16, 2),
                    K2m=[Rot(TT, f"bK2m{c_}{d_}", [128, 512], BF16, 2) for c_ in range(2)],
                    QsT=Rot(TT, f"bQs{d_}", [128, 4, 128], BF16, 2),
                    Qsm=[Rot(TT, f"bQsm{c_}{d_}", [128, 4, 128], BF16, 2) for c_ in range(2)],
                    K2T=Rot(TT, f"bK2T{d_}", [128, 4, 128], BF16, 2),
                    Am=Rot(TT, f"bAm{d_}", [128, 4, 128], BF16, 2),
                    S=[TT(f"bS{d_}_{j_}", [128, 4, 128], F32) for j_ in range(2)], cur=[0],
                    Sp=Rot(TT, f"bSp{d_}", [128, 4, 128], F32, 2), Spb=Rot(TT, f"bSpb{d_}", [128, 4, 128], BF16, 4),
                    osb=Rot(TT, f"bos{d_}", [128, 512], F32, 2)))
                S.memset('dve', bt[d_]['S'][0][:], 0.0, [f'bS{d_}_0h{h}' for h in range(4)])
                for c_ in range(2):
                    oth = slice(64, 128) if c_ == 0 else slice(0, 64)
                    for j_ in range(2):
                        S.memset('pool', bt[d_]['K2m'][c_].t[j_][oth, :], 0.0, [bt[d_]['K2m'][c_].k[j_]])
                        S.memset('pool', bt[d_]['Qsm'][c_].t[j_][:, :, oth], 0.0, [bt[d_]['Qsm'][c_].k[j_]])
            pD1 = PP("pD1", [128, 512], F32); pD2 = PP("pD2", [128, 4, 128], F32)
            pKT = PP("pKT", [128, 4, 128], BF16); pBL = PP("pBL", [128, 4, 2], F32)
            pAT = PP("pAT", [128, 4, 128], F32)
            pOb = PP("pOb", [128, 4, 128], F32)
            pSNr = Rot(PP, "pSN", [128, 4, 128], F32, 2)

            def hgrn_load(cx, ti, d_):
                B_ = bt[d_]
                row0 = ti * 128
                g, gk = B_['g'].next(); kg, kgk = B_['kg'].next(); v, vk = B_['v'].next(); hq, hqk = B_['hq'].next()
                S.dma('sp', g[:], GG[d_][row0:row0 + 128, :], (), [gk])
                S.dma('sp', kg[:], KG[d_][row0:row0 + 128, :], (), [kgk])
                S.dma('sp', v[:], HV[row0:row0 + 128, :], (), [vk])
                S.dma('sp', hq[:], HQ[:, :, row0:row0 + 128].rearrange("h p t -> p h t"), (), [hqk])
                cx['ld'] = (g, gk, kg, kgk, v, vk, hq, hqk)

            def hgrn_pre(cx, ti, d_):
                B_ = bt[d_]
                is_lat = ti >= 2
                row0 = ti * 128
                U_, R_, M_ = (Uf, Rf, Mf) if d_ == 0 else (Ub, Rb, Mb)
                uk, rk, mk_ = ('Uf', 'Rf', 'Mf') if d_ == 0 else ('Ub', 'Rb', 'Mb')
                g, gk, kg, kgk, v, vk, hq, hqk = cx['ld']
                Ek, Ekk = B_['Ek'].next(); EqT, Eqk = B_['EqT'].next(); eb, ebk = B_['eb'].next()
                K2, K2k = B_['K2'].next(); QsT, Qsk = B_['QsT'].next(); K2T, K2Tk = B_['K2T'].next()
                S.mm(pD1[:], U_[:], g[:], True, True, [uk, gk], ['pD1'])
                for h in range(4):
                    S.mm(pD2[:, h, :], g[:, h * 128:(h + 1) * 128], R_[:], True, True, [gk, rk], ['pD2'])
                for h in range(4):
                    S.mm(pBL[:, h, :], g[:, h * 128:(h + 1) * 128], Ind[:], True, True, [gk, 'Ind'], ['pBL'])
                S.act(Ek[:], pD1[:], AF.Exp, ['pD1'], [Ekk])
                S.act(EqT[:], pD2[:], AF.Exp, ['pD2'], [Eqk])
                S.act(eb[:], pBL[:], AF.Exp, ['pBL'], [ebk])
                S.tt('dve', K2[:], kg[:], Ek[:], ALU.mult, [kgk, Ekk], [K2k])
                K2m = []
                for c_ in range(2):
                    rs_ = slice(c_ * 64, (c_ + 1) * 64)
                    km, kmk = B_['K2m'][c_].next()
                    S.copy('act', km[rs_, :], K2[rs_, :], [K2k], [kmk])
                    K2m.append((km, kmk))
                cx.update(v=v, vk=vk, eb=eb, ebk=ebk, K2m=K2m)
                if is_lat:
                    S.tt('pool', QsT[:], hq[:], EqT[:], ALU.mult, [hqk, Eqk], [Qsk])
                    Qsm = []
                    for c_ in range(2):
                        cs_ = slice(c_ * 64, (c_ + 1) * 64)
                        qm, qmk = B_['Qsm'][c_].next()
                        S.copy('pool', qm[:, :, cs_], QsT[:, :, cs_], [Qsk], [qmk])
                        Qsm.append((qm, qmk))
                    Am, Amk = B_['Am'].next()
                    for h in range(4):
                        S.tr(pKT[:, h, :], K2[:, h * 128:(h + 1) * 128], identb[:], [K2k, 'identb'], ['pKT'])
                    S.copy('act', K2T[:], pKT[:], ['pKT'], [K2Tk])
                    for h in range(4):
                        S.mm(pAT[:, h, :], K2T[:, h, :], QsT[:, h, :], True, True, [K2Tk, Qsk], ['pAT'])
                    S.tt('dve', Am[:], pAT[:], M_[:].unsqueeze(1).to_broadcast([128, 4, 128]), ALU.mult,
                         ['pAT', mk_], [Amk])
                    cx.update(Am=Am, Amk=Amk, Qsm=Qsm)

            def hgrn_chain(cx, ti, d_):
                B_ = bt[d_]
                is_lat = ti >= 2
                row0 = ti * 128
                v, vk, eb, ebk, K2m = (cx[k_] for k_ in ('v', 'vk', 'eb', 'ebk', 'K2m'))
                order = (0, 1) if d_ == 0 else (1, 0)
                spbs = {}
                for c_ in order:
                    ci = B_['cur'][0]
                    Sc, Sn = B_['S'][ci], B_['S'][1 - ci]
                    Sck = [f'bS{d_}_{ci}h{h}' for h in range(4)]
                    Snk = [f'bS{d_}_{1 - ci}h{h}' for h in range(4)]
                    B_['cur'][0] = 1 - ci
                    km, kmk = K2m[c_]
                    psn, psnk = pSNr.next()
                    for h in range(4):
                        S.mm(psn[:, h, :], km[:, h * 128:(h + 1) * 128], v[:, h * 128:(h + 1) * 128], True, True,
                             [kmk, vk], [psnk])
                    for h in range(4):
                        S.stt('dve', Sn[:, h, :], Sc[:, h, :], eb[:, h, c_:c_ + 1], psn[:, h, :], ALU.mult, ALU.add,
                              [Sck[h], ebk, psnk], [Snk[h]])
                    if is_lat:
                        Spb, Spbk = B_['Spb'].next()
                        S.tt('pool', Spb[:], Sc[:], eb[:, :, c_:c_ + 1].to_broadcast([128, 4, 128]), ALU.mult,
                             Sck + [ebk], [Spbk])
                        spbs[c_] = (Spb, Spbk)
                if is_lat:
                    Am, Amk, Qsm = cx['Am'], cx['Amk'], cx['Qsm']
                    for h in range(4):
                        S.mm(pOb[:, h, :], Am[:, h, :], v[:, h * 128:(h + 1) * 128], True, False, [Amk, vk], ['pOb'])
                        for n_, c_ in enumerate(order):
                            qm, qmk = Qsm[c_]
                            Spb, Spbk = spbs[c_]
                            S.mm(pOb[:, h, :], qm[:, h, :], Spb[:, h, :], False, n_ == 1, [qmk, Spbk], ['pOb'])
                    ob, obk = B_['osb'].next()
                    S.copy('act', ob[:], pOb[:].rearrange("p h d -> p (h d)"), ['pOb'], [obk])
                    S.dma('sp', OO[d_][row0 - L:row0 - L + 128, :], ob[:], [obk], ())

            fwd_order = list(range(NT))
            bwd_order = [1, 0] + list(range(NT - 1, 1, -1))
            cxs = [[dict() for _ in range(NT)] for _ in range(2)]
            orders = (fwd_order, bwd_order)
            for i_ in range(2):
                for d_ in range(2):
                    hgrn_load(cxs[d_][i_], orders[d_][i_], d_)
            for d_ in range(2):
                hgrn_pre(cxs[d_][0], orders[d_][0], d_)
            for i_ in range(NT):
                for d_ in range(2):
                    if i_ + 2 < NT:
                        hgrn_load(cxs[d_][i_ + 2], orders[d_][i_ + 2], d_)
                for d_ in range(2):
                    if i_ + 1 < NT:
                        hgrn_pre(cxs[d_][i_ + 1], orders[d_][i_ + 1], d_)
                for d_ in range(2):
                    hgrn_chain(cxs[d_][i_], orders[d_][i_], d_)
            S.emit()
        if stop_after == 3:
            return nc

        with ExitStack() as es:
            def TT(name, shape, dt):
                return es.enter_context(nc.sbuf_tensor(name, shape, dt))

            def PP(name, shape, dt):
                return es.enter_context(nc.psum_tensor(name, shape, dt))

            KNs = TT("KNs", [128, 4, T], BF16); KRs = TT("KRs", [128, T], BF16); Vs = TT("Vs", [128, NT, 512], BF16)
            sqKR = TT("sqKR", [64, T], BF16)
            sqn = TT("sqn", [128, 512], BF16); sqr = TT("sqr", [64, 512], BF16)
            sqnR = Rot(TT, "csqn", [128, 512], BF16, 2); sqrR = Rot(TT, "csqr", [64, 512], BF16, 2)
            km2 = TT("km2", [128, 4], F32); tmx = TT("tmx", [128, 1], F32)
            tmxR = Rot(TT, "ctmx", [128, 1], F32, 3); nshR = Rot(TT, "cnsh", [128, 1], F32, 3)
            qnr = Rot(TT, "cqn", [128, 512], BF16, 3); qrr = Rot(TT, "cqr", [128, 512], BF16, 3)
            PTr = Rot(TT, "cPT", [128, 1024], BF16, 4); osr = Rot(TT, "cos", [128, 512], BF16, 2)
            rinv = TT("rinv", [128, 512], F32)
            raccR = [Rot(TT, f"racc{i}_", [128, 1024], F32, 2) for i in range(2)]
            rsumR = [Rot(TT, f"rsum{i}_", [128, 512], F32, 2) for i in range(2)]
            pScR = Rot(PP, "pSc", [128, 1024], F32, 2)
            pOaR = Rot(PP, "pOa", [128, 512], F32, 2)
            pMiR = Rot(PP, "pMi", [128, 512], F32, 2)
            pNm = pMiR.t[0]
            for h in range(4):
                S.dma('sp', KNs[:, h, :], KN[h], (), ['KNs'])
            S.memset('pool', KRs[64:128, :], 0.0, ['KRs'])
            S.dma('sp', KRs[0:64, :], KR, (), ['KRs'])
            for i_ in range(3):
                S.memset('pool', qrr.t[i_][64:128, :], 0.0, [qrr.k[i_]])
            VVv = VV.rearrange("(t p) n -> p t n", p=128)
            for j in range(0, NT, 4):
                je = min(NT, j + 4)
                S.dma('sp', Vs[:, j:je, :], VVv[:, j:je, :], (), ['Vs'])
            S.memset('dve', km2[:], 0.0, ['km2'])
            S.act(sqKR[:], KRs[0:64, :], AF.Square, ['KRs'], ['sqKR'])
            for j0 in range(0, T, 512):
                w_ = min(512, T - j0)
                for h in range(4):
                    S.act(sqn[:, :w_], KNs[:, h, j0:j0 + w_], AF.Square, ['KNs'], ['sqn'])
                    S.mm(pNm[:, :w_], onesb[:], sqn[:, :w_], True, False, ['onesb', 'sqn'], ['pMi0'])
                    S.mm(pNm[:, :w_], onesb[0:64, :], sqKR[:, j0:j0 + w_], False, True, ['onesb', 'sqKR'], ['pMi0'])
                    S.op('dve', (lambda ww: (lambda e: e.reduce_max(out=tmx[:], in_=pNm[:, :ww], axis=AX.X)))(w_),
                         ['pMi0'], ['tmx'])
                    S.tt('dve', km2[:, h:h + 1], km2[:, h:h + 1], tmx[:], ALU.max, ['km2', 'tmx'], ['km2'])
            items = [(g_, h) for g_ in range(8) for h in range(4)]
            cxs = [dict() for _ in items]
            NP_ = NT // 2

            def c_pro(i):
                g_, h = items[i]
                q0 = g_ * 512
                cx = cxs[i]
                qn, qnk = qnr.next(); qr, qrk = qrr.next()
                sqn_, sqnk = sqnR.next(); sqr_, sqrk = sqrR.next(); tm_, tmk = tmxR.next(); nsh, nshk = nshR.next()
                pm, pmk = pMiR.next()
                S.dma('sp', qn[:], QN[h][:, q0:q0 + 512], (), [qnk])
                S.dma('sp', qr[0:64, :], QR[h][:, q0:q0 + 512], (), [qrk])
                S.act(sqn_[:], qn[:], AF.Square, [qnk], [sqnk])
                S.act(sqr_[:], qr[0:64, :], AF.Square, [qrk], [sqrk])
                S.mm(pm[:], onesb[:], sqn_[:], True, False, ['onesb', sqnk], [pmk])
                S.mm(pm[:], onesb[0:64, :], sqr_[:], False, True, ['onesb', sqrk], [pmk])
                S.op('dve', (lambda o_, i_: (lambda e: e.reduce_max(out=o_[:], in_=i_[:], axis=AX.X)))(tm_, pm), [pmk], [tmk])
                S.ts('dve', nsh[:], tm_[:], km2[:, h:h + 1], -0.5 * SCALE, ALU.add, ALU.mult, [tmk, 'km2'], [nshk])
                cx.update(qn=qn, qnk=qnk, qr=qr, qrk=qrk, nsh=nsh, nshk=nshk, ps={})

            def c_qk(i, j):
                g_, h = items[i]
                cx = cxs[i]
                ps, pk = pScR.next()
                cx['ps'][j] = (ps, pk)
                for u_ in range(2):
                    kt = 2 * j + u_
                    S.mm(ps[:, u_ * 512:(u_ + 1) * 512], KNs[:, h, kt * 128:(kt + 1) * 128], cx['qn'][:], True, False,
                         ['KNs', cx['qnk']], [pk])
                    S.mm(ps[:, u_ * 512:(u_ + 1) * 512], KRs[:, kt * 128:(kt + 1) * 128], cx['qr'][:], False, True,
                         ['KRs', cx['qrk']], [pk])

            def c_main(i):
                g_, h = items[i]
                cx = cxs[i]
                pOa, pOak = pOaR.next()
                racc = [raccR[0].next(), raccR[1].next()]
                cx.update(pOa=pOa, pOak=pOak, racc=racc)
                for j in range(NP_):
                    if j + 1 < NP_:
                        c_qk(i, j + 1)
                    ps, pk = cx['ps'].pop(j)
                    PT, ptk = PTr.next()
                    S.act(PT[:], ps[:], AF.Exp, [pk, cx['nshk']], [ptk], bias=cx['nsh'][:], scale=SCALE)
                    for u_ in range(2):
                        kt = 2 * j + u_
                        S.mm(pOa[:], Vs[:, kt, h * 128:(h + 1) * 128], PT[:, u_ * 512:(u_ + 1) * 512],
                             kt == 0, kt == NT - 1, ['Vs', ptk], [pOak])
                    ae = 'dve' if j % 2 == 0 else 'pool'
                    ra, rak = racc[j % 2]
                    if j < 2:
                        S.copy(ae, ra[:], PT[:], [ptk], [rak])
                    else:
                        S.tt(ae, ra[:], ra[:], PT[:], ALU.add, [rak, ptk], [rak])

            def c_epi(i):
                g_, h = items[i]
                q0 = g_ * 512
                cx = cxs[i]
                pOa, pOak, racc = cx['pOa'], cx['pOak'], cx['racc']
                rs0, rs0k = rsumR[0].next(); rs1, rs1k = rsumR[1].next()
                pm, pmk = pMiR.next()
                S.tt('dve', rs0[:], racc[0][0][:, 0:512], racc[0][0][:, 512:1024], ALU.add, [racc[0][1]], [rs0k])
                S.tt('pool', rs1[:], racc[1][0][:, 0:512], racc[1][0][:, 512:1024], ALU.add, [racc[1][1]], [rs1k])
                S.mm(pm[:], onesf[:], rs0[:], True, False, ['onesf', rs0k], [pmk])
                S.mm(pm[:], onesf[:], rs1[:], False, True, ['onesf', rs1k], [pmk])
                S.op('dve', (lambda i_: (lambda e: e.reciprocal(out=rinv[:], in_=i_[:])))(pm), [pmk], ['rinv'])
                ob, obk = osr.next()
                S.tt('dve', ob[:], pOa[:], rinv[:], ALU.mult, [pOak, 'rinv'], [obk])
                S.dma('sp', MIXA[h][:, q0:q0 + 512], ob[:], [obk], ())
                cxs[i] = None

            c_pro(0)
            c_qk(0, 0)
            for i in range(len(items)):
                if i + 1 < len(items):
                    c_pro(i + 1)
                c_main(i)
                if i + 1 < len(items):
                    c_qk(i + 1, 0)
                c_epi(i)
            S.emit()
        if stop_after == 4:
            return nc

        with ExitStack() as es:
            def TT(name, shape, dt):
                return es.enter_context(nc.sbuf_tensor(name, shape, dt))

            def PP(name, shape, dt):
                return es.enter_context(nc.psum_tensor(name, shape, dt))

            Wout = TT("Wout", [128, 8, D], BF16)
            mD = TT("mD", [128, 3, D], F32)
            gon = TT("gon", [128, 128], F32)
            woutv = w_out.rearrange("(k p) n -> p k n", p=128)
            for k in range(8):
                S.dma('pool', Wout[:, k, :], woutv[:, k, :], (), ['Wout'])
            for i, mi in enumerate((4, 5, 6)):
                S.dma('sp', mD[:, i, :], MOD[mi], (), ['mD'])
            S.dma('sp', gon[:], g_on.partition_broadcast(128), (), ['gon'])
            ofr = Rot(TT, "dof", [128, 512], F32, 3); obr = Rot(TT, "dob", [128, 512], F32, 3); hgr = Rot(TT, "dhg", [128, 512], F32, 3)
            mar = Rot(TT, "dma_", [128, 4, 128], BF16, 4); xtr = Rot(TT, "dxt", [128, D], F32, 5)
            osumr = Rot(TT, "osum", [128, 512], F32, 2); osqr = Rot(TT, "osq", [128, 512], F32, 2); ss4r = Rot(TT, "ss4", [128, 4], F32, 3)
            tBr = Rot(TT, "tB", [128, 512], F32, 2); hgbr = Rot(TT, "hgb", [128, 512], BF16, 3); mixBr = Rot(TT, "mixB", [128, 4, 128], BF16, 2)
            tmpDr = Rot(TT, "tmpD", [128, D], F32, 2); x1r = Rot(TT, "dx1", [128, D], F32, 3)
            junkD = TT("junkD", [128, D], BF16); ssDr = Rot(TT, "ssD", [128, 1], F32, 3)
            t2Dr = Rot(TT, "t2D", [128, D], F32, 2); h2r = Rot(TT, "h2", [128, D], BF16, 3); h2Tr = Rot(TT, "dh2T", [128, 8, 128], BF16, 3)
            pTBr = Rot(PP, "pTB", [128, 4, 128], BF16, 2)
            pLOr = Rot(PP, "pLO", [128, 512], F32, 4)
            pT8r = Rot(PP, "pT8", [128, 8, 128], BF16, 2)

            def d1_s0(cx, ti):
                r0 = ti * 128
                for nm, rr, src in (('of', ofr, OO[0][r0:r0 + 128, :]), ('ob', obr, OO[1][r0:r0 + 128, :]),
                                    ('hg', hgr, HG[r0:r0 + 128, :]),
                                    ('ma', mar, MIXA[:, :, r0:r0 + 128].rearrange("h p t -> p h t")),
                                    ('xt', xtr, x[r0:r0 + 128, :])):
                    cx[nm], cx[nm + 'k'] = rr.next()
                    S.dma('sp', cx[nm][:], src, (), [cx[nm + 'k']])

            def d1_s1(cx, ti):
                of, ofk, ob, obk, hg, hgk = cx['of'], cx['ofk'], cx['ob'], cx['obk'], cx['hg'], cx['hgk']
                osum, osumk = osumr.next(); osq, osqk = osqr.next(); ss4, ss4k = ss4r.next(); tB, tBk = tBr.next()
                hgb, hgbk = hgbr.next()
                cx['hgb'], cx['hgbk'] = hgb, hgbk
                S.tt('pool', osum[:], of[:], ob[:], ALU.add, [ofk, obk], [osumk])
                S.tt('pool', osq[:], osum[:], osum[:], ALU.mult, [osumk], [osqk])
                S.op('dve', (lambda o_, i_: (lambda e: e.reduce_sum(out=o_[:], in_=i_[:].rearrange("p (h d) -> p h d", h=4),
                                                                    axis=AX.X)))(ss4, osq), [osqk], [ss4k])
                S.act(ss4[:], ss4[:], AF.Sqrt, [ss4k], [ss4k], bias=EPS, scale=1.0 / 128)
                S.op('dve', (lambda o_: (lambda e: e.reciprocal(out=o_[:], in_=o_[:])))(ss4), [ss4k], [ss4k])
                o3 = osum[:].rearrange("p (h d) -> p h d", h=4)
                t3 = tB[:].rearrange("p (h d) -> p h d", h=4)
                S.tt('dve', t3, o3, ss4[:].unsqueeze(2).to_broadcast([128, 4, 128]), ALU.mult, [osumk, ss4k], [tBk])
                S.tt('pool', t3, t3, gon[:].unsqueeze(1).to_broadcast([128, 4, 128]), ALU.mult, [tBk, 'gon'], [tBk])
                S.tt('pool', hgb[:], tB[:], hg[:], ALU.mult, [tBk, hgk], [hgbk])

            def d1_s2(cx, ti):
                hgb, hgbk, ma, mak = cx['hgb'], cx['hgbk'], cx['ma'], cx['mak']
                pTB, pTBk = pTBr.next(); mixB, mixBk = mixBr.next()
                for h in range(4):
                    S.tr(pTB[:, h, :], hgb[:, h * 128:(h + 1) * 128], identb[:], [hgbk, 'identb'], [pTBk])
                S.copy('act', mixB[:], pTB[:], [pTBk], [mixBk])
                cx['pLO'] = []
                for hf in range(2):
                    pl, plk = pLOr.next()
                    cx['pLO'].append((pl, plk))
                    for k in range(4):
                        S.mm(pl[:], ma[:, k, :], Wout[:, k, hf * 512:(hf + 1) * 512], k == 0, False, [mak, 'Wout'], [plk])
                    for k in range(4):
                        S.mm(pl[:], mixB[:, k, :], Wout[:, 4 + k, hf * 512:(hf + 1) * 512], False, k == 3,
                             [mixBk, 'Wout'], [plk])

            def d1_s3(cx, ti):
                r0 = ti * 128
                xt_, xtk = cx['xt'], cx['xtk']
                tmpD, tmpDk = tmpDr.next(); ssD, ssDk = ssDr.next(); t2D, t2Dk = t2Dr.next(); h2, h2k = h2r.next()
                x1, x1k = x1r.next()
                cx['h2'], cx['h2k'] = h2, h2k
                for hf in range(2):
                    pl, plk = cx['pLO'][hf]
                    S.tt('dve', tmpD[:, hf * 512:(hf + 1) * 512], pl[:], mD[:, 0, hf * 512:(hf + 1) * 512], ALU.mult,
                         [plk, 'mD'], [tmpDk])
                S.tt('pool', x1[:], tmpD[:], xt_[:], ALU.add, [tmpDk, xtk], [x1k])
                S.dma('sp', X1[r0:r0 + 128, :], x1[:], [x1k], ())
                S.memset('dve', ssD[:], 0.0, [ssDk])
                S.act(junkD[:], x1[:], AF.Square, [x1k], ['junkD', ssDk], accum=ssD[:])
                S.act(ssD[:], ssD[:], AF.Sqrt, [ssDk], [ssDk], bias=EPS, scale=1.0 / D)
                S.op('dve', (lambda o_: (lambda e: e.reciprocal(out=o_[:], in_=o_[:])))(ssD), [ssDk], [ssDk])
                S.stt('dve', t2D[:], x1[:], ssD[:, 0:1], mD[:, 1, :], ALU.mult, ALU.mult, [x1k, ssDk, 'mD'], [t2Dk])
                S.tt('pool', h2[:], t2D[:], mD[:, 2, :], ALU.add, [t2Dk, 'mD'], [h2k])

            def d1_s4(cx, ti):
                r0 = ti * 128
                h2, h2k = cx['h2'], cx['h2k']
                pT8, pT8k = pT8r.next(); hT2, hT2k = h2Tr.next()
                for k in range(8):
                    S.tr(pT8[:, k, :], h2[:, k * 128:(k + 1) * 128], identb[:], [h2k, 'identb'], [pT8k])
                S.copy('dve' if ti % 2 else 'act', hT2[:], pT8[:], [pT8k], [hT2k])
                S.dma('sp', H2T[:, :, r0:r0 + 128], hT2[:], [hT2k], ())
            run_pipeline(N // 128, [d1_s0, d1_s1, d1_s2, d1_s3, d1_s4])
            S.emit()
        if stop_after == 5:
            return nc

        with ExitStack() as es:
            def TT(name, shape, dt):
                return es.enter_context(nc.sbuf_tensor(name, shape, dt))

            def PP(name, shape, dt):
                return es.enter_context(nc.psum_tensor(name, shape, dt))

            Wg = TT("Wg", [128, 8, DFF], BF16); Wu = TT("Wu", [128, 8, DFF], BF16); Wd = TT("Wd", [128, NCF, D], BF16)
            mE = TT("mE", [128, 2, D], F32)
            wgv = w_gate.rearrange("(k p) n -> p k n", p=128); wuv = w_up.rearrange("(k p) n -> p k n", p=128)
            wdv = w_down.rearrange("(c p) n -> p c n", p=128)
            for k in range(8):
                S.dma('pool', Wg[:, k, :], wgv[:, k, :], (), ['Wg'])
                S.dma('pool', Wu[:, k, :], wuv[:, k, :], (), ['Wu'])
            for c_ in range(NCF):
                S.dma('pool', Wd[:, c_, :], wdv[:, c_, :], (), ['Wd'])
            S.dma('sp', mE[:, 0, :], MOD[7], (), ['mE'])
            S.dma('sp', mE[:, 1, :], g_fin.partition_broadcast(128), (), ['mE'])
            GN = 512
            h2g = Rot(TT, "eh2", [128, 8, GN], BF16, 1)
            aT = TT("aT", [128, NCF, GN], BF16)
            sgr = Rot(TT, "esg", [128, GN], F32, 2)
            x1r = Rot(TT, "ex1", [128, D], F32, 1); tmr = Rot(TT, "etm", [128, D], F32, 2)
            ssE = TT("ssE", [128, 1], F32)
            pG = [PP(f"pG{i}", [128, GN], F32) for i in range(2)]
            pU = [PP(f"pU{i}", [128, GN], F32) for i in range(2)]
            pY = [PP(f"pY{i}", [128, 512], F32) for i in range(2)]
            for g_ in range(N // GN):
                q0 = g_ * GN
                hh, hhk = h2g.next()
                S.dma('sp', hh[:], H2T[:, :, q0:q0 + GN], (), [hhk])
                for c_ in range(NCF):
                    pg, pgk = pG[c_ % 2], f'pG{c_ % 2}'
                    pu, puk = pU[c_ % 2], f'pU{c_ % 2}'
                    for k in range(8):
                        S.mm(pg[:], Wg[:, k, c_ * 128:(c_ + 1) * 128], hh[:, k, :], k == 0, k == 7, ['Wg', hhk], [pgk])
                    for k in range(8):
                        S.mm(pu[:], Wu[:, k, c_ * 128:(c_ + 1) * 128], hh[:, k, :], k == 0, k == 7, ['Wu', hhk], [puk])
                    sg, sgk = sgr.next()
                    S.act(sg[:], pg[:], AF.Silu, [pgk], [sgk])
                    S.tt('dve', aT[:, c_, :], sg[:], pu[:], ALU.mult, [sgk, puk], ['aT'])
                for sb in range(GN // 128):
                    r0 = q0 + sb * 128
                    x1, x1k = x1r.next(); tm, tmk = tmr.next()
                    S.dma('sp!', x1[:], X1[r0:r0 + 128, :], (), [x1k])
                    for hf in range(2):
                        for c_ in range(NCF):
                            S.mm(pY[hf][:], aT[:, c_, sb * 128:(sb + 1) * 128], Wd[:, c_, hf * 512:(hf + 1) * 512],
                                 c_ == 0, c_ == NCF - 1, ['aT', 'Wd'], [f'pY{hf}'])
                        S.tt('dve', tm[:, hf * 512:(hf + 1) * 512], pY[hf][:], mE[:, 0, hf * 512:(hf + 1) * 512], ALU.mult,
                             [f'pY{hf}', 'mE'], [tmk])
                    S.tt('pool', x1[:], tm[:], x1[:], ALU.add, [tmk, x1k], [x1k])
                    S.memset('dve', ssE[:], 0.0, ['ssE'])
                    S.act(tm[:], x1[:], AF.Square, [x1k, tmk], [tmk, 'ssE'], accum=ssE[:])
                    S.act(ssE[:], ssE[:], AF.Sqrt, ['ssE'], ['ssE'], bias=EPS, scale=1.0 / D)
                    S.op('dve', lambda e: e.reciprocal(out=ssE[:], in_=ssE[:]), ['ssE'], ['ssE'])
                    S.stt('dve', tm[:], x1[:], ssE[:, 0:1], mE[:, 1, :], ALU.mult, ALU.mult, [x1k, 'ssE', 'mE'], [tmk])
                    S.dma('sp', out[r0:r0 + 128, :], tm[:], [tmk], ())
            S.emit()
    return nc


_NC_CACHE = {}


def kernel(**inputs):
    if 'nc' not in _NC_CACHE:
        _NC_CACHE['nc'] = build_nc()
    nc = _NC_CACHE['nc']
    f = lambda a: np.ascontiguousarray(np.asarray(a, dtype=np.float32))
    shared = {
        "c_ctx": f(inputs["c_ctx"]), "w_mod": f(inputs["w_mod"][0]), "b_mod": f(inputs["b_mod"][0]),
        "g_norm_mix": f(inputs["g_norm_mix"][0]), "g_norm_ffn": f(inputs["g_norm_ffn"][0]),
        "w_in": f(inputs["w_in"][0]), "g_q_norm": f(inputs["g_q_norm"][0]), "w_uq": f(inputs["w_uq"][0]),
        "g_kv_norm": f(inputs["g_kv_norm"][0]), "w_ukv": f(inputs["w_ukv"][0]),
        "lb_fwd": f(inputs["lb_fwd"]), "lb_bwd": f(inputs["lb_bwd"]), "g_hgrn_norm": f(inputs["g_hgrn_norm"][0]),
        "w_out": f(inputs["w_out"][0]), "w_gate": f(inputs["w_gate"][0]), "w_up": f(inputs["w_up"][0]),
        "w_down": f(inputs["w_down"][0]), "g_final": f(inputs["g_final"]),
    }
    xs, cs, ctxs = f(inputs["x"]), f(inputs["c"]), f(inputs["ctx"])
    in_maps = []
    for b in range(NB):
        m = dict(shared)
        m["x"] = xs[b]; m["c"] = cs[b]; m["ctx"] = ctxs[b]
        in_maps.append(m)
    res = run_bass_kernel_spmd(nc, in_maps, core_ids=list(range(NB)))
    return np.stack([np.asarray(r["out"], dtype=np.float32) for r in res.results], axis=0)
```

```python
import math
from contextlib import ExitStack

import numpy as np
import concourse.bass as bass
import concourse.mybir as mybir
from concourse.bass_utils import run_bass_kernel_spmd

F32 = mybir.dt.float32
BF16 = mybir.dt.bfloat16
AF = mybir.ActivationFunctionType
ALU = mybir.AluOpType
AX = mybir.AxisListType

NB, N, L, D = 8, 4096, 256, 1024
T = N + L
NT = T // 128
DFF = 2816
NCF = DFF // 128
EPS = 1e-6
INC = 3136
SCALE = 1.0 / math.sqrt(192.0)


class Sched:
    ENG = ('pe', 'act', 'dve', 'pool', 'sp')

    def __init__(self, nc, n_dma_sems=24):
        self.nc = nc
        self.ops = {e: [] for e in self.ENG}
        self.sems = {}
        for e in ('pe', 'act', 'dve', 'pool'):
            self.sems[('e', e)] = nc.alloc_semaphore(name=f"s_{e}")
        for i in range(n_dma_sems):
            self.sems[('d', i)] = nc.alloc_semaphore(name=f"s_dma{i}")
        self.nw = 6
        for i in range(self.nw):
            self.sems[('w', i)] = nc.alloc_semaphore(name=f"s_swdma{i}")
        self.wnext = 0
        self.cnt = {k: 0 for k in self.sems}
        self.nd = n_dma_sems
        self.dnext = 0
        self.waited = {e: {} for e in self.ENG}
        self.res = {}
        self.enabled = True
        self.load_q = None

    def _deps(self, reads, writes):
        t = []
        for r in reads:
            st = self.res.get(r)
            if st and st['w']:
                t.append(st['w'])
        for w in writes:
            st = self.res.get(w)
            if st:
                if st['w']:
                    t.append(st['w'])
                t.extend(st['r'].values())
        return t

    def _need(self, eng, tickets):
        best = {}
        for key, val in tickets:
            if key == ('e', 'pe') and eng == 'pe':
                continue
            if self.waited[eng].get(key, 0) >= val:
                continue
            if best.get(key, 0) < val:
                best[key] = val
        for key, val in best.items():
            self.waited[eng][key] = val
        return list(best.items())

    def _commit(self, ticket, reads, writes):
        for r in reads:
            st = self.res.setdefault(r, {'w': None, 'r': {}})
            st['r'][ticket[0]] = ticket
        for w in writes:
            self.res[w] = {'w': ticket, 'r': {}}

    def op(self, eng, fn, reads=(), writes=()):
        if not self.enabled:
            return None
        waits = self._need(eng, self._deps(reads, writes))
        key = ('e', eng)
        self.cnt[key] += 1
        ticket = (key, self.cnt[key])
        self.ops[eng].append((waits, fn, key, 1))
        self._commit(ticket, reads, writes)
        return ticket

    def dma(self, q, out, in_, reads=(), writes=()):
        if not self.enabled:
            return None
        if q == 'sp!':
            q = 'sp'
        elif q == 'sp' and not reads and self.load_q:
            q = self.load_q
        if q == 'pool':
            key = ('w', self.wnext)
            self.wnext = (self.wnext + 1) % self.nw
        else:
            key = ('d', self.dnext)
            self.dnext = (self.dnext + 1) % self.nd
        tickets = self._deps(reads, writes)
        if self.cnt[key] > 0:
            tickets.append((key, self.cnt[key]))
        waits = self._need(q, tickets)
        self.cnt[key] += 16
        ticket = (key, self.cnt[key])
        self.ops[q].append((waits, lambda e: e.dma_start(out=out, in_=in_), key, 16))
        self._commit(ticket, reads, writes)
        return ticket

    def barrier(self):
        allt = [(k, v) for k, v in self.cnt.items() if v > 0]
        for e in self.ENG:
            waits = self._need(e, allt)
            if waits:
                self.ops[e].append((waits, None, None, 0))

    def emit(self):
        self.barrier()
        with self.nc.Block() as block:
            def mk(engname):
                def body(e):
                    for waits, fn, semkey, inc in self.ops[engname]:
                        for key, val in waits:
                            e.wait_ge(self.sems[key], val)
                        if fn is not None:
                            fn(e).then_inc(self.sems[semkey], inc)
                return body
            block.tensor(mk('pe'))
            block.scalar(mk('act'))
            block.vector(mk('dve'))
            block.gpsimd(mk('pool'))
            block.sync(mk('sp'))
        self.ops = {e: [] for e in self.ENG}

    def mm(self, out, lhsT, rhs, start, stop, r, w):
        return self.op('pe', lambda e: e.matmul(out, lhsT=lhsT, rhs=rhs, start=start, stop=stop), r, w)

    def tr(self, out, in_, ident, r, w):
        return self.op('pe', lambda e: e.transpose(out, in_, ident), r, w)

    def act(self, out, in_, func, r, w, bias=None, scale=None, accum=None):
        kw = {}
        if bias is not None:
            kw['bias'] = bias
        if scale is not None:
            kw['scale'] = scale
        if accum is not None:
            kw['accum_out'] = accum
        return self.op('act', lambda e: e.activation(out=out, in_=in_, func=func, **kw), r, w)

    def tt(self, eng, out, in0, in1, op, r, w):
        return self.op(eng, lambda e: e.tensor_tensor(out=out, in0=in0, in1=in1, op=op), r, w)

    def ts(self, eng, out, in0, s1, s2, op0, op1, r, w):
        if s2 is None:
            return self.op(eng, lambda e: e.tensor_scalar(out=out, in0=in0, scalar1=s1, scalar2=None, op0=op0), r, w)
        return self.op(eng, lambda e: e.tensor_scalar(out=out, in0=in0, scalar1=s1, scalar2=s2, op0=op0, op1=op1), r, w)

    def stt(self, eng, out, in0, scalar, in1, op0, op1, r, w):
        return self.op(eng, lambda e: e.scalar_tensor_tensor(out=out, in0=in0, scalar=scalar, in1=in1, op0=op0, op1=op1), r, w)

    def copy(self, eng, out, in_, r, w):
        if eng == 'act':
            return self.act(out, in_, AF.Copy, r, w)
        return self.op(eng, lambda e: e.tensor_copy(out=out, in_=in_), r, w)

    def memset(self, eng, ap, val, w):
        return self.op(eng, lambda e: e.memset(ap, val), (), w)


def run_pipeline(n, stages):
    ctxs = [dict() for _ in range(n)]
    K = len(stages)
    for t in range(n + K - 1):
        for k in range(K - 1, -1, -1):
            i = t - k
            if 0 <= i < n:
                stages[k](ctxs[i], i)


class Rot:
    def __init__(self, alloc, name, shape, dt, n):
        self.t = [alloc(f"{name}{i}", shape, dt) for i in range(n)]
        self.k = [f"{name}{i}" for i in range(n)]
        self.i = 0

    def next(self):
        j = self.i % len(self.t)
        self.i += 1
        return self.t[j], self.k[j]


def _rstd(S, ss, out, dim, tag):
    S.act(out, ss, AF.Sqrt, [tag + 'ss'], [tag + 'rs'], bias=EPS, scale=1.0 / dim)
    S.op('dve', lambda e: e.reciprocal(out=out, in_=out), [tag + 'rs'], [tag + 'rs'])


def build_nc(stop_after=None, debug=False, a2_groups=None, a2_parts='abcde', STQ='sp'):
    nc = bass.Bass("TRN2", target_bir_lowering=False)
    S = Sched(nc)

    def din(name, shape):
        return nc.dram_tensor(name, shape, F32, kind="ExternalInput").ap()

    x = din("x", [N, D]); c = din("c", [D]); ctx = din("ctx", [L, D]); c_ctx = din("c_ctx", [D])
    w_mod = din("w_mod", [D, 6 * D]); b_mod = din("b_mod", [6 * D])
    g_mix = din("g_norm_mix", [D]); g_ffn = din("g_norm_ffn", [D])
    w_in = din("w_in", [D, INC]); g_qn = din("g_q_norm", [256]); w_uq = din("w_uq", [256, 768])
    g_kvn = din("g_kv_norm", [256]); w_ukv = din("w_ukv", [256, 1024])
    lb_f = din("lb_fwd", [2, 512]); lb_b = din("lb_bwd", [2, 512]); g_on = din("g_hgrn_norm", [128])
    w_out = din("w_out", [D, D]); w_gate = din("w_gate", [D, DFF]); w_up = din("w_up", [D, DFF])
    w_down = din("w_down", [DFF, D]); g_fin = din("g_final", [D])
    out = nc.dram_tensor("out", [N, D], F32, kind="ExternalOutput").ap()

    dbg = set(debug) if debug else set()

    def scr(name, shape, dt):
        return nc.dram_tensor(name, shape, dt, kind="ExternalOutput" if name in dbg else "Internal").ap()

    MOD = scr("s_mod", [8, 128, D], F32)
    HT = scr("s_ht", [128, 8, T], BF16)
    QN = scr("s_qn", [4, 128, N], BF16); QR = scr("s_qr", [4, 64, N], BF16)
    KN = scr("s_kn", [4, 128, T], BF16); KR = scr("s_kr", [64, T], BF16)
    VV = scr("s_v", [T, 512], BF16)
    HQ = scr("s_hq", [4, 128, T], F32); HV = scr("s_hv", [T, 512], BF16); HG = scr("s_hg", [N, 512], F32)
    GG = scr("s_g", [2, T, 512], F32); KG = scr("s_kg", [2, T, 512], F32)
    OO = scr("s_o", [2, N, 512], F32)
    MIXA = scr("s_mixa", [4, 128, N], BF16)
    X1 = scr("s_x1", [N, D], F32); H2T = scr("s_h2t", [128, 8, N], BF16)

    outer = ExitStack()
    with outer:
        def CT(name, shape, dt):
            return outer.enter_context(nc.sbuf_tensor(name, shape, dt))
        identb = CT("identb", [128, 128], BF16); identf = CT("identf", [128, 128], F32)
        onesb = CT("onesb", [128, 128], BF16); onesf = CT("onesf", [128, 128], F32)
        Uf = CT("Uf", [128, 128], F32); Ub = CT("Ub", [128, 128], F32)
        Rf = CT("Rf", [128, 128], F32); Rb = CT("Rb", [128, 128], F32)
        Mf = CT("Mf", [128, 128], F32); Mb = CT("Mb", [128, 128], F32)
        Ind = CT("Ind", [128, 2], F32)

        def sel(tile, pattern, cm, cmp_op, key):
            S.op('pool', lambda e: e.affine_select(out=tile[:], in_=tile[:], pattern=pattern, compare_op=cmp_op,
                                                   fill=0.0, base=0, channel_multiplier=cm), [key], [key])
        for tl, key in ((identb, 'identb'), (identf, 'identf')):
            S.memset('pool', tl[:], 1.0, [key])
            sel(tl, [[-1, 128]], 1, ALU.is_equal, key)
        S.memset('pool', onesb[:], 1.0, ['onesb'])
        S.memset('pool', onesf[:], 1.0, ['onesf'])
        S.memset('pool', Uf[:], 1.0, ['Uf']); sel(Uf, [[-1, 128]], 1, ALU.is_gt, 'Uf')
        S.memset('pool', Uf[64:128, 0:64], 0.0, ['Uf'])
        S.memset('pool', Mb[:], 1.0, ['Mb']); sel(Mb, [[-1, 128]], 1, ALU.is_ge, 'Mb')
        S.memset('pool', Mb[64:128, 0:64], 0.0, ['Mb'])
        S.memset('pool', Ub[:], 1.0, ['Ub']); sel(Ub, [[1, 128]], -1, ALU.is_gt, 'Ub')
        S.memset('pool', Ub[0:64, 64:128], 0.0, ['Ub'])
        S.memset('pool', Mf[:], 1.0, ['Mf']); sel(Mf, [[1, 128]], -1, ALU.is_ge, 'Mf')
        S.memset('pool', Mf[0:64, 64:128], 0.0, ['Mf'])
        S.ts('pool', Rf[:], Uf[:], -1.0, None, ALU.mult, None, ['Uf'], ['Rf'])
        S.ts('pool', Rb[:], Ub[:], -1.0, None, ALU.mult, None, ['Ub'], ['Rb'])
        S.memset('pool', Ind[:], 0.0, ['Ind'])
        S.memset('pool', Ind[0:64, 0:1], 1.0, ['Ind'])
        S.memset('pool', Ind[64:128, 1:2], 1.0, ['Ind'])

        wstack = ExitStack()
        Win = wstack.enter_context(nc.sbuf_tensor("Win", [128, 8, INC], BF16))
        Wkpe = wstack.enter_context(nc.sbuf_tensor("Wkpe", [128, 8, 128], BF16))
        winv = w_in.rearrange("(k p) n -> p k n", p=128)
        S.memset('dve', Wkpe[:, :, 64:128], 0.0, ['Wkpe'])
        for k in range(8):
            S.dma('pool', Win[:, k, :], winv[:, k, :], (), ['Win'])
            for f_ in range(2):
                S.dma('pool', Wkpe[:, k, f_ * 32:(f_ + 1) * 32].rearrange("p (a i) -> p a i", a=2),
                      winv[:, k, 512:576].rearrange("p (a f i) -> p f a i", a=2, f=2)[:, f_, :, :], (), ['Wkpe'])

        with ExitStack() as es:
            def TT(name, shape, dt):
                return es.enter_context(nc.sbuf_tensor(name, shape, dt))

            def PP(name, shape, dt):
                return es.enter_context(nc.psum_tensor(name, shape, dt))
            crow = TT("crow", [128, 2, D], F32)
            cb = TT("cb", [128, 2, 8, 128], F32)
            bmod = TT("bmod", [128, 6 * D], F32)
            wm = [TT(f"wm{i}", [128, 8, 512], F32) for i in range(2)]
            modl = TT("modl", [128, 6 * D], F32); modc = TT("modc", [128, 2 * D], F32)
            gm = TT("gm", [128, D], F32); gf = TT("gf", [128, D], F32)
            tmpA = [TT(f"tmpA{i}", [128, D], F32) for i in range(3)]
            pcb = PP("pcb", [128, 128], F32)
            pm = [PP(f"pm{i}", [128, 512], F32) for i in range(2)]
            S.dma('sp', crow[:, 0, :], c.partition_broadcast(128), (), ['crow'])
            S.dma('sp', crow[:, 1, :], c_ctx.partition_broadcast(128), (), ['crow'])
            S.dma('sp', bmod[:], b_mod.partition_broadcast(128), (), ['bmod'])
            S.dma('sp', gm[:], g_mix.partition_broadcast(128), (), ['gm'])
            S.dma('sp', gf[:], g_ffn.partition_broadcast(128), (), ['gf'])
            S.act(crow[:], crow[:], AF.Silu, ['crow'], ['crow'])
            for w_ in range(2):
                for k in range(8):
                    S.mm(pcb[:], crow[:, w_, k * 128:(k + 1) * 128], identf[:], True, True, ['crow', 'identf'], ['pcb'])
                    S.copy('dve', cb[:, w_, k, :], pcb[:], ['pcb'], ['cb'])
            wmv = w_mod.rearrange("(k p) n -> p k n", p=128)
            for j in range(12):
                wt = wm[j % 2]; wk = f"wm{j % 2}"
                S.dma('sp' if j % 2 == 0 else 'act', wt[:], wmv[:, :, j * 512:(j + 1) * 512], (), [wk])
                for k in range(8):
                    S.mm(pm[0][:], cb[:, 0, k, :], wt[:, k, :], k == 0, k == 7, ['cb', wk], ['pm0'])
                S.tt('dve', modl[:, j * 512:(j + 1) * 512], pm[0][:], bmod[:, j * 512:(j + 1) * 512], ALU.add,
                     ['pm0', 'bmod'], ['modl'])
                if j < 4:
                    for k in range(8):
                        S.mm(pm[1][:], cb[:, 1, k, :], wt[:, k, :], k == 0, k == 7, ['cb', wk], ['pm1'])
                    S.tt('dve', modc[:, j * 512:(j + 1) * 512], pm[1][:], bmod[:, j * 512:(j + 1) * 512], ALU.add,
                         ['pm1', 'bmod'], ['modc'])
            S.stt('dve', tmpA[0][:], modl[:, D:2 * D], 1.0, gm[:], ALU.add, ALU.mult, ['modl', 'gm'], ['tA0'])
            S.stt('dve', tmpA[1][:], modc[:, D:2 * D], 1.0, gm[:], ALU.add, ALU.mult, ['modc', 'gm'], ['tA1'])
            S.stt('dve', tmpA[2][:], modl[:, 4 * D:5 * D], 1.0, gf[:], ALU.add, ALU.mult, ['modl', 'gf'], ['tA2'])
            S.dma('sp', MOD[0], tmpA[0][:], ['tA0'], ())
            S.dma('sp', MOD[1], modl[:, 0:D], ['modl'], ())
            S.dma('sp', MOD[2], tmpA[1][:], ['tA1'], ())
            S.dma('sp', MOD[3], modc[:, 0:D], ['modc'], ())
            S.dma('sp', MOD[4], modl[:, 2 * D:3 * D], ['modl'], ())
            S.dma('sp', MOD[5], tmpA[2][:], ['tA2'], ())
            S.dma('sp', MOD[6], modl[:, 3 * D:4 * D], ['modl'], ())
            S.dma('sp', MOD[7], modl[:, 5 * D:6 * D], ['modl'], ())
            S.emit()
        S.load_q = 'act'
        if stop_after == 0:
            return nc

        with ExitStack() as es:
            def TT(name, shape, dt):
                return es.enter_context(nc.sbuf_tensor(name, shape, dt))

            def PP(name, shape, dt):
                return es.enter_context(nc.psum_tensor(name, shape, dt))
            mA = TT("mA", [128, 4, D], F32)
            junk = TT("junk", [128, D], BF16)
            xtR = Rot(TT, "xt", [128, D], F32, 4); ssR = Rot(TT, "ssa", [128, 1], F32, 4)
            t1R = Rot(TT, "t1_", [128, D], F32, 2); hbR = Rot(TT, "hb", [128, D], BF16, 3)
            hTR = Rot(TT, "hT", [128, 8, 128], BF16, 3); ptrR = Rot(PP, "ptr", [128, 8, 128], BF16, 2)
            for i in range(4):
                S.dma('sp', mA[:, i, :], MOD[i], (), ['mA'])

            def a1_s0(cx, ti):
                cx['xt'], cx['xk'] = xtR.next()
                src = ctx[ti * 128:(ti + 1) * 128, :] if ti < 2 else x[(ti - 2) * 128:(ti - 1) * 128, :]
                S.dma('sp', cx['xt'][:], src, (), [cx['xk']])

            def a1_s1(cx, ti):
                xt_, xk = cx['xt'], cx['xk']
                ss, sk = ssR.next()
                cx['ss'], cx['sk'] = ss, sk
                S.memset('dve', ss[:], 0.0, [sk])
                S.act(junk[:], xt_[:], AF.Square, [xk], ['junk', sk], accum=ss[:])
                S.act(ss[:], ss[:], AF.Sqrt, [sk], [sk], bias=EPS, scale=1.0 / D)
                S.op('dve', (lambda o_: (lambda e: e.reciprocal(out=o_[:], in_=o_[:])))(ss), [sk], [sk])

            def a1_s1b(cx, ti):
                xt_, xk, ss, sk = cx['xt'], cx['xk'], cx['ss'], cx['sk']
                mi = 2 if ti < 2 else 0
                t1, t1k = t1R.next(); hb, hbk = hbR.next()
                cx['hb'], cx['hbk'] = hb, hbk
                S.stt('dve', t1[:], xt_[:], ss[:, 0:1], mA[:, mi, :], ALU.mult, ALU.mult, [xk, sk, 'mA'], [t1k])
                S.tt('pool', hb[:], t1[:], mA[:, mi + 1, :], ALU.add, [t1k, 'mA'], [hbk])

            def a1_s2(cx, ti):
                hb, hbk = cx['hb'], cx['hbk']
                pt, ptk = ptrR.next(); hT, hTk = hTR.next()
                for k in range(8):
                    S.tr(pt[:, k, :], hb[:, k * 128:(k + 1) * 128], identb[:], [hbk, 'identb'], [ptk])
                S.copy('dve' if ti % 2 else 'act', hT[:], pt[:], [ptk], [hTk])
                S.dma('sp', HT[:, :, ti * 128:(ti + 1) * 128], hT[:], [hTk], ())
            run_pipeline(NT, [a1_s0, a1_s1, a1_s1b, a1_s2])
            S.emit()
        if stop_after == 1:
            return nc

        with ExitStack() as es:
            def TT(name, shape, dt):
                return es.enter_context(nc.sbuf_tensor(name, shape, dt))

            def PP(name, shape, dt):
                return es.enter_context(nc.psum_tensor(name, shape, dt))
            Wkrot = TT("Wkrot", [128, 8, 128], BF16)
            Wqn = TT("Wqn", [128, 2, 4, 128], BF16); Wqr = TT("Wqr", [128, 2, 4, 128], BF16)
            Wqrot = TT("Wqrot", [128, 2, 4, 128], BF16)
            Wkn = TT("Wkn", [128, 2, 4, 128], BF16); Wv = TT("Wv", [128, 2, 4, 128], BF16)
            Ct = TT("Ct", [64, N], F32); St = TT("St", [64, N], F32)
            lbt = TT("lbt", [128, 2, 512], F32); oml = TT("oml", [128, 2, 512], F32)
            with ExitStack() as es2:
                def T2(name, shape, dt):
                    return es2.enter_context(nc.sbuf_tensor(name, shape, dt))
                S.memset('dve', Wkrot[:, :, 64:128], 0.0, ['Wkrot'])
                for tl_, k_ in ((Wqr, 'Wqr'), (Wqrot, 'Wqrot')):
                    S.memset('dve', tl_[:, :, :, 64:128], 0.0, [k_])
                S.ts('dve', Wkrot[:, :, 0:32], Wkpe[:, :, 32:64], -1.0, None, ALU.mult, None, ['Wkpe'], ['Wkrot'])
                S.copy('dve', Wkrot[:, :, 32:64], Wkpe[:, :, 0:32], ['Wkpe'], ['Wkrot'])
                stq = T2("stq", [128, 2, 768], F32); stkv = T2("stkv", [128, 2, 1024], F32)
                gq = T2("gq", [128, 2], F32); gkv = T2("gkv", [128, 2], F32)
                S.dma('sp', stq[:], w_uq.rearrange("(c p) n -> p c n", p=128), (), ['stq'])
                S.dma('sp', stkv[:], w_ukv.rearrange("(c p) n -> p c n", p=128), (), ['stkv'])
                for c_ in range(2):
                    S.dma('sp', gq[:, c_:c_ + 1], g_qn[c_ * 128:(c_ + 1) * 128].rearrange("(p o) -> p o", o=1), (), ['gq'])
                    S.dma('sp', gkv[:, c_:c_ + 1], g_kvn[c_ * 128:(c_ + 1) * 128].rearrange("(p o) -> p o", o=1), (), ['gkv'])
                for c_ in range(2):
                    sq_v = stq[:, c_, :].rearrange("p (h d) -> p h d", h=4)
                    S.ts('dve', Wqn[:, c_, :, :], sq_v[:, :, 0:128], gq[:, c_:c_ + 1], None, ALU.mult, None,
                         ['stq', 'gq'], ['Wqn'])
                    for f_ in range(2):
                        for a_ in range(2):
                            so = 128 + a_ * 32 + f_ * 16
                            do = f_ * 32 + a_ * 16
                            S.ts('dve', Wqr[:, c_, :, do:do + 16], sq_v[:, :, so:so + 16], gq[:, c_:c_ + 1], None,
                                 ALU.mult, None, ['stq', 'gq'], ['Wqr'])
                    S.ts('dve', Wqrot[:, c_, :, 0:32], Wqr[:, c_, :, 32:64], -1.0, None, ALU.mult, None, ['Wqr'], ['Wqrot'])
                    S.copy('dve', Wqrot[:, c_, :, 32:64], Wqr[:, c_, :, 0:32], ['Wqr'], ['Wqrot'])
                    skv_v = stkv[:, c_, :].rearrange("p (h t d) -> p h t d", h=4, t=2)
                    S.ts('dve', Wkn[:, c_, :, :], skv_v[:, :, 0, :], gkv[:, c_:c_ + 1], None, ALU.mult, None,
                         ['stkv', 'gkv'], ['Wkn'])
                    S.ts('dve', Wv[:, c_, :, :], skv_v[:, :, 1, :], gkv[:, c_:c_ + 1], None, ALU.mult, None,
                         ['stkv', 'gkv'], ['Wv'])
                lraw = T2("lraw", [128, 2, 2, 512], F32)
                for d_, lbx in enumerate((lb_f, lb_b)):
                    for r_ in range(2):
                        S.dma('sp', lraw[:, d_, r_, :], lbx[r_].partition_broadcast(128), (), ['lraw'])
                S.tt('dve', lbt[:], lraw[:, :, 0, :], lraw[:, :, 1, :], ALU.subtract, ['lraw'], ['lbt'])
                S.act(lbt[:], lbt[:], AF.Sigmoid, ['lbt'], ['lbt'])
                S.ts('dve', oml[:], lbt[:], -0.5, 0.5, ALU.mult, ALU.add, ['lbt'], ['oml'])
                S.tt('dve', lbt[:], lbt[:], oml[:], ALU.add, ['lbt', 'oml'], ['lbt'])
                pidx = T2("pidx", [64, 1], F32); i16 = T2("i16", [64, 1], F32); mrow = T2("mrow", [64, 1], F32)
                arow = T2("arow", [64, 1], F32); acol = T2("acol", [64, 1], F32)
                rowpos = T2("rowpos", [64, N], F32); colpos = T2("colpos", [64, N], F32); ang = T2("ang", [64, N], F32)
                S.op('pool', lambda e: e.iota(pidx[:], [[0, 1]], base=0, channel_multiplier=1,
                                              allow_small_or_imprecise_dtypes=True), (), ['pidx'])
                S.op('pool', lambda e: e.iota(rowpos[:], [[1, 64], [0, 64]], base=0, channel_multiplier=0,
                                              allow_small_or_imprecise_dtypes=True), (), ['rowpos'])
                S.op('pool', lambda e: e.iota(colpos[:], [[0, 64], [1, 64]], base=0, channel_multiplier=0,
                                              allow_small_or_imprecise_dtypes=True), (), ['colpos'])
                msk = T2("msk", [64, 3], F32)
                S.memset('pool', msk[:], 1.0, ['msk'])
                for j_ in range(3):
                    S.op('pool', (lambda jj: (lambda e: e.affine_select(
                        out=msk[:, jj:jj + 1], in_=msk[:, jj:jj + 1], pattern=[[0, 1]], compare_op=ALU.is_ge, fill=0.0,
                        base=-16 * (jj + 1), channel_multiplier=1)))(j_), ['msk'], ['msk'])
                S.tt('dve', mrow[:], msk[:, 0:1], msk[:, 1:2], ALU.add, ['msk'], ['mrow'])
                S.tt('dve', mrow[:], mrow[:], msk[:, 2:3], ALU.add, ['msk', 'mrow'], ['mrow'])
                S.stt('dve', i16[:], mrow[:], -16.0, pidx[:], ALU.mult, ALU.add, ['mrow', 'pidx'], ['i16'])
                S.tt('dve', mrow[:], msk[:, 1:2], msk[:, 0:1], ALU.subtract, ['msk'], ['mrow'])
                S.tt('dve', mrow[:], mrow[:], msk[:, 2:3], ALU.subtract, ['msk', 'mrow'], ['mrow'])
                S.ts('dve', mrow[:], mrow[:], 1.0, None, ALU.add, None, ['mrow'], ['mrow'])
                S.act(i16[:], i16[:], AF.Exp, ['i16'], ['i16'], scale=-math.log(10000.0) / 16.0)
                S.tt('dve', arow[:], i16[:], mrow[:], ALU.mult, ['i16', 'mrow'], ['arow'])
                S.tt('dve', acol[:], i16[:], arow[:], ALU.subtract, ['i16', 'arow'], ['acol'])
                S.ts('dve', ang[:], rowpos[:], arow[:, 0:1], None, ALU.mult, None, ['rowpos', 'arow'], ['ang'])
                S.stt('dve', ang[:], colpos[:], acol[:, 0:1], ang[:], ALU.mult, ALU.add, ['colpos', 'acol', 'ang'], ['ang'])
                sc_ = 1.0 - 1e-6
                ki = T2("ki", [64, N], mybir.dt.int32)
                for tab, shift in ((St, 0.0), (Ct, 0.5 * math.pi)):
                    S.ts('dve', rowpos[:], ang[:], shift, 1.0 / (2 * math.pi), ALU.add, ALU.mult, ['ang'], ['rowpos'])
                    S.copy('dve', ki[:], rowpos[:], ['rowpos'], ['ki'])
                    S.copy('dve', colpos[:], ki[:], ['ki'], ['colpos'])
                    S.ts('dve', rowpos[:], ang[:], shift, None, ALU.add, None, ['ang'], ['rowpos'])
                    S.stt('dve', rowpos[:], colpos[:], -2 * math.pi, rowpos[:], ALU.mult, ALU.add, ['colpos', 'rowpos'], ['rowpos'])
                    S.act(tab[:], rowpos[:], AF.Sin, ['rowpos'], ['St' if shift == 0.0 else 'Ct'], scale=sc_)
                if stop_after == 15:
                    for nm, tl, shp, dt_ in (("d_Ct", Ct, [64, N], F32), ("d_St", St, [64, N], F32),
                                             ("d_Wkpe", Wkpe, [128, 8, 128], BF16), ("d_Wkrot", Wkrot, [128, 8, 128], BF16),
                                             ("d_Wqn", Wqn, [128, 2, 4, 128], BF16), ("d_Wqr", Wqr, [128, 2, 4, 128], BF16),
                                             ("d_Wqrot", Wqrot, [128, 2, 4, 128], BF16), ("d_Wkn", Wkn, [128, 2, 4, 128], BF16),
                                             ("d_Wv", Wv, [128, 2, 4, 128], BF16), ("d_lbt", lbt, [128, 2, 512], F32),
                                             ("d_oml", oml, [128, 2, 512], F32), ("d_Win", Win, [128, 8, INC], BF16)):
                        dd = nc.dram_tensor(nm, shp, dt_, kind="ExternalOutput").ap()
                        S.dma('sp', dd, tl[:], [nm[2:]], ())
                S.emit()
                if stop_after == 15:
                    return nc

            hTg = Rot(TT, "hTg", [128, 8, 512], BF16, 2)
            cT = TT("cT", [128, 2, 512], BF16); sq = TT("sq", [128, 2, 512], BF16)
            rbc = TT("rbc", [128, 512], F32); rtk = TT("rtk", [128, 4], F32)
            o_bf = Rot(TT, "o_bf", [128, 512], BF16, 3)
            o_f = Rot(TT, "o_f", [128, 512], F32, 4)
            u_f = Rot(TT, "u_f", [64, 512], F32, 4)
            sgb = Rot(TT, "sgb", [128, 512], F32, 3); fb_ = Rot(TT, "fb_", [128, 512], F32, 3)
            pA = [PP(f"pA{i}", [128, 512], F32) for i in range(2)]
            pB = [PP(f"pB{i}", [128, 512], F32) for i in range(2)]
            pS = PP("pS", [128, 512], F32)
            pT = [PP(f"pT{i}", [128, 512], F32) for i in range(2)]
            pV = PP("pV", [128, 4, 128], F32)
            groups = [(0, 256)] + [(256 + i * 512, 512) for i in range(8)]
            if a2_groups is not None:
                groups = groups[:a2_groups]
            for (tok0, n) in groups:
                is_lat = tok0 >= L
                lo = tok0 - L
                nsub = n // 128
                S.enabled = True
                hT_, hk = hTg.next()
                S.dma('sp', hT_[:, :, :n], HT[:, :, tok0:tok0 + n], (), [hk])

                def fm_proj(ps, pk, wt, wk, col0, ncols):
                    for k in range(8):
                        S.mm(ps[0:ncols, :n], wt[:, k, col0:col0 + ncols], hT_[:, k, :n], k == 0, k == 7, [wk, hk], [pk])

                def lowrank(col0, want_tok):
                    for c_ in range(2):
                        fm_proj(pA[c_], f'pA{c_}', Win, 'Win', col0 + c_ * 128, 128)
                        S.copy('act', cT[:, c_, :n], pA[c_][:, :n], [f'pA{c_}'], ['cT'])
                        S.act(sq[:, c_, :n], pA[c_][:, :n], AF.Square, [f'pA{c_}'], ['sq'])
                    for c_ in range(2):
                        S.mm(pS[:, :n], onesb[:], sq[:, c_, :n], c_ == 0, c_ == 1, ['onesb', 'sq'], ['pS'])
                    S.act(rbc[:, :n], pS[:, :n], AF.Sqrt, ['pS'], ['rbc'], bias=EPS, scale=1.0 / 256)
                    S.op('dve', (lambda nn: (lambda e: e.reciprocal(out=rbc[:, :nn], in_=rbc[:, :nn])))(n), ['rbc'], ['rbc'])
                    if want_tok:
                        for s_ in range(nsub):
                            for c_ in range(2):
                                S.mm(pV[:, s_, :], sq[:, c_, s_ * 128:(s_ + 1) * 128], onesb[:], c_ == 0, c_ == 1,
                                     ['sq', 'onesb'], ['pV'])
                        S.act(rtk[:, :nsub], pV[:, :nsub, 0], AF.Sqrt, ['pV'], ['rtk'], bias=EPS, scale=1.0 / 256)
                        S.op('dve', (lambda ns: (lambda e: e.reciprocal(out=rtk[:, :ns], in_=rtk[:, :ns])))(nsub), ['rtk'], ['rtk'])

                def rope_out(p0, k0, p1, k1, dst, scale_rows):
                    u1, uk1 = u_f.next(); u2, uk2 = u_f.next()
                    S.tt('dve', u1[:, :n], p0[0:64, :n], Ct[:, lo:lo + n], ALU.mult, [k0, 'Ct'], [uk1])
                    S.tt('dve', u2[:, :n], p1[0:64, :n], St[:, lo:lo + n], ALU.mult, [k1, 'St'], [uk2])
                    ob, ok = o_bf.next()
                    if scale_rows:
                        S.tt('pool', u1[:, :n], u1[:, :n], u2[:, :n], ALU.add, [uk1, uk2], [uk1])
                        S.tt('pool', ob[0:64, :n], u1[:, :n], rbc[0:64, :n], ALU.mult, [uk1, 'rbc'], [ok])
                    else:
                        S.tt('pool', ob[0:64, :n], u1[:, :n], u2[:, :n], ALU.add, [uk1, uk2], [ok])
                    S.dma(STQ, dst, ob[0:64, :n], [ok], ())

                if is_lat and 'a' in a2_parts:
                    lowrank(0, False)
                    for h in range(4):
                        pb, pk = pB[h % 2], f'pB{h % 2}'
                        for c_ in range(2):
                            S.mm(pb[:, :n], Wqn[:, c_, h, :], cT[:, c_, :n], c_ == 0, c_ == 1, ['Wqn', 'cT'], [pk])
                        ob, ok = o_bf.next()
                        S.tt('dve', ob[:, :n], pb[:, :n], rbc[:, :n], ALU.mult, [pk, 'rbc'], [ok])
                        S.dma(STQ, QN[h][:, lo:lo + n], ob[:, :n], [ok], ())
                    for h in range(4):
                        for c_ in range(2):
                            S.mm(pB[0][:, :n], Wqr[:, c_, h, :], cT[:, c_, :n], c_ == 0, c_ == 1, ['Wqr', 'cT'], ['pB0'])
                        for c_ in range(2):
                            S.mm(pB[1][:, :n], Wqrot[:, c_, h, :], cT[:, c_, :n], c_ == 0, c_ == 1, ['Wqrot', 'cT'], ['pB1'])
                        rope_out(pB[0], 'pB0', pB[1], 'pB1', QR[h][:, lo:lo + n], True)
                S.enabled = 'b' in a2_parts
                lowrank(256, True)
                for h in range(4):
                    pb, pk = pB[h % 2], f'pB{h % 2}'
                    for c_ in range(2):
                        S.mm(pb[:, :n], Wkn[:, c_, h, :], cT[:, c_, :n], c_ == 0, c_ == 1, ['Wkn', 'cT'], [pk])
                    ob, ok = o_bf.next()
                    S.tt('dve', ob[:, :n], pb[:, :n], rbc[:, :n], ALU.mult, [pk, 'rbc'], [ok])
                    S.dma(STQ, KN[h][:, tok0:tok0 + n], ob[:, :n], [ok], ())
                Wv2 = Wv[:].rearrange("p c h d -> p c (h d)")
                for s_ in range(nsub):
                    pt, pk = pT[s_ % 2], f'pT{s_ % 2}'
                    for c_ in range(2):
                        S.mm(pt[:], cT[:, c_, s_ * 128:(s_ + 1) * 128], Wv2[:, c_, :], c_ == 0, c_ == 1, ['cT', 'Wv'], [pk])
                    ob, ok = o_bf.next()
                    S.act(ob[:], pt[:], AF.Copy, [pk, 'rtk'], [ok], scale=rtk[:, s_:s_ + 1])
                    S.dma(STQ, VV[tok0 + s_ * 128:tok0 + (s_ + 1) * 128, :], ob[:], [ok], ())
                S.enabled = 'c' in a2_parts
                fm_proj(pA[0], 'pA0', Wkpe, 'Wkpe', 0, 128)
                if is_lat:
                    fm_proj(pA[1], 'pA1', Wkrot, 'Wkrot', 0, 128)
                    rope_out(pA[0], 'pA0', pA[1], 'pA1', KR[:, tok0:tok0 + n], False)
                else:
                    ob, ok = o_bf.next()
                    S.copy('act', ob[0:64, :n], pA[0][0:64, :n], ['pA0'], [ok])
                    S.dma(STQ, KR[:, tok0:tok0 + n], ob[0:64, :n], [ok], ())
                S.enabled = 'd' in a2_parts
                for h in range(4):
                    pa, pk = pA[h % 2], f'pA{h % 2}'
                    fm_proj(pa, pk, Win, 'Win', 576 + h * 128, 128)
                    of, ok = o_f.next()
                    S.copy('act' if h % 2 else 'dve', of[:, :n], pa[:, :n], [pk], [ok])
                    S.dma(STQ, HQ[h][:, tok0:tok0 + n], of[:, :n], [ok], ())
                S.enabled = 'e' in a2_parts
                for s_ in range(nsub):
                    row0 = tok0 + s_ * 128

                    def tm_proj(ps, pk, col0):
                        for k in range(8):
                            S.mm(ps[:], hT_[:, k, s_ * 128:(s_ + 1) * 128], Win[:, k, col0:col0 + 512], k == 0, k == 7,
                                 [hk, 'Win'], [pk])
                    tm_proj(pT[0], 'pT0', 1088)
                    ob, ok = o_bf.next()
                    S.copy('dve', ob[:], pT[0][:], ['pT0'], [ok])
                    S.dma(STQ, HV[row0:row0 + 128, :], ob[:], [ok], ())
                    if is_lat:
                        tm_proj(pT[1], 'pT1', 1600)
                        th, thk = sgb.next(); uh, uhk = fb_.next()
                        S.act(th[:], pT[1][:], AF.Tanh, ['pT1'], [thk], scale=0.5)
                        S.act(uh[:], pT[1][:], AF.Copy, ['pT1'], [uhk], scale=0.5)
                        of, ok = o_f.next()
                        S.tt('pool', th[:], th[:], uh[:], ALU.mult, [thk, uhk], [thk])
                        S.tt('pool', of[:], th[:], uh[:], ALU.add, [thk, uhk], [ok])
                        S.dma(STQ, HG[row0 - L:row0 - L + 128, :], of[:], [ok], ())
                    fts = []
                    for d_ in range(2):
                        pt, pk = pT[d_], f'pT{d_}'
                        tm_proj(pt, pk, 2112 + d_ * 512)
                        sg, sk = sgb.next(); ff, fk = fb_.next()
                        S.act(sg[:], pt[:], AF.Tanh, [pk], [sk], scale=0.5)
                        S.tt('dve', ff[:], sg[:], oml[:, d_, :], ALU.mult, [sk, 'oml'], [fk])
                        S.tt('pool', ff[:], ff[:], lbt[:, d_, :], ALU.add, [fk, 'lbt'], [fk])
                        fts.append((ff, fk))
                    for d_ in range(2):
                        ff, fk = fts[d_]
                        of, ok = o_f.next()
                        S.act(of[:], ff[:], AF.Ln, [fk], [ok])
                        S.dma(STQ, GG[d_][row0:row0 + 128, :], of[:], [ok], ())
                        of2, ok2 = o_f.next()
                        S.ts('pool', of2[:], ff[:], -1.0, 1.0, ALU.mult, ALU.add, [fk], [ok2])
                        S.dma(STQ, KG[d_][row0:row0 + 128, :], of2[:], [ok2], ())
            S.enabled = True
            S.emit()
        wstack.close()
        if stop_after == 2:
            return nc

        with ExitStack() as es:
            def TT(name, shape, dt):
                return es.enter_context(nc.sbuf_tensor(name, shape, dt))

            def PP(name, shape, dt):
                return es.enter_context(nc.psum_tensor(name, shape, dt))

            bt = []
            for d_ in range(2):
                bt.append(dict(
                    g=Rot(TT, f"bg{d_}", [128, 512], F32, 4), kg=Rot(TT, f"bkg{d_}", [128, 512], F32, 4),
                    v=Rot(TT, f"bv{d_}", [128, 512], BF16, 5), hq=Rot(TT, f"bhq{d_}", [128, 4, 128], F32, 4),
                    Ek=Rot(TT, f"bEk{d_}", [128, 512], F32, 2), EqT=Rot(TT, f"bEq{d_}", [128, 4, 128], F32, 2),
                    eb=Rot(TT, f"beb{d_}", [128, 4, 2], F32, 3), K2=Rot(TT, f"bK2{d_}", [128, 512], BF16, 2),
                    K2m=[Rot(TT, f"bK2m{c_}{d_}", [128, 512], BF16, 2) for c_ in range(2)],
                    QsT=Rot(TT, f"bQs{d_}", [128, 4, 128], BF16, 2),
                    Qsm=[Rot(TT, f"bQsm{c_}{d_}", [128, 4, 128], BF16, 2) for c_ in range(2)],
                    K2T=Rot(TT, f"bK2T{d_}", [128, 4, 128], BF16, 2),
                    Am=Rot(TT, f"bAm{d_}", [128, 4, 128], BF16, 2),
                    S=[TT(f"bS{d_}_{j_}", [128, 4, 128], F32) for j_ in range(2)], cur=[0],
                    Sp=Rot(TT, f"bSp{d_}", [128, 4, 128], F32, 2), Spb=Rot(TT, f"bSpb{d_}", [128, 4, 128], BF16, 4),
                    osb=Rot(TT, f"bos{d_}", [128, 512], F32, 2)))
                S.memset('dve', bt[d_]['S'][0][:], 0.0, [f'bS{d_}_0h{h}' for h in range(4)])
                for c_ in range(2):
                    oth = slice(64, 128) if c_ == 0 else slice(0, 64)
                    for j_ in range(2):
                        S.memset('pool', bt[d_]['K2m'][c_].t[j_][oth, :], 0.0, [bt[d_]['K2m'][c_].k[j_]])
                        S.memset('pool', bt[d_]['Qsm'][c_].t[j_][:, :, oth], 0.0, [bt[d_]['Qsm'][c_].k[j_]])
            pD1 = PP("pD1", [128, 512], F32); pD2 = PP("pD2", [128, 4, 128], F32)
            pKT = PP("pKT", [128, 4, 128], BF16); pBL = PP("pBL", [128, 4, 2], F32)
            pAT = PP("pAT", [128, 4, 128], F32)
            pOb = PP("pOb", [128, 4, 128], F32)
            pSNr = Rot(PP, "pSN", [128, 4, 128], F32, 2)

            def hgrn_load(cx, ti, d_):
                B_ = bt[d_]
                row0 = ti * 128
                g, gk = B_['g'].next(); kg, kgk = B_['kg'].next(); v, vk = B_['v'].next(); hq, hqk = B_['hq'].next()
                S.dma('sp', g[:], GG[d_][row0:row0 + 128, :], (), [gk])
                S.dma('sp', kg[:], KG[d_][row0:row0 + 128, :], (), [kgk])
                S.dma('sp', v[:], HV[row0:row0 + 128, :], (), [vk])
                S.dma('sp', hq[:], HQ[:, :, row0:row0 + 128].rearrange("h p t -> p h t"), (), [hqk])
                cx['ld'] = (g, gk, kg, kgk, v, vk, hq, hqk)

            def hgrn_pre(cx, ti, d_):
                B_ = bt[d_]
                is_lat = ti >= 2
                row0 = ti * 128
                U_, R_, M_ = (Uf, Rf, Mf) if d_ == 0 else (Ub, Rb, Mb)
                uk, rk, mk_ = ('Uf', 'Rf', 'Mf') if d_ == 0 else ('Ub', 'Rb', 'Mb')
                g, gk, kg, kgk, v, vk, hq, hqk = cx['ld']
                Ek, Ekk = B_['Ek'].next(); EqT, Eqk = B_['EqT'].next(); eb, ebk = B_['eb'].next()
                K2, K2k = B_['K2'].next(); QsT, Qsk = B_['QsT'].next(); K2T, K2Tk = B_['K2T'].next()
                S.mm(pD1[:], U_[:], g[:], True, True, [uk, gk], ['pD1'])
                for h in range(4):
                    S.mm(pD2[:, h, :], g[:, h * 128:(h + 1) * 128], R_[:], True, True, [gk, rk], ['pD2'])
                for h in range(4):
                    S.mm(pBL[:, h, :], g[:, h * 128:(h + 1) * 128], Ind[:], True, True, [gk, 'Ind'], ['pBL'])
                S.act(Ek[:], pD1[:], AF.Exp, ['pD1'], [Ekk])
                S.act(EqT[:], pD2[:], AF.Exp, ['pD2'], [Eqk])
                S.act(eb[:], pBL[:], AF.Exp, ['pBL'], [ebk])
                S.tt('dve', K2[:], kg[:], Ek[:], ALU.mult, [kgk, Ekk], [K2k])
                K2m = []
                for c_ in range(2):
                    rs_ = slice(c_ * 64, (c_ + 1) * 64)
                    km, kmk = B_['K2m'][c_].next()
                    S.copy('act', km[rs_, :], K2[rs_, :], [K2k], [kmk])
                    K2m.append((km, kmk))
                cx.update(v=v, vk=vk, eb=eb, ebk=ebk, K2m=K2m)
                if is_lat:
                    S.tt('pool', QsT[:], hq[:], EqT[:], ALU.mult, [hqk, Eqk], [Qsk])
                    Qsm = []
                    for c_ in range(2):
                        cs_ = slice(c_ * 64, (c_ + 1) * 64)
                        qm, qmk = B_['Qsm'][c_].next()
                        S.copy('pool', qm[:, :, cs_], QsT[:, :, cs_], [Qsk], [qmk])
                        Qsm.append((qm, qmk))
                    Am, Amk = B_['Am'].next()
                    for h in range(4):
                        S.tr(pKT[:, h, :], K2[:, h * 128:(h + 1) * 128], identb[:], [K2k, 'identb'], ['pKT'])
                    S.copy('act', K2T[:], pKT[:], ['pKT'], [K2Tk])
                    for h in range(4):
                        S.mm(pAT[:, h, :], K2T[:, h, :], QsT[:, h, :], True, True, [K2Tk, Qsk], ['pAT'])
                    S.tt('dve', Am[:], pAT[:], M_[:].unsqueeze(1).to_broadcast([128, 4, 128]), ALU.mult,
                         ['pAT', mk_], [Amk])
                    cx.update(Am=Am, Amk=Amk, Qsm=Qsm)

            def hgrn_chain(cx, ti, d_):
                B_ = bt[d_]
                is_lat = ti >= 2
                row0 = ti * 128
                v, vk, eb, ebk, K2m = (cx[k_] for k_ in ('v', 'vk', 'eb', 'ebk', 'K2m'))
                order = (0, 1) if d_ == 0 else (1, 0)
                spbs = {}
                for c_ in order:
                    ci = B_['cur'][0]
                    Sc, Sn = B_['S'][ci], B_['S'][1 - ci]
                    Sck = [f'bS{d_}_{ci}h{h}' for h in range(4)]
                    Snk = [f'bS{d_}_{1 - ci}h{h}' for h in range(4)]
                    B_['cur'][0] = 1 - ci
                    km, kmk = K2m[c_]
                    psn, psnk = pSNr.next()
                    for h in range(4):
                        S.mm(psn[:, h, :], km[:, h * 128:(h + 1) * 128], v[:, h * 128:(h + 1) * 128], True, True,
                             [kmk, vk], [psnk])
                    for h in range(4):
                        S.stt('dve', Sn[:, h, :], Sc[:, h, :], eb[:, h, c_:c_ + 1], psn[:, h, :], ALU.mult, ALU.add,
                              [Sck[h], ebk, psnk], [Snk[h]])
                    if is_lat:
                        Spb, Spbk = B_['Spb'].next()
                        S.tt('pool', Spb[:], Sc[:], eb[:, :, c_:c_ + 1].to_broadcast([128, 4, 128]), ALU.mult,
                             Sck + [ebk], [Spbk])
                        spbs[c_] = (Spb, Spbk)
                if is_lat:
                    Am, Amk, Qsm = cx['Am'], cx['Amk'], cx['Qsm']
                    for h in range(4):
                        S.mm(pOb[:, h, :], Am[:, h, :], v[:, h * 128:(h + 1) * 128], True, False, [Amk, vk], ['pOb'])
                        for n_, c_ in enumerate(order):
                            qm, qmk = Qsm[c_]
                            Spb, Spbk = spbs[c_]
                            S.mm(pOb[:, h, :], qm[:, h, :], Spb[:, h, :], False, n_ == 1, [qmk, Spbk], ['pOb'])
                    ob, obk = B_['osb'].next()
                    S.copy('act', ob[:], pOb[:].rearrange("p h d -> p (h d)"), ['pOb'], [obk])
                    S.dma('sp', OO[d_][row0 - L:row0 - L + 128, :], ob[:], [obk], ())

            fwd_order = list(range(NT))
            bwd_order = [1, 0] + list(range(NT - 1, 1, -1))
            cxs = [[dict() for _ in range(NT)] for _ in range(2)]
            orders = (fwd_order, bwd_order)
            for i_ in range(2):
                for d_ in range(2):
                    hgrn_load(cxs[d_][i_], orders[d_][i_], d_)
            for d_ in range(2):
                hgrn_pre(cxs[d_][0], orders[d_][0], d_)
            for i_ in range(NT):
                for d_ in range(2):
                    if i_ + 2 < NT:
                        hgrn_load(cxs[d_][i_ + 2], orders[d_][i_ + 2], d_)
                for d_ in range(2):
                    if i_ + 1 < NT:
                        hgrn_pre(cxs[d_][i_ + 1], orders[d_][i_ + 1], d_)
                for d_ in range(2):
                    hgrn_chain(cxs[d_][i_], orders[d_][i_], d_)
            S.emit()
        if stop_after == 3:
            return nc

        with ExitStack() as es:
            def TT(name, shape, dt):
                return es.enter_context(nc.sbuf_tensor(name, shape, dt))

            def PP(name, shape, dt):
                return es.enter_context(nc.psum_tensor(name, shape, dt))

            KNs = TT("KNs", [128, 4, T], BF16); KRs = TT("KRs", [128, T], BF16); Vs = TT("Vs", [128, NT, 512], BF16)
            sqKR = TT("sqKR", [64, T], BF16)
            sqn = TT("sqn", [128, 512], BF16); sqr = TT("sqr", [64, 512], BF16)
            sqnR = Rot(TT, "csqn", [128, 512], BF16, 2); sqrR = Rot(TT, "csqr", [64, 512], BF16, 2)
            km2 = TT("km2", [128, 4], F32); tmx = TT("tmx", [128, 1], F32)
            tmxR = Rot(TT, "ctmx", [128, 1], F32, 3); nshR = Rot(TT, "cnsh", [128, 1], F32, 3)
            qnr = Rot(TT, "cqn", [128, 512], BF16, 3); qrr = Rot(TT, "cqr", [128, 512], BF16, 3)
            PTr = Rot(TT, "cPT", [128, 1024], BF16, 4); osr = Rot(TT, "cos", [128, 512], BF16, 2)
            rinv = TT("rinv", [128, 512], F32)
            raccR = [Rot(TT, f"racc{i}_", [128, 1024], F32, 2) for i in range(2)]
            rsumR = [Rot(TT, f"rsum{i}_", [128, 512], F32, 2) for i in range(2)]
            pScR = Rot(PP, "pSc", [128, 1024], F32, 2)
            pOaR = Rot(PP, "pOa", [128, 512], F32, 2)
            pMiR = Rot(PP, "pMi", [128, 512], F32, 2)
            pNm = pMiR.t[0]
            for h in range(4):
                S.dma('sp' if h % 2 else 'sp!', KNs[:, h, :], KN[h], (), ['KNs'])
            S.memset('pool', KRs[64:128, :], 0.0, ['KRs'])
            S.dma('sp', KRs[0:64, :], KR, (), ['KRs'])
            for i_ in range(3):
                S.memset('pool', qrr.t[i_][64:128, :], 0.0, [qrr.k[i_]])
            VVv = VV.rearrange("(t p) n -> p t n", p=128)
            for j in range(0, NT, 4):
                je = min(NT, j + 4)
                S.dma('sp' if (j // 4) % 2 else 'sp!', Vs[:, j:je, :], VVv[:, j:je, :], (), ['Vs'])
            S.memset('dve', km2[:], 0.0, ['km2'])
            S.act(sqKR[:], KRs[0:64, :], AF.Square, ['KRs'], ['sqKR'])
            for j0 in range(0, T, 512):
                w_ = min(512, T - j0)
                for h in range(4):
                    S.act(sqn[:, :w_], KNs[:, h, j0:j0 + w_], AF.Square, ['KNs'], ['sqn'])
                    S.mm(pNm[:, :w_], onesb[:], sqn[:, :w_], True, False, ['onesb', 'sqn'], ['pMi0'])
                    S.mm(pNm[:, :w_], onesb[0:64, :], sqKR[:, j0:j0 + w_], False, True, ['onesb', 'sqKR'], ['pMi0'])
                    S.op('dve', (lambda ww: (lambda e: e.reduce_max(out=tmx[:], in_=pNm[:, :ww], axis=AX.X)))(w_),
                         ['pMi0'], ['tmx'])
                    S.tt('dve', km2[:, h:h + 1], km2[:, h:h + 1], tmx[:], ALU.max, ['km2', 'tmx'], ['km2'])
            items = [(g_, h) for g_ in range(8) for h in range(4)]
            cxs = [dict() for _ in items]
            NP_ = NT // 2

            def c_pro(i):
                g_, h = items[i]
                q0 = g_ * 512
                cx = cxs[i]
                qn, qnk = qnr.next(); qr, qrk = qrr.next()
                sqn_, sqnk = sqnR.next(); sqr_, sqrk = sqrR.next(); tm_, tmk = tmxR.next(); nsh, nshk = nshR.next()
                pm, pmk = pMiR.next()
                S.dma('sp', qn[:], QN[h][:, q0:q0 + 512], (), [qnk])
                S.dma('sp', qr[0:64, :], QR[h][:, q0:q0 + 512], (), [qrk])
                S.act(sqn_[:], qn[:], AF.Square, [qnk], [sqnk])
                S.act(sqr_[:], qr[0:64, :], AF.Square, [qrk], [sqrk])
                S.mm(pm[:], onesb[:], sqn_[:], True, False, ['onesb', sqnk], [pmk])
                S.mm(pm[:], onesb[0:64, :], sqr_[:], False, True, ['onesb', sqrk], [pmk])
                S.op('dve', (lambda o_, i_: (lambda e: e.reduce_max(out=o_[:], in_=i_[:], axis=AX.X)))(tm_, pm), [pmk], [tmk])
                S.ts('dve', nsh[:], tm_[:], km2[:, h:h + 1], -0.5 * SCALE, ALU.add, ALU.mult, [tmk, 'km2'], [nshk])
                cx.update(qn=qn, qnk=qnk, qr=qr, qrk=qrk, nsh=nsh, nshk=nshk, ps={})

            def c_qk(i, j):
                g_, h = items[i]
                cx = cxs[i]
                ps, pk = pScR.next()
                cx['ps'][j] = (ps, pk)
                for u_ in range(2):
                    kt = 2 * j + u_
                    S.mm(ps[:, u_ * 512:(u_ + 1) * 512], KNs[:, h, kt * 128:(kt + 1) * 128], cx['qn'][:], True, False,
                         ['KNs', cx['qnk']], [pk])
                    S.mm(ps[:, u_ * 512:(u_ + 1) * 512], KRs[:, kt * 128:(kt + 1) * 128], cx['qr'][:], False, True,
                         ['KRs', cx['qrk']], [pk])

            def c_main(i):
                g_, h = items[i]
                cx = cxs[i]
                pOa, pOak = pOaR.next()
                racc = [raccR[0].next(), raccR[1].next()]
                cx.update(pOa=pOa, pOak=pOak, racc=racc)
                for j in range(NP_):
                    if j + 1 < NP_:
                        c_qk(i, j + 1)
                    ps, pk = cx['ps'].pop(j)
                    PT, ptk = PTr.next()
                    S.act(PT[:], ps[:], AF.Exp, [pk, cx['nshk']], [ptk], bias=cx['nsh'][:], scale=SCALE)
                    for u_ in range(2):
                        kt = 2 * j + u_
                        S.mm(pOa[:], Vs[:, kt, h * 128:(h + 1) * 128], PT[:, u_ * 512:(u_ + 1) * 512],
                             kt == 0, kt == NT - 1, ['Vs', ptk], [pOak])
                    ae = 'dve' if j % 2 == 0 else 'pool'
                    ra, rak = racc[j % 2]
                    if j < 2:
                        S.copy(ae, ra[:], PT[:], [ptk], [rak])
                    else:
                        S.tt(ae, ra[:], ra[:], PT[:], ALU.add, [rak, ptk], [rak])

            def c_epi(i):
                g_, h = items[i]
                q0 = g_ * 512
                cx = cxs[i]
                pOa, pOak, racc = cx['pOa'], cx['pOak'], cx['racc']
                rs0, rs0k = rsumR[0].next(); rs1, rs1k = rsumR[1].next()
                pm, pmk = pMiR.next()
                S.tt('dve', rs0[:], racc[0][0][:, 0:512], racc[0][0][:, 512:1024], ALU.add, [racc[0][1]], [rs0k])
                S.tt('pool', rs1[:], racc[1][0][:, 0:512], racc[1][0][:, 512:1024], ALU.add, [racc[1][1]], [rs1k])
                S.mm(pm[:], onesf[:], rs0[:], True, False, ['onesf', rs0k], [pmk])
                S.mm(pm[:], onesf[:], rs1[:], False, True, ['onesf', rs1k], [pmk])
                S.op('dve', (lambda i_: (lambda e: e.reciprocal(out=rinv[:], in_=i_[:])))(pm), [pmk], ['rinv'])
                ob, obk = osr.next()
                S.tt('dve', ob[:], pOa[:], rinv[:], ALU.mult, [pOak, 'rinv'], [obk])
                S.dma('sp', MIXA[h][:, q0:q0 + 512], ob[:], [obk], ())
                cxs[i] = None

            c_pro(0)
            c_qk(0, 0)
            for i in range(len(items)):
                if i + 1 < len(items):
                    c_pro(i + 1)
                c_main(i)
                if i + 1 < len(items):
                    c_qk(i + 1, 0)
                c_epi(i)
            S.emit()
        if stop_after == 4:
            return nc

        with ExitStack() as es:
            def TT(name, shape, dt):
                return es.enter_context(nc.sbuf_tensor(name, shape, dt))

            def PP(name, shape, dt):
                return es.enter_context(nc.psum_tensor(name, shape, dt))

            Wout = TT("Wout", [128, 8, D], BF16)
            mD = TT("mD", [128, 3, D], F32)
            gon = TT("gon", [128, 128], F32)
            woutv = w_out.rearrange("(k p) n -> p k n", p=128)
            for k in range(8):
                S.dma('pool', Wout[:, k, :], woutv[:, k, :], (), ['Wout'])
            for i, mi in enumerate((4, 5, 6)):
                S.dma('sp', mD[:, i, :], MOD[mi], (), ['mD'])
            S.dma('sp', gon[:], g_on.partition_broadcast(128), (), ['gon'])
            ofr = Rot(TT, "dof", [128, 512], F32, 3); obr = Rot(TT, "dob", [128, 512], F32, 3); hgr = Rot(TT, "dhg", [128, 512], F32, 4)
            mar = Rot(TT, "dma_", [128, 4, 128], BF16, 5); xtr = Rot(TT, "dxt", [128, D], F32, 6)
            osumr = Rot(TT, "osum", [128, 512], F32, 3); osqr = Rot(TT, "osq", [128, 512], F32, 2); ss4r = Rot(TT, "ss4", [128, 4], F32, 3)
            tBr = Rot(TT, "tB", [128, 512], F32, 2); hgbr = Rot(TT, "hgb", [128, 512], BF16, 3); mixBr = Rot(TT, "mixB", [128, 4, 128], BF16, 2)
            tmpDr = Rot(TT, "tmpD", [128, D], F32, 2); x1r = Rot(TT, "dx1", [128, D], F32, 4)
            junkD = TT("junkD", [128, D], BF16); ssDr = Rot(TT, "ssD", [128, 1], F32, 3)
            t2Dr = Rot(TT, "t2D", [128, D], F32, 2); h2r = Rot(TT, "h2", [128, D], BF16, 3); h2Tr = Rot(TT, "dh2T", [128, 8, 128], BF16, 3)
            pTBr = Rot(PP, "pTB", [128, 4, 128], BF16, 2)
            pLOr = Rot(PP, "pLO", [128, 512], F32, 4)
            pT8r = Rot(PP, "pT8", [128, 8, 128], BF16, 2)

            def d1_s0(cx, ti):
                r0 = ti * 128
                for nm, rr, src in (('of', ofr, OO[0][r0:r0 + 128, :]), ('ob', obr, OO[1][r0:r0 + 128, :]),
                                    ('hg', hgr, HG[r0:r0 + 128, :]),
                                    ('ma', mar, MIXA[:, :, r0:r0 + 128].rearrange("h p t -> p h t")),
                                    ('xt', xtr, x[r0:r0 + 128, :])):
                    cx[nm], cx[nm + 'k'] = rr.next()
                    S.dma('sp', cx[nm][:], src, (), [cx[nm + 'k']])

            def d1_s1(cx, ti):
                of, ofk, ob, obk, hg, hgk = cx['of'], cx['ofk'], cx['ob'], cx['obk'], cx['hg'], cx['hgk']
                osum, osumk = osumr.next(); osq, osqk = osqr.next(); ss4, ss4k = ss4r.next(); tB, tBk = tBr.next()
                hgb, hgbk = hgbr.next()
                cx['hgb'], cx['hgbk'] = hgb, hgbk
                S.tt('pool', osum[:], of[:], ob[:], ALU.add, [ofk, obk], [osumk])
                S.tt('pool', osq[:], osum[:], osum[:], ALU.mult, [osumk], [osqk])
                S.op('dve', (lambda o_, i_: (lambda e: e.reduce_sum(out=o_[:], in_=i_[:].rearrange("p (h d) -> p h d", h=4),
                                                                    axis=AX.X)))(ss4, osq), [osqk], [ss4k])
                S.act(ss4[:], ss4[:], AF.Sqrt, [ss4k], [ss4k], bias=EPS, scale=1.0 / 128)
                S.op('dve', (lambda o_: (lambda e: e.reciprocal(out=o_[:], in_=o_[:])))(ss4), [ss4k], [ss4k])
                cx.update(osum=osum, osumk=osumk, ss4=ss4, ss4k=ss4k, tB=tB, tBk=tBk)

            def d1_s1b(cx, ti):
                hg, hgk = cx['hg'], cx['hgk']
                osum, osumk, ss4, ss4k, tB, tBk = (cx[k_] for k_ in ('osum', 'osumk', 'ss4', 'ss4k', 'tB', 'tBk'))
                hgb, hgbk = cx['hgb'], cx['hgbk']
                o3 = osum[:].rearrange("p (h d) -> p h d", h=4)
                t3 = tB[:].rearrange("p (h d) -> p h d", h=4)
                S.tt('dve', t3, o3, ss4[:].unsqueeze(2).to_broadcast([128, 4, 128]), ALU.mult, [osumk, ss4k], [tBk])
                S.tt('pool', t3, t3, gon[:].unsqueeze(1).to_broadcast([128, 4, 128]), ALU.mult, [tBk, 'gon'], [tBk])
                S.tt('pool', hgb[:], tB[:], hg[:], ALU.mult, [tBk, hgk], [hgbk])

            def d1_s2(cx, ti):
                hgb, hgbk, ma, mak = cx['hgb'], cx['hgbk'], cx['ma'], cx['mak']
                pTB, pTBk = pTBr.next(); mixB, mixBk = mixBr.next()
                for h in range(4):
                    S.tr(pTB[:, h, :], hgb[:, h * 128:(h + 1) * 128], identb[:], [hgbk, 'identb'], [pTBk])
                S.copy('act', mixB[:], pTB[:], [pTBk], [mixBk])
                cx['pLO'] = []
                for hf in range(2):
                    pl, plk = pLOr.next()
                    cx['pLO'].append((pl, plk))
                    for k in range(4):
                        S.mm(pl[:], ma[:, k, :], Wout[:, k, hf * 512:(hf + 1) * 512], k == 0, False, [mak, 'Wout'], [plk])
                    for k in range(4):
                        S.mm(pl[:], mixB[:, k, :], Wout[:, 4 + k, hf * 512:(hf + 1) * 512], False, k == 3,
                             [mixBk, 'Wout'], [plk])

            def d1_s3(cx, ti):
                r0 = ti * 128
                xt_, xtk = cx['xt'], cx['xtk']
                tmpD, tmpDk = tmpDr.next(); ssD, ssDk = ssDr.next(); t2D, t2Dk = t2Dr.next(); h2, h2k = h2r.next()
                x1, x1k = x1r.next()
                cx['h2'], cx['h2k'] = h2, h2k
                for hf in range(2):
                    pl, plk = cx['pLO'][hf]
                    S.tt('dve', tmpD[:, hf * 512:(hf + 1) * 512], pl[:], mD[:, 0, hf * 512:(hf + 1) * 512], ALU.mult,
                         [plk, 'mD'], [tmpDk])
                S.tt('pool', x1[:], tmpD[:], xt_[:], ALU.add, [tmpDk, xtk], [x1k])
                S.dma('sp', X1[r0:r0 + 128, :], x1[:], [x1k], ())
                S.memset('dve', ssD[:], 0.0, [ssDk])
                S.act(junkD[:], x1[:], AF.Square, [x1k], ['junkD', ssDk], accum=ssD[:])
                S.act(ssD[:], ssD[:], AF.Sqrt, [ssDk], [ssDk], bias=EPS, scale=1.0 / D)
                S.op('dve', (lambda o_: (lambda e: e.reciprocal(out=o_[:], in_=o_[:])))(ssD), [ssDk], [ssDk])
                cx.update(x1=x1, x1k=x1k, ssD=ssD, ssDk=ssDk, t2D=t2D, t2Dk=t2Dk)

            def d1_s3b(cx, ti):
                x1, x1k, ssD, ssDk, t2D, t2Dk, h2, h2k = (cx[k_] for k_ in ('x1', 'x1k', 'ssD', 'ssDk', 't2D', 't2Dk', 'h2', 'h2k'))
                S.stt('dve', t2D[:], x1[:], ssD[:, 0:1], mD[:, 1, :], ALU.mult, ALU.mult, [x1k, ssDk, 'mD'], [t2Dk])
                S.tt('pool', h2[:], t2D[:], mD[:, 2, :], ALU.add, [t2Dk, 'mD'], [h2k])

            def d1_s4(cx, ti):
                r0 = ti * 128
                h2, h2k = cx['h2'], cx['h2k']
                pT8, pT8k = pT8r.next(); hT2, hT2k = h2Tr.next()
                for k in range(8):
                    S.tr(pT8[:, k, :], h2[:, k * 128:(k + 1) * 128], identb[:], [h2k, 'identb'], [pT8k])
                S.copy('dve' if ti % 2 else 'act', hT2[:], pT8[:], [pT8k], [hT2k])
                S.dma('sp', H2T[:, :, r0:r0 + 128], hT2[:], [hT2k], ())
            run_pipeline(N // 128, [d1_s0, d1_s1, d1_s1b, d1_s2, d1_s3, d1_s3b, d1_s4])
            S.emit()
        if stop_after == 5:
            return nc

        with ExitStack() as es:
            def TT(name, shape, dt):
                return es.enter_context(nc.sbuf_tensor(name, shape, dt))

            def PP(name, shape, dt):
                return es.enter_context(nc.psum_tensor(name, shape, dt))

            Wg = TT("Wg", [128, 8, DFF], BF16); Wu = TT("Wu", [128, 8, DFF], BF16); Wd = TT("Wd", [128, NCF, D], BF16)
            mE = TT("mE", [128, 2, D], F32)
            wgv = w_gate.rearrange("(k p) n -> p k n", p=128); wuv = w_up.rearrange("(k p) n -> p k n", p=128)
            wdv = w_down.rearrange("(c p) n -> p c n", p=128)
            for k in range(8):
                S.dma('pool', Wg[:, k, :], wgv[:, k, :], (), ['Wg'])
                S.dma('pool', Wu[:, k, :], wuv[:, k, :], (), ['Wu'])
            for c_ in range(NCF):
                S.dma('pool', Wd[:, c_, :], wdv[:, c_, :], (), ['Wd'])
            S.dma('sp', mE[:, 0, :], MOD[7], (), ['mE'])
            S.dma('sp', mE[:, 1, :], g_fin.partition_broadcast(128), (), ['mE'])
            GN = 512
            h2g = Rot(TT, "eh2", [128, 8, GN], BF16, 1)
            aT = TT("aT", [128, NCF, GN], BF16)
            sgr = Rot(TT, "esg", [128, GN], F32, 2)
            x1r = Rot(TT, "ex1", [128, D], F32, 1); tmr = Rot(TT, "etm", [128, D], F32, 2)
            ssE = TT("ssE", [128, 1], F32)
            pG = [PP(f"pG{i}", [128, GN], F32) for i in range(2)]
            pU = [PP(f"pU{i}", [128, GN], F32) for i in range(2)]
            pY = [PP(f"pY{i}", [128, 512], F32) for i in range(2)]
            for g_ in range(N // GN):
                q0 = g_ * GN
                hh, hhk = h2g.next()
                S.dma('sp', hh[:], H2T[:, :, q0:q0 + GN], (), [hhk])
                for c_ in range(NCF):
                    pg, pgk = pG[c_ % 2], f'pG{c_ % 2}'
                    pu, puk = pU[c_ % 2], f'pU{c_ % 2}'
                    for k in range(8):
                        S.mm(pg[:], Wg[:, k, c_ * 128:(c_ + 1) * 128], hh[:, k, :], k == 0, k == 7, ['Wg', hhk], [pgk])
                    for k in range(8):
                        S.mm(pu[:], Wu[:, k, c_ * 128:(c_ + 1) * 128], hh[:, k, :], k == 0, k == 7, ['Wu', hhk], [puk])
                    sg, sgk = sgr.next()
                    S.act(sg[:], pg[:], AF.Silu, [pgk], [sgk])
                    S.tt('dve', aT[:, c_, :], sg[:], pu[:], ALU.mult, [sgk, puk], ['aT'])
                for sb in range(GN // 128):
                    r0 = q0 + sb * 128
                    x1, x1k = x1r.next(); tm, tmk = tmr.next()
                    S.dma('sp!', x1[:], X1[r0:r0 + 128, :], (), [x1k])
                    for hf in range(2):
                        for c_ in range(NCF):
                            S.mm(pY[hf][:], aT[:, c_, sb * 128:(sb + 1) * 128], Wd[:, c_, hf * 512:(hf + 1) * 512],
                                 c_ == 0, c_ == NCF - 1, ['aT', 'Wd'], [f'pY{hf}'])
                        S.tt('dve', tm[:, hf * 512:(hf + 1) * 512], pY[hf][:], mE[:, 0, hf * 512:(hf + 1) * 512], ALU.mult,
                             [f'pY{hf}', 'mE'], [tmk])
                    S.tt('pool', x1[:], tm[:], x1[:], ALU.add, [tmk, x1k], [x1k])
                    S.memset('dve', ssE[:], 0.0, ['ssE'])
                    S.act(tm[:], x1[:], AF.Square, [x1k, tmk], [tmk, 'ssE'], accum=ssE[:])
                    S.act(ssE[:], ssE[:], AF.Sqrt, ['ssE'], ['ssE'], bias=EPS, scale=1.0 / D)
                    S.op('dve', lambda e: e.reciprocal(out=ssE[:], in_=ssE[:]), ['ssE'], ['ssE'])
                    S.stt('dve', tm[:], x1[:], ssE[:, 0:1], mE[:, 1, :], ALU.mult, ALU.mult, [x1k, 'ssE', 'mE'], [tmk])
                    S.dma('sp', out[r0:r0 + 128, :], tm[:], [tmk], ())
            S.emit()
    return nc


_NC_CACHE = {}


def kernel(**inputs):
    if 'nc' not in _NC_CACHE:
        _NC_CACHE['nc'] = build_nc()
    nc = _NC_CACHE['nc']
    f = lambda a: np.ascontiguousarray(np.asarray(a, dtype=np.float32))
    shared = {
        "c_ctx": f(inputs["c_ctx"]), "w_mod": f(inputs["w_mod"][0]), "b_mod": f(inputs["b_mod"][0]),
        "g_norm_mix": f(inputs["g_norm_mix"][0]), "g_norm_ffn": f(inputs["g_norm_ffn"][0]),
        "w_in": f(inputs["w_in"][0]), "g_q_norm": f(inputs["g_q_norm"][0]), "w_uq": f(inputs["w_uq"][0]),
        "g_kv_norm": f(inputs["g_kv_norm"][0]), "w_ukv": f(inputs["w_ukv"][0]),
        "lb_fwd": f(inputs["lb_fwd"]), "lb_bwd": f(inputs["lb_bwd"]), "g_hgrn_norm": f(inputs["g_hgrn_norm"][0]),
        "w_out": f(inputs["w_out"][0]), "w_gate": f(inputs["w_gate"][0]), "w_up": f(inputs["w_up"][0]),
        "w_down": f(inputs["w_down"][0]), "g_final": f(inputs["g_final"]),
    }
    xs, cs, ctxs = f(inputs["x"]), f(inputs["c"]), f(inputs["ctx"])
    in_maps = []
    for b in range(NB):
        m = dict(shared)
        m["x"] = xs[b]; m["c"] = cs[b]; m["ctx"] = ctxs[b]
        in_maps.append(m)
    res = run_bass_kernel_spmd(nc, in_maps, core_ids=list(range(NB)))
    return np.stack([np.asarray(r["out"], dtype=np.float32) for r in res.results], axis=0)
```
